# Optimizing a Trainium2 kernel written in Bass

```python
import math
import jax
import jax.numpy as jnp
from jax import lax
import numpy as np

D_MODEL = 1024
BATCH = 8
SEQ = 2048
DEPTH = 4
DEC_BATCH = 128
DEC_SEQ = 8
PAST_LEN = 16384
PAGE_SIZE = 128

N_META = 16
N_MIXERS = 3
CHUNK = 64
ALPHA = (2 * DEPTH) ** 0.25
BETA = (8 * DEPTH) ** -0.25
LN_EPS = 1e-5
D_FF = ((8 * D_MODEL // 3 + 127) // 128) * 128
HG_EXPAND = 128
HG_HEADS = D_MODEL // HG_EXPAND
HG_DK = HG_EXPAND
HG_DV = D_MODEL // HG_HEADS
RET_HEADS = 4
RET_DK = D_MODEL // RET_HEADS
RET_DV = 2 * D_MODEL // RET_HEADS
ROPE_BASE = 10000.0
M_DI = 2 * D_MODEL
M_HEADDIM = 64
M_HEADS = M_DI // M_HEADDIM
M_GROUPS = 8
M_DSTATE = 128
M_CONV = 4
M_CONV_DIM = M_DI + 2 * M_GROUPS * M_DSTATE
N_HG_LAYERS = len(range(0, DEPTH, N_MIXERS))
N_RET_LAYERS = len(range(1, DEPTH, N_MIXERS))
N_SSM_LAYERS = len(range(2, DEPTH, N_MIXERS))

kernel_name = 'hybrid_hgrn2_retnet_mamba2_macaron_deepnorm_step'

F32 = jnp.float32


def layer_norm(x, g, b):
    xf = x.astype(F32)
    mu = jnp.mean(xf, axis=-1, keepdims=True)
    var = jnp.mean(jnp.square(xf - mu), axis=-1, keepdims=True)
    return ((xf - mu) * lax.rsqrt(var + LN_EPS) * g.astype(F32) + b.astype(F32)).astype(x.dtype)


def head_layer_norm(o, g):
    mu = jnp.mean(o, axis=-1, keepdims=True)
    var = jnp.mean(jnp.square(o - mu), axis=-1, keepdims=True)
    return (o - mu) * lax.rsqrt(var + LN_EPS) * g.astype(F32).reshape(o.shape[2:])


def head_rms_norm(o, g):
    ms = jnp.mean(jnp.square(o), axis=-1, keepdims=True)
    return o * lax.rsqrt(ms + LN_EPS) * g.astype(F32).reshape(o.shape[2:])


def swiglu(x, w_gate, w_up, w_down):
    return (jax.nn.silu(x @ w_gate) * (x @ w_up)) @ w_down


def rotary(x, pos):
    half = x.shape[-1] // 2
    inv_freq = ROPE_BASE ** (-jnp.arange(half, dtype=F32) / half)
    ang = pos.astype(F32)[:, None] * inv_freq[None, :]
    cos = jnp.cos(ang)[None, :, None, :]
    sin = jnp.sin(ang)[None, :, None, :]
    xf = x.astype(F32)
    x1, x2 = xf[..., :half], xf[..., half:]
    return jnp.concatenate([x1 * cos - x2 * sin, x1 * sin + x2 * cos], axis=-1).astype(x.dtype)


def _chunk_scan(q, k, v, log_a, s0):
    bsz, t = q.shape[:2]
    c = math.gcd(t, CHUNK)
    n = t // c

    def blocks(a):
        a = a.astype(F32)
        return a.reshape((bsz, n, c) + a.shape[2:]).swapaxes(0, 1)

    mask = jnp.tril(jnp.ones((c, c), dtype=bool))
    vector_decay = log_a.shape[-1] > 1

    def step(s, blk):
        qc, kc, vc, gc = blk
        g = jnp.cumsum(gc, axis=1)
        if vector_decay:
            diff = g[:, :, None] - g[:, None]
            dec = jnp.exp(jnp.where(mask[None, :, :, None, None], diff, -jnp.inf))
            att = jnp.einsum('bihk,bijhk,bjhk->bijh', qc, dec, kc)
        else:
            gs = g[..., 0]
            diff = gs[:, :, None] - gs[:, None]
            dec = jnp.exp(jnp.where(mask[None, :, :, None], diff, -jnp.inf))
            att = jnp.einsum('bihk,bjhk->bijh', qc, kc) * dec
        o = jnp.einsum('bijh,bjhv->bihv', att, vc) + jnp.einsum('bihk,bhkv->bihv', qc * jnp.exp(g), s)
        g_last = g[:, -1]
        k_dec = kc * jnp.exp(g_last[:, None] - g)
        s = jnp.exp(g_last)[..., None] * s + jnp.einsum('bjhk,bjhv->bhkv', k_dec, vc)
        return s, o

    s, o = lax.scan(step, s0.astype(F32), (blocks(q), blocks(k), blocks(v), blocks(log_a)))
    return o.swapaxes(0, 1).reshape((bsz, t) + o.shape[3:]), s


def recur(q, k, v, log_a, s0, n_lead):
    t = q.shape[1]
    outs = []
    s = s0
    for lo, hi in ((0, n_lead), (n_lead, t)):
        if hi > lo:
            o, s = _chunk_scan(q[:, lo:hi], k[:, lo:hi], v[:, lo:hi], log_a[:, lo:hi], s)
            outs.append(o)
    return jnp.concatenate(outs, axis=1), s


def causal_depthwise_conv(xcat, w, b):
    y = lax.conv_general_dilated(
        xcat, w.astype(xcat.dtype)[:, None, :], window_strides=(1,), padding='VALID',
        dimension_numbers=('NWC', 'WIO', 'NWC'), feature_group_count=xcat.shape[-1])
    return y + b.astype(xcat.dtype)


def hgrn2_mixer(x, s0, w_in, norm_g, w_o, lb, n_lead):
    bsz, t, _ = x.shape
    fw = HG_HEADS * HG_DK
    vw = HG_HEADS * HG_DV
    proj = x @ w_in
    q = jax.nn.silu(proj[..., :fw]).reshape(bsz, t, HG_HEADS, HG_DK)
    z = proj[..., fw:2 * fw].astype(F32)
    inp = proj[..., 2 * fw:2 * fw + vw].reshape(bsz, t, HG_HEADS, HG_DV)
    gate = proj[..., 2 * fw + vw:]
    f = lb + (1.0 - lb) * jax.nn.sigmoid(z)
    k = ((1.0 - lb) * jax.nn.sigmoid(-z)).reshape(bsz, t, HG_HEADS, HG_DK)
    log_f = jnp.log(f).reshape(bsz, t, HG_HEADS, HG_DK)
    o, s = recur(q, k, inp, log_f, s0, n_lead)
    o = head_rms_norm(o, norm_g).reshape(bsz, t, vw).astype(x.dtype) * jax.nn.silu(gate)
    return o @ w_o, s


def retention_mixer(x, s0, w_in, norm_g, w_o, pos, n_lead):
    bsz, t, _ = x.shape
    qk = RET_HEADS * RET_DK
    vw = RET_HEADS * RET_DV
    proj = x @ w_in
    q = rotary(proj[..., :qk].reshape(bsz, t, RET_HEADS, RET_DK), pos)
    k = rotary(proj[..., qk:2 * qk].reshape(bsz, t, RET_HEADS, RET_DK), pos) * (RET_DK ** -0.5)
    v = proj[..., 2 * qk:2 * qk + vw].reshape(bsz, t, RET_HEADS, RET_DV)
    gate = proj[..., 2 * qk + vw:]
    log_gamma = jnp.log(1.0 - 2.0 ** (-5.0 - jnp.arange(RET_HEADS, dtype=F32)))
    log_a = jnp.broadcast_to(log_gamma[None, None, :, None], (bsz, t, RET_HEADS, 1))
    o, s = recur(q, k, v, log_a, s0, n_lead)
    o = head_layer_norm(o, norm_g).reshape(bsz, t, vw).astype(x.dtype) * jax.nn.silu(gate)
    return o @ w_o, s


def mamba2_mixer(x, s0, conv0, w_in, conv_w, conv_b, dt_bias, a_log, d_skip, norm_g, w_o, n_lead):
    bsz, t, _ = x.shape
    proj = x @ w_in
    z = proj[..., :M_DI]
    xbc = proj[..., M_DI:M_DI + M_CONV_DIM]
    dt_raw = proj[..., M_DI + M_CONV_DIM:]
    xcat = jnp.concatenate([conv0.astype(xbc.dtype), xbc], axis=1)
    new_conv = xcat[:, -(M_CONV - 1):]
    xbc = jax.nn.silu(causal_depthwise_conv(xcat, conv_w, conv_b))
    gn = M_GROUPS * M_DSTATE
    rep = M_HEADS // M_GROUPS
    xs = xbc[..., :M_DI].reshape(bsz, t, M_HEADS, M_HEADDIM)
    bm = jnp.repeat(xbc[..., M_DI:M_DI + gn].reshape(bsz, t, M_GROUPS, M_DSTATE), rep, axis=2)
    cm = jnp.repeat(xbc[..., M_DI + gn:].reshape(bsz, t, M_GROUPS, M_DSTATE), rep, axis=2)
    dt = jax.nn.softplus(dt_raw.astype(F32) + dt_bias.astype(F32))
    log_a = (dt * -jnp.exp(a_log.astype(F32)))[..., None]
    xf = xs.astype(F32)
    o, s = recur(cm, bm, xf * dt[..., None], log_a, s0, n_lead)
    y = o + d_skip.astype(F32)[:, None] * xf
    y = y.reshape(bsz, t, M_DI) * jax.nn.silu(z.astype(F32))
    yg = y.reshape(bsz, t, M_GROUPS, M_DI // M_GROUPS)
    yg = yg * lax.rsqrt(jnp.mean(jnp.square(yg), axis=-1, keepdims=True) + LN_EPS)
    y = (yg.reshape(bsz, t, M_DI) * norm_g.astype(F32)).astype(x.dtype)
    return y @ w_o, s, new_conv


def run_trunk(h, st_hg, st_ret, st_ssm, st_conv, pos, n_lead, p):
    lb_all = jnp.cumsum(jax.nn.softmax(p['hg_lb_logits'].astype(F32), axis=0), axis=0)
    lb_all = lb_all - lb_all[0]
    new_hg, new_ret, new_ssm, new_conv = [], [], [], []
    for i in range(DEPTH):
        h = layer_norm(ALPHA * h + 0.5 * swiglu(h, p['ffn_w_gate'][i, 0], p['ffn_w_up'][i, 0], p['ffn_w_down'][i, 0]),
                       p['ln_g'][i, 0], p['ln_b'][i, 0])
        kind, j = i % N_MIXERS, i // N_MIXERS
        if kind == 0:
            m, s = hgrn2_mixer(h, st_hg[j], p['hg_w_in'][j], p['hg_norm_g'][j], p['hg_w_o'][j], lb_all[i], n_lead)
            new_hg.append(s)
        elif kind == 1:
            m, s = retention_mixer(h, st_ret[j], p['ret_w_in'][j], p['ret_norm_g'][j], p['ret_w_o'][j], pos, n_lead)
            new_ret.append(s)
        else:
            m, s, c = mamba2_mixer(h, st_ssm[j], st_conv[j], p['m_w_in'][j], p['m_conv_w'][j], p['m_conv_b'][j],
                                   p['m_dt_bias'][j], p['m_a_log'][j], p['m_d'][j], p['m_norm_g'][j], p['m_w_o'][j], n_lead)
            new_ssm.append(s)
            new_conv.append(c)
        h = layer_norm(ALPHA * h + m, p['ln_g'][i, 1], p['ln_b'][i, 1])
        h = layer_norm(ALPHA * h + 0.5 * swiglu(h, p['ffn_w_gate'][i, 1], p['ffn_w_up'][i, 1], p['ffn_w_down'][i, 1]),
                       p['ln_g'][i, 2], p['ln_b'][i, 2])
    return h, jnp.stack(new_hg), jnp.stack(new_ret), jnp.stack(new_ssm), jnp.stack(new_conv)


def setup_inputs(seed: int = 0) -> dict:
    key = jax.random.key(seed)
    ks = jax.random.split(key, 32)

    def nrm(k, shape, scale):
        return jax.random.normal(k, shape, F32) * scale

    fw = HG_HEADS * HG_DK
    hvw = HG_HEADS * HG_DV
    hg_cols = 2 * fw + 2 * hvw
    hg_scale = jnp.ones((hg_cols,), F32).at[2 * fw:2 * fw + hvw].set(BETA) * D_MODEL ** -0.5
    qk = RET_HEADS * RET_DK
    rvw = RET_HEADS * RET_DV
    ret_cols = 2 * qk + 2 * rvw
    ret_scale = jnp.ones((ret_cols,), F32).at[2 * qk:2 * qk + rvw].set(BETA) * D_MODEL ** -0.5
    m_cols = M_DI + M_CONV_DIM + M_HEADS
    dt0 = jnp.exp(jax.random.uniform(ks[22], (N_SSM_LAYERS, M_HEADS), F32)
                  * (math.log(0.1) - math.log(0.001)) + math.log(0.001))
    return {
        'x_prompt': nrm(ks[0], (BATCH, SEQ, D_MODEL), 1.0),
        'x_sample': nrm(ks[1], (DEC_BATCH, DEC_SEQ, D_MODEL), 1.0),
        'state_hgrn': nrm(ks[2], (N_HG_LAYERS, DEC_BATCH, HG_HEADS, HG_DK, HG_DV), 0.5),
        'state_ret': nrm(ks[3], (N_RET_LAYERS, DEC_BATCH, RET_HEADS, RET_DK, RET_DV), 1.0),
        'state_ssm': nrm(ks[4], (N_SSM_LAYERS, DEC_BATCH, M_HEADS, M_DSTATE, M_HEADDIM), 0.5),
        'state_conv': nrm(ks[5], (N_SSM_LAYERS, DEC_BATCH, M_CONV - 1, M_CONV_DIM), 1.0),
        'meta_tokens': nrm(ks[6], (N_META, D_MODEL), 1.0),
        'ln_g': 1.0 + nrm(ks[7], (DEPTH, 3, D_MODEL), 0.02),
        'ln_b': nrm(ks[8], (DEPTH, 3, D_MODEL), 0.02),
        'ffn_w_gate': nrm(ks[9], (DEPTH, 2, D_MODEL, D_FF), D_MODEL ** -0.5),
        'ffn_w_up': nrm(ks[10], (DEPTH, 2, D_MODEL, D_FF), D_MODEL ** -0.5),
        'ffn_w_down': nrm(ks[11], (DEPTH, 2, D_FF, D_MODEL), BETA * D_FF ** -0.5),
        'hg_lb_logits': nrm(ks[12], (DEPTH, fw), 1.0),
        'hg_w_in': nrm(ks[13], (N_HG_LAYERS, D_MODEL, hg_cols), 1.0) * hg_scale,
        'hg_norm_g': 1.0 + nrm(ks[14], (N_HG_LAYERS, hvw), 0.02),
        'hg_w_o': nrm(ks[15], (N_HG_LAYERS, hvw, D_MODEL), BETA * hvw ** -0.5),
        'ret_w_in': nrm(ks[16], (N_RET_LAYERS, D_MODEL, ret_cols), 1.0) * ret_scale,
        'ret_norm_g': 1.0 + nrm(ks[17], (N_RET_LAYERS, rvw), 0.02),
        'ret_w_o': nrm(ks[18], (N_RET_LAYERS, rvw, D_MODEL), BETA * rvw ** -0.5),
        'm_w_in': nrm(ks[19], (N_SSM_LAYERS, D_MODEL, m_cols), D_MODEL ** -0.5),
        'm_conv_w': nrm(ks[20], (N_SSM_LAYERS, M_CONV, M_CONV_DIM), M_CONV ** -0.5),
        'm_conv_b': nrm(ks[21], (N_SSM_LAYERS, M_CONV_DIM), 0.02),
        'm_dt_bias': dt0 + jnp.log(-jnp.expm1(-dt0)),
        'm_a_log': jnp.log(jax.random.uniform(ks[23], (N_SSM_LAYERS, M_HEADS), F32, minval=1.0, maxval=16.0)),
        'm_d': 1.0 + nrm(ks[24], (N_SSM_LAYERS, M_HEADS), 0.1),
        'm_norm_g': 1.0 + nrm(ks[25], (N_SSM_LAYERS, M_DI), 0.02),
        'm_w_o': nrm(ks[26], (N_SSM_LAYERS, M_DI, D_MODEL), BETA * M_DI ** -0.5),
    }


def reference(x_prompt, x_sample, state_hgrn, state_ret, state_ssm, state_conv, meta_tokens, ln_g, ln_b,
              ffn_w_gate, ffn_w_up, ffn_w_down, hg_lb_logits, hg_w_in, hg_norm_g, hg_w_o,
              ret_w_in, ret_norm_g, ret_w_o, m_w_in, m_conv_w, m_conv_b, m_dt_bias, m_a_log, m_d,
              m_norm_g, m_w_o):
    p = {
        'ln_g': ln_g, 'ln_b': ln_b, 'ffn_w_gate': ffn_w_gate, 'ffn_w_up': ffn_w_up, 'ffn_w_down': ffn_w_down,
        'hg_lb_logits': hg_lb_logits, 'hg_w_in': hg_w_in, 'hg_norm_g': hg_norm_g, 'hg_w_o': hg_w_o,
        'ret_w_in': ret_w_in, 'ret_norm_g': ret_norm_g, 'ret_w_o': ret_w_o,
        'm_w_in': m_w_in, 'm_conv_w': m_conv_w, 'm_conv_b': m_conv_b, 'm_dt_bias': m_dt_bias,
        'm_a_log': m_a_log, 'm_d': m_d, 'm_norm_g': m_norm_g, 'm_w_o': m_w_o,
    }
    bp, sp, _ = x_prompt.shape
    h_p = jnp.concatenate(
        [jnp.broadcast_to(meta_tokens[None].astype(x_prompt.dtype), (bp, N_META, D_MODEL)), x_prompt], axis=1)
    pos_p = jnp.arange(N_META + sp)
    z_hg = jnp.zeros((N_HG_LAYERS, bp, HG_HEADS, HG_DK, HG_DV), F32)
    z_ret = jnp.zeros((N_RET_LAYERS, bp, RET_HEADS, RET_DK, RET_DV), F32)
    z_ssm = jnp.zeros((N_SSM_LAYERS, bp, M_HEADS, M_DSTATE, M_HEADDIM), F32)
    z_conv = jnp.zeros((N_SSM_LAYERS, bp, M_CONV - 1, M_CONV_DIM), x_prompt.dtype)
    h_p, hgrn_prompt, ret_prompt, ssm_prompt, conv_prompt = run_trunk(
        h_p, z_hg, z_ret, z_ssm, z_conv, pos_p, N_META, p)
    y_prompt = h_p[:, N_META:]
    pos_s = PAST_LEN + jnp.arange(x_sample.shape[1])
    y_sample, hgrn_sample, ret_sample, ssm_sample, conv_sample = run_trunk(
        x_sample, state_hgrn, state_ret, state_ssm, state_conv, pos_s, 0, p)
    return (y_prompt, y_sample, hgrn_prompt, hgrn_sample, ret_prompt, ret_sample,
            ssm_prompt, ssm_sample, conv_prompt, conv_sample)
```

```python
import math
import numpy as np
import ml_dtypes
import concourse.bass as bass
import concourse.mybir as mybir
from concourse.bass_utils import run_bass_kernel_spmd

F32 = mybir.dt.float32
BF16 = mybir.dt.bfloat16
AF = mybir.ActivationFunctionType
ALU = mybir.AluOpType

NCORES = 8
D = 1024
DFF = 2816
DEPTH = 4
T = 2192
NTILES = [(0, 144)] + [(144 + 512 * i, 512) for i in range(4)]
ALPHA = (2 * DEPTH) ** 0.25
EPS = 1e-5
TYPE_S, TYPE_M, TYPE_P = 0, 1, 2


def token_tiles(nt):
    if nt == 0:
        return [(TYPE_S, 0, 128, 16, 8), (TYPE_M, 128, 16, 1, 16)]
    c0 = NTILES[nt][0]
    return [(TYPE_P, c0 + 128 * r, 128, 2, 64) for r in range(4)]


C_ID = 0
C_MASK = 128
C_NEG = C_MASK + 384
C_LTRI = C_NEG + 384
C_BLK = C_LTRI + 384
C_RM = C_BLK + 48
C_SEL = C_RM + 656
C_END = C_SEL + 48

V_LNG, V_LNB, V_LB, V_HGN, V_RETN, V_CW, V_CB, V_MN = 0, 12, 24, 28, 30, 32, 48, 52
V_ROWS = 54


def host_consts():
    c = np.zeros((128, C_END), np.float32)
    c[:, C_ID:C_ID + 128] = np.eye(128, dtype=np.float32)
    j = np.arange(128)[:, None]
    i = np.arange(128)[None, :]
    for ty, L, n in ((TYPE_S, 8, 128), (TYPE_M, 16, 16), (TYPE_P, 64, 128)):
        same = (j // L == i // L) & (j < n) & (i < n)
        m = (same & (j <= i)).astype(np.float32)
        c[:, C_MASK + 128 * ty:C_MASK + 128 * ty + 128] = m
        c[:, C_NEG + 128 * ty:C_NEG + 128 * ty + 128] = (m - 1.0) * 30000.0
        c[:, C_LTRI + 128 * ty:C_LTRI + 128 * ty + 128] = same.astype(np.float32)
        s = np.arange(16)[None, :]
        c[:, C_BLK + 16 * ty:C_BLK + 16 * ty + 16] = ((j // L == s) & (j < n)).astype(np.float32)
    rm = np.ones(656, np.float32)
    rm[0:128:8] = 0.0
    rm[128] = 0.0
    rm[144::64] = 0.0
    c[:, C_RM:C_RM + 656] = rm[None, :]
    for s_ in range(16):
        for r_ in range(3):
            c[s_ * 8 + 5 + r_, C_SEL + s_ * 3 + r_] = 1.0
    half = 128
    inv_freq = (np.float32(10000.0) ** (-np.arange(half, dtype=np.float32) / np.float32(half))).astype(np.float32)
    pos = np.zeros(T, np.float32)
    pos[0:128] = 16384 + (np.arange(128) % 8)
    pos[128:144] = np.arange(16)
    pos[144:] = 16 + np.arange(2048)
    ang = (pos[None, :].astype(np.float32) * inv_freq[:, None]).astype(np.float32)
    cs = np.stack([np.cos(ang), np.sin(ang)], axis=1).astype(np.float32)
    pidx = np.zeros(208, np.float64)
    pidx[0:128] = np.arange(128) % 8
    pidx[128:144] = np.arange(16)
    pidx[144:208] = np.arange(64)
    ret = np.zeros((128, 4, 2, 208), np.float32)
    gam = []
    for h in range(4):
        lg = math.log(1.0 - 2.0 ** (-5.0 - h))
        gam.append(lg)
        g = (pidx + 1.0) * lg
        ret[:, h, 0, :] = np.exp(g)[None, :]
        ret[:, h, 1, :] = (np.exp(-g) / 16.0)[None, :]
    return c, cs, ret, gam


class Res:
    __slots__ = ("name", "w", "rd", "dsem", "dcnt", "excl")

    def __init__(self, name, excl=False):
        self.name = name
        self.excl = excl
        self.w = None
        self.rd = {}
        self.dsem = None
        self.dcnt = 0


class Eng:
    def __init__(self, name, eng, sem):
        self.name, self.eng, self.sem = name, eng, sem
        self.cnt = 0
        self.seen = {}


class Builder:
    def __init__(self):
        nc = bass.Bass("TRN2", target_bir_lowering=False)
        self.nc = nc
        self.pe = Eng("pe", nc.tensor, nc.semaphore("s_pe").__enter__())
        self.dve = Eng("dve", nc.vector, nc.semaphore("s_dve").__enter__())
        self.act = Eng("act", nc.scalar, nc.semaphore("s_act").__enter__())
        self.pool = Eng("pool", nc.gpsimd, nc.semaphore("s_pool").__enter__())
        self.sp = Eng("sp", nc.sync, None)
        self.compute = [self.pe, self.dve, self.act, self.pool]
        self.nsem = 4
        self.out_waits = []
        self.dma_tags = {}
        self.semcache = {}
        self.arena = nc.alloc_sbuf_tensor("arena", [128, 53000], F32)
        self.aoff = 0
        self.psum = nc.alloc_psum_tensor("psum", [128, 4096], F32)
        self.bank = [Res("bank%d" % b, excl=True) for b in range(8)]

    def alloc(self, nfloats):
        o = self.aoff
        self.aoff += nfloats
        assert self.aoff <= 53000, self.aoff
        return o

    def view(self, off, shape, dt=F32):
        n = int(np.prod(shape))
        if dt == F32:
            a = self.arena[:, off:off + n]
        else:
            a = self.arena[:, off:off + (n + 1) // 2].bitcast(BF16)
        if len(shape) == 2:
            return a.rearrange("p (a b) -> p a b", b=shape[1])
        if len(shape) == 3:
            return a.rearrange("p (a b c) -> p a b c", b=shape[1], c=shape[2])
        return a

    def pb(self, b, n=512, p0=0, p1=128, c0=0):
        return self.psum[p0:p1, b * 512 + c0:b * 512 + c0 + n]

    def _wait(self, E, tag):
        key, sem, val = tag
        if E is self.pe and key == "pe":
            return
        if E.seen.get(key, 0) >= val:
            return
        E.eng.wait_ge(sem, val)
        E.seen[key] = val

    def _deps(self, E, reads, writes):
        for r in reads:
            if r.w is not None:
                self._wait(E, r.w)
            if r.excl:
                for key, tag in list(r.rd.items()):
                    if key != E.name:
                        self._wait(E, tag)
        for w in writes:
            if w.w is not None:
                self._wait(E, w.w)
            for tag in list(w.rd.values()):
                self._wait(E, tag)

    def op(self, E, emit, r=(), w=(), sig=True):
        self._deps(E, r, w)
        ins = emit()
        if sig:
            E.cnt += 1
            ins.then_inc(E.sem, 1)
            tag = (E.name, E.sem, E.cnt)
        else:
            tag = (E.name, E.sem, E.cnt + 1)
        for x in r:
            o = x.rd.get(E.name)
            if o is None or o[2] < tag[2]:
                x.rd[E.name] = tag
        for x in w:
            x.w = tag
            x.rd = {}
        return ins

    def _dsem(self, res):
        if res.dsem is None:
            ent = self.semcache.get(res.name)
            if ent is None:
                ent = [self.nc.semaphore("d_" + res.name).__enter__(), 0]
                self.semcache[res.name] = ent
                self.nsem += 1
            res.dsem = ent[0]
            res.dcnt = ent[1]
        return res.dsem

    def dma_in(self, Q, out_ap, in_ap, res, **kw):
        self._deps(Q, (), (res,))
        sem = self._dsem(res)
        Q.eng.dma_start(out=out_ap, in_=in_ap, **kw).then_inc(sem, 16)
        res.dcnt += 16
        self.semcache[res.name][1] = res.dcnt
        res.w = ("d_" + res.name, sem, res.dcnt)
        res.rd = {}
        self.dma_tags[res.w[0]] = res.w

    def dma_out(self, Q, out_ap, in_ap, res, final=True, **kw):
        self._deps(Q, (res,), ())
        sem = self._dsem(res)
        Q.eng.dma_start(out=out_ap, in_=in_ap, **kw).then_inc(sem, 16)
        res.dcnt += 16
        self.semcache[res.name][1] = res.dcnt
        tag = ("d_" + res.name, sem, res.dcnt)
        res.rd["dma"] = tag
        self.out_waits.append(tag)
        self.dma_tags[tag[0]] = tag

    def barrier(self, touch=()):
        for E in self.compute + [self.sp]:
            for F in self.compute:
                if E is not F and F.cnt > 0:
                    self._wait(E, (F.name, F.sem, F.cnt))
            for tag in self.dma_tags.values():
                self._wait(E, tag)
        for res in touch:
            for F in self.compute:
                if F.cnt > 0:
                    res.rd[F.name] = (F.name, F.sem, F.cnt)
            for tag in self.dma_tags.values():
                res.rd[tag[0]] = tag
        self.dma_tags = {}

    def finish(self):
        for E in self.compute:
            if E.cnt > 0:
                self._wait(self.sp, (E.name, E.sem, E.cnt))
        last = {}
        for key, sem, val in self.out_waits:
            if key not in last or last[key][1] < val:
                last[key] = (sem, val)
        for key, (sem, val) in last.items():
            self.sp.eng.wait_ge(sem, val)
            self.act.eng.wait_ge(sem, val)


def bcast_mid(ap, n):
    pat = [list(x) for x in ap.ap]
    return bass.AP(ap.tensor, ap.offset, [pat[0], [0, n]] + pat[1:])


def bcast_last(ap, n):
    pat = [list(x) for x in ap.ap]
    if len(pat) == 3:
        pat = pat[:2]
    return bass.AP(ap.tensor, ap.offset, pat + [[0, n]])


def build_program(stages=None):
    K = Builder()
    nc = K.nc
    pe, dve, act, pool, sp = K.pe, K.dve, K.act, K.pool, K.sp
    TT = nc.tensor
    V = nc.vector
    A = nc.scalar
    G = nc.gpsimd

    def din(name, shape, dt=F32):
        return nc.dram_tensor(name, list(shape), dt, kind="ExternalInput").ap()

    def dout(name, shape):
        return nc.dram_tensor(name, list(shape), F32, kind="ExternalOutput").ap()

    xin = din("xin", [T, D])
    c128_d = din("c128", [128, C_END])
    cs_d = din("cs", [128, 2, T])
    rett_d = din("rett", [128, 4, 2, 208])
    vecs_d = din("vecs", [64, D])
    mhead_d = din("mhead", [3, 32])
    wg_d = din("ffn_w_gate", [DEPTH, 2, D, DFF])
    wu_d = din("ffn_w_up", [DEPTH, 2, D, DFF])
    wd_d = din("ffn_w_down", [DEPTH, 2, DFF, D])
    hgwi_d = din("hg_w_in", [2, D, 4096])
    hgwo_d = din("hg_w_o", [2, D, D])
    rwi_d = din("ret_w_in", [1, D, 6144])
    rwo_d = din("ret_w_o", [1, 2048, D])
    mwi_d = din("m_w_in", [1, D, 6176])
    mwo_d = din("m_w_o", [1, 2048, D])
    sthg_d = din("st_hg", [2, 16, 8, 128, 128])
    stret_d = din("st_ret", [1, 16, 4, 256, 512])
    stssm_d = din("st_ssm", [1, 16, 32, 128, 64])
    stconv_d = din("st_conv", [1, 16, 3, 4096])

    y_d = dout("y", [T, D])
    ohgp_d = dout("o_hg_p", [2, 8, 128, 128])
    ohgs_d = dout("o_hg_s", [2, 16, 8, 128, 128])
    oretp_d = dout("o_ret_p", [1, 4, 256, 512])
    orets_d = dout("o_ret_s", [1, 16, 4, 256, 512])
    ossmp_d = dout("o_ssm_p", [1, 32, 128, 64])
    ossms_d = dout("o_ssm_s", [1, 16, 32, 128, 64])
    oconvp_d = dout("o_conv_p", [1, 3, 4096])
    oconvs_d = dout("o_conv_s", [1, 16, 3, 4096])

    o_hf = K.alloc(8 * T)
    o_hb = K.alloc(8 * T // 2)
    o_c = K.alloc(C_END)
    o_vt = K.alloc(8 * 64)
    o_small = K.alloc(512)
    o_wq = [K.alloc(3072) for _ in range(4)]
    o_scr = K.alloc(0)
    SCR_END = 53000

    hf = K.view(o_hf, [8, T])
    hb = K.view(o_hb, [8, T], BF16)
    c128 = K.arena[:, o_c:o_c + C_END]
    vecT = K.view(o_vt, [8, 64])
    small = K.arena[:, o_small:o_small + 512]
    R_hf = [[Res("hf%d_%d" % (c, n)) for n in range(5)] for c in range(8)]
    R_hb = [Res("hb%d" % n) for n in range(5)]
    R_c = Res("consts")
    R_vt = Res("vecT")
    R_small = Res("small")
    R_wq = [Res("wq%d" % i) for i in range(4)]

    ident = c128[:, C_ID:C_ID + 128]

    def cmask(ty, n):
        return c128[0:n, C_MASK + 128 * ty:C_MASK + 128 * ty + n]

    def cneg(ty, n):
        return c128[0:n, C_NEG + 128 * ty:C_NEG + 128 * ty + n]

    def cltri(ty, n):
        return c128[0:n, C_LTRI + 128 * ty:C_LTRI + 128 * ty + n]

    def cblk(ty, n, nsub):
        return c128[0:n, C_BLK + 16 * ty:C_BLK + 16 * ty + nsub]

    ones_f = small[:, 0:128]
    ones_b = small[:, 128:192].bitcast(BF16)
    lbcol = small[:, 192:208].rearrange("p (j h) -> p j h", h=8)
    omlcol = small[:, 208:224].rearrange("p (j h) -> p j h", h=8)
    nomlcol = small[:, 224:240].rearrange("p (j h) -> p j h", h=8)
    mh_bc = small[:, 240:336].rearrange("p (r h) -> p r h", h=32)
    negA = small[:, 336:368]
    dcol = small[:, 368:384]
    lbtmp = small[:, 384:448]

    K.dma_in(sp, c128, c128_d, R_c)
    K.op(dve, lambda: V.memset(ones_f, 1.0), w=(R_small,))
    K.op(dve, lambda: V.memset(ones_b, 1.0), w=(R_small,))
    import os as _os
    if "nomh" not in _os.environ.get("KDBG", ""):
        K.dma_in(sp, mh_bc, bass.AP(mhead_d.tensor, 0, [[0, 128], [32, 3], [1, 32]]), R_small)
        with nc.allow_non_contiguous_dma(reason="tiny const"):
            K.dma_in(sp, dcol[0:64, :], bass.AP(mhead_d.tensor, 64, [[0, 64], [2, 16]]), R_small)
            K.dma_in(sp, dcol[64:128, :], bass.AP(mhead_d.tensor, 65, [[0, 64], [2, 16]]), R_small)

    class Scr:
        def __init__(self, extra=False):
            self.off = o_scr
            self.end = SCR_END
            self.extra = [o_wq[2], o_wq[2] + 6144] if extra else None

        def get(self, n):
            if self.off + n <= self.end:
                o = self.off
                self.off += n
                return o
            assert self.extra is not None and self.extra[0] + n <= self.extra[1], "scratch overflow"
            o = self.extra[0]
            self.extra[0] += n
            return o

    units = []
    wstate = {"next": 0}

    def emit_loads(upto):
        while wstate["next"] <= min(upto, len(units) - 1):
            u = units[wstate["next"]]
            for res_list, dst, src in u:
                K._deps(pool, (), res_list)
                sem = K._dsem(res_list[0])
                G.dma_start(out=dst, in_=src).then_inc(sem, 16)
                res_list[0].dcnt += 16
                K.semcache[res_list[0].name][1] = res_list[0].dcnt
                tag = ("d_" + res_list[0].name, sem, res_list[0].dcnt)
                for rr in res_list:
                    rr.w = tag
                    rr.rd = {}
            wstate["next"] += 1

    def wview(slot_floats_off, shape):
        return K.view(slot_floats_off, shape, BF16)

    plan = []
    for i in range(DEPTH):
        plan.append(("ffn", i, 0))
        kind = i % 3
        plan.append((("hg", i // 3), ("ret", 0), ("mam", 0))[kind])
        plan.append(("ffn", i, 1))

    ffn_groups = [(4 * g, 4) for g in range(5)] + [(20, 2)]
    unit_index = {}
    big = [0]

    def add_unit(key, entries):
        unit_index[key] = len(units)
        units.append(entries)

    for ph in plan:
        if ph[0] == "ffn":
            _, i, s = ph
            for gi, (j0, gn) in enumerate(ffn_groups):
                slot = big[0] % 2
                big[0] += 1
                base = o_wq[2 * slot]
                rl = [R_wq[2 * slot], R_wq[2 * slot + 1]]
                wg_v = wview(base, [8, 512])
                wu_v = wview(base + 2048, [8, 512])
                wd_v = wview(base + 4096, [4, 1024])
                ent = [
                    (rl, wg_v[:, :, 0:gn * 128], wg_d[i, s].rearrange("(k p) n -> p k n", p=128)[:, :, j0 * 128:(j0 + gn) * 128]),
                    (rl, wu_v[:, :, 0:gn * 128], wu_d[i, s].rearrange("(k p) n -> p k n", p=128)[:, :, j0 * 128:(j0 + gn) * 128]),
                    (rl, wd_v[:, 0:gn, :], wd_d[i, s].rearrange("(j p) n -> p j n", p=128)[:, j0:j0 + gn, :]),
                ]
                add_unit(("ffn", i, s, gi), ent)
            if big[0] % 2 == 1:
                pass
        else:
            msl = [0]

            def mslot():
                s_ = msl[0] % 2
                msl[0] += 1
                return o_wq[s_], [R_wq[s_]]

            if ph[0] == "hg":
                j = ph[1]
                wsrc = hgwi_d[j].rearrange("(k p) n -> p k n", p=128)
                for h in range(8):
                    base, rl = mslot()
                    wi_v = wview(base, [8, 512])
                    wo_v = wview(base + 2048, [1024])
                    ent = []
                    for b4 in range(4):
                        ent.append((rl, wi_v[:, :, b4 * 128:(b4 + 1) * 128], wsrc[:, :, b4 * 1024 + h * 128:b4 * 1024 + (h + 1) * 128]))
                    ent.append((rl, wo_v, hgwo_d[j, h * 128:(h + 1) * 128, :]))
                    add_unit(("hg", j, h), ent)
            elif ph[0] == "ret":
                wsrc = rwi_d[0].rearrange("(k p) n -> p k n", p=128)
                for h in range(4):
                    v_ = wview(o_wq[3], [8, 512])
                    add_unit(("ret", h, "qk"), [
                        ([R_wq[3]], v_[:, :, 0:256], wsrc[:, :, h * 256:(h + 1) * 256]),
                        ([R_wq[3]], v_[:, :, 256:512], wsrc[:, :, 1024 + h * 256:1024 + (h + 1) * 256])])
                    v_ = wview(o_wq[1], [8, 512])
                    add_unit(("ret", h, "v"), [([R_wq[1]], v_, wsrc[:, :, 2048 + h * 512:2048 + (h + 1) * 512])])
                    v_ = wview(o_wq[2], [8, 512])
                    add_unit(("ret", h, "g"), [([R_wq[2]], v_, wsrc[:, :, 4096 + h * 512:4096 + (h + 1) * 512])])
                    v_ = wview(o_wq[0], [4, 1024])
                    add_unit(("ret", h, "o"), [([R_wq[0]], v_, rwo_d[0, h * 512:(h + 1) * 512, :].rearrange("(c p) n -> p c n", p=128))])
            else:
                wsrc = mwi_d[0].rearrange("(k p) n -> p k n", p=128)
                base, rl = mslot()
                v_ = wview(base, [8, 32])
                add_unit(("mam", "dt"), [(rl, v_, wsrc[:, :, 6144:6176])])
                for g in range(8):
                    base, rl = mslot()
                    v_ = wview(base, [8, 768])
                    ent = [
                        (rl, v_[:, :, 0:256], wsrc[:, :, g * 256:(g + 1) * 256]),
                        (rl, v_[:, :, 256:512], wsrc[:, :, 2048 + g * 256:2048 + (g + 1) * 256]),
                        (rl, v_[:, :, 512:640], wsrc[:, :, 4096 + g * 128:4096 + (g + 1) * 128]),
                        (rl, v_[:, :, 640:768], wsrc[:, :, 5120 + g * 128:5120 + (g + 1) * 128]),
                    ]
                    add_unit(("mam", g, "in"), ent)
                    base, rl = mslot()
                    v_ = wview(base, [2, 1024])
                    add_unit(("mam", g, "o"), [(rl, v_, mwo_d[0, g * 256:(g + 1) * 256, :].rearrange("(c p) n -> p c n", p=128))])

    def use_unit(key):
        idx = unit_index[key]
        emit_loads(idx)
        return units[idx]

    def done_unit(key):
        emit_loads(unit_index[key] + 1)

    first_acc = {"v": True}

    def accumulate(m, nt, ps_ap, bank_res):
        c0, n = NTILES[nt]
        dst = hf[:, m, c0:c0 + n]
        if first_acc["v"]:
            K.op(dve, lambda: V.scalar_tensor_tensor(dst, dst, ALPHA, ps_ap, ALU.mult, ALU.add),
                 r=(bank_res, R_hf[m][nt]), w=(R_hf[m][nt],))
        else:
            K.op(dve, lambda: V.tensor_tensor(dst, dst, ps_ap, ALU.add),
                 r=(bank_res, R_hf[m][nt]), w=(R_hf[m][nt],))

    def load_vecs():
        scr = Scr()
        o = scr.get(1024)
        vtok = K.arena[0:64, o:o + 1024]
        R = Res("vecstage")
        K.dma_in(sp, vtok, vecs_d, R)
        for c in range(8):
            ps = K.pb(c % 4, 64)
            K.op(pe, lambda: TT.transpose(ps, vtok[:, c * 128:(c + 1) * 128], ident[0:64, 0:64]),
                 r=(R, R_c), w=(K.bank[c % 4],))
            K.op(act, lambda: A.copy(vecT[:, c, :], ps), r=(K.bank[c % 4],), w=(R_vt,))
        lg = vecT[:, :, V_LB:V_LB + 4]
        mx = lbtmp[:, 0:8]
        ex = lbtmp[:, 8:40].rearrange("p (h d) -> p h d", d=4)
        sm = lbtmp[:, 40:48]
        K.op(dve, lambda: V.tensor_reduce(mx, lg, mybir.AxisListType.X, ALU.max), r=(R_vt,), w=(R_small,))
        K.op(dve, lambda: V.tensor_tensor(ex, lg, bcast_last(mx, 4), ALU.subtract), r=(R_vt, R_small), w=(R_small,))
        K.op(act, lambda: A.activation(ex, ex, AF.Exp), r=(R_small,), w=(R_small,))
        K.op(dve, lambda: V.tensor_reduce(sm, ex, mybir.AxisListType.X, ALU.add), r=(R_small,), w=(R_small,))
        K.op(dve, lambda: V.reciprocal(sm, sm), r=(R_small,), w=(R_small,))
        K.op(dve, lambda: V.memset(lbcol[:, 0, :], 0.0), w=(R_small,))
        t3 = lbtmp[:, 48:56]
        K.op(dve, lambda: V.tensor_tensor(t3, ex[:, :, 1], ex[:, :, 2], ALU.add), r=(R_small,), w=(R_small,))
        K.op(dve, lambda: V.tensor_tensor(t3, t3, ex[:, :, 3], ALU.add), r=(R_small,), w=(R_small,))
        K.op(dve, lambda: V.tensor_tensor(lbcol[:, 1, :], t3, sm, ALU.mult), r=(R_small,), w=(R_small,))
        lb_all = small[:, 192:208]
        K.op(dve, lambda: V.tensor_scalar(small[:, 208:224], lb_all, -1.0, 1.0, ALU.mult, ALU.add), r=(R_small,), w=(R_small,))
        K.op(dve, lambda: V.tensor_scalar(small[:, 224:240], lb_all, 1.0, -1.0, ALU.mult, ALU.add), r=(R_small,), w=(R_small,))
        K.op(act, lambda: A.activation(negA, mh_bc[:, 1, :], AF.Exp), r=(R_small,), w=(R_small,))
        K.op(dve, lambda: V.tensor_scalar(negA, negA, -1.0, None, ALU.mult), r=(R_small,), w=(R_small,))

    def load_input():
        scr = Scr()
        xs = [K.arena[:, o:o + 1024] for o in (scr.get(1024), scr.get(1024))]
        Rx = [Res("xs0"), Res("xs1")]
        tiles = [(0, 128), (128, 16)] + [(144 + 128 * r, 128) for r in range(16)]
        import os
        if "nometa" in os.environ.get("KDBG", ""):
            tiles = [t_ for t_ in tiles if t_[1] == 128]
        if "ntiles" in os.environ.get("KDBG", ""):
            tiles = tiles[:int(os.environ["KNT"])]
        for ti, (c0, n) in enumerate(tiles):
            s = ti % 2
            K.dma_in(sp, xs[s][0:n, :], xin[c0:c0 + n, :], Rx[s])
            nt = 0 if c0 < 144 else 1 + (c0 - 144) // 512
            for half in range(2):
                b = (2 * ti + half) % 4
                ps = K.psum[:, b * 512:b * 512 + 512].rearrange("p (c n) -> p c n", c=4)
                for cc in range(4):
                    c = half * 4 + cc
                    K.op(pe, lambda: TT.transpose(ps[:, cc, 0:n], xs[s][0:n, c * 128:(c + 1) * 128], ident[0:n, 0:n]),
                         r=(Rx[s], R_c), w=(K.bank[b],), sig=(cc == 3))
                wr = [R_hf[half * 4 + cc][nt] for cc in range(4)]
                if "nohf" not in os.environ.get("KDBG", ""):
                    K.op(act, lambda: A.copy(hf[:, half * 4:half * 4 + 4, c0:c0 + n], ps[:, :, 0:n]), r=(K.bank[b],), w=wr)
                if "nohb" not in os.environ.get("KDBG", ""):
                    K.op(dve, lambda: V.tensor_copy(hb[:, half * 4:half * 4 + 4, c0:c0 + n], ps[:, :, 0:n]), r=(K.bank[b],), w=(R_hb[nt],))

    def store_output():
        scr = Scr()
        ys = [K.arena[:, o:o + 1024] for o in (scr.get(1024), scr.get(1024))]
        Ry = [Res("ys0"), Res("ys1")]
        tiles = [(0, 128), (128, 16)] + [(144 + 128 * r, 128) for r in range(16)]
        for ti, (c0, n) in enumerate(tiles):
            s = ti % 2
            nt = 0 if c0 < 144 else 1 + (c0 - 144) // 512
            for half in range(2):
                b = (2 * ti + half) % 4
                ps = K.psum[:, b * 512:b * 512 + 512]
                for cc in range(4):
                    c = half * 4 + cc
                    K.op(pe, lambda: TT.transpose(ps[0:n, cc * 128:(cc + 1) * 128], hf[:, c, c0:c0 + n], ident),
                         r=(R_hf[c][nt], R_c), w=(K.bank[b],), sig=(cc == 3))
                if half == 0:
                    K.op(act, lambda: A.copy(ys[s][0:n, 0:512], ps[0:n, :]), r=(K.bank[b],), w=(Ry[s],))
                else:
                    K.op(dve, lambda: V.tensor_copy(ys[s][0:n, 512:1024], ps[0:n, :]), r=(K.bank[b],), w=(Ry[s],))
            K.dma_out(sp, y_d[c0:c0 + n, :], ys[s][0:n, :], Ry[s])

    LN_OFF = o_scr + 3072
    ln_xb = K.view(LN_OFF, [8, 512], BF16)
    ln_sq = K.view(LN_OFF + 2048, [8, 512], BF16)
    ln_st = [K.view(LN_OFF + 4096, [4, 512]), K.view(LN_OFF + 6144, [4, 512])]
    LN_R = {"xb": Res("ln_xb"), "sq": Res("ln_sq"), "st": [Res("ln_st0"), Res("ln_st1")]}

    def ln_closures(li, lj):
        xb, sq = ln_xb, ln_sq
        Rxb, Rsq = LN_R["xb"], LN_R["sq"]
        row = li * 3 + lj

        def stats(nt):
            c0, n = NTILES[nt]
            st, Rst = ln_st[nt % 2], LN_R["st"][nt % 2]
            hft = hf[:, :, c0:c0 + n]
            rall = [R_hf[c][nt] for c in range(8)]
            K.op(pool, lambda: G.tensor_copy(xb[:, :, 0:n], hft), r=rall, w=(Rxb,))
            K.op(act, lambda: A.activation(sq[:, :, 0:n], hft, AF.Square), r=rall, w=(Rsq,))
            for c in range(8):
                K.op(pe, lambda: TT.matmul(K.pb(0, n), ones_b, xb[:, c, 0:n], start=(c == 0), stop=(c == 7)),
                     r=(Rxb, R_small), w=(K.bank[0],), sig=(c == 7))
            for c in range(8):
                K.op(pe, lambda: TT.matmul(K.pb(1, n), ones_b, sq[:, c, 0:n], start=(c == 0), stop=(c == 7)),
                     r=(Rsq, R_small), w=(K.bank[1],), sig=(c == 7))
            mean, var, rstd, nmr = (st[:, q, 0:n] for q in range(4))
            K.op(dve, lambda: V.tensor_scalar(mean, K.pb(0, n), 1.0 / D, None, ALU.mult), r=(K.bank[0],), w=(Rst,))
            K.op(dve, lambda: V.tensor_tensor(var, mean, mean, ALU.mult), r=(Rst,), w=(Rst,))
            K.op(dve, lambda: V.scalar_tensor_tensor(var, K.pb(1, n), 1.0 / D, var, ALU.mult, ALU.subtract), r=(K.bank[1], Rst), w=(Rst,))
            K.op(act, lambda: A.activation(rstd, var, AF.Ln, bias=EPS), r=(Rst,), w=(Rst,))
            K.op(act, lambda: A.activation(rstd, rstd, AF.Exp, scale=-0.5), r=(Rst,), w=(Rst,))
            K.op(dve, lambda: V.scalar_tensor_tensor(nmr, mean, -1.0, rstd, ALU.mult, ALU.mult), r=(Rst,), w=(Rst,))

        def apply(nt):
            c0, n = NTILES[nt]
            st, Rst = ln_st[nt % 2], LN_R["st"][nt % 2]
            hft = hf[:, :, c0:c0 + n]
            rall = [R_hf[c][nt] for c in range(8)]
            rstd, nmr = st[:, 2, 0:n], st[:, 3, 0:n]
            K.op(dve, lambda: V.tensor_tensor(hft, hft, bcast_mid(rstd, 8), ALU.mult), r=rall + [Rst], w=rall)
            K.op(pool, lambda: G.tensor_tensor(hft, hft, bcast_mid(nmr, 8), ALU.add), r=rall + [Rst], w=rall)
            for c in range(8):
                K.op(act, lambda: A.activation(hf[:, c, c0:c0 + n], hf[:, c, c0:c0 + n], AF.Identity,
                                               bias=vecT[:, c, V_LNB + row:V_LNB + row + 1],
                                               scale=vecT[:, c, V_LNG + row:V_LNG + row + 1]),
                     r=(R_hf[c][nt], R_vt), w=(R_hf[c][nt],))
            K.op(dve, lambda: V.tensor_copy(hb[:, :, c0:c0 + n], hft), r=rall, w=(R_hb[nt],))

        return stats, apply

    def layernorm(li, lj):
        stats, apply = ln_closures(li, lj)
        stats(0)
        for nt in range(5):
            if nt + 1 < 5:
                stats(nt + 1)
            apply(nt)
        first_acc["v"] = True

    ffn_actb = [K.view(o_scr + o, [4, 512], BF16) for o in (0, 1024)]
    ffn_sgb = [K.arena[:, o_scr + o:o_scr + o + 512] for o in (2048, 2560)]
    FFN_R = {"act": [Res("act0"), Res("act1")], "sg": [Res("sg0"), Res("sg1")]}

    def ffn(i, s, ln=None):
        actb, sgb = ffn_actb, ffn_sgb
        Ract, Rsg = FFN_R["act"], FFN_R["sg"]
        cnt = {"gu": 0, "y": 0, "a": 0}
        for gi, (j0, gn) in enumerate(ffn_groups):
            u = use_unit(("ffn", i, s, gi))
            rl = u[0][0]
            wg_v, wu_v, wd_v = u[0][1], u[1][1], u[2][1]
            for nt, (c0, n) in enumerate(NTILES):
                a = cnt["a"] % 2
                cnt["a"] += 1
                for jj in range(gn):
                    p = cnt["gu"] % 2
                    cnt["gu"] += 1
                    bg, bu = p, 2 + p
                    for k in range(8):
                        K.op(pe, lambda: TT.matmul(K.pb(bg, n), wg_v[:, k, jj * 128:(jj + 1) * 128], hb[:, k, c0:c0 + n], start=(k == 0), stop=(k == 7)),
                             r=rl + [R_hb[nt]], w=(K.bank[bg],), sig=(k == 7))
                    for k in range(8):
                        K.op(pe, lambda: TT.matmul(K.pb(bu, n), wu_v[:, k, jj * 128:(jj + 1) * 128], hb[:, k, c0:c0 + n], start=(k == 0), stop=(k == 7)),
                             r=rl + [R_hb[nt]], w=(K.bank[bu],), sig=(k == 7))
                    K.op(act, lambda: A.activation(sgb[p][:, 0:n], K.pb(bg, n), AF.Silu), r=(K.bank[bg],), w=(Rsg[p],))
                    K.op(dve, lambda: V.scalar_tensor_tensor(actb[a][:, jj, 0:n], K.pb(bu, n), 0.5, sgb[p][:, 0:n], ALU.mult, ALU.mult),
                         r=(K.bank[bu], Rsg[p]), w=(Ract[a],))
                for m in range(8):
                    by = 4 + cnt["y"] % 4
                    cnt["y"] += 1
                    for jj in range(gn):
                        K.op(pe, lambda: TT.matmul(K.pb(by, n), wd_v[:, jj, m * 128:(m + 1) * 128], actb[a][:, jj, 0:n], start=(jj == 0), stop=(jj == gn - 1)),
                             r=rl + [Ract[a]], w=(K.bank[by],), sig=(jj == gn - 1))
                    accumulate(m, nt, K.pb(by, n), K.bank[by])
                if ln is not None and gi == len(ffn_groups) - 1 and nt >= 1:
                    ln[0](nt - 1)
                    ln[1](nt - 1)
            first_acc["v"] = False
            done_unit(("ffn", i, s, gi))
        if ln is not None:
            ln[0](4)
            ln[1](4)
            first_acc["v"] = True

    def hgrn(j):
        K.barrier(touch=(R_wq[2], R_wq[3]))
        scr = Scr(extra=True)

        def buf(n):
            o = scr.get(n)
            return K.arena[:, o:o + n]
        sig_, lf, gcs, eng_ = (buf(512) for _ in range(4))
        setA = []
        for q in range(2):
            setA.append({
                "qh": buf(512), "kt": buf(512), "eg": buf(512), "sgate": buf(512),
                "vtok": K.view(scr.get(512), [4, 128]),
                "R": {nm: Res("hg_%s_%d" % (nm, q)) for nm in ("qh", "kt", "eg", "sgate", "vtok")},
            })
        attsb2 = [buf(128), buf(128)]
        ktok2 = [buf(128), buf(128)]
        vblk_s = K.view(scr.get(2048), [16, 128])
        upr_s = K.view(scr.get(2048), [16, 128])
        vblk_p = [K.view(scr.get(256), [2, 128]), K.view(scr.get(256), [2, 128])]
        upr_p = [K.view(scr.get(256), [2, 128]), K.view(scr.get(256), [2, 128])]
        Scur = [buf(128), buf(128)]
        S0 = K.view(scr.get(2048), [16, 128])
        osq = buf(512)
        t1 = osq
        rstd = buf(512)
        yT = K.view(scr.get(256), [512], BF16)
        R = {nm: Res("hg_" + nm) for nm in "sig lf gcs eng S0 osq rstd yT".split()}
        R["t1"] = R["osq"]
        R2 = {nm: [Res("hg_%s_a" % nm), Res("hg_%s_b" % nm)] for nm in ("attsb", "ktok", "vblk", "upr")}
        RS = [Res("hg_S0_"), Res("hg_S1_")]
        rmt = c128[:, C_RM:C_RM + 656]
        tcount = [0]
        st8 = {"sidx": 0}
        units_h = [None] * 8

        def stageA(h, nt, q):
            c0, n = NTILES[nt]
            SA = setA[q]
            RA = SA["R"]
            if nt == 0:
                units_h[h] = use_unit(("hg", j, h))
                K.dma_in(sp, S0, sthg_d[j, :, h].rearrange("s k v -> k s v"), R["S0"])
            if nt == 1 and h < 7:
                emit_loads(unit_index[("hg", j, h + 1)])
            u = units_h[h]
            rl = u[0][0]
            wq_, wz_, wi_, wgt_ = (u[b][1] for b in range(4))
            lb_c = lbcol[:, j, h:h + 1]
            oml_c = omlcol[:, j, h:h + 1]
            noml_c = nomlcol[:, j, h:h + 1]
            for bi, wv in enumerate((wq_, wz_, wgt_)):
                for k in range(8):
                    K.op(pe, lambda: TT.matmul(K.pb(bi, n), wv[:, k, :], hb[:, k, c0:c0 + n], start=(k == 0), stop=(k == 7)),
                         r=rl + [R_hb[nt]], w=(K.bank[bi],), sig=(k == 7))
            tts = token_tiles(nt)
            for ti, (ty, tc0, tn, nsub, L) in enumerate(tts):
                for k in range(8):
                    K.op(pe, lambda: TT.matmul(K.pb(3, 128, 0, tn, ti * 128), hb[:, k, tc0:tc0 + tn], wi_[:, k, :], start=(k == 0), stop=(k == 7)),
                         r=rl + [R_hb[nt]], w=(K.bank[3],), sig=(k == 7))
            qh, kt, eg, sgate, vtok = SA["qh"], SA["kt"], SA["eg"], SA["sgate"], SA["vtok"]
            K.op(act, lambda: A.activation(qh[:, 0:n], K.pb(0, n), AF.Silu), r=(K.bank[0],), w=(RA["qh"],))
            K.op(act, lambda: A.activation(sgate[:, 0:n], K.pb(2, n), AF.Silu), r=(K.bank[2],), w=(RA["sgate"],))
            K.op(act, lambda: A.activation(sig_[:, 0:n], K.pb(1, n), AF.Sigmoid), r=(K.bank[1],), w=(R["sig"],))
            for ti, (ty, tc0, tn, nsub, L) in enumerate(tts):
                K.op(act, lambda: A.copy(vtok[0:tn, ti, :], K.pb(3, 128, 0, tn, ti * 128)), r=(K.bank[3],), w=(RA["vtok"],))
            K.op(act, lambda: A.activation(lf[:, 0:n], sig_[:, 0:n], AF.Ln, bias=lb_c, scale=oml_c), r=(R["sig"], R_small), w=(R["lf"],))
            K.op(dve, lambda: V.tensor_scalar(kt[:, 0:n], sig_[:, 0:n], noml_c, oml_c, ALU.mult, ALU.add), r=(R["sig"], R_small), w=(RA["kt"],))
            rmo = 0 if nt == 0 else 144
            K.op(dve, lambda: V.tensor_tensor_scan(gcs[:, 0:n], rmt[:, rmo:rmo + n], lf[:, 0:n], 0.0, ALU.mult, ALU.add),
                 r=(R["lf"], R_c), w=(R["gcs"],))
            K.op(act, lambda: A.activation(eg[:, 0:n], gcs[:, 0:n], AF.Exp), r=(R["gcs"],), w=(RA["eg"],))
            K.op(act, lambda: A.activation(eng_[:, 0:n], gcs[:, 0:n], AF.Exp, scale=-1.0), r=(R["gcs"],), w=(R["eng"],))
            K.op(dve, lambda: V.tensor_tensor(qh[:, 0:n], qh[:, 0:n], eg[:, 0:n], ALU.mult), r=(RA["qh"], RA["eg"]), w=(RA["qh"],))
            K.op(pool, lambda: G.tensor_tensor(kt[:, 0:n], kt[:, 0:n], eng_[:, 0:n], ALU.mult), r=(RA["kt"], R["eng"]), w=(RA["kt"],))

        def stageB(h, nt, q):
            c0, n = NTILES[nt]
            SA = setA[q]
            RA = SA["R"]
            qh, kt, eg, sgate, vtok = SA["qh"], SA["kt"], SA["eg"], SA["sgate"], SA["vtok"]
            u = units_h[h]
            rl = u[0][0]
            wo_ = u[4][1]
            ng_c = vecT[:, h, V_HGN + j:V_HGN + j + 1]
            if nt == 0:
                st8["sidx"] = 0
                K.op(dve, lambda: V.memset(Scur[0], 0.0), w=(RS[0],))
            for ti, (ty, tc0, tn, nsub, L) in enumerate(token_tiles(nt)):
                lo = tc0 - c0
                pp = tcount[0] % 2
                tcount[0] += 1
                attsb, ktok = attsb2[pp], ktok2[pp]
                Ratt, Rkt = R2["attsb"][pp], R2["ktok"][pp]
                if ty == TYPE_S:
                    vblk, upr = vblk_s, upr_s
                    Rvb, Rup = R2["vblk"][0], R2["upr"][0]
                else:
                    vblk, upr = vblk_p[pp], upr_p[pp]
                    Rvb, Rup = R2["vblk"][pp], R2["upr"][pp]
                K.op(pe, lambda: TT.matmul(K.pb(4, tn, 0, tn), kt[:, lo:lo + tn], qh[:, lo:lo + tn], start=True, stop=True),
                     r=(RA["kt"], RA["qh"]), w=(K.bank[4],))
                K.op(dve, lambda: V.tensor_tensor(attsb[0:tn, 0:tn], K.pb(4, tn, 0, tn), cmask(ty, tn), ALU.mult), r=(K.bank[4], R_c), w=(Ratt,))
                K.op(pe, lambda: TT.transpose(K.pb(5, 128, 0, tn), kt[:, lo:lo + tn], ident), r=(RA["kt"], R_c), w=(K.bank[5],))
                K.op(act, lambda: A.copy(ktok[0:tn, :], K.pb(5, 128, 0, tn)), r=(K.bank[5],), w=(Rkt,))
                K.op(pool, lambda: G.tensor_tensor(vblk[0:tn, 0:nsub, :], bcast_mid(vtok[0:tn, ti, :], nsub), bcast_last(cblk(ty, tn, nsub), 128), ALU.mult),
                     r=(RA["vtok"], R_c), w=(Rvb,))
                K.op(pe, lambda: TT.matmul(K.pb(7, tn, 0, 128, lo), vtok[0:tn, ti, :], attsb[0:tn, 0:tn], start=True, stop=False),
                     r=(RA["vtok"], Ratt), w=(K.bank[7],), sig=False)
                for s0 in range(0, nsub, 4):
                    sn = min(4, nsub - s0)
                    K.op(pe, lambda: TT.matmul(K.pb(6, sn * 128), ktok[0:tn, :], vblk[0:tn, s0:s0 + sn, :], start=True, stop=True),
                         r=(Rkt, Rvb), w=(K.bank[6],))
                    dview = eg[:, lo + (s0 + 1) * L - 1:lo + (s0 + sn) * L:L] if sn > 1 else eg[:, lo + (s0 + 1) * L - 1:lo + (s0 + 1) * L]
                    K.op(dve, lambda: V.tensor_tensor(upr[:, s0:s0 + sn, :], K.pb(6, sn * 128).rearrange("p (s v) -> p s v", v=128), bcast_last(dview, 128), ALU.mult),
                         r=(K.bank[6], RA["eg"]), w=(Rup,))
                if ty == TYPE_S:
                    for s_ in range(16):
                        K.op(pe, lambda: TT.matmul(K.pb(7, L, 0, 128, lo + s_ * L), S0[:, s_, :], qh[:, lo + s_ * L:lo + (s_ + 1) * L], start=False, stop=(s_ == 15)),
                             r=(R["S0"], RA["qh"]), w=(K.bank[7],), sig=(s_ == 15))
                    dv_ = eg[:, lo + L - 1:lo + 16 * L:L]
                    K.op(dve, lambda: V.tensor_tensor(S0, S0, bcast_last(dv_, 128), ALU.mult), r=(R["S0"], RA["eg"]), w=(R["S0"],))
                    K.op(dve, lambda: V.tensor_tensor(S0, S0, upr, ALU.add), r=(Rup, R["S0"]), w=(R["S0"],))
                    K.dma_out(sp, ohgs_d[j, :, h].rearrange("s k v -> k s v"), S0, R["S0"])
                else:
                    for s_ in range(nsub):
                        cur = st8["sidx"] % 2
                        K.op(pe, lambda: TT.matmul(K.pb(7, L, 0, 128, lo + s_ * L), Scur[cur], qh[:, lo + s_ * L:lo + (s_ + 1) * L], start=False, stop=(s_ == nsub - 1)),
                             r=(RS[cur], RA["qh"]), w=(K.bank[7],), sig=(s_ == nsub - 1))
                        dcl = eg[:, lo + (s_ + 1) * L - 1:lo + (s_ + 1) * L]
                        K.op(dve, lambda: V.scalar_tensor_tensor(Scur[1 - cur], Scur[cur], dcl, upr[:, s_, :], ALU.mult, ALU.add),
                             r=(RS[cur], RA["eg"], Rup), w=(RS[1 - cur],))
                        st8["sidx"] += 1
            K.op(act, lambda: A.activation(osq[:, 0:n], K.pb(7, n), AF.Square), r=(K.bank[7],), w=(R["osq"],))
            K.op(pe, lambda: TT.matmul(K.pb(4, n), ones_f, osq[:, 0:n], start=True, stop=True), r=(R["osq"], R_small), w=(K.bank[4],))
            K.op(act, lambda: A.activation(rstd[:, 0:n], K.pb(4, n), AF.Ln, bias=EPS, scale=1.0 / 128), r=(K.bank[4],), w=(R["rstd"],))
            K.op(act, lambda: A.activation(rstd[:, 0:n], rstd[:, 0:n], AF.Exp, scale=-0.5), r=(R["rstd"],), w=(R["rstd"],))
            K.op(dve, lambda: V.tensor_tensor(t1[:, 0:n], K.pb(7, n), rstd[:, 0:n], ALU.mult), r=(K.bank[7], R["rstd"]), w=(R["t1"],))
            K.op(dve, lambda: V.scalar_tensor_tensor(yT[:, 0:n], t1[:, 0:n], ng_c, sgate[:, 0:n], ALU.mult, ALU.mult), r=(R["t1"], RA["sgate"], R_vt), w=(R["yT"],))
            for m in range(8):
                b = 4 + (m % 3)
                K.op(pe, lambda: TT.matmul(K.pb(b, n), wo_[:, m * 128:(m + 1) * 128], yT[:, 0:n], start=True, stop=True),
                     r=rl + [R["yT"]], w=(K.bank[b],))
                accumulate(m, nt, K.pb(b, n), K.bank[b])
            if nt == 4:
                first_acc["v"] = False
                fin = st8["sidx"] % 2
                K.dma_out(sp, ohgp_d[j, h], Scur[fin], RS[fin])

        steps = [(h, nt) for h in range(8) for nt in range(5)]
        stageA(steps[0][0], steps[0][1], 0)
        for i_, (h, nt) in enumerate(steps):
            if i_ + 1 < len(steps):
                stageA(steps[i_ + 1][0], steps[i_ + 1][1], (i_ + 1) % 2)
            stageB(h, nt, i_ % 2)
        K.barrier(touch=(R_wq[2], R_wq[3]))
        done_unit(("hg", j, 7))

    def retention(rconst):
        K.barrier()
        scr = Scr(extra=False)

        def buf(n):
            o = scr.get(n)
            return K.arena[:, o:o + n]
        cs = K.view(scr.get(1024), [2, 512])
        rt = K.view(scr.get(416), [2, 208])
        qk = K.view(scr.get(2048), [4, 512])
        ta, tb = buf(512), buf(512)
        vtok = buf(512)
        attsb, ktok = buf(128), buf(256)
        vb = buf(512)
        Sc = K.view(scr.get(1024), [2, 512])
        S0 = K.view(scr.get(1024), [2, 512])
        osb = K.view(scr.get(512), [4, 128])
        osq = K.view(scr.get(512), [4, 128])
        stt = K.view(scr.get(512), [4, 128])
        sgate = K.view(scr.get(512), [4, 128])
        yT = K.view(scr.get(256), [4, 128], BF16)
        names = "cs rt qk ta tb vtok attsb ktok vb Sc S0 osb osq stt sgate yT".split()
        R = {nm: Res("rt_" + nm) for nm in names}
        gam = rconst
        U2 = K.psum[:, 1024:2048].rearrange("p (c v) -> p c v", v=512)
        for h in range(4):
            uqk = use_unit(("ret", h, "qk"))
            uv = use_unit(("ret", h, "v"))
            ug = use_unit(("ret", h, "g"))
            uo = use_unit(("ret", h, "o"))
            wq_v, wk_v = uqk[0][1], uqk[1][1]
            rl_qk = uqk[0][0]
            wv_v, wg_v, wo_v = uv[0][1], ug[0][1], uo[0][1]
            rl_v, rl_g, rl_o = uv[0][0], ug[0][0], uo[0][0]
            K.dma_in(sp, rt, rett_d[:, h], R["rt"])
            dS, dM, dP = math.exp(8 * gam[h]), math.exp(16 * gam[h]), math.exp(64 * gam[h])
            K.op(dve, lambda: V.memset(Sc, 0.0), w=(R["Sc"],))
            for nt, (c0, n) in enumerate(NTILES):
                K.dma_in(sp, cs[:, :, 0:n], cs_d[:, :, c0:c0 + n], R["cs"])
                for qi, wv in enumerate((wq_v, wk_v)):
                    for half in range(2):
                        b = qi * 2 + half
                        for k in range(8):
                            K.op(pe, lambda: TT.matmul(K.pb(b, n), wv[:, k, half * 128:(half + 1) * 128], hb[:, k, c0:c0 + n], start=(k == 0), stop=(k == 7)),
                                 r=rl_qk + [R_hb[nt]], w=(K.bank[b],), sig=(k == 7))
                cosv, sinv = cs[:, 0, 0:n], cs[:, 1, 0:n]
                for qi in range(2):
                    b1, b2 = qi * 2, qi * 2 + 1
                    x1o, x2o = qk[:, qi * 2, 0:n], qk[:, qi * 2 + 1, 0:n]
                    K.op(dve, lambda: V.tensor_tensor(ta[:, 0:n], K.pb(b1, n), cosv, ALU.mult), r=(K.bank[b1], R["cs"]), w=(R["ta"],))
                    K.op(dve, lambda: V.tensor_tensor(tb[:, 0:n], K.pb(b2, n), sinv, ALU.mult), r=(K.bank[b2], R["cs"]), w=(R["tb"],))
                    K.op(pool, lambda: G.tensor_tensor(x1o, ta[:, 0:n], tb[:, 0:n], ALU.subtract), r=(R["ta"], R["tb"]), w=(R["qk"],))
                    K.op(dve, lambda: V.tensor_tensor(ta[:, 0:n], K.pb(b1, n), sinv, ALU.mult), r=(K.bank[b1], R["cs"]), w=(R["ta"],))
                    K.op(dve, lambda: V.tensor_tensor(tb[:, 0:n], K.pb(b2, n), cosv, ALU.mult), r=(K.bank[b2], R["cs"]), w=(R["tb"],))
                    K.op(pool, lambda: G.tensor_tensor(x2o, ta[:, 0:n], tb[:, 0:n], ALU.add), r=(R["ta"], R["tb"]), w=(R["qk"],))
                    for xo in (x1o, x2o):
                        if nt == 0:
                            K.op(dve, lambda: V.tensor_tensor(xo, xo, rt[:, qi, 0:144], ALU.mult), r=(R["qk"], R["rt"]), w=(R["qk"],))
                        else:
                            x3 = xo.rearrange("p (a b) -> p a b", b=64)
                            K.op(dve, lambda: V.tensor_tensor(x3, x3, bcast_mid(rt[:, qi, 144:208], 8), ALU.mult), r=(R["qk"], R["rt"]), w=(R["qk"],))
                for ti, (ty, tc0, tn, nsub, L) in enumerate(token_tiles(nt)):
                    lo = tc0 - c0
                    dd = (dS, dM, dP)[ty]
                    for k in range(8):
                        K.op(pe, lambda: TT.matmul(K.pb(4, 512, 0, tn), hb[:, k, tc0:tc0 + tn], wv_v[:, k, :], start=(k == 0), stop=(k == 7)),
                             r=rl_v + [R_hb[nt]], w=(K.bank[4],), sig=(k == 7))
                    K.op(act, lambda: A.copy(vtok[0:tn, :], K.pb(4, 512, 0, tn)), r=(K.bank[4],), w=(R["vtok"],))
                    for vc in range(4):
                        for k in range(8):
                            K.op(pe, lambda: TT.matmul(K.pb(5, tn, 0, 128, vc * 128), wg_v[:, k, vc * 128:(vc + 1) * 128], hb[:, k, tc0:tc0 + tn], start=(k == 0), stop=(k == 7)),
                                 r=rl_g + [R_hb[nt]], w=(K.bank[5],), sig=(k == 7))
                    K.op(act, lambda: A.activation(sgate[:, :, 0:tn], K.pb(5, 512).rearrange("p (c t) -> p c t", t=128)[:, :, 0:tn], AF.Silu), r=(K.bank[5],), w=(R["sgate"],))
                    for kc in range(2):
                        K.op(pe, lambda: TT.matmul(K.pb(6, tn, 0, tn), qk[:, 2 + kc, lo:lo + tn], qk[:, kc, lo:lo + tn], start=(kc == 0), stop=(kc == 1)),
                             r=(R["qk"],), w=(K.bank[6],), sig=(kc == 1))
                    K.op(dve, lambda: V.tensor_tensor(attsb[0:tn, 0:tn], K.pb(6, tn, 0, tn), cmask(ty, tn), ALU.mult), r=(K.bank[6], R_c), w=(R["attsb"],))
                    for kc in range(2):
                        K.op(pe, lambda: TT.transpose(K.pb(6, 128, 0, tn, 128 + kc * 128), qk[:, 2 + kc, lo:lo + tn], ident), r=(R["qk"], R_c), w=(K.bank[6],), sig=(kc == 1))
                    K.op(act, lambda: A.copy(ktok[0:tn, :], K.pb(6, 256, 0, tn, 128)), r=(K.bank[6],), w=(R["ktok"],))
                    for vc in range(4):
                        K.op(pe, lambda: TT.matmul(K.pb(7, tn, 0, 128, vc * 128), vtok[0:tn, vc * 128:(vc + 1) * 128], attsb[0:tn, 0:tn], start=True, stop=True),
                             r=(R["vtok"], R["attsb"]), w=(K.bank[7],), sig=(vc == 3))
                    K.op(act, lambda: A.copy(osb[:, :, 0:tn], K.pb(7, 512).rearrange("p (c t) -> p c t", t=128)[:, :, 0:tn]), r=(K.bank[7],), w=(R["osb"],))
                    for s_ in range(nsub):
                        if ty == TYPE_S:
                            K.dma_in(sp, S0, stret_d[0, s_, h].rearrange("(c p) v -> p c v", p=128), R["S0"])
                            Sx, Rx_ = S0, R["S0"]
                        else:
                            Sx, Rx_ = Sc, R["Sc"]
                        for vc in range(4):
                            for kc in range(2):
                                K.op(pe, lambda: TT.matmul(K.pb(4, L, 0, 128, vc * 64), Sx[:, kc, vc * 128:(vc + 1) * 128], qk[:, kc, lo + s_ * L:lo + (s_ + 1) * L], start=(kc == 0), stop=(kc == 1)),
                                     r=(Rx_, R["qk"]), w=(K.bank[4],), sig=(kc == 1 and vc == 3))
                        K.op(dve, lambda: V.tensor_tensor(osb[:, :, s_ * L:(s_ + 1) * L], osb[:, :, s_ * L:(s_ + 1) * L], K.pb(4, 256).rearrange("p (c t) -> p c t", t=64)[:, :, 0:L], ALU.add),
                             r=(K.bank[4], R["osb"]), w=(R["osb"],))
                        if nsub > 1:
                            K.op(act, lambda: A.mul(vb[0:tn, :], vtok[0:tn, :], cblk(ty, tn, nsub)[:, s_:s_ + 1]), r=(R["vtok"], R_c), w=(R["vb"],))
                            vsrc, rv = vb, R["vb"]
                        else:
                            vsrc, rv = vtok, R["vtok"]
                        for kc in range(2):
                            K.op(pe, lambda: TT.matmul(K.pb(2 + kc, 512), ktok[0:tn, kc * 128:(kc + 1) * 128], vsrc[0:tn, :], start=True, stop=True),
                                 r=(R["ktok"], rv), w=(K.bank[2 + kc],))
                        K.op(dve, lambda: V.tensor_tensor(Sx, Sx, U2, ALU.add), r=(Rx_, K.bank[2], K.bank[3]), w=(Rx_,))
                        K.op(act, lambda: A.mul(Sx, Sx, dd), r=(Rx_,), w=(Rx_,))
                        if ty == TYPE_S:
                            K.dma_out(sp, orets_d[0, s_, h].rearrange("(c p) v -> p c v", p=128), S0, R["S0"])
                    K.op(act, lambda: A.activation(osq[:, :, 0:tn], osb[:, :, 0:tn], AF.Square), r=(R["osb"],), w=(R["osq"],))
                    for vc in range(4):
                        K.op(pe, lambda: TT.matmul(K.pb(6, tn), ones_f, osb[:, vc, 0:tn], start=(vc == 0), stop=(vc == 3)), r=(R["osb"], R_small), w=(K.bank[6],), sig=(vc == 3))
                    mean, var, rstd, nmr = (stt[:, q, 0:tn] for q in range(4))
                    K.op(dve, lambda: V.tensor_scalar(mean, K.pb(6, tn), 1.0 / 512, None, ALU.mult), r=(K.bank[6],), w=(R["stt"],))
                    for vc in range(4):
                        K.op(pe, lambda: TT.matmul(K.pb(6, tn), ones_f, osq[:, vc, 0:tn], start=(vc == 0), stop=(vc == 3)), r=(R["osq"], R_small), w=(K.bank[6],), sig=(vc == 3))
                    K.op(dve, lambda: V.tensor_tensor(var, mean, mean, ALU.mult), r=(R["stt"],), w=(R["stt"],))
                    K.op(dve, lambda: V.scalar_tensor_tensor(var, K.pb(6, tn), 1.0 / 512, var, ALU.mult, ALU.subtract), r=(K.bank[6], R["stt"]), w=(R["stt"],))
                    K.op(act, lambda: A.activation(rstd, var, AF.Ln, bias=EPS), r=(R["stt"],), w=(R["stt"],))
                    K.op(act, lambda: A.activation(rstd, rstd, AF.Exp, scale=-0.5), r=(R["stt"],), w=(R["stt"],))
                    K.op(dve, lambda: V.scalar_tensor_tensor(nmr, mean, -1.0, rstd, ALU.mult, ALU.mult), r=(R["stt"],), w=(R["stt"],))
                    K.op(dve, lambda: V.tensor_tensor(osb[:, :, 0:tn], osb[:, :, 0:tn], bcast_mid(rstd, 4), ALU.mult), r=(R["osb"], R["stt"]), w=(R["osb"],))
                    K.op(dve, lambda: V.tensor_tensor(osb[:, :, 0:tn], osb[:, :, 0:tn], bcast_mid(nmr, 4), ALU.add), r=(R["osb"], R["stt"]), w=(R["osb"],))
                    for vc in range(4):
                        ci_ = h * 4 + vc
                        ngc = vecT[:, ci_ % 8, V_RETN + ci_ // 8:V_RETN + ci_ // 8 + 1]
                        K.op(dve, lambda: V.scalar_tensor_tensor(yT[:, vc, 0:tn], osb[:, vc, 0:tn], ngc, sgate[:, vc, 0:tn], ALU.mult, ALU.mult),
                             r=(R["osb"], R["sgate"], R_vt), w=(R["yT"],))
                    for m in range(8):
                        b = m % 4
                        for vc in range(4):
                            K.op(pe, lambda: TT.matmul(K.pb(b, tn), wo_v[:, vc, m * 128:(m + 1) * 128], yT[:, vc, 0:tn], start=(vc == 0), stop=(vc == 3)),
                                 r=rl_o + [R["yT"]], w=(K.bank[b],), sig=(vc == 3))
                        dst = hf[:, m, tc0:tc0 + tn]
                        ps_ = K.pb(b, tn)
                        if first_acc["v"]:
                            K.op(dve, lambda: V.scalar_tensor_tensor(dst, dst, ALPHA, ps_, ALU.mult, ALU.add), r=(K.bank[b], R_hf[m][nt]), w=(R_hf[m][nt],))
                        else:
                            K.op(dve, lambda: V.tensor_tensor(dst, dst, ps_, ALU.add), r=(K.bank[b], R_hf[m][nt]), w=(R_hf[m][nt],))
            first_acc["v"] = False
            K.dma_out(sp, oretp_d[0, h].rearrange("(c p) v -> p c v", p=128), Sc, R["Sc"])
            if h < 3:
                done_unit(("ret", h, "o"))
        K.barrier()
        done_unit(("ret", 3, "o"))

    def mamba():
        K.barrier(touch=(R_wq[2], R_wq[3]))
        scr = Scr(extra=True)

        def buf(n):
            o = scr.get(n)
            return K.arena[:, o:o + n]
        NTT = 18
        dt_t = K.view(scr.get(NTT * 32), [NTT, 32])
        g_t = K.view(scr.get(NTT * 32), [NTT, 32])
        dtw_t = K.view(scr.get(NTT * 32), [NTT, 32])
        tmp32 = buf(32)
        o_pre = scr.get(4 * 520)
        pre = K.view(o_pre, [4, 520])
        y2 = K.view(o_pre, [2, 512])
        rstd = K.arena[:, o_pre + 1024:o_pre + 1536]
        spre = K.view(scr.get(4 * 176), [4, 176])
        mpre = K.view(scr.get(4 * 19), [4, 19])
        halo = K.view(scr.get(4 * 3), [4, 3])
        cv = K.view(scr.get(4 * 512), [4, 512])
        cacc = buf(128)
        sz = K.view(scr.get(1024), [2, 512])
        o_hist = scr.get(512)
        hist_tok = K.arena[0:48, o_hist:o_hist + 512]
        o_c48 = scr.get(512)
        c48 = K.arena[0:48, o_c48:o_c48 + 512]
        ptok = K.arena[0:3, o_c48:o_c48 + 512]
        ctmp = buf(128)
        xtok = buf(256)
        btok = buf(128)
        rhsd4 = K.view(scr.get(512), [4, 128])
        tmpd4 = K.view(scr.get(512), [4, 128])
        egbc4 = K.view(scr.get(512), [4, 128])
        attsb4 = K.view(scr.get(512), [4, 128])
        qh4 = K.view(scr.get(512), [4, 128])
        vdt4 = K.view(scr.get(256), [4, 64])
        vpr4 = K.view(scr.get(256), [4, 64])
        rhsd, tmpd, egbc, attsb, qh = rhsd4[:, 0, :], tmpd4[:, 0, :], egbc4[:, 0, :], attsb4[:, 0, :], qh4[:, 0, :]
        decT = tmpd4[:, 1, :]
        vpr, vdt = vpr4[:, 0, :], vdt4[:, 0, :]
        o_vblk = scr.get(1024)
        vblk = K.view(o_vblk, [16, 64])
        vblk4 = K.arena[:, o_vblk:o_vblk + 512]
        ysq = K.view(o_vblk, [2, 512])
        o_upr = scr.get(1024)
        upr = K.view(o_upr, [16, 64])
        upr4 = K.view(o_upr, [4, 2, 64])
        S4 = [K.view(scr.get(256), [4, 64]), K.view(scr.get(256), [4, 64])]
        S0 = K.view(scr.get(1024), [16, 64])
        yT = K.view(scr.get(512), [2, 512], BF16)
        names = "vdt dt g dtw tmp32 pre spre mpre halo cv cacc sz hist c48 ctmp xtok btok rhsd tmpd egbc attsb qh vpr vblk upr S0 yT".split()
        R = {nm: Res("mb_" + nm) for nm in names}
        R["ysq"] = R["vblk"]
        R["y2"] = R["pre"]
        R["rstd"] = R["pre"]
        R["ptok"] = R["c48"]
        R["decT"] = R["tmpd"]
        RS4 = [Res("mb_S4_0"), Res("mb_S4_1")]
        all_tt = []
        for nt in range(5):
            for tt_ in token_tiles(nt):
                all_tt.append((nt,) + tt_)
        sel = c128[:, C_SEL:C_SEL + 48]
        udt = use_unit(("mam", "dt"))
        wdt = udt[0][1]
        for tix, (nt, ty, tc0, tn, nsub, L) in enumerate(all_tt):
            for k in range(8):
                K.op(pe, lambda: TT.matmul(K.pb(0, 32, 0, tn), hb[:, k, tc0:tc0 + tn], wdt[:, k, :], start=(k == 0), stop=(k == 7)),
                     r=udt[0][0] + [R_hb[nt]], w=(K.bank[0],), sig=(k == 7))
            d_ = dt_t[0:tn, tix, :]
            K.op(dve, lambda: V.tensor_tensor(d_, K.pb(0, 32, 0, tn), mh_bc[0:tn, 0, :], ALU.add), r=(K.bank[0], R_small), w=(R["dt"],))
            K.op(act, lambda: A.activation(d_, d_, AF.Exp), r=(R["dt"],), w=(R["dt"],))
            K.op(act, lambda: A.activation(d_, d_, AF.Ln, bias=1.0), r=(R["dt"],), w=(R["dt"],))
            la = tmp32[0:tn, :]
            K.op(dve, lambda: V.tensor_tensor(la, d_, negA[0:tn, :], ALU.mult), r=(R["dt"], R_small), w=(R["tmp32"],))
            K.op(pe, lambda: TT.matmul(K.pb(1, 32, 0, tn), cmask(ty, tn), la, start=True, stop=True), r=(R["tmp32"], R_c), w=(K.bank[1],))
            K.op(pe, lambda: TT.matmul(K.pb(2, 32, 0, tn), cltri(ty, tn), la, start=True, stop=True), r=(R["tmp32"], R_c), w=(K.bank[2],))
            K.op(act, lambda: A.copy(g_t[0:tn, tix, :], K.pb(1, 32, 0, tn)), r=(K.bank[1],), w=(R["g"],))
            w_ = dtw_t[0:tn, tix, :]
            K.op(dve, lambda: V.tensor_tensor(w_, K.pb(2, 32, 0, tn), g_t[0:tn, tix, :], ALU.subtract), r=(K.bank[2], R["g"]), w=(R["dtw"],))
            K.op(act, lambda: A.activation(w_, w_, AF.Exp), r=(R["dtw"],), w=(R["dtw"],))
            K.op(dve, lambda: V.tensor_tensor(w_, w_, d_, ALU.mult), r=(R["dtw"], R["dt"]), w=(R["dtw"],))
        done_unit(("mam", "dt"))
        hist_rows = stconv_d[0].rearrange("s r c -> (s r) c")
        oconvs_rows = oconvs_d[0].rearrange("s r c -> (s r) c")
        for g in range(8):
            uin = use_unit(("mam", g, "in"))
            rl = uin[0][0]
            wz_, wx_, wB_, wC_ = (uin[b][1] for b in range(4))
            uo = use_unit(("mam", g, "o"))
            wo_v, rl_o = uo[0][1], uo[0][0]
            chunks = [(wx_[:, :, 0:128], 2 * g), (wx_[:, :, 128:256], 2 * g + 1), (wB_, 16 + g), (wC_, 24 + g)]
            sidx = 0
            K.op(dve, lambda: V.memset(S4[0], 0.0), w=(RS4[0],))
            for ci, (wv, cch) in enumerate(chunks):
                K.dma_in(sp, hist_tok[:, ci * 128:(ci + 1) * 128], hist_rows[:, cch * 128:(cch + 1) * 128], R["hist"])
            for ci, (wv, cch) in enumerate(chunks):
                K.op(pe, lambda: TT.transpose(K.pb(6, 48), hist_tok[:, ci * 128:(ci + 1) * 128], ident[0:48, 0:48]), r=(R["hist"], R_c), w=(K.bank[6],))
                K.op(act, lambda: A.copy(spre[:, ci, :].rearrange("p (s t) -> p s t", t=11)[:, :, 0:3], K.pb(6, 48).rearrange("p (s r) -> p s r", r=3)),
                     r=(K.bank[6],), w=(R["spre"],))
            K.op(dve, lambda: V.memset(mpre[:, :, 0:3], 0.0), w=(R["mpre"],))
            tix = 0
            for nt, (c0, n) in enumerate(NTILES):
                for ci, (wv, cch) in enumerate(chunks):
                    for k in range(8):
                        K.op(pe, lambda: TT.matmul(K.pb(ci, n), wv[:, k, :], hb[:, k, c0:c0 + n], start=(k == 0), stop=(k == 7)),
                             r=rl + [R_hb[nt]], w=(K.bank[ci],), sig=(k == 7))
                for zc in range(2):
                    for k in range(8):
                        K.op(pe, lambda: TT.matmul(K.pb(4 + zc, n), wz_[:, k, zc * 128:(zc + 1) * 128], hb[:, k, c0:c0 + n], start=(k == 0), stop=(k == 7)),
                             r=rl + [R_hb[nt]], w=(K.bank[4 + zc],), sig=(k == 7))
                    K.op(act, lambda: A.activation(sz[:, zc, 0:n], K.pb(4 + zc, n), AF.Silu), r=(K.bank[4 + zc],), w=(R["sz"],))
                for ci, (wv, cch) in enumerate(chunks):
                    cc8, rr = cch % 8, cch // 8
                    wcol = [vecT[:, cc8, V_CW + 4 * t_ + rr:V_CW + 4 * t_ + rr + 1] for t_ in range(4)]
                    bcol = vecT[:, cc8, V_CB + rr:V_CB + rr + 1]
                    if nt == 0:
                        sp3 = spre[:, ci, :].rearrange("p (s t) -> p s t", t=11)
                        K.op(act, lambda: A.copy(sp3[:, :, 3:11], K.pb(ci, 128).rearrange("p (s t) -> p s t", t=8)), r=(K.bank[ci],), w=(R["spre"],))
                        K.op(act, lambda: A.copy(mpre[:, ci, 3:19], K.pb(ci, 16, 0, 128, 128)), r=(K.bank[ci],), w=(R["mpre"],))
                        K.op(dve, lambda: V.tensor_copy(cacc, K.pb(ci, 128)), r=(K.bank[ci],), w=(R["cacc"],))
                        K.op(pe, lambda: TT.transpose(K.pb(6, 128), cacc, ident), r=(R["cacc"], R_c), w=(K.bank[6],))
                        K.op(act, lambda: A.copy(ctmp, K.pb(6, 128)), r=(K.bank[6],), w=(R["ctmp"],))
                        K.op(pe, lambda: TT.matmul(K.pb(6, 128, 0, 48, 128), sel, ctmp, start=True, stop=True), r=(R["ctmp"], R_c), w=(K.bank[6],))
                        K.op(act, lambda: A.copy(c48[:, ci * 128:(ci + 1) * 128], K.pb(6, 128, 0, 48, 128)), r=(K.bank[6],), w=(R["c48"],))
                        co = cv[:, ci, 0:128].rearrange("p (s t) -> p s t", t=8)
                        K.op(act, lambda: A.activation(co, sp3[:, :, 3:11], AF.Identity, bias=bcol, scale=wcol[3]), r=(R["spre"], R_vt), w=(R["cv"],))
                        for t_ in range(3):
                            K.op(dve, lambda: V.scalar_tensor_tensor(co, sp3[:, :, t_:t_ + 8], wcol[t_], co, ALU.mult, ALU.add), r=(R["spre"], R["cv"], R_vt), w=(R["cv"],))
                        cm = cv[:, ci, 128:144]
                        K.op(act, lambda: A.activation(cm, mpre[:, ci, 3:19], AF.Identity, bias=bcol, scale=wcol[3]), r=(R["mpre"], R_vt), w=(R["cv"],))
                        for t_ in range(3):
                            K.op(dve, lambda: V.scalar_tensor_tensor(cm, mpre[:, ci, t_:t_ + 16], wcol[t_], cm, ALU.mult, ALU.add), r=(R["mpre"], R["cv"], R_vt), w=(R["cv"],))
                        K.op(dve, lambda: V.tensor_copy(halo[:, ci, :], mpre[:, ci, 16:19]), r=(R["mpre"],), w=(R["halo"],))
                    else:
                        K.op(dve, lambda: V.tensor_copy(pre[:, ci, 0:3], halo[:, ci, :]), r=(R["halo"],), w=(R["pre"],))
                        K.op(act, lambda: A.copy(pre[:, ci, 3:3 + n], K.pb(ci, n)), r=(K.bank[ci],), w=(R["pre"],))
                        K.op(dve, lambda: V.tensor_copy(halo[:, ci, :], pre[:, ci, n:n + 3]), r=(R["pre"],), w=(R["halo"],))
                        co = cv[:, ci, 0:n]
                        K.op(act, lambda: A.activation(co, pre[:, ci, 3:3 + n], AF.Identity, bias=bcol, scale=wcol[3]), r=(R["pre"], R_vt), w=(R["cv"],))
                        for t_ in range(3):
                            K.op(dve, lambda: V.scalar_tensor_tensor(co, pre[:, ci, t_:t_ + n], wcol[t_], co, ALU.mult, ALU.add), r=(R["pre"], R["cv"], R_vt), w=(R["cv"],))
                        if nt == 4:
                            K.op(pe, lambda: TT.transpose(K.pb(6, 128, 0, 3), pre[:, ci, n:n + 3], ident), r=(R["pre"], R_c), w=(K.bank[6],))
                            K.op(act, lambda: A.copy(ptok[:, ci * 128:(ci + 1) * 128], K.pb(6, 128, 0, 3)), r=(K.bank[6],), w=(R["ptok"],))
                    K.op(act, lambda: A.activation(cv[:, ci, 0:n], cv[:, ci, 0:n], AF.Silu), r=(R["cv"],), w=(R["cv"],))
                if nt == 0:
                    for ci, (wv, cch) in enumerate(chunks):
                        K.dma_out(sp, oconvs_rows[:, cch * 128:(cch + 1) * 128], c48[:, ci * 128:(ci + 1) * 128], R["c48"])
                if nt == 4:
                    for ci, (wv, cch) in enumerate(chunks):
                        K.dma_out(sp, oconvp_d[0, :, cch * 128:(cch + 1) * 128], ptok[:, ci * 128:(ci + 1) * 128], R["ptok"])
                for ti, (ty, tc0, tn, nsub, L) in enumerate(token_tiles(nt)):
                    lo = tc0 - c0
                    for xc in range(2):
                        K.op(pe, lambda: TT.transpose(K.pb(6, 128, 0, tn, xc * 128), cv[:, xc, lo:lo + tn], ident), r=(R["cv"], R_c), w=(K.bank[6],), sig=False)
                    K.op(pe, lambda: TT.transpose(K.pb(6, 128, 0, tn, 256), cv[:, 2, lo:lo + tn], ident), r=(R["cv"], R_c), w=(K.bank[6],))
                    K.op(act, lambda: A.copy(xtok[0:tn, :], K.pb(6, 256, 0, tn)), r=(K.bank[6],), w=(R["xtok"],))
                    K.op(act, lambda: A.copy(btok[0:tn, :], K.pb(6, 128, 0, tn, 256)), r=(K.bank[6],), w=(R["btok"],))
                    K.op(pe, lambda: TT.matmul(K.pb(7, tn, 0, tn), cv[:, 2, lo:lo + tn], cv[:, 3, lo:lo + tn], start=True, stop=True), r=(R["cv"],), w=(K.bank[7],))
                    if ty == TYPE_S:
                        for hh in range(4):
                            hd = 4 * g + hh
                            gcol = g_t[0:tn, tix, hd:hd + 1]
                            K.op(dve, lambda: V.tensor_scalar(rhsd[0:tn, 0:tn], ident[0:tn, 0:tn], gcol, None, ALU.mult), r=(R["g"], R_c), w=(R["rhsd"],))
                            K.op(pe, lambda: TT.matmul(K.pb(5, tn, 0, 128, 0), ones_f[0:tn, :], rhsd[0:tn, 0:tn], start=True, stop=True), r=(R["rhsd"], R_small), w=(K.bank[5],))
                            K.op(act, lambda: A.activation(egbc[:, 0:tn], K.pb(5, tn), AF.Exp), r=(K.bank[5],), w=(R["egbc"],))
                            K.op(dve, lambda: V.scalar_tensor_tensor(tmpd[0:tn, 0:tn], K.pb(5, tn, 0, tn), gcol, cneg(ty, tn), ALU.subtract, ALU.add), r=(K.bank[5], R_c, R["g"]), w=(R["tmpd"],))
                            K.op(act, lambda: A.activation(decT[0:tn, 0:tn], tmpd[0:tn, 0:tn], AF.Exp), r=(R["tmpd"],), w=(R["decT"],))
                            K.op(dve, lambda: V.tensor_tensor(attsb[0:tn, 0:tn], K.pb(7, tn, 0, tn), decT[0:tn, 0:tn], ALU.mult), r=(K.bank[7], R["decT"]), w=(R["attsb"],))
                            K.op(pool, lambda: G.tensor_tensor(qh[:, 0:tn], cv[:, 3, lo:lo + tn], egbc[:, 0:tn], ALU.mult), r=(R["cv"], R["egbc"]), w=(R["qh"],))
                            K.op(dve, lambda: V.tensor_scalar(vpr[0:tn, :], xtok[0:tn, hh * 64:(hh + 1) * 64], dtw_t[0:tn, tix, hd:hd + 1], None, ALU.mult), r=(R["xtok"], R["dtw"]), w=(R["vpr"],))
                            K.op(dve, lambda: V.tensor_scalar(vdt[0:tn, :], xtok[0:tn, hh * 64:(hh + 1) * 64], dt_t[0:tn, tix, hd:hd + 1], None, ALU.mult), r=(R["xtok"], R["dt"]), w=(R["vdt"],))
                            K.op(pool, lambda: G.tensor_tensor(vblk[0:tn, 0:nsub, :], bcast_mid(vpr[0:tn, :], nsub), bcast_last(cblk(ty, tn, nsub), 64), ALU.mult), r=(R["vpr"], R_c), w=(R["vblk"],))
                            for s0 in range(0, nsub, 8):
                                sn = min(8, nsub - s0)
                                K.op(pe, lambda: TT.matmul(K.pb(4, sn * 64), btok[0:tn, :], vblk[0:tn, s0:s0 + sn, :], start=True, stop=True), r=(R["btok"], R["vblk"]), w=(K.bank[4],))
                                K.op(act, lambda: A.copy(upr[:, s0:s0 + sn, :], K.pb(4, sn * 64).rearrange("p (s v) -> p s v", v=64)), r=(K.bank[4],), w=(R["upr"],))
                            po = 64 * (hh % 2)
                            ob = 2 + (hh // 2)
                            K.op(pe, lambda: TT.matmul(K.pb(ob, tn, po, po + 64, lo), vdt[0:tn, :], attsb[0:tn, 0:tn], start=True, stop=False),
                                 r=(R["vdt"], R["attsb"]), w=(K.bank[ob],), sig=False)
                            K.dma_in(sp, S0, stssm_d[0, :, hd].rearrange("s k v -> k s v"), R["S0"])
                            for s_ in range(16):
                                K.op(pe, lambda: TT.matmul(K.pb(ob, L, po, po + 64, lo + s_ * L), S0[:, s_, :], qh[:, s_ * L:(s_ + 1) * L], start=False, stop=(s_ == 15)),
                                     r=(R["S0"], R["qh"]), w=(K.bank[ob],), sig=(s_ == 15))
                            dv_ = egbc[:, L - 1:16 * L:L]
                            K.op(dve, lambda: V.tensor_tensor(S0, S0, bcast_last(dv_, 64), ALU.mult), r=(R["S0"], R["egbc"]), w=(R["S0"],))
                            K.op(dve, lambda: V.tensor_tensor(S0, S0, upr, ALU.add), r=(R["upr"], R["S0"]), w=(R["S0"],))
                            K.dma_out(sp, ossms_d[0, :, hd].rearrange("s k v -> k s v"), S0, R["S0"])
                    else:
                        gq = g_t[0:tn, tix, 4 * g:4 * g + 4]
                        dq = dt_t[0:tn, tix, 4 * g:4 * g + 4]
                        wq4 = dtw_t[0:tn, tix, 4 * g:4 * g + 4]
                        K.op(dve, lambda: V.tensor_tensor(rhsd4[0:tn, :, 0:tn], bcast_mid(ident[0:tn, 0:tn], 4), bcast_last(gq, tn), ALU.mult), r=(R["g"], R_c), w=(R["rhsd"],))
                        gb4 = K.pb(5, 512).rearrange("p (h t) -> p h t", t=128)
                        for hh in range(4):
                            K.op(pe, lambda: TT.matmul(K.pb(5, tn, 0, 128, hh * 128), ones_f[0:tn, :], rhsd4[0:tn, hh, 0:tn], start=True, stop=True), r=(R["rhsd"], R_small), w=(K.bank[5],), sig=(hh == 3))
                        K.op(act, lambda: A.activation(egbc4[:, :, 0:tn], gb4[:, :, 0:tn], AF.Exp), r=(K.bank[5],), w=(R["egbc"],))
                        K.op(dve, lambda: V.tensor_tensor(tmpd4[0:tn, :, 0:tn], gb4[0:tn, :, 0:tn], bcast_last(gq, tn), ALU.subtract), r=(K.bank[5], R["g"]), w=(R["tmpd"],))
                        K.op(pool, lambda: G.tensor_tensor(tmpd4[0:tn, :, 0:tn], tmpd4[0:tn, :, 0:tn], bcast_mid(cneg(ty, tn), 4), ALU.add), r=(R["tmpd"], R_c), w=(R["tmpd"],))
                        K.op(act, lambda: A.activation(tmpd4[0:tn, :, 0:tn], tmpd4[0:tn, :, 0:tn], AF.Exp), r=(R["tmpd"],), w=(R["tmpd"],))
                        K.op(dve, lambda: V.tensor_tensor(attsb4[0:tn, :, 0:tn], tmpd4[0:tn, :, 0:tn], bcast_mid(K.pb(7, tn, 0, tn), 4), ALU.mult), r=(K.bank[7], R["tmpd"]), w=(R["attsb"],))
                        K.op(pool, lambda: G.tensor_tensor(qh4[:, :, 0:tn], egbc4[:, :, 0:tn], bcast_mid(cv[:, 3, lo:lo + tn], 4), ALU.mult), r=(R["cv"], R["egbc"]), w=(R["qh"],))
                        x4 = xtok[0:tn, :].rearrange("p (h v) -> p h v", v=64)
                        K.op(dve, lambda: V.tensor_tensor(vdt4[0:tn], x4, bcast_last(dq, 64), ALU.mult), r=(R["xtok"], R["dt"]), w=(R["vdt"],))
                        K.op(pool, lambda: G.tensor_tensor(vpr4[0:tn], x4, bcast_last(wq4, 64), ALU.mult), r=(R["xtok"], R["dtw"]), w=(R["vpr"],))
                        if nsub == 1:
                            urhs, rur = vpr4[0:tn].rearrange("p h v -> p (h v)"), R["vpr"]
                        else:
                            vp = vpr4[0:tn]
                            pat = [list(x_) for x_ in vp.ap]
                            in0 = bass.AP(vp.tensor, vp.offset, [pat[0], pat[1], [0, nsub], pat[2]])
                            bk = cblk(ty, tn, nsub)
                            pb_ = [list(x_) for x_ in bk.ap]
                            in1 = bass.AP(bk.tensor, bk.offset, [pb_[0], [0, 4], pb_[1], [0, 64]])
                            v4 = vblk4[0:tn, :].rearrange("p (h s v) -> p h s v", s=nsub, v=64)
                            K.op(dve, lambda: V.tensor_tensor(v4, in0, in1, ALU.mult), r=(R["vpr"], R_c), w=(R["vblk"],))
                            urhs, rur = vblk4[0:tn, :], R["vblk"]
                        K.op(pe, lambda: TT.matmul(K.pb(4, 4 * nsub * 64), btok[0:tn, :], urhs, start=True, stop=True), r=(R["btok"], rur), w=(K.bank[4],))
                        K.op(act, lambda: A.copy(upr4[:, :, 0:nsub, :], K.pb(4, 4 * nsub * 64).rearrange("p (h s v) -> p h s v", s=nsub, v=64)), r=(K.bank[4],), w=(R["upr"],))
                        for hh in range(4):
                            po = 64 * (hh % 2)
                            ob = 2 + (hh // 2)
                            K.op(pe, lambda: TT.matmul(K.pb(ob, tn, po, po + 64, lo), vdt4[0:tn, hh, :], attsb4[0:tn, hh, 0:tn], start=True, stop=False),
                                 r=(R["vdt"], R["attsb"]), w=(K.bank[ob],), sig=False)
                        for s_ in range(nsub):
                            cur = sidx % 2
                            for hh in range(4):
                                po = 64 * (hh % 2)
                                ob = 2 + (hh // 2)
                                K.op(pe, lambda: TT.matmul(K.pb(ob, L, po, po + 64, lo + s_ * L), S4[cur][:, hh, :], qh4[:, hh, s_ * L:(s_ + 1) * L], start=False, stop=(s_ == nsub - 1)),
                                     r=(RS4[cur], R["qh"]), w=(K.bank[ob],), sig=(hh == 3))
                            dcl = egbc4[:, :, (s_ + 1) * L - 1:(s_ + 1) * L]
                            K.op(dve, lambda: V.tensor_tensor(S4[1 - cur], S4[cur], bcast_last(dcl, 64), ALU.mult), r=(RS4[cur], R["egbc"]), w=(RS4[1 - cur],))
                            K.op(dve, lambda: V.tensor_tensor(S4[1 - cur], S4[1 - cur], upr4[:, :, s_, :], ALU.add), r=(R["upr"], RS4[1 - cur]), w=(RS4[1 - cur],))
                            sidx += 1
                    tix += 1
                for pc in range(2):
                    cch = 2 * g + pc
                    K.op(dve, lambda: V.scalar_tensor_tensor(y2[:, pc, 0:n], cv[:, pc, 0:n], dcol[:, cch:cch + 1], K.pb(2 + pc, n), ALU.mult, ALU.add),
                         r=(R["cv"], K.bank[2 + pc], R_small), w=(R["y2"],))
                    K.op(pool, lambda: G.tensor_tensor(y2[:, pc, 0:n], y2[:, pc, 0:n], sz[:, pc, 0:n], ALU.mult), r=(R["y2"], R["sz"]), w=(R["y2"],))
                    K.op(act, lambda: A.activation(ysq[:, pc, 0:n], y2[:, pc, 0:n], AF.Square), r=(R["y2"],), w=(R["ysq"],))
                for pc in range(2):
                    K.op(pe, lambda: TT.matmul(K.pb(7, n), ones_f, ysq[:, pc, 0:n], start=(pc == 0), stop=(pc == 1)), r=(R["ysq"], R_small), w=(K.bank[7],), sig=(pc == 1))
                K.op(act, lambda: A.activation(rstd[:, 0:n], K.pb(7, n), AF.Ln, bias=EPS, scale=1.0 / 256), r=(K.bank[7],), w=(R["rstd"],))
                K.op(act, lambda: A.activation(rstd[:, 0:n], rstd[:, 0:n], AF.Exp, scale=-0.5), r=(R["rstd"],), w=(R["rstd"],))
                for pc in range(2):
                    cch = 2 * g + pc
                    ngc = vecT[:, cch % 8, V_MN + cch // 8:V_MN + cch // 8 + 1]
                    K.op(dve, lambda: V.scalar_tensor_tensor(yT[:, pc, 0:n], y2[:, pc, 0:n], ngc, rstd[:, 0:n], ALU.mult, ALU.mult), r=(R["y2"], R["rstd"], R_vt), w=(R["yT"],))
                for m in range(8):
                    b = m % 2
                    for pc in range(2):
                        K.op(pe, lambda: TT.matmul(K.pb(b, n), wo_v[:, pc, m * 128:(m + 1) * 128], yT[:, pc, 0:n], start=(pc == 0), stop=(pc == 1)),
                             r=rl_o + [R["yT"]], w=(K.bank[b],), sig=(pc == 1))
                    accumulate(m, nt, K.pb(b, n), K.bank[b])
            first_acc["v"] = False
            fin = sidx % 2
            for hh in range(4):
                K.dma_out(sp, ossmp_d[0, 4 * g + hh], S4[fin][:, hh, :], RS4[fin])
            if g < 7:
                done_unit(("mam", g, "o"))
        K.barrier(touch=(R_wq[2], R_wq[3]))
        done_unit(("mam", 7, "o"))

    rconst = host_consts()[3]
    import os
    dbg = os.environ.get("KDBG", "")
    if "novecs" not in dbg:
        load_vecs()
        K.barrier()
    if "noin" not in dbg:
        load_input()
    K.barrier()
    first_acc["v"] = True
    nsub_done = 0
    for i in range(DEPTH):
        for sub in range(3):
            if stages is not None and nsub_done >= stages:
                break
            if sub == 0:
                ffn(i, 0, ln_closures(i, 0))
            elif sub == 1:
                kind = i % 3
                if kind == 0:
                    hgrn(i // 3)
                elif kind == 1:
                    retention(rconst)
                else:
                    mamba()
                layernorm(i, 1)
            else:
                ffn(i, 1, ln_closures(i, 2))
            nsub_done += 1
    K.barrier()
    if "noout" not in dbg:
        store_output()
    K.finish()
    return nc


_CACHE = {}


def kernel(x_prompt, x_sample, state_hgrn, state_ret, state_ssm, state_conv, meta_tokens, ln_g, ln_b,
           ffn_w_gate, ffn_w_up, ffn_w_down, hg_lb_logits, hg_w_in, hg_norm_g, hg_w_o,
           ret_w_in, ret_norm_g, ret_w_o, m_w_in, m_conv_w, m_conv_b, m_dt_bias, m_a_log, m_d,
           m_norm_g, m_w_o):
    f = lambda a: np.ascontiguousarray(np.asarray(a, dtype=np.float32))
    x_prompt, x_sample, meta_tokens = f(x_prompt), f(x_sample), f(meta_tokens)
    c128, cs, rett, _ = host_consts()
    vecs = np.zeros((64, D), np.float32)
    vecs[V_LNG:V_LNG + 12] = f(ln_g).reshape(12, D)
    vecs[V_LNB:V_LNB + 12] = f(ln_b).reshape(12, D)
    vecs[V_LB:V_LB + 4] = f(hg_lb_logits)
    vecs[V_HGN:V_HGN + 2] = f(hg_norm_g)
    vecs[V_RETN:V_RETN + 2] = f(ret_norm_g).reshape(2, D)
    vecs[V_CW:V_CW + 16] = f(m_conv_w).reshape(16, D)
    vecs[V_CB:V_CB + 4] = f(m_conv_b).reshape(4, D)
    vecs[V_MN:V_MN + 2] = f(m_norm_g).reshape(2, D)
    mhead = np.stack([f(m_dt_bias)[0], f(m_a_log)[0], f(m_d)[0]], axis=0)
    shared = {
        "c128": c128, "cs": cs, "rett": rett, "vecs": vecs, "mhead": mhead,
        "ffn_w_gate": f(ffn_w_gate), "ffn_w_up": f(ffn_w_up), "ffn_w_down": f(ffn_w_down),
        "hg_w_in": f(hg_w_in), "hg_w_o": f(hg_w_o), "ret_w_in": f(ret_w_in), "ret_w_o": f(ret_w_o),
        "m_w_in": f(m_w_in), "m_w_o": f(m_w_o),
    }
    state_hgrn, state_ret, state_ssm, state_conv = f(state_hgrn), f(state_ret), f(state_ssm), f(state_conv)
    in_maps = []
    for c in range(NCORES):
        sl = slice(16 * c, 16 * c + 16)
        xin = np.concatenate([x_sample[sl].reshape(128, D), meta_tokens, x_prompt[c]], axis=0)
        m = dict(shared)
        m["xin"] = np.ascontiguousarray(xin)
        m["st_hg"] = np.ascontiguousarray(state_hgrn[:, sl])
        m["st_ret"] = np.ascontiguousarray(state_ret[:, sl])
        m["st_ssm"] = np.ascontiguousarray(state_ssm[:, sl])
        m["st_conv"] = np.ascontiguousarray(state_conv[:, sl])
        in_maps.append(m)
    if "nc" not in _CACHE:
        _CACHE["nc"] = build_program()
    res = run_bass_kernel_spmd(_CACHE["nc"], in_maps, core_ids=list(range(NCORES)))
    rs = res.results
    y = [r["y"] for r in rs]
    y_prompt = np.stack([yy[144:] for yy in y], axis=0)
    y_sample = np.concatenate([yy[0:128].reshape(16, 8, D) for yy in y], axis=0)
    cat1 = lambda k: np.concatenate([r[k] for r in rs], axis=1)
    stk1 = lambda k: np.stack([r[k] for r in rs], axis=1)
    return (y_prompt.astype(np.float32), y_sample.astype(np.float32),
            stk1("o_hg_p"), cat1("o_hg_s"), stk1("o_ret_p"), cat1("o_ret_s"),
            stk1("o_ssm_p"), cat1("o_ssm_s"), stk1("o_conv_p"), cat1("o_conv_s"))
```

```python
import math
import numpy as np
import ml_dtypes
import concourse.bass as bass
import concourse.mybir as mybir
from concourse.bass_utils import run_bass_kernel_spmd

F32 = mybir.dt.float32
BF16 = mybir.dt.bfloat16
AF = mybir.ActivationFunctionType
ALU = mybir.AluOpType

NCORES = 8
D = 1024
DFF = 2816
DEPTH = 4
T = 2192
NTILES = [(0, 144)] + [(144 + 512 * i, 512) for i in range(4)]
ALPHA = (2 * DEPTH) ** 0.25
EPS = 1e-5
TYPE_S, TYPE_M, TYPE_P = 0, 1, 2


def token_tiles(nt):
    if nt == 0:
        return [(TYPE_S, 0, 128, 16, 8), (TYPE_M, 128, 16, 1, 16)]
    c0 = NTILES[nt][0]
    return [(TYPE_P, c0 + 128 * r, 128, 2, 64) for r in range(4)]


C_ID = 0
C_MASK = 128
C_NEG = C_MASK + 384
C_LTRI = C_NEG + 384
C_BLK = C_LTRI + 384
C_RM = C_BLK + 48
C_SEL = C_RM + 656
C_END = C_SEL + 48

V_LNG, V_LNB, V_LB, V_HGN, V_RETN, V_CW, V_CB, V_MN = 0, 12, 24, 28, 30, 32, 48, 52
V_ROWS = 54


def host_consts():
    c = np.zeros((128, C_END), np.float32)
    c[:, C_ID:C_ID + 128] = np.eye(128, dtype=np.float32)
    j = np.arange(128)[:, None]
    i = np.arange(128)[None, :]
    for ty, L, n in ((TYPE_S, 8, 128), (TYPE_M, 16, 16), (TYPE_P, 64, 128)):
        same = (j // L == i // L) & (j < n) & (i < n)
        m = (same & (j <= i)).astype(np.float32)
        c[:, C_MASK + 128 * ty:C_MASK + 128 * ty + 128] = m
        c[:, C_NEG + 128 * ty:C_NEG + 128 * ty + 128] = (m - 1.0) * 30000.0
        c[:, C_LTRI + 128 * ty:C_LTRI + 128 * ty + 128] = same.astype(np.float32)
        s = np.arange(16)[None, :]
        c[:, C_BLK + 16 * ty:C_BLK + 16 * ty + 16] = ((j // L == s) & (j < n)).astype(np.float32)
    rm = np.ones(656, np.float32)
    rm[0:128:8] = 0.0
    rm[128] = 0.0
    rm[144::64] = 0.0
    c[:, C_RM:C_RM + 656] = rm[None, :]
    for s_ in range(16):
        for r_ in range(3):
            c[s_ * 8 + 5 + r_, C_SEL + s_ * 3 + r_] = 1.0
    half = 128
    inv_freq = (np.float32(10000.0) ** (-np.arange(half, dtype=np.float32) / np.float32(half))).astype(np.float32)
    pos = np.zeros(T, np.float32)
    pos[0:128] = 16384 + (np.arange(128) % 8)
    pos[128:144] = np.arange(16)
    pos[144:] = 16 + np.arange(2048)
    ang = (pos[None, :].astype(np.float32) * inv_freq[:, None]).astype(np.float32)
    cs = np.stack([np.cos(ang), np.sin(ang)], axis=1).astype(np.float32)
    pidx = np.zeros(208, np.float64)
    pidx[0:128] = np.arange(128) % 8
    pidx[128:144] = np.arange(16)
    pidx[144:208] = np.arange(64)
    ret = np.zeros((128, 4, 2, 208), np.float32)
    gam = []
    for h in range(4):
        lg = math.log(1.0 - 2.0 ** (-5.0 - h))
        gam.append(lg)
        g = (pidx + 1.0) * lg
        ret[:, h, 0, :] = np.exp(g)[None, :]
        ret[:, h, 1, :] = (np.exp(-g) / 16.0)[None, :]
    return c, cs, ret, gam


class Res:
    __slots__ = ("name", "w", "rd", "dsem", "dcnt", "excl")

    def __init__(self, name, excl=False):
        self.name = name
        self.excl = excl
        self.w = None
        self.rd = {}
        self.dsem = None
        self.dcnt = 0


class Eng:
    def __init__(self, name, eng, sem):
        self.name, self.eng, self.sem = name, eng, sem
        self.cnt = 0
        self.seen = {}


class Builder:
    def __init__(self):
        nc = bass.Bass("TRN2", target_bir_lowering=False)
        self.nc = nc
        self.pe = Eng("pe", nc.tensor, nc.semaphore("s_pe").__enter__())
        self.dve = Eng("dve", nc.vector, nc.semaphore("s_dve").__enter__())
        self.act = Eng("act", nc.scalar, nc.semaphore("s_act").__enter__())
        self.pool = Eng("pool", nc.gpsimd, nc.semaphore("s_pool").__enter__())
        self.sp = Eng("sp", nc.sync, None)
        self.compute = [self.pe, self.dve, self.act, self.pool]
        self.nsem = 4
        self.out_waits = []
        self.dma_tags = {}
        self.semcache = {}
        self.arena = nc.alloc_sbuf_tensor("arena", [128, 53000], F32)
        self.aoff = 0
        self.psum = nc.alloc_psum_tensor("psum", [128, 4096], F32)
        self.bank = [Res("bank%d" % b, excl=True) for b in range(8)]

    def alloc(self, nfloats):
        o = self.aoff
        self.aoff += nfloats
        assert self.aoff <= 53000, self.aoff
        return o

    def view(self, off, shape, dt=F32):
        n = int(np.prod(shape))
        if dt == F32:
            a = self.arena[:, off:off + n]
        else:
            a = self.arena[:, off:off + (n + 1) // 2].bitcast(BF16)
        if len(shape) == 2:
            return a.rearrange("p (a b) -> p a b", b=shape[1])
        if len(shape) == 3:
            return a.rearrange("p (a b c) -> p a b c", b=shape[1], c=shape[2])
        return a

    def pb(self, b, n=512, p0=0, p1=128, c0=0):
        return self.psum[p0:p1, b * 512 + c0:b * 512 + c0 + n]

    def _wait(self, E, tag):
        key, sem, val = tag
        if E is self.pe and key == "pe":
            return
        if E.seen.get(key, 0) >= val:
            return
        E.eng.wait_ge(sem, val)
        E.seen[key] = val

    def _deps(self, E, reads, writes):
        for r in reads:
            if r.w is not None:
                self._wait(E, r.w)
            if r.excl:
                for key, tag in list(r.rd.items()):
                    if key != E.name:
                        self._wait(E, tag)
        for w in writes:
            if w.w is not None:
                self._wait(E, w.w)
            for tag in list(w.rd.values()):
                self._wait(E, tag)

    def op(self, E, emit, r=(), w=(), sig=True):
        self._deps(E, r, w)
        ins = emit()
        if sig:
            E.cnt += 1
            ins.then_inc(E.sem, 1)
            tag = (E.name, E.sem, E.cnt)
        else:
            tag = (E.name, E.sem, E.cnt + 1)
        for x in r:
            o = x.rd.get(E.name)
            if o is None or o[2] < tag[2]:
                x.rd[E.name] = tag
        for x in w:
            x.w = tag
            x.rd = {}
        return ins

    def _dsem(self, res):
        if res.dsem is None:
            ent = self.semcache.get(res.name)
            if ent is None:
                ent = [self.nc.semaphore("d_" + res.name).__enter__(), 0]
                self.semcache[res.name] = ent
                self.nsem += 1
            res.dsem = ent[0]
            res.dcnt = ent[1]
        return res.dsem

    def dma_in(self, Q, out_ap, in_ap, res, **kw):
        self._deps(Q, (), (res,))
        sem = self._dsem(res)
        Q.eng.dma_start(out=out_ap, in_=in_ap, **kw).then_inc(sem, 16)
        res.dcnt += 16
        self.semcache[res.name][1] = res.dcnt
        res.w = ("d_" + res.name, sem, res.dcnt)
        res.rd = {}
        self.dma_tags[res.w[0]] = res.w

    def dma_out(self, Q, out_ap, in_ap, res, final=True, **kw):
        self._deps(Q, (res,), ())
        sem = self._dsem(res)
        Q.eng.dma_start(out=out_ap, in_=in_ap, **kw).then_inc(sem, 16)
        res.dcnt += 16
        self.semcache[res.name][1] = res.dcnt
        tag = ("d_" + res.name, sem, res.dcnt)
        res.rd["dma"] = tag
        self.out_waits.append(tag)
        self.dma_tags[tag[0]] = tag

    def barrier(self, touch=()):
        for E in self.compute + [self.sp]:
            for F in self.compute:
                if E is not F and F.cnt > 0:
                    self._wait(E, (F.name, F.sem, F.cnt))
            for tag in self.dma_tags.values():
                self._wait(E, tag)
        for res in touch:
            for F in self.compute:
                if F.cnt > 0:
                    res.rd[F.name] = (F.name, F.sem, F.cnt)
            for tag in self.dma_tags.values():
                res.rd[tag[0]] = tag
        self.dma_tags = {}

    def finish(self):
        for E in self.compute:
            if E.cnt > 0:
                self._wait(self.sp, (E.name, E.sem, E.cnt))
        last = {}
        for key, sem, val in self.out_waits:
            if key not in last or last[key][1] < val:
                last[key] = (sem, val)
        for key, (sem, val) in last.items():
            self.sp.eng.wait_ge(sem, val)
            self.act.eng.wait_ge(sem, val)


def bcast_mid(ap, n):
    pat = [list(x) for x in ap.ap]
    return bass.AP(ap.tensor, ap.offset, [pat[0], [0, n]] + pat[1:])


def bcast_last(ap, n):
    pat = [list(x) for x in ap.ap]
    if len(pat) == 3:
        pat = pat[:2]
    return bass.AP(ap.tensor, ap.offset, pat + [[0, n]])


def build_program(stages=None):
    K = Builder()
    nc = K.nc
    pe, dve, act, pool, sp = K.pe, K.dve, K.act, K.pool, K.sp
    TT = nc.tensor
    V = nc.vector
    A = nc.scalar
    G = nc.gpsimd

    def din(name, shape, dt=F32):
        return nc.dram_tensor(name, list(shape), dt, kind="ExternalInput").ap()

    def dout(name, shape):
        return nc.dram_tensor(name, list(shape), F32, kind="ExternalOutput").ap()

    xin = din("xin", [T, D])
    c128_d = din("c128", [128, C_END])
    cs_d = din("cs", [128, 2, T])
    rett_d = din("rett", [128, 4, 2, 208])
    vecs_d = din("vecs", [64, D])
    mhead_d = din("mhead", [3, 32])
    wg_d = din("ffn_w_gate", [DEPTH, 2, D, DFF])
    wu_d = din("ffn_w_up", [DEPTH, 2, D, DFF])
    wd_d = din("ffn_w_down", [DEPTH, 2, DFF, D])
    hgwi_d = din("hg_w_in", [2, D, 4096])
    hgwo_d = din("hg_w_o", [2, D, D])
    rwi_d = din("ret_w_in", [1, D, 6144])
    rwo_d = din("ret_w_o", [1, 2048, D])
    mwi_d = din("m_w_in", [1, D, 6176])
    mwo_d = din("m_w_o", [1, 2048, D])
    sthg_d = din("st_hg", [2, 16, 8, 128, 128])
    stret_d = din("st_ret", [1, 16, 4, 256, 512])
    stssm_d = din("st_ssm", [1, 16, 32, 128, 64])
    stconv_d = din("st_conv", [1, 16, 3, 4096])

    y_d = dout("y", [T, D])
    ohgp_d = dout("o_hg_p", [2, 8, 128, 128])
    ohgs_d = dout("o_hg_s", [2, 16, 8, 128, 128])
    oretp_d = dout("o_ret_p", [1, 4, 256, 512])
    orets_d = dout("o_ret_s", [1, 16, 4, 256, 512])
    ossmp_d = dout("o_ssm_p", [1, 32, 128, 64])
    ossms_d = dout("o_ssm_s", [1, 16, 32, 128, 64])
    oconvp_d = dout("o_conv_p", [1, 3, 4096])
    oconvs_d = dout("o_conv_s", [1, 16, 3, 4096])

    o_hf = K.alloc(8 * T)
    o_hb = K.alloc(8 * T // 2)
    o_c = K.alloc(C_END)
    o_vt = K.alloc(8 * 64)
    o_small = K.alloc(512)
    o_wq = [K.alloc(3072) for _ in range(4)]
    o_scr = K.alloc(0)
    SCR_END = 53000

    hf = K.view(o_hf, [8, T])
    hb = K.view(o_hb, [8, T], BF16)
    c128 = K.arena[:, o_c:o_c + C_END]
    vecT = K.view(o_vt, [8, 64])
    small = K.arena[:, o_small:o_small + 512]
    R_hf = [[Res("hf%d_%d" % (c, n)) for n in range(5)] for c in range(8)]
    R_hb = [Res("hb%d" % n) for n in range(5)]
    R_c = Res("consts")
    R_vt = Res("vecT")
    R_small = Res("small")
    R_wq = [Res("wq%d" % i) for i in range(4)]

    ident = c128[:, C_ID:C_ID + 128]

    def cmask(ty, n):
        return c128[0:n, C_MASK + 128 * ty:C_MASK + 128 * ty + n]

    def cneg(ty, n):
        return c128[0:n, C_NEG + 128 * ty:C_NEG + 128 * ty + n]

    def cltri(ty, n):
        return c128[0:n, C_LTRI + 128 * ty:C_LTRI + 128 * ty + n]

    def cblk(ty, n, nsub):
        return c128[0:n, C_BLK + 16 * ty:C_BLK + 16 * ty + nsub]

    ones_f = small[:, 0:128]
    ones_b = small[:, 128:192].bitcast(BF16)
    lbcol = small[:, 192:208].rearrange("p (j h) -> p j h", h=8)
    omlcol = small[:, 208:224].rearrange("p (j h) -> p j h", h=8)
    nomlcol = small[:, 224:240].rearrange("p (j h) -> p j h", h=8)
    mh_bc = small[:, 240:336].rearrange("p (r h) -> p r h", h=32)
    negA = small[:, 336:368]
    dcol = small[:, 368:384]
    lbtmp = small[:, 384:448]

    K.dma_in(sp, c128, c128_d, R_c)
    K.op(dve, lambda: V.memset(ones_f, 1.0), w=(R_small,))
    K.op(dve, lambda: V.memset(ones_b, 1.0), w=(R_small,))
    import os as _os
    if "nomh" not in _os.environ.get("KDBG", ""):
        K.dma_in(sp, mh_bc, bass.AP(mhead_d.tensor, 0, [[0, 128], [32, 3], [1, 32]]), R_small)
        with nc.allow_non_contiguous_dma(reason="tiny const"):
            K.dma_in(sp, dcol[0:64, :], bass.AP(mhead_d.tensor, 64, [[0, 64], [2, 16]]), R_small)
            K.dma_in(sp, dcol[64:128, :], bass.AP(mhead_d.tensor, 65, [[0, 64], [2, 16]]), R_small)

    class Scr:
        def __init__(self, extra=False):
            self.off = o_scr
            self.end = SCR_END
            self.extra = [o_wq[2], o_wq[2] + 6144] if extra else None

        def get(self, n):
            if self.off + n <= self.end:
                o = self.off
                self.off += n
                return o
            assert self.extra is not None and self.extra[0] + n <= self.extra[1], "scratch overflow"
            o = self.extra[0]
            self.extra[0] += n
            return o

    units = []
    wstate = {"next": 0}

    def emit_loads(upto):
        while wstate["next"] <= min(upto, len(units) - 1):
            u = units[wstate["next"]]
            for res_list, dst, src in u:
                K._deps(pool, (), res_list)
                sem = K._dsem(res_list[0])
                G.dma_start(out=dst, in_=src).then_inc(sem, 16)
                res_list[0].dcnt += 16
                K.semcache[res_list[0].name][1] = res_list[0].dcnt
                tag = ("d_" + res_list[0].name, sem, res_list[0].dcnt)
                for rr in res_list:
                    rr.w = tag
                    rr.rd = {}
            wstate["next"] += 1

    def wview(slot_floats_off, shape):
        return K.view(slot_floats_off, shape, BF16)

    plan = []
    for i in range(DEPTH):
        plan.append(("ffn", i, 0))
        kind = i % 3
        plan.append((("hg", i // 3), ("ret", 0), ("mam", 0))[kind])
        plan.append(("ffn", i, 1))

    ffn_groups = [(4 * g, 4) for g in range(5)] + [(20, 2)]
    unit_index = {}
    big = [0]

    def add_unit(key, entries):
        unit_index[key] = len(units)
        units.append(entries)

    for ph in plan:
        if ph[0] == "ffn":
            _, i, s = ph
            for gi, (j0, gn) in enumerate(ffn_groups):
                slot = big[0] % 2
                big[0] += 1
                base = o_wq[2 * slot]
                rl = [R_wq[2 * slot], R_wq[2 * slot + 1]]
                wg_v = wview(base, [8, 512])
                wu_v = wview(base + 2048, [8, 512])
                wd_v = wview(base + 4096, [4, 1024])
                ent = [
                    (rl, wg_v[:, :, 0:gn * 128], wg_d[i, s].rearrange("(k p) n -> p k n", p=128)[:, :, j0 * 128:(j0 + gn) * 128]),
                    (rl, wu_v[:, :, 0:gn * 128], wu_d[i, s].rearrange("(k p) n -> p k n", p=128)[:, :, j0 * 128:(j0 + gn) * 128]),
                    (rl, wd_v[:, 0:gn, :], wd_d[i, s].rearrange("(j p) n -> p j n", p=128)[:, j0:j0 + gn, :]),
                ]
                add_unit(("ffn", i, s, gi), ent)
            if big[0] % 2 == 1:
                pass
        else:
            msl = [0]

            def mslot():
                s_ = msl[0] % 2
                msl[0] += 1
                return o_wq[s_], [R_wq[s_]]

            if ph[0] == "hg":
                j = ph[1]
                wsrc = hgwi_d[j].rearrange("(k p) n -> p k n", p=128)
                for h in range(8):
                    base, rl = mslot()
                    wi_v = wview(base, [8, 512])
                    wo_v = wview(base + 2048, [1024])
                    ent = []
                    for b4 in range(4):
                        ent.append((rl, wi_v[:, :, b4 * 128:(b4 + 1) * 128], wsrc[:, :, b4 * 1024 + h * 128:b4 * 1024 + (h + 1) * 128]))
                    ent.append((rl, wo_v, hgwo_d[j, h * 128:(h + 1) * 128, :]))
                    add_unit(("hg", j, h), ent)
            elif ph[0] == "ret":
                wsrc = rwi_d[0].rearrange("(k p) n -> p k n", p=128)
                for h in range(4):
                    v_ = wview(o_wq[3], [8, 512])
                    add_unit(("ret", h, "qk"), [
                        ([R_wq[3]], v_[:, :, 0:256], wsrc[:, :, h * 256:(h + 1) * 256]),
                        ([R_wq[3]], v_[:, :, 256:512], wsrc[:, :, 1024 + h * 256:1024 + (h + 1) * 256])])
                    v_ = wview(o_wq[1], [8, 512])
                    add_unit(("ret", h, "v"), [([R_wq[1]], v_, wsrc[:, :, 2048 + h * 512:2048 + (h + 1) * 512])])
                    v_ = wview(o_wq[2], [8, 512])
                    add_unit(("ret", h, "g"), [([R_wq[2]], v_, wsrc[:, :, 4096 + h * 512:4096 + (h + 1) * 512])])
                    v_ = wview(o_wq[0], [4, 1024])
                    add_unit(("ret", h, "o"), [([R_wq[0]], v_, rwo_d[0, h * 512:(h + 1) * 512, :].rearrange("(c p) n -> p c n", p=128))])
            else:
                wsrc = mwi_d[0].rearrange("(k p) n -> p k n", p=128)
                base, rl = mslot()
                v_ = wview(base, [8, 32])
                add_unit(("mam", "dt"), [(rl, v_, wsrc[:, :, 6144:6176])])
                for g in range(8):
                    base, rl = mslot()
                    v_ = wview(base, [8, 768])
                    ent = [
                        (rl, v_[:, :, 0:256], wsrc[:, :, g * 256:(g + 1) * 256]),
                        (rl, v_[:, :, 256:512], wsrc[:, :, 2048 + g * 256:2048 + (g + 1) * 256]),
                        (rl, v_[:, :, 512:640], wsrc[:, :, 4096 + g * 128:4096 + (g + 1) * 128]),
                        (rl, v_[:, :, 640:768], wsrc[:, :, 5120 + g * 128:5120 + (g + 1) * 128]),
                    ]
                    add_unit(("mam", g, "in"), ent)
                    base, rl = mslot()
                    v_ = wview(base, [2, 1024])
                    add_unit(("mam", g, "o"), [(rl, v_, mwo_d[0, g * 256:(g + 1) * 256, :].rearrange("(c p) n -> p c n", p=128))])

    def use_unit(key):
        idx = unit_index[key]
        emit_loads(idx)
        return units[idx]

    def done_unit(key):
        emit_loads(unit_index[key] + 1)

    first_acc = {"v": True}

    def accumulate(m, nt, ps_ap, bank_res):
        c0, n = NTILES[nt]
        dst = hf[:, m, c0:c0 + n]
        if first_acc["v"]:
            K.op(dve, lambda: V.scalar_tensor_tensor(dst, dst, ALPHA, ps_ap, ALU.mult, ALU.add),
                 r=(bank_res, R_hf[m][nt]), w=(R_hf[m][nt],))
        else:
            K.op(dve, lambda: V.tensor_tensor(dst, dst, ps_ap, ALU.add),
                 r=(bank_res, R_hf[m][nt]), w=(R_hf[m][nt],))

    def load_vecs():
        scr = Scr()
        o = scr.get(1024)
        vtok = K.arena[0:64, o:o + 1024]
        R = Res("vecstage")
        K.dma_in(sp, vtok, vecs_d, R)
        for c in range(8):
            ps = K.pb(c % 4, 64)
            K.op(pe, lambda: TT.transpose(ps, vtok[:, c * 128:(c + 1) * 128], ident[0:64, 0:64]),
                 r=(R, R_c), w=(K.bank[c % 4],))
            K.op(act, lambda: A.copy(vecT[:, c, :], ps), r=(K.bank[c % 4],), w=(R_vt,))
        lg = vecT[:, :, V_LB:V_LB + 4]
        mx = lbtmp[:, 0:8]
        ex = lbtmp[:, 8:40].rearrange("p (h d) -> p h d", d=4)
        sm = lbtmp[:, 40:48]
        K.op(dve, lambda: V.tensor_reduce(mx, lg, mybir.AxisListType.X, ALU.max), r=(R_vt,), w=(R_small,))
        K.op(dve, lambda: V.tensor_tensor(ex, lg, bcast_last(mx, 4), ALU.subtract), r=(R_vt, R_small), w=(R_small,))
        K.op(act, lambda: A.activation(ex, ex, AF.Exp), r=(R_small,), w=(R_small,))
        K.op(dve, lambda: V.tensor_reduce(sm, ex, mybir.AxisListType.X, ALU.add), r=(R_small,), w=(R_small,))
        K.op(dve, lambda: V.reciprocal(sm, sm), r=(R_small,), w=(R_small,))
        K.op(dve, lambda: V.memset(lbcol[:, 0, :], 0.0), w=(R_small,))
        t3 = lbtmp[:, 48:56]
        K.op(dve, lambda: V.tensor_tensor(t3, ex[:, :, 1], ex[:, :, 2], ALU.add), r=(R_small,), w=(R_small,))
        K.op(dve, lambda: V.tensor_tensor(t3, t3, ex[:, :, 3], ALU.add), r=(R_small,), w=(R_small,))
        K.op(dve, lambda: V.tensor_tensor(lbcol[:, 1, :], t3, sm, ALU.mult), r=(R_small,), w=(R_small,))
        lb_all = small[:, 192:208]
        K.op(dve, lambda: V.tensor_scalar(small[:, 208:224], lb_all, -1.0, 1.0, ALU.mult, ALU.add), r=(R_small,), w=(R_small,))
        K.op(dve, lambda: V.tensor_scalar(small[:, 224:240], lb_all, 1.0, -1.0, ALU.mult, ALU.add), r=(R_small,), w=(R_small,))
        K.op(act, lambda: A.activation(negA, mh_bc[:, 1, :], AF.Exp), r=(R_small,), w=(R_small,))
        K.op(dve, lambda: V.tensor_scalar(negA, negA, -1.0, None, ALU.mult), r=(R_small,), w=(R_small,))

    def load_input():
        scr = Scr()
        xs = [K.arena[:, o:o + 1024] for o in (scr.get(1024), scr.get(1024))]
        Rx = [Res("xs0"), Res("xs1")]
        tiles = [(0, 128), (128, 16)] + [(144 + 128 * r, 128) for r in range(16)]
        import os
        if "nometa" in os.environ.get("KDBG", ""):
            tiles = [t_ for t_ in tiles if t_[1] == 128]
        if "ntiles" in os.environ.get("KDBG", ""):
            tiles = tiles[:int(os.environ["KNT"])]
        for ti, (c0, n) in enumerate(tiles):
            s = ti % 2
            K.dma_in(sp, xs[s][0:n, :], xin[c0:c0 + n, :], Rx[s])
            nt = 0 if c0 < 144 else 1 + (c0 - 144) // 512
            for half in range(2):
                b = (2 * ti + half) % 4
                ps = K.psum[:, b * 512:b * 512 + 512].rearrange("p (c n) -> p c n", c=4)
                for cc in range(4):
                    c = half * 4 + cc
                    K.op(pe, lambda: TT.transpose(ps[:, cc, 0:n], xs[s][0:n, c * 128:(c + 1) * 128], ident[0:n, 0:n]),
                         r=(Rx[s], R_c), w=(K.bank[b],), sig=(cc == 3))
                wr = [R_hf[half * 4 + cc][nt] for cc in range(4)]
                if "nohf" not in os.environ.get("KDBG", ""):
                    K.op(act, lambda: A.copy(hf[:, half * 4:half * 4 + 4, c0:c0 + n], ps[:, :, 0:n]), r=(K.bank[b],), w=wr)
                if "nohb" not in os.environ.get("KDBG", ""):
                    K.op(dve, lambda: V.tensor_copy(hb[:, half * 4:half * 4 + 4, c0:c0 + n], ps[:, :, 0:n]), r=(K.bank[b],), w=(R_hb[nt],))

    def store_output():
        scr = Scr()
        ys = [K.arena[:, o:o + 1024] for o in (scr.get(1024), scr.get(1024))]
        Ry = [Res("ys0"), Res("ys1")]
        tiles = [(0, 128), (128, 16)] + [(144 + 128 * r, 128) for r in range(16)]
        for ti, (c0, n) in enumerate(tiles):
            s = ti % 2
            nt = 0 if c0 < 144 else 1 + (c0 - 144) // 512
            for half in range(2):
                b = (2 * ti + half) % 4
                ps = K.psum[:, b * 512:b * 512 + 512]
                for cc in range(4):
                    c = half * 4 + cc
                    K.op(pe, lambda: TT.transpose(ps[0:n, cc * 128:(cc + 1) * 128], hf[:, c, c0:c0 + n], ident),
                         r=(R_hf[c][nt], R_c), w=(K.bank[b],), sig=(cc == 3))
                if half == 0:
                    K.op(act, lambda: A.copy(ys[s][0:n, 0:512], ps[0:n, :]), r=(K.bank[b],), w=(Ry[s],))
                else:
                    K.op(dve, lambda: V.tensor_copy(ys[s][0:n, 512:1024], ps[0:n, :]), r=(K.bank[b],), w=(Ry[s],))
            K.dma_out(sp, y_d[c0:c0 + n, :], ys[s][0:n, :], Ry[s])

    LN_OFF = o_scr + 3072
    ln_xb = K.view(LN_OFF, [8, 512], BF16)
    ln_sq = K.view(LN_OFF + 2048, [8, 512], BF16)
    ln_st = [K.view(LN_OFF + 4096, [4, 512]), K.view(LN_OFF + 6144, [4, 512])]
    LN_R = {"xb": Res("ln_xb"), "sq": Res("ln_sq"), "st": [Res("ln_st0"), Res("ln_st1")]}

    def ln_closures(li, lj):
        xb, sq = ln_xb, ln_sq
        Rxb, Rsq = LN_R["xb"], LN_R["sq"]
        row = li * 3 + lj

        def stats(nt):
            c0, n = NTILES[nt]
            st, Rst = ln_st[nt % 2], LN_R["st"][nt % 2]
            hft = hf[:, :, c0:c0 + n]
            rall = [R_hf[c][nt] for c in range(8)]
            K.op(pool, lambda: G.tensor_copy(xb[:, :, 0:n], hft), r=rall, w=(Rxb,))
            K.op(act, lambda: A.activation(sq[:, :, 0:n], hft, AF.Square), r=rall, w=(Rsq,))
            for c in range(8):
                K.op(pe, lambda: TT.matmul(K.pb(0, n), ones_b, xb[:, c, 0:n], start=(c == 0), stop=(c == 7)),
                     r=(Rxb, R_small), w=(K.bank[0],), sig=(c == 7))
            for c in range(8):
                K.op(pe, lambda: TT.matmul(K.pb(1, n), ones_b, sq[:, c, 0:n], start=(c == 0), stop=(c == 7)),
                     r=(Rsq, R_small), w=(K.bank[1],), sig=(c == 7))
            mean, var, rstd, nmr = (st[:, q, 0:n] for q in range(4))
            K.op(dve, lambda: V.tensor_scalar(mean, K.pb(0, n), 1.0 / D, None, ALU.mult), r=(K.bank[0],), w=(Rst,))
            K.op(dve, lambda: V.tensor_tensor(var, mean, mean, ALU.mult), r=(Rst,), w=(Rst,))
            K.op(dve, lambda: V.scalar_tensor_tensor(var, K.pb(1, n), 1.0 / D, var, ALU.mult, ALU.subtract), r=(K.bank[1], Rst), w=(Rst,))
            K.op(act, lambda: A.activation(rstd, var, AF.Ln, bias=EPS), r=(Rst,), w=(Rst,))
            K.op(act, lambda: A.activation(rstd, rstd, AF.Exp, scale=-0.5), r=(Rst,), w=(Rst,))
            K.op(dve, lambda: V.scalar_tensor_tensor(nmr, mean, -1.0, rstd, ALU.mult, ALU.mult), r=(Rst,), w=(Rst,))

        def apply(nt):
            c0, n = NTILES[nt]
            st, Rst = ln_st[nt % 2], LN_R["st"][nt % 2]
            hft = hf[:, :, c0:c0 + n]
            rall = [R_hf[c][nt] for c in range(8)]
            rstd, nmr = st[:, 2, 0:n], st[:, 3, 0:n]
            K.op(dve, lambda: V.tensor_tensor(hft, hft, bcast_mid(rstd, 8), ALU.mult), r=rall + [Rst], w=rall)
            K.op(pool, lambda: G.tensor_tensor(hft, hft, bcast_mid(nmr, 8), ALU.add), r=rall + [Rst], w=rall)
            for c in range(8):
                K.op(act, lambda: A.activation(hf[:, c, c0:c0 + n], hf[:, c, c0:c0 + n], AF.Identity,
                                               bias=vecT[:, c, V_LNB + row:V_LNB + row + 1],
                                               scale=vecT[:, c, V_LNG + row:V_LNG + row + 1]),
                     r=(R_hf[c][nt], R_vt), w=(R_hf[c][nt],))
            K.op(dve, lambda: V.tensor_copy(hb[:, :, c0:c0 + n], hft), r=rall, w=(R_hb[nt],))

        return stats, apply

    def layernorm(li, lj):
        stats, apply = ln_closures(li, lj)
        stats(0)
        for nt in range(5):
            if nt + 1 < 5:
                stats(nt + 1)
            apply(nt)
        first_acc["v"] = True

    ffn_actb = [K.view(o_scr + o, [4, 512], BF16) for o in (0, 1024)]
    ffn_sgb = [K.arena[:, o_scr + o:o_scr + o + 512] for o in (2048, 2560)]
    FFN_R = {"act": [Res("act0"), Res("act1")], "sg": [Res("sg0"), Res("sg1")]}

    def ffn(i, s, ln=None):
        actb, sgb = ffn_actb, ffn_sgb
        Ract, Rsg = FFN_R["act"], FFN_R["sg"]
        cnt = {"gu": 0, "y": 0, "a": 0}
        for gi, (j0, gn) in enumerate(ffn_groups):
            u = use_unit(("ffn", i, s, gi))
            rl = u[0][0]
            wg_v, wu_v, wd_v = u[0][1], u[1][1], u[2][1]
            for nt, (c0, n) in enumerate(NTILES):
                a = cnt["a"] % 2
                cnt["a"] += 1
                for jj in range(gn):
                    p = cnt["gu"] % 2
                    cnt["gu"] += 1
                    bg, bu = p, 2 + p
                    for k in range(8):
                        K.op(pe, lambda: TT.matmul(K.pb(bg, n), wg_v[:, k, jj * 128:(jj + 1) * 128], hb[:, k, c0:c0 + n], start=(k == 0), stop=(k == 7)),
                             r=rl + [R_hb[nt]], w=(K.bank[bg],), sig=(k == 7))
                    for k in range(8):
                        K.op(pe, lambda: TT.matmul(K.pb(bu, n), wu_v[:, k, jj * 128:(jj + 1) * 128], hb[:, k, c0:c0 + n], start=(k == 0), stop=(k == 7)),
                             r=rl + [R_hb[nt]], w=(K.bank[bu],), sig=(k == 7))
                    K.op(act, lambda: A.activation(sgb[p][:, 0:n], K.pb(bg, n), AF.Silu), r=(K.bank[bg],), w=(Rsg[p],))
                    K.op(dve, lambda: V.scalar_tensor_tensor(actb[a][:, jj, 0:n], K.pb(bu, n), 0.5, sgb[p][:, 0:n], ALU.mult, ALU.mult),
                         r=(K.bank[bu], Rsg[p]), w=(Ract[a],))
                for m in range(8):
                    by = 4 + cnt["y"] % 4
                    cnt["y"] += 1
                    for jj in range(gn):
                        K.op(pe, lambda: TT.matmul(K.pb(by, n), wd_v[:, jj, m * 128:(m + 1) * 128], actb[a][:, jj, 0:n], start=(jj == 0), stop=(jj == gn - 1)),
                             r=rl + [Ract[a]], w=(K.bank[by],), sig=(jj == gn - 1))
                    accumulate(m, nt, K.pb(by, n), K.bank[by])
                if ln is not None and gi == len(ffn_groups) - 1 and nt >= 1:
                    ln[0](nt - 1)
                    ln[1](nt - 1)
            first_acc["v"] = False
            done_unit(("ffn", i, s, gi))
        if ln is not None:
            ln[0](4)
            ln[1](4)
            first_acc["v"] = True

    def hgrn(j):
        K.barrier(touch=(R_wq[2], R_wq[3]))
        scr = Scr(extra=True)

        def buf(n):
            o = scr.get(n)
            return K.arena[:, o:o + n]
        sig_, lf, gcs, eng_ = (buf(512) for _ in range(4))
        setA = []
        for q in range(2):
            setA.append({
                "qh": buf(512), "kt": buf(512), "eg": buf(512), "sgate": buf(512),
                "vtok": K.view(scr.get(512), [4, 128]),
                "R": {nm: Res("hg_%s_%d" % (nm, q)) for nm in ("qh", "kt", "eg", "sgate", "vtok")},
            })
        attsb2 = [buf(128), buf(128)]
        ktok2 = [buf(128), buf(128)]
        vblk_s = K.view(scr.get(2048), [16, 128])
        upr_s = K.view(scr.get(2048), [16, 128])
        vblk_p = [K.view(scr.get(256), [2, 128]), K.view(scr.get(256), [2, 128])]
        upr_p = [K.view(scr.get(256), [2, 128]), K.view(scr.get(256), [2, 128])]
        Scur = [buf(128), buf(128)]
        S0 = K.view(scr.get(2048), [16, 128])
        osq = buf(512)
        t1 = osq
        rstd = buf(512)
        yT = K.view(scr.get(256), [512], BF16)
        R = {nm: Res("hg_" + nm) for nm in "sig lf gcs eng S0 osq rstd yT".split()}
        R["t1"] = R["osq"]
        R2 = {nm: [Res("hg_%s_a" % nm), Res("hg_%s_b" % nm)] for nm in ("attsb", "ktok", "vblk", "upr")}
        RS = [Res("hg_S0_"), Res("hg_S1_")]
        rmt = c128[:, C_RM:C_RM + 656]
        tcount = [0]
        st8 = {"sidx": 0}
        units_h = [None] * 8

        def stageA(h, nt, q):
            c0, n = NTILES[nt]
            SA = setA[q]
            RA = SA["R"]
            if nt == 0:
                units_h[h] = use_unit(("hg", j, h))
                K.dma_in(sp, S0, sthg_d[j, :, h].rearrange("s k v -> k s v"), R["S0"])
            if nt == 1 and h < 7:
                emit_loads(unit_index[("hg", j, h + 1)])
            u = units_h[h]
            rl = u[0][0]
            wq_, wz_, wi_, wgt_ = (u[b][1] for b in range(4))
            lb_c = lbcol[:, j, h:h + 1]
            oml_c = omlcol[:, j, h:h + 1]
            noml_c = nomlcol[:, j, h:h + 1]
            for bi, wv in enumerate((wq_, wz_, wgt_)):
                for k in range(8):
                    K.op(pe, lambda: TT.matmul(K.pb(bi, n), wv[:, k, :], hb[:, k, c0:c0 + n], start=(k == 0), stop=(k == 7)),
                         r=rl + [R_hb[nt]], w=(K.bank[bi],), sig=(k == 7))
            tts = token_tiles(nt)
            for ti, (ty, tc0, tn, nsub, L) in enumerate(tts):
                for k in range(8):
                    K.op(pe, lambda: TT.matmul(K.pb(3, 128, 0, tn, ti * 128), hb[:, k, tc0:tc0 + tn], wi_[:, k, :], start=(k == 0), stop=(k == 7)),
                         r=rl + [R_hb[nt]], w=(K.bank[3],), sig=(k == 7))
            qh, kt, eg, sgate, vtok = SA["qh"], SA["kt"], SA["eg"], SA["sgate"], SA["vtok"]
            K.op(act, lambda: A.activation(qh[:, 0:n], K.pb(0, n), AF.Silu), r=(K.bank[0],), w=(RA["qh"],))
            K.op(act, lambda: A.activation(sgate[:, 0:n], K.pb(2, n), AF.Silu), r=(K.bank[2],), w=(RA["sgate"],))
            K.op(act, lambda: A.activation(sig_[:, 0:n], K.pb(1, n), AF.Sigmoid), r=(K.bank[1],), w=(R["sig"],))
            for ti, (ty, tc0, tn, nsub, L) in enumerate(tts):
                K.op(act, lambda: A.copy(vtok[0:tn, ti, :], K.pb(3, 128, 0, tn, ti * 128)), r=(K.bank[3],), w=(RA["vtok"],))
            K.op(act, lambda: A.activation(lf[:, 0:n], sig_[:, 0:n], AF.Ln, bias=lb_c, scale=oml_c), r=(R["sig"], R_small), w=(R["lf"],))
            K.op(dve, lambda: V.tensor_scalar(kt[:, 0:n], sig_[:, 0:n], noml_c, oml_c, ALU.mult, ALU.add), r=(R["sig"], R_small), w=(RA["kt"],))
            rmo = 0 if nt == 0 else 144
            K.op(dve, lambda: V.tensor_tensor_scan(gcs[:, 0:n], rmt[:, rmo:rmo + n], lf[:, 0:n], 0.0, ALU.mult, ALU.add),
                 r=(R["lf"], R_c), w=(R["gcs"],))
            K.op(act, lambda: A.activation(eg[:, 0:n], gcs[:, 0:n], AF.Exp), r=(R["gcs"],), w=(RA["eg"],))
            K.op(act, lambda: A.activation(eng_[:, 0:n], gcs[:, 0:n], AF.Exp, scale=-1.0), r=(R["gcs"],), w=(R["eng"],))
            K.op(dve, lambda: V.tensor_tensor(qh[:, 0:n], qh[:, 0:n], eg[:, 0:n], ALU.mult), r=(RA["qh"], RA["eg"]), w=(RA["qh"],))
            K.op(pool, lambda: G.tensor_tensor(kt[:, 0:n], kt[:, 0:n], eng_[:, 0:n], ALU.mult), r=(RA["kt"], R["eng"]), w=(RA["kt"],))

        def stageB(h, nt, q):
            c0, n = NTILES[nt]
            SA = setA[q]
            RA = SA["R"]
            qh, kt, eg, sgate, vtok = SA["qh"], SA["kt"], SA["eg"], SA["sgate"], SA["vtok"]
            u = units_h[h]
            rl = u[0][0]
            wo_ = u[4][1]
            ng_c = vecT[:, h, V_HGN + j:V_HGN + j + 1]
            if nt == 0:
                st8["sidx"] = 0
                K.op(dve, lambda: V.memset(Scur[0], 0.0), w=(RS[0],))
            for ti, (ty, tc0, tn, nsub, L) in enumerate(token_tiles(nt)):
                lo = tc0 - c0
                pp = tcount[0] % 2
                tcount[0] += 1
                attsb, ktok = attsb2[pp], ktok2[pp]
                Ratt, Rkt = R2["attsb"][pp], R2["ktok"][pp]
                if ty == TYPE_S:
                    vblk, upr = vblk_s, upr_s
                    Rvb, Rup = R2["vblk"][0], R2["upr"][0]
                else:
                    vblk, upr = vblk_p[pp], upr_p[pp]
                    Rvb, Rup = R2["vblk"][pp], R2["upr"][pp]
                K.op(pe, lambda: TT.matmul(K.pb(4, tn, 0, tn), kt[:, lo:lo + tn], qh[:, lo:lo + tn], start=True, stop=True),
                     r=(RA["kt"], RA["qh"]), w=(K.bank[4],))
                K.op(dve, lambda: V.tensor_tensor(attsb[0:tn, 0:tn], K.pb(4, tn, 0, tn), cmask(ty, tn), ALU.mult), r=(K.bank[4], R_c), w=(Ratt,))
                K.op(pe, lambda: TT.transpose(K.pb(5, 128, 0, tn), kt[:, lo:lo + tn], ident), r=(RA["kt"], R_c), w=(K.bank[5],))
                K.op(act, lambda: A.copy(ktok[0:tn, :], K.pb(5, 128, 0, tn)), r=(K.bank[5],), w=(Rkt,))
                K.op(pool, lambda: G.tensor_tensor(vblk[0:tn, 0:nsub, :], bcast_mid(vtok[0:tn, ti, :], nsub), bcast_last(cblk(ty, tn, nsub), 128), ALU.mult),
                     r=(RA["vtok"], R_c), w=(Rvb,))
                K.op(pe, lambda: TT.matmul(K.pb(7, tn, 0, 128, lo), vtok[0:tn, ti, :], attsb[0:tn, 0:tn], start=True, stop=False),
                     r=(RA["vtok"], Ratt), w=(K.bank[7],), sig=False)
                for s0 in range(0, nsub, 4):
                    sn = min(4, nsub - s0)
                    K.op(pe, lambda: TT.matmul(K.pb(6, sn * 128), ktok[0:tn, :], vblk[0:tn, s0:s0 + sn, :], start=True, stop=True),
                         r=(Rkt, Rvb), w=(K.bank[6],))
                    dview = eg[:, lo + (s0 + 1) * L - 1:lo + (s0 + sn) * L:L] if sn > 1 else eg[:, lo + (s0 + 1) * L - 1:lo + (s0 + 1) * L]
                    K.op(dve, lambda: V.tensor_tensor(upr[:, s0:s0 + sn, :], K.pb(6, sn * 128).rearrange("p (s v) -> p s v", v=128), bcast_last(dview, 128), ALU.mult),
                         r=(K.bank[6], RA["eg"]), w=(Rup,))
                if ty == TYPE_S:
                    for s_ in range(16):
                        K.op(pe, lambda: TT.matmul(K.pb(7, L, 0, 128, lo + s_ * L), S0[:, s_, :], qh[:, lo + s_ * L:lo + (s_ + 1) * L], start=False, stop=(s_ == 15)),
                             r=(R["S0"], RA["qh"]), w=(K.bank[7],), sig=(s_ == 15))
                    dv_ = eg[:, lo + L - 1:lo + 16 * L:L]
                    K.op(dve, lambda: V.tensor_tensor(S0, S0, bcast_last(dv_, 128), ALU.mult), r=(R["S0"], RA["eg"]), w=(R["S0"],))
                    K.op(dve, lambda: V.tensor_tensor(S0, S0, upr, ALU.add), r=(Rup, R["S0"]), w=(R["S0"],))
                    K.dma_out(sp, ohgs_d[j, :, h].rearrange("s k v -> k s v"), S0, R["S0"])
                else:
                    for s_ in range(nsub):
                        cur = st8["sidx"] % 2
                        K.op(pe, lambda: TT.matmul(K.pb(7, L, 0, 128, lo + s_ * L), Scur[cur], qh[:, lo + s_ * L:lo + (s_ + 1) * L], start=False, stop=(s_ == nsub - 1)),
                             r=(RS[cur], RA["qh"]), w=(K.bank[7],), sig=(s_ == nsub - 1))
                        dcl = eg[:, lo + (s_ + 1) * L - 1:lo + (s_ + 1) * L]
                        K.op(dve, lambda: V.scalar_tensor_tensor(Scur[1 - cur], Scur[cur], dcl, upr[:, s_, :], ALU.mult, ALU.add),
                             r=(RS[cur], RA["eg"], Rup), w=(RS[1 - cur],))
                        st8["sidx"] += 1
            K.op(act, lambda: A.activation(osq[:, 0:n], K.pb(7, n), AF.Square), r=(K.bank[7],), w=(R["osq"],))
            K.op(pe, lambda: TT.matmul(K.pb(4, n), ones_f, osq[:, 0:n], start=True, stop=True), r=(R["osq"], R_small), w=(K.bank[4],))
            K.op(act, lambda: A.activation(rstd[:, 0:n], K.pb(4, n), AF.Ln, bias=EPS, scale=1.0 / 128), r=(K.bank[4],), w=(R["rstd"],))
            K.op(act, lambda: A.activation(rstd[:, 0:n], rstd[:, 0:n], AF.Exp, scale=-0.5), r=(R["rstd"],), w=(R["rstd"],))
            K.op(dve, lambda: V.tensor_tensor(t1[:, 0:n], K.pb(7, n), rstd[:, 0:n], ALU.mult), r=(K.bank[7], R["rstd"]), w=(R["t1"],))
            K.op(dve, lambda: V.scalar_tensor_tensor(yT[:, 0:n], t1[:, 0:n], ng_c, sgate[:, 0:n], ALU.mult, ALU.mult), r=(R["t1"], RA["sgate"], R_vt), w=(R["yT"],))
            for m in range(8):
                b = 4 + (m % 3)
                K.op(pe, lambda: TT.matmul(K.pb(b, n), wo_[:, m * 128:(m + 1) * 128], yT[:, 0:n], start=True, stop=True),
                     r=rl + [R["yT"]], w=(K.bank[b],))
                accumulate(m, nt, K.pb(b, n), K.bank[b])
            if nt == 4:
                first_acc["v"] = False
                fin = st8["sidx"] % 2
                K.dma_out(sp, ohgp_d[j, h], Scur[fin], RS[fin])

        steps = [(h, nt) for h in range(8) for nt in range(5)]
        stageA(steps[0][0], steps[0][1], 0)
        for i_, (h, nt) in enumerate(steps):
            if i_ + 1 < len(steps):
                stageA(steps[i_ + 1][0], steps[i_ + 1][1], (i_ + 1) % 2)
            stageB(h, nt, i_ % 2)
        K.barrier(touch=(R_wq[2], R_wq[3]))
        done_unit(("hg", j, 7))

    def retention(rconst):
        K.barrier()
        scr = Scr(extra=False)

        def buf(n):
            o = scr.get(n)
            return K.arena[:, o:o + n]
        cs = K.view(scr.get(1024), [2, 512])
        rt = K.view(scr.get(416), [2, 208])
        qk = K.view(scr.get(2048), [4, 512])
        ta, tb = buf(512), buf(512)
        vtok = buf(512)
        attsb, ktok = buf(128), buf(256)
        vb = buf(512)
        Sc = K.view(scr.get(1024), [2, 512])
        S0_2 = [K.view(scr.get(1024), [2, 512]), K.view(scr.get(1024), [2, 512])]
        osb = K.view(scr.get(512), [4, 128])
        osq = K.view(scr.get(512), [4, 128])
        stt = K.view(scr.get(512), [4, 128])
        sgate = K.view(scr.get(512), [4, 128])
        yT = K.view(scr.get(256), [4, 128], BF16)
        names = "cs rt qk ta tb vtok attsb ktok vb Sc S0 osb osq stt sgate yT".split()
        R = {nm: Res("rt_" + nm) for nm in names}
        RS0 = [R["S0"], Res("rt_S0b")]
        gam = rconst
        U2 = K.psum[:, 1024:2048].rearrange("p (c v) -> p c v", v=512)
        for h in range(4):
            uqk = use_unit(("ret", h, "qk"))
            uv = use_unit(("ret", h, "v"))
            ug = use_unit(("ret", h, "g"))
            uo = use_unit(("ret", h, "o"))
            wq_v, wk_v = uqk[0][1], uqk[1][1]
            rl_qk = uqk[0][0]
            wv_v, wg_v, wo_v = uv[0][1], ug[0][1], uo[0][1]
            rl_v, rl_g, rl_o = uv[0][0], ug[0][0], uo[0][0]
            K.dma_in(sp, rt, rett_d[:, h], R["rt"])
            dS, dM, dP = math.exp(8 * gam[h]), math.exp(16 * gam[h]), math.exp(64 * gam[h])
            K.op(dve, lambda: V.memset(Sc, 0.0), w=(R["Sc"],))
            for nt, (c0, n) in enumerate(NTILES):
                K.dma_in(sp, cs[:, :, 0:n], cs_d[:, :, c0:c0 + n], R["cs"])
                for qi, wv in enumerate((wq_v, wk_v)):
                    for half in range(2):
                        b = qi * 2 + half
                        for k in range(8):
                            K.op(pe, lambda: TT.matmul(K.pb(b, n), wv[:, k, half * 128:(half + 1) * 128], hb[:, k, c0:c0 + n], start=(k == 0), stop=(k == 7)),
                                 r=rl_qk + [R_hb[nt]], w=(K.bank[b],), sig=(k == 7))
                cosv, sinv = cs[:, 0, 0:n], cs[:, 1, 0:n]
                for qi in range(2):
                    b1, b2 = qi * 2, qi * 2 + 1
                    x1o, x2o = qk[:, qi * 2, 0:n], qk[:, qi * 2 + 1, 0:n]
                    K.op(dve, lambda: V.tensor_tensor(ta[:, 0:n], K.pb(b1, n), cosv, ALU.mult), r=(K.bank[b1], R["cs"]), w=(R["ta"],))
                    K.op(dve, lambda: V.tensor_tensor(tb[:, 0:n], K.pb(b2, n), sinv, ALU.mult), r=(K.bank[b2], R["cs"]), w=(R["tb"],))
                    K.op(pool, lambda: G.tensor_tensor(x1o, ta[:, 0:n], tb[:, 0:n], ALU.subtract), r=(R["ta"], R["tb"]), w=(R["qk"],))
                    K.op(dve, lambda: V.tensor_tensor(ta[:, 0:n], K.pb(b1, n), sinv, ALU.mult), r=(K.bank[b1], R["cs"]), w=(R["ta"],))
                    K.op(dve, lambda: V.tensor_tensor(tb[:, 0:n], K.pb(b2, n), cosv, ALU.mult), r=(K.bank[b2], R["cs"]), w=(R["tb"],))
                    K.op(pool, lambda: G.tensor_tensor(x2o, ta[:, 0:n], tb[:, 0:n], ALU.add), r=(R["ta"], R["tb"]), w=(R["qk"],))
                    for xo in (x1o, x2o):
                        if nt == 0:
                            K.op(dve, lambda: V.tensor_tensor(xo, xo, rt[:, qi, 0:144], ALU.mult), r=(R["qk"], R["rt"]), w=(R["qk"],))
                        else:
                            x3 = xo.rearrange("p (a b) -> p a b", b=64)
                            K.op(dve, lambda: V.tensor_tensor(x3, x3, bcast_mid(rt[:, qi, 144:208], 8), ALU.mult), r=(R["qk"], R["rt"]), w=(R["qk"],))
                for ti, (ty, tc0, tn, nsub, L) in enumerate(token_tiles(nt)):
                    lo = tc0 - c0
                    dd = (dS, dM, dP)[ty]
                    for k in range(8):
                        K.op(pe, lambda: TT.matmul(K.pb(4, 512, 0, tn), hb[:, k, tc0:tc0 + tn], wv_v[:, k, :], start=(k == 0), stop=(k == 7)),
                             r=rl_v + [R_hb[nt]], w=(K.bank[4],), sig=(k == 7))
                    K.op(act, lambda: A.copy(vtok[0:tn, :], K.pb(4, 512, 0, tn)), r=(K.bank[4],), w=(R["vtok"],))
                    for vc in range(4):
                        for k in range(8):
                            K.op(pe, lambda: TT.matmul(K.pb(5, tn, 0, 128, vc * 128), wg_v[:, k, vc * 128:(vc + 1) * 128], hb[:, k, tc0:tc0 + tn], start=(k == 0), stop=(k == 7)),
                                 r=rl_g + [R_hb[nt]], w=(K.bank[5],), sig=(k == 7))
                    K.op(act, lambda: A.activation(sgate[:, :, 0:tn], K.pb(5, 512).rearrange("p (c t) -> p c t", t=128)[:, :, 0:tn], AF.Silu), r=(K.bank[5],), w=(R["sgate"],))
                    for kc in range(2):
                        K.op(pe, lambda: TT.matmul(K.pb(6, tn, 0, tn), qk[:, 2 + kc, lo:lo + tn], qk[:, kc, lo:lo + tn], start=(kc == 0), stop=(kc == 1)),
                             r=(R["qk"],), w=(K.bank[6],), sig=(kc == 1))
                    K.op(dve, lambda: V.tensor_tensor(attsb[0:tn, 0:tn], K.pb(6, tn, 0, tn), cmask(ty, tn), ALU.mult), r=(K.bank[6], R_c), w=(R["attsb"],))
                    for kc in range(2):
                        K.op(pe, lambda: TT.transpose(K.pb(6, 128, 0, tn, 128 + kc * 128), qk[:, 2 + kc, lo:lo + tn], ident), r=(R["qk"], R_c), w=(K.bank[6],), sig=(kc == 1))
                    K.op(act, lambda: A.copy(ktok[0:tn, :], K.pb(6, 256, 0, tn, 128)), r=(K.bank[6],), w=(R["ktok"],))
                    for vc in range(4):
                        K.op(pe, lambda: TT.matmul(K.pb(7, tn, 0, 128, vc * 128), vtok[0:tn, vc * 128:(vc + 1) * 128], attsb[0:tn, 0:tn], start=True, stop=True),
                             r=(R["vtok"], R["attsb"]), w=(K.bank[7],), sig=(vc == 3))
                    K.op(act, lambda: A.copy(osb[:, :, 0:tn], K.pb(7, 512).rearrange("p (c t) -> p c t", t=128)[:, :, 0:tn]), r=(K.bank[7],), w=(R["osb"],))
                    for s_ in range(nsub):
                        if ty == TYPE_S:
                            S0, RS0_ = S0_2[s_ % 2], RS0[s_ % 2]
                            K.dma_in(sp, S0, stret_d[0, s_, h].rearrange("(c p) v -> p c v", p=128), RS0_)
                            Sx, Rx_ = S0, RS0_
                        else:
                            Sx, Rx_ = Sc, R["Sc"]
                        for vc in range(4):
                            for kc in range(2):
                                K.op(pe, lambda: TT.matmul(K.pb(4, L, 0, 128, vc * 64), Sx[:, kc, vc * 128:(vc + 1) * 128], qk[:, kc, lo + s_ * L:lo + (s_ + 1) * L], start=(kc == 0), stop=(kc == 1)),
                                     r=(Rx_, R["qk"]), w=(K.bank[4],), sig=(kc == 1 and vc == 3))
                        K.op(dve, lambda: V.tensor_tensor(osb[:, :, s_ * L:(s_ + 1) * L], osb[:, :, s_ * L:(s_ + 1) * L], K.pb(4, 256).rearrange("p (c t) -> p c t", t=64)[:, :, 0:L], ALU.add),
                             r=(K.bank[4], R["osb"]), w=(R["osb"],))
                        if nsub > 1:
                            K.op(act, lambda: A.mul(vb[0:tn, :], vtok[0:tn, :], cblk(ty, tn, nsub)[:, s_:s_ + 1]), r=(R["vtok"], R_c), w=(R["vb"],))
                            vsrc, rv = vb, R["vb"]
                        else:
                            vsrc, rv = vtok, R["vtok"]
                        for kc in range(2):
                            K.op(pe, lambda: TT.matmul(K.pb(2 + kc, 512), ktok[0:tn, kc * 128:(kc + 1) * 128], vsrc[0:tn, :], start=True, stop=True),
                                 r=(R["ktok"], rv), w=(K.bank[2 + kc],))
                        K.op(dve, lambda: V.tensor_tensor(Sx, Sx, U2, ALU.add), r=(Rx_, K.bank[2], K.bank[3]), w=(Rx_,))
                        K.op(act, lambda: A.mul(Sx, Sx, dd), r=(Rx_,), w=(Rx_,))
                        if ty == TYPE_S:
                            K.dma_out(sp, orets_d[0, s_, h].rearrange("(c p) v -> p c v", p=128), S0, RS0_)
                    K.op(act, lambda: A.activation(osq[:, :, 0:tn], osb[:, :, 0:tn], AF.Square), r=(R["osb"],), w=(R["osq"],))
                    for vc in range(4):
                        K.op(pe, lambda: TT.matmul(K.pb(6, tn), ones_f, osb[:, vc, 0:tn], start=(vc == 0), stop=(vc == 3)), r=(R["osb"], R_small), w=(K.bank[6],), sig=(vc == 3))
                    mean, var, rstd, nmr = (stt[:, q, 0:tn] for q in range(4))
                    K.op(dve, lambda: V.tensor_scalar(mean, K.pb(6, tn), 1.0 / 512, None, ALU.mult), r=(K.bank[6],), w=(R["stt"],))
                    for vc in range(4):
                        K.op(pe, lambda: TT.matmul(K.pb(6, tn), ones_f, osq[:, vc, 0:tn], start=(vc == 0), stop=(vc == 3)), r=(R["osq"], R_small), w=(K.bank[6],), sig=(vc == 3))
                    K.op(dve, lambda: V.tensor_tensor(var, mean, mean, ALU.mult), r=(R["stt"],), w=(R["stt"],))
                    K.op(dve, lambda: V.scalar_tensor_tensor(var, K.pb(6, tn), 1.0 / 512, var, ALU.mult, ALU.subtract), r=(K.bank[6], R["stt"]), w=(R["stt"],))
                    K.op(act, lambda: A.activation(rstd, var, AF.Ln, bias=EPS), r=(R["stt"],), w=(R["stt"],))
                    K.op(act, lambda: A.activation(rstd, rstd, AF.Exp, scale=-0.5), r=(R["stt"],), w=(R["stt"],))
                    K.op(dve, lambda: V.scalar_tensor_tensor(nmr, mean, -1.0, rstd, ALU.mult, ALU.mult), r=(R["stt"],), w=(R["stt"],))
                    K.op(dve, lambda: V.tensor_tensor(osb[:, :, 0:tn], osb[:, :, 0:tn], bcast_mid(rstd, 4), ALU.mult), r=(R["osb"], R["stt"]), w=(R["osb"],))
                    K.op(dve, lambda: V.tensor_tensor(osb[:, :, 0:tn], osb[:, :, 0:tn], bcast_mid(nmr, 4), ALU.add), r=(R["osb"], R["stt"]), w=(R["osb"],))
                    for vc in range(4):
                        ci_ = h * 4 + vc
                        ngc = vecT[:, ci_ % 8, V_RETN + ci_ // 8:V_RETN + ci_ // 8 + 1]
                        K.op(dve, lambda: V.scalar_tensor_tensor(yT[:, vc, 0:tn], osb[:, vc, 0:tn], ngc, sgate[:, vc, 0:tn], ALU.mult, ALU.mult),
                             r=(R["osb"], R["sgate"], R_vt), w=(R["yT"],))
                    for m in range(8):
                        b = m % 4
                        for vc in range(4):
                            K.op(pe, lambda: TT.matmul(K.pb(b, tn), wo_v[:, vc, m * 128:(m + 1) * 128], yT[:, vc, 0:tn], start=(vc == 0), stop=(vc == 3)),
                                 r=rl_o + [R["yT"]], w=(K.bank[b],), sig=(vc == 3))
                        dst = hf[:, m, tc0:tc0 + tn]
                        ps_ = K.pb(b, tn)
                        if first_acc["v"]:
                            K.op(dve, lambda: V.scalar_tensor_tensor(dst, dst, ALPHA, ps_, ALU.mult, ALU.add), r=(K.bank[b], R_hf[m][nt]), w=(R_hf[m][nt],))
                        else:
                            K.op(dve, lambda: V.tensor_tensor(dst, dst, ps_, ALU.add), r=(K.bank[b], R_hf[m][nt]), w=(R_hf[m][nt],))
            first_acc["v"] = False
            K.dma_out(sp, oretp_d[0, h].rearrange("(c p) v -> p c v", p=128), Sc, R["Sc"])
            if h < 3:
                done_unit(("ret", h, "o"))
        K.barrier()
        done_unit(("ret", 3, "o"))

    def mamba():
        K.barrier(touch=(R_wq[2], R_wq[3]))
        scr = Scr(extra=True)

        def buf(n):
            o = scr.get(n)
            return K.arena[:, o:o + n]
        NTT = 18
        dt_t = K.view(scr.get(NTT * 32), [NTT, 32])
        g_t = K.view(scr.get(NTT * 32), [NTT, 32])
        dtw_t = K.view(scr.get(NTT * 32), [NTT, 32])
        tmp32 = buf(32)
        o_pre = scr.get(4 * 520)
        pre = K.view(o_pre, [4, 520])
        y2 = K.view(o_pre, [2, 512])
        rstd = K.arena[:, o_pre + 1024:o_pre + 1536]
        spre = K.view(scr.get(4 * 176), [4, 176])
        mpre = K.view(scr.get(4 * 19), [4, 19])
        halo = K.view(scr.get(4 * 3), [4, 3])
        cv = K.view(scr.get(4 * 512), [4, 512])
        cacc = buf(128)
        sz = K.view(scr.get(1024), [2, 512])
        ctmp = buf(128)
        xtok = buf(256)
        btok = buf(128)
        rhsd4 = K.view(scr.get(512), [4, 128])
        tmpd4_2 = [K.view(scr.get(512), [4, 128]), K.view(scr.get(512), [4, 128])]
        egbc4_2 = [K.view(scr.get(512), [4, 128]), K.view(scr.get(512), [4, 128])]
        tmpd4, egbc4 = tmpd4_2[0], egbc4_2[0]
        attsb4 = K.view(scr.get(512), [4, 128])
        qh4 = K.view(scr.get(512), [4, 128])
        vdt4 = K.view(scr.get(256), [4, 64])
        vpr4 = K.view(scr.get(256), [4, 64])
        rhsd, tmpd, egbc, attsb, qh = rhsd4[:, 0, :], tmpd4[:, 0, :], egbc4[:, 0, :], attsb4[:, 0, :], qh4[:, 0, :]
        decT = tmpd4[:, 1, :]
        vpr, vdt = vpr4[:, 0, :], vdt4[:, 0, :]
        o_vblk = scr.get(1024)
        vblk = K.view(o_vblk, [16, 64])
        vblk4 = K.arena[:, o_vblk:o_vblk + 512]
        ysq = K.view(o_vblk, [2, 512])
        hist_tok = K.arena[0:48, o_vblk:o_vblk + 512]
        o_upr = scr.get(1024)
        upr = K.view(o_upr, [16, 64])
        upr4 = K.view(o_upr, [4, 2, 64])
        c48 = K.arena[0:48, o_upr:o_upr + 512]
        ptok = K.arena[0:3, o_upr + 512:o_upr + 1024]
        S4 = [K.view(scr.get(256), [4, 64]), K.view(scr.get(256), [4, 64])]
        S0 = K.view(scr.get(1024), [16, 64])
        yT = K.view(scr.get(512), [2, 512], BF16)
        names = "vdt dt g dtw tmp32 pre spre mpre halo cv cacc sz hist c48 ctmp xtok btok rhsd tmpd egbc attsb qh vpr vblk upr S0 yT".split()
        R = {nm: Res("mb_" + nm) for nm in names}
        R["ysq"] = R["vblk"]
        R["hist"] = R["vblk"]
        R["c48"] = R["upr"]
        R["ptok"] = R["upr"]
        R["y2"] = R["pre"]
        R["rstd"] = R["pre"]
        R["decT"] = R["tmpd"]
        R_td = [R["tmpd"], Res("mb_tmpd1")]
        R_eg = [R["egbc"], Res("mb_egbc1")]
        RS4 = [Res("mb_S4_0"), Res("mb_S4_1")]
        all_tt = []
        for nt in range(5):
            for tt_ in token_tiles(nt):
                all_tt.append((nt,) + tt_)
        sel = c128[:, C_SEL:C_SEL + 48]
        udt = use_unit(("mam", "dt"))
        wdt = udt[0][1]
        for tix, (nt, ty, tc0, tn, nsub, L) in enumerate(all_tt):
            for k in range(8):
                K.op(pe, lambda: TT.matmul(K.pb(0, 32, 0, tn), hb[:, k, tc0:tc0 + tn], wdt[:, k, :], start=(k == 0), stop=(k == 7)),
                     r=udt[0][0] + [R_hb[nt]], w=(K.bank[0],), sig=(k == 7))
            d_ = dt_t[0:tn, tix, :]
            K.op(dve, lambda: V.tensor_tensor(d_, K.pb(0, 32, 0, tn), mh_bc[0:tn, 0, :], ALU.add), r=(K.bank[0], R_small), w=(R["dt"],))
            K.op(act, lambda: A.activation(d_, d_, AF.Exp), r=(R["dt"],), w=(R["dt"],))
            K.op(act, lambda: A.activation(d_, d_, AF.Ln, bias=1.0), r=(R["dt"],), w=(R["dt"],))
            la = tmp32[0:tn, :]
            K.op(dve, lambda: V.tensor_tensor(la, d_, negA[0:tn, :], ALU.mult), r=(R["dt"], R_small), w=(R["tmp32"],))
            K.op(pe, lambda: TT.matmul(K.pb(1, 32, 0, tn), cmask(ty, tn), la, start=True, stop=True), r=(R["tmp32"], R_c), w=(K.bank[1],))
            K.op(pe, lambda: TT.matmul(K.pb(2, 32, 0, tn), cltri(ty, tn), la, start=True, stop=True), r=(R["tmp32"], R_c), w=(K.bank[2],))
            K.op(act, lambda: A.copy(g_t[0:tn, tix, :], K.pb(1, 32, 0, tn)), r=(K.bank[1],), w=(R["g"],))
            w_ = dtw_t[0:tn, tix, :]
            K.op(dve, lambda: V.tensor_tensor(w_, K.pb(2, 32, 0, tn), g_t[0:tn, tix, :], ALU.subtract), r=(K.bank[2], R["g"]), w=(R["dtw"],))
            K.op(act, lambda: A.activation(w_, w_, AF.Exp), r=(R["dtw"],), w=(R["dtw"],))
            K.op(dve, lambda: V.tensor_tensor(w_, w_, d_, ALU.mult), r=(R["dtw"], R["dt"]), w=(R["dtw"],))
        done_unit(("mam", "dt"))
        hist_rows = stconv_d[0].rearrange("s r c -> (s r) c")
        oconvs_rows = oconvs_d[0].rearrange("s r c -> (s r) c")
        btiles = [(tix_, tt_[1], tt_[3]) for tix_, tt_ in enumerate(all_tt) if tt_[1] != TYPE_S]
        btile_idx = {bt[0]: i_ for i_, bt in enumerate(btiles)}

        def decay_stage(g, bt, slot):
            tix_, ty_, tn_ = bt
            gq = g_t[0:tn_, tix_, 4 * g:4 * g + 4]
            td, eg_ = tmpd4_2[slot], egbc4_2[slot]
            Rtd, Reg = R_td[slot], R_eg[slot]
            K.op(dve, lambda: V.tensor_tensor(rhsd4[0:tn_, :, 0:tn_], bcast_mid(ident[0:tn_, 0:tn_], 4), bcast_last(gq, tn_), ALU.mult), r=(R["g"], R_c), w=(R["rhsd"],))
            gb4 = K.pb(5, 512).rearrange("p (h t) -> p h t", t=128)
            for hh in range(4):
                K.op(pe, lambda: TT.matmul(K.pb(5, tn_, 0, 128, hh * 128), ones_f[0:tn_, :], rhsd4[0:tn_, hh, 0:tn_], start=True, stop=True), r=(R["rhsd"], R_small), w=(K.bank[5],), sig=(hh == 3))
            K.op(act, lambda: A.activation(eg_[:, :, 0:tn_], gb4[:, :, 0:tn_], AF.Exp), r=(K.bank[5],), w=(Reg,))
            K.op(dve, lambda: V.tensor_tensor(td[0:tn_, :, 0:tn_], gb4[0:tn_, :, 0:tn_], bcast_last(gq, tn_), ALU.subtract), r=(K.bank[5], R["g"]), w=(Rtd,))
            K.op(pool, lambda: G.tensor_tensor(td[0:tn_, :, 0:tn_], td[0:tn_, :, 0:tn_], bcast_mid(cneg(ty_, tn_), 4), ALU.add), r=(Rtd, R_c), w=(Rtd,))
            K.op(act, lambda: A.activation(td[0:tn_, :, 0:tn_], td[0:tn_, :, 0:tn_], AF.Exp), r=(Rtd,), w=(Rtd,))

        for g in range(8):
            uin = use_unit(("mam", g, "in"))
            rl = uin[0][0]
            wz_, wx_, wB_, wC_ = (uin[b][1] for b in range(4))
            uo = use_unit(("mam", g, "o"))
            wo_v, rl_o = uo[0][1], uo[0][0]
            chunks = [(wx_[:, :, 0:128], 2 * g), (wx_[:, :, 128:256], 2 * g + 1), (wB_, 16 + g), (wC_, 24 + g)]
            sidx = 0
            K.op(dve, lambda: V.memset(S4[0], 0.0), w=(RS4[0],))
            for ci, (wv, cch) in enumerate(chunks):
                K.dma_in(sp, hist_tok[:, ci * 128:(ci + 1) * 128], hist_rows[:, cch * 128:(cch + 1) * 128], R["hist"])
            for ci, (wv, cch) in enumerate(chunks):
                K.op(pe, lambda: TT.transpose(K.pb(6, 48), hist_tok[:, ci * 128:(ci + 1) * 128], ident[0:48, 0:48]), r=(R["hist"], R_c), w=(K.bank[6],))
                K.op(act, lambda: A.copy(spre[:, ci, :].rearrange("p (s t) -> p s t", t=11)[:, :, 0:3], K.pb(6, 48).rearrange("p (s r) -> p s r", r=3)),
                     r=(K.bank[6],), w=(R["spre"],))
            K.op(dve, lambda: V.memset(mpre[:, :, 0:3], 0.0), w=(R["mpre"],))
            tix = 0
            for nt, (c0, n) in enumerate(NTILES):
                if nt == 0:
                    pass
                for ci, (wv, cch) in enumerate(chunks):
                    for k in range(8):
                        K.op(pe, lambda: TT.matmul(K.pb(ci, n), wv[:, k, :], hb[:, k, c0:c0 + n], start=(k == 0), stop=(k == 7)),
                             r=rl + [R_hb[nt]], w=(K.bank[ci],), sig=(k == 7))
                for zc in range(2):
                    for k in range(8):
                        K.op(pe, lambda: TT.matmul(K.pb(4 + zc, n), wz_[:, k, zc * 128:(zc + 1) * 128], hb[:, k, c0:c0 + n], start=(k == 0), stop=(k == 7)),
                             r=rl + [R_hb[nt]], w=(K.bank[4 + zc],), sig=(k == 7))
                    K.op(act, lambda: A.activation(sz[:, zc, 0:n], K.pb(4 + zc, n), AF.Silu), r=(K.bank[4 + zc],), w=(R["sz"],))
                for ci, (wv, cch) in enumerate(chunks):
                    cc8, rr = cch % 8, cch // 8
                    wcol = [vecT[:, cc8, V_CW + 4 * t_ + rr:V_CW + 4 * t_ + rr + 1] for t_ in range(4)]
                    bcol = vecT[:, cc8, V_CB + rr:V_CB + rr + 1]
                    if nt == 0:
                        sp3 = spre[:, ci, :].rearrange("p (s t) -> p s t", t=11)
                        K.op(act, lambda: A.copy(sp3[:, :, 3:11], K.pb(ci, 128).rearrange("p (s t) -> p s t", t=8)), r=(K.bank[ci],), w=(R["spre"],))
                        K.op(act, lambda: A.copy(mpre[:, ci, 3:19], K.pb(ci, 16, 0, 128, 128)), r=(K.bank[ci],), w=(R["mpre"],))
                        K.op(dve, lambda: V.tensor_copy(cacc, K.pb(ci, 128)), r=(K.bank[ci],), w=(R["cacc"],))
                        K.op(pe, lambda: TT.transpose(K.pb(6, 128), cacc, ident), r=(R["cacc"], R_c), w=(K.bank[6],))
                        K.op(act, lambda: A.copy(ctmp, K.pb(6, 128)), r=(K.bank[6],), w=(R["ctmp"],))
                        K.op(pe, lambda: TT.matmul(K.pb(6, 128, 0, 48, 128), sel, ctmp, start=True, stop=True), r=(R["ctmp"], R_c), w=(K.bank[6],))
                        K.op(act, lambda: A.copy(c48[:, ci * 128:(ci + 1) * 128], K.pb(6, 128, 0, 48, 128)), r=(K.bank[6],), w=(R["c48"],))
                        co = cv[:, ci, 0:128].rearrange("p (s t) -> p s t", t=8)
                        K.op(act, lambda: A.activation(co, sp3[:, :, 3:11], AF.Identity, bias=bcol, scale=wcol[3]), r=(R["spre"], R_vt), w=(R["cv"],))
                        for t_ in range(3):
                            K.op(dve, lambda: V.scalar_tensor_tensor(co, sp3[:, :, t_:t_ + 8], wcol[t_], co, ALU.mult, ALU.add), r=(R["spre"], R["cv"], R_vt), w=(R["cv"],))
                        cm = cv[:, ci, 128:144]
                        K.op(act, lambda: A.activation(cm, mpre[:, ci, 3:19], AF.Identity, bias=bcol, scale=wcol[3]), r=(R["mpre"], R_vt), w=(R["cv"],))
                        for t_ in range(3):
                            K.op(dve, lambda: V.scalar_tensor_tensor(cm, mpre[:, ci, t_:t_ + 16], wcol[t_], cm, ALU.mult, ALU.add), r=(R["mpre"], R["cv"], R_vt), w=(R["cv"],))
                        K.op(dve, lambda: V.tensor_copy(halo[:, ci, :], mpre[:, ci, 16:19]), r=(R["mpre"],), w=(R["halo"],))
                    else:
                        K.op(dve, lambda: V.tensor_copy(pre[:, ci, 0:3], halo[:, ci, :]), r=(R["halo"],), w=(R["pre"],))
                        K.op(act, lambda: A.copy(pre[:, ci, 3:3 + n], K.pb(ci, n)), r=(K.bank[ci],), w=(R["pre"],))
                        K.op(dve, lambda: V.tensor_copy(halo[:, ci, :], pre[:, ci, n:n + 3]), r=(R["pre"],), w=(R["halo"],))
                        co = cv[:, ci, 0:n]
                        K.op(act, lambda: A.activation(co, pre[:, ci, 3:3 + n], AF.Identity, bias=bcol, scale=wcol[3]), r=(R["pre"], R_vt), w=(R["cv"],))
                        for t_ in range(3):
                            K.op(dve, lambda: V.scalar_tensor_tensor(co, pre[:, ci, t_:t_ + n], wcol[t_], co, ALU.mult, ALU.add), r=(R["pre"], R["cv"], R_vt), w=(R["cv"],))
                        if nt == 4:
                            K.op(pe, lambda: TT.transpose(K.pb(6, 128, 0, 3), pre[:, ci, n:n + 3], ident), r=(R["pre"], R_c), w=(K.bank[6],))
                            K.op(act, lambda: A.copy(ptok[:, ci * 128:(ci + 1) * 128], K.pb(6, 128, 0, 3)), r=(K.bank[6],), w=(R["ptok"],))
                    K.op(act, lambda: A.activation(cv[:, ci, 0:n], cv[:, ci, 0:n], AF.Silu), r=(R["cv"],), w=(R["cv"],))
                if nt == 0:
                    for ci, (wv, cch) in enumerate(chunks):
                        K.dma_out(sp, oconvs_rows[:, cch * 128:(cch + 1) * 128], c48[:, ci * 128:(ci + 1) * 128], R["c48"])
                if nt == 4:
                    for ci, (wv, cch) in enumerate(chunks):
                        K.dma_out(sp, oconvp_d[0, :, cch * 128:(cch + 1) * 128], ptok[:, ci * 128:(ci + 1) * 128], R["ptok"])
                for ti, (ty, tc0, tn, nsub, L) in enumerate(token_tiles(nt)):
                    lo = tc0 - c0
                    for xc in range(2):
                        K.op(pe, lambda: TT.transpose(K.pb(6, 128, 0, tn, xc * 128), cv[:, xc, lo:lo + tn], ident), r=(R["cv"], R_c), w=(K.bank[6],), sig=False)
                    K.op(pe, lambda: TT.transpose(K.pb(6, 128, 0, tn, 256), cv[:, 2, lo:lo + tn], ident), r=(R["cv"], R_c), w=(K.bank[6],))
                    K.op(act, lambda: A.copy(xtok[0:tn, :], K.pb(6, 256, 0, tn)), r=(K.bank[6],), w=(R["xtok"],))
                    K.op(act, lambda: A.copy(btok[0:tn, :], K.pb(6, 128, 0, tn, 256)), r=(K.bank[6],), w=(R["btok"],))
                    K.op(pe, lambda: TT.matmul(K.pb(7, tn, 0, tn), cv[:, 2, lo:lo + tn], cv[:, 3, lo:lo + tn], start=True, stop=True), r=(R["cv"],), w=(K.bank[7],))
                    if ty == TYPE_S:
                        for hh in range(4):
                            hd = 4 * g + hh
                            gcol = g_t[0:tn, tix, hd:hd + 1]
                            K.op(dve, lambda: V.tensor_scalar(rhsd[0:tn, 0:tn], ident[0:tn, 0:tn], gcol, None, ALU.mult), r=(R["g"], R_c), w=(R["rhsd"],))
                            K.op(pe, lambda: TT.matmul(K.pb(5, tn, 0, 128, 0), ones_f[0:tn, :], rhsd[0:tn, 0:tn], start=True, stop=True), r=(R["rhsd"], R_small), w=(K.bank[5],))
                            K.op(act, lambda: A.activation(egbc[:, 0:tn], K.pb(5, tn), AF.Exp), r=(K.bank[5],), w=(R["egbc"],))
                            K.op(dve, lambda: V.scalar_tensor_tensor(tmpd[0:tn, 0:tn], K.pb(5, tn, 0, tn), gcol, cneg(ty, tn), ALU.subtract, ALU.add), r=(K.bank[5], R_c, R["g"]), w=(R["tmpd"],))
                            K.op(act, lambda: A.activation(decT[0:tn, 0:tn], tmpd[0:tn, 0:tn], AF.Exp), r=(R["tmpd"],), w=(R["decT"],))
                            K.op(dve, lambda: V.tensor_tensor(attsb[0:tn, 0:tn], K.pb(7, tn, 0, tn), decT[0:tn, 0:tn], ALU.mult), r=(K.bank[7], R["decT"]), w=(R["attsb"],))
                            K.op(pool, lambda: G.tensor_tensor(qh[:, 0:tn], cv[:, 3, lo:lo + tn], egbc[:, 0:tn], ALU.mult), r=(R["cv"], R["egbc"]), w=(R["qh"],))
                            K.op(dve, lambda: V.tensor_scalar(vpr[0:tn, :], xtok[0:tn, hh * 64:(hh + 1) * 64], dtw_t[0:tn, tix, hd:hd + 1], None, ALU.mult), r=(R["xtok"], R["dtw"]), w=(R["vpr"],))
                            K.op(dve, lambda: V.tensor_scalar(vdt[0:tn, :], xtok[0:tn, hh * 64:(hh + 1) * 64], dt_t[0:tn, tix, hd:hd + 1], None, ALU.mult), r=(R["xtok"], R["dt"]), w=(R["vdt"],))
                            K.op(pool, lambda: G.tensor_tensor(vblk[0:tn, 0:nsub, :], bcast_mid(vpr[0:tn, :], nsub), bcast_last(cblk(ty, tn, nsub), 64), ALU.mult), r=(R["vpr"], R_c), w=(R["vblk"],))
                            for s0 in range(0, nsub, 8):
                                sn = min(8, nsub - s0)
                                K.op(pe, lambda: TT.matmul(K.pb(4, sn * 64), btok[0:tn, :], vblk[0:tn, s0:s0 + sn, :], start=True, stop=True), r=(R["btok"], R["vblk"]), w=(K.bank[4],))
                                K.op(act, lambda: A.copy(upr[:, s0:s0 + sn, :], K.pb(4, sn * 64).rearrange("p (s v) -> p s v", v=64)), r=(K.bank[4],), w=(R["upr"],))
                            po = 64 * (hh % 2)
                            ob = 2 + (hh // 2)
                            K.op(pe, lambda: TT.matmul(K.pb(ob, tn, po, po + 64, lo), vdt[0:tn, :], attsb[0:tn, 0:tn], start=True, stop=False),
                                 r=(R["vdt"], R["attsb"]), w=(K.bank[ob],), sig=False)
                            K.dma_in(sp, S0, stssm_d[0, :, hd].rearrange("s k v -> k s v"), R["S0"])
                            for s_ in range(16):
                                K.op(pe, lambda: TT.matmul(K.pb(ob, L, po, po + 64, lo + s_ * L), S0[:, s_, :], qh[:, s_ * L:(s_ + 1) * L], start=False, stop=(s_ == 15)),
                                     r=(R["S0"], R["qh"]), w=(K.bank[ob],), sig=(s_ == 15))
                            dv_ = egbc[:, L - 1:16 * L:L]
                            K.op(dve, lambda: V.tensor_tensor(S0, S0, bcast_last(dv_, 64), ALU.mult), r=(R["S0"], R["egbc"]), w=(R["S0"],))
                            K.op(dve, lambda: V.tensor_tensor(S0, S0, upr, ALU.add), r=(R["upr"], R["S0"]), w=(R["S0"],))
                            K.dma_out(sp, ossms_d[0, :, hd].rearrange("s k v -> k s v"), S0, R["S0"])
                    else:
                        bi = btile_idx[tix]
                        slot = bi % 2
                        if bi == 0:
                            decay_stage(g, btiles[0], 0)
                        if bi + 1 < len(btiles):
                            decay_stage(g, btiles[bi + 1], (bi + 1) % 2)
                        tmpd4, egbc4 = tmpd4_2[slot], egbc4_2[slot]
                        Rtd, Reg = R_td[slot], R_eg[slot]
                        dq = dt_t[0:tn, tix, 4 * g:4 * g + 4]
                        wq4 = dtw_t[0:tn, tix, 4 * g:4 * g + 4]
                        K.op(dve, lambda: V.tensor_tensor(attsb4[0:tn, :, 0:tn], tmpd4[0:tn, :, 0:tn], bcast_mid(K.pb(7, tn, 0, tn), 4), ALU.mult), r=(K.bank[7], Rtd), w=(R["attsb"],))
                        K.op(pool, lambda: G.tensor_tensor(qh4[:, :, 0:tn], egbc4[:, :, 0:tn], bcast_mid(cv[:, 3, lo:lo + tn], 4), ALU.mult), r=(R["cv"], Reg), w=(R["qh"],))
                        x4 = xtok[0:tn, :].rearrange("p (h v) -> p h v", v=64)
                        K.op(dve, lambda: V.tensor_tensor(vdt4[0:tn], x4, bcast_last(dq, 64), ALU.mult), r=(R["xtok"], R["dt"]), w=(R["vdt"],))
                        K.op(pool, lambda: G.tensor_tensor(vpr4[0:tn], x4, bcast_last(wq4, 64), ALU.mult), r=(R["xtok"], R["dtw"]), w=(R["vpr"],))
                        if nsub == 1:
                            urhs, rur = vpr4[0:tn].rearrange("p h v -> p (h v)"), R["vpr"]
                        else:
                            vp = vpr4[0:tn]
                            pat = [list(x_) for x_ in vp.ap]
                            in0 = bass.AP(vp.tensor, vp.offset, [pat[0], pat[1], [0, nsub], pat[2]])
                            bk = cblk(ty, tn, nsub)
                            pb_ = [list(x_) for x_ in bk.ap]
                            in1 = bass.AP(bk.tensor, bk.offset, [pb_[0], [0, 4], pb_[1], [0, 64]])
                            v4 = vblk4[0:tn, :].rearrange("p (h s v) -> p h s v", s=nsub, v=64)
                            K.op(dve, lambda: V.tensor_tensor(v4, in0, in1, ALU.mult), r=(R["vpr"], R_c), w=(R["vblk"],))
                            urhs, rur = vblk4[0:tn, :], R["vblk"]
                        K.op(pe, lambda: TT.matmul(K.pb(4, 4 * nsub * 64), btok[0:tn, :], urhs, start=True, stop=True), r=(R["btok"], rur), w=(K.bank[4],))
                        K.op(act, lambda: A.copy(upr4[:, :, 0:nsub, :], K.pb(4, 4 * nsub * 64).rearrange("p (h s v) -> p h s v", s=nsub, v=64)), r=(K.bank[4],), w=(R["upr"],))
                        for hh in range(4):
                            po = 64 * (hh % 2)
                            ob = 2 + (hh // 2)
                            K.op(pe, lambda: TT.matmul(K.pb(ob, tn, po, po + 64, lo), vdt4[0:tn, hh, :], attsb4[0:tn, hh, 0:tn], start=True, stop=False),
                                 r=(R["vdt"], R["attsb"]), w=(K.bank[ob],), sig=False)
                        for s_ in range(nsub):
                            cur = sidx % 2
                            for hh in range(4):
                                po = 64 * (hh % 2)
                                ob = 2 + (hh // 2)
                                K.op(pe, lambda: TT.matmul(K.pb(ob, L, po, po + 64, lo + s_ * L), S4[cur][:, hh, :], qh4[:, hh, s_ * L:(s_ + 1) * L], start=False, stop=(s_ == nsub - 1)),
                                     r=(RS4[cur], R["qh"]), w=(K.bank[ob],), sig=(hh == 3))
                            dcl = egbc4[:, :, (s_ + 1) * L - 1:(s_ + 1) * L]
                            K.op(dve, lambda: V.tensor_tensor(S4[1 - cur], S4[cur], bcast_last(dcl, 64), ALU.mult), r=(RS4[cur], Reg), w=(RS4[1 - cur],))
                            K.op(dve, lambda: V.tensor_tensor(S4[1 - cur], S4[1 - cur], upr4[:, :, s_, :], ALU.add), r=(R["upr"], RS4[1 - cur]), w=(RS4[1 - cur],))
                            sidx += 1
                    tix += 1
                for pc in range(2):
                    cch = 2 * g + pc
                    K.op(dve, lambda: V.scalar_tensor_tensor(y2[:, pc, 0:n], cv[:, pc, 0:n], dcol[:, cch:cch + 1], K.pb(2 + pc, n), ALU.mult, ALU.add),
                         r=(R["cv"], K.bank[2 + pc], R_small), w=(R["y2"],))
                    K.op(pool, lambda: G.tensor_tensor(y2[:, pc, 0:n], y2[:, pc, 0:n], sz[:, pc, 0:n], ALU.mult), r=(R["y2"], R["sz"]), w=(R["y2"],))
                    K.op(act, lambda: A.activation(ysq[:, pc, 0:n], y2[:, pc, 0:n], AF.Square), r=(R["y2"],), w=(R["ysq"],))
                for pc in range(2):
                    K.op(pe, lambda: TT.matmul(K.pb(7, n), ones_f, ysq[:, pc, 0:n], start=(pc == 0), stop=(pc == 1)), r=(R["ysq"], R_small), w=(K.bank[7],), sig=(pc == 1))
                K.op(act, lambda: A.activation(rstd[:, 0:n], K.pb(7, n), AF.Ln, bias=EPS, scale=1.0 / 256), r=(K.bank[7],), w=(R["rstd"],))
                K.op(act, lambda: A.activation(rstd[:, 0:n], rstd[:, 0:n], AF.Exp, scale=-0.5), r=(R["rstd"],), w=(R["rstd"],))
                for pc in range(2):
                    cch = 2 * g + pc
                    ngc = vecT[:, cch % 8, V_MN + cch // 8:V_MN + cch // 8 + 1]
                    K.op(dve, lambda: V.scalar_tensor_tensor(yT[:, pc, 0:n], y2[:, pc, 0:n], ngc, rstd[:, 0:n], ALU.mult, ALU.mult), r=(R["y2"], R["rstd"], R_vt), w=(R["yT"],))
                for m in range(8):
                    b = m % 2
                    for pc in range(2):
                        K.op(pe, lambda: TT.matmul(K.pb(b, n), wo_v[:, pc, m * 128:(m + 1) * 128], yT[:, pc, 0:n], start=(pc == 0), stop=(pc == 1)),
                             r=rl_o + [R["yT"]], w=(K.bank[b],), sig=(pc == 1))
                    accumulate(m, nt, K.pb(b, n), K.bank[b])
            first_acc["v"] = False
            fin = sidx % 2
            for hh in range(4):
                K.dma_out(sp, ossmp_d[0, 4 * g + hh], S4[fin][:, hh, :], RS4[fin])
            if g < 7:
                done_unit(("mam", g, "o"))
        K.barrier(touch=(R_wq[2], R_wq[3]))
        done_unit(("mam", 7, "o"))

    rconst = host_consts()[3]
    import os
    dbg = os.environ.get("KDBG", "")
    if "novecs" not in dbg:
        load_vecs()
        K.barrier()
    if "noin" not in dbg:
        load_input()
    K.barrier()
    first_acc["v"] = True
    nsub_done = 0
    for i in range(DEPTH):
        for sub in range(3):
            if stages is not None and nsub_done >= stages:
                break
            if sub == 0:
                ffn(i, 0, ln_closures(i, 0))
            elif sub == 1:
                kind = i % 3
                if kind == 0:
                    hgrn(i // 3)
                elif kind == 1:
                    retention(rconst)
                else:
                    mamba()
                layernorm(i, 1)
            else:
                ffn(i, 1, ln_closures(i, 2))
            nsub_done += 1
    K.barrier()
    if "noout" not in dbg:
        store_output()
    K.finish()
    return nc


_CACHE = {}


def kernel(x_prompt, x_sample, state_hgrn, state_ret, state_ssm, state_conv, meta_tokens, ln_g, ln_b,
           ffn_w_gate, ffn_w_up, ffn_w_down, hg_lb_logits, hg_w_in, hg_norm_g, hg_w_o,
           ret_w_in, ret_norm_g, ret_w_o, m_w_in, m_conv_w, m_conv_b, m_dt_bias, m_a_log, m_d,
           m_norm_g, m_w_o):
    f = lambda a: np.ascontiguousarray(np.asarray(a, dtype=np.float32))
    x_prompt, x_sample, meta_tokens = f(x_prompt), f(x_sample), f(meta_tokens)
    c128, cs, rett, _ = host_consts()
    vecs = np.zeros((64, D), np.float32)
    vecs[V_LNG:V_LNG + 12] = f(ln_g).reshape(12, D)
    vecs[V_LNB:V_LNB + 12] = f(ln_b).reshape(12, D)
    vecs[V_LB:V_LB + 4] = f(hg_lb_logits)
    vecs[V_HGN:V_HGN + 2] = f(hg_norm_g)
    vecs[V_RETN:V_RETN + 2] = f(ret_norm_g).reshape(2, D)
    vecs[V_CW:V_CW + 16] = f(m_conv_w).reshape(16, D)
    vecs[V_CB:V_CB + 4] = f(m_conv_b).reshape(4, D)
    vecs[V_MN:V_MN + 2] = f(m_norm_g).reshape(2, D)
    mhead = np.stack([f(m_dt_bias)[0], f(m_a_log)[0], f(m_d)[0]], axis=0)
    shared = {
        "c128": c128, "cs": cs, "rett": rett, "vecs": vecs, "mhead": mhead,
        "ffn_w_gate": f(ffn_w_gate), "ffn_w_up": f(ffn_w_up), "ffn_w_down": f(ffn_w_down),
        "hg_w_in": f(hg_w_in), "hg_w_o": f(hg_w_o), "ret_w_in": f(ret_w_in), "ret_w_o": f(ret_w_o),
        "m_w_in": f(m_w_in), "m_w_o": f(m_w_o),
    }
    state_hgrn, state_ret, state_ssm, state_conv = f(state_hgrn), f(state_ret), f(state_ssm), f(state_conv)
    in_maps = []
    for c in range(NCORES):
        sl = slice(16 * c, 16 * c + 16)
        xin = np.concatenate([x_sample[sl].reshape(128, D), meta_tokens, x_prompt[c]], axis=0)
        m = dict(shared)
        m["xin"] = np.ascontiguousarray(xin)
        m["st_hg"] = np.ascontiguousarray(state_hgrn[:, sl])
        m["st_ret"] = np.ascontiguousarray(state_ret[:, sl])
        m["st_ssm"] = np.ascontiguousarray(state_ssm[:, sl])
        m["st_conv"] = np.ascontiguousarray(state_conv[:, sl])
        in_maps.append(m)
    if "nc" not in _CACHE:
        _CACHE["nc"] = build_program()
    res = run_bass_kernel_spmd(_CACHE["nc"], in_maps, core_ids=list(range(NCORES)))
    rs = res.results
    y = [r["y"] for r in rs]
    y_prompt = np.stack([yy[144:] for yy in y], axis=0)
    y_sample = np.concatenate([yy[0:128].reshape(16, 8, D) for yy in y], axis=0)
    cat1 = lambda k: np.concatenate([r[k] for r in rs], axis=1)
    stk1 = lambda k: np.stack([r[k] for r in rs], axis=1)
    return (y_prompt.astype(np.float32), y_sample.astype(np.float32),
            stk1("o_hg_p"), cat1("o_hg_s"), stk1("o_ret_p"), cat1("o_ret_s"),
            stk1("o_ssm_p"), cat1("o_ssm_s"), stk1("o_conv_p"), cat1("o_conv_s"))
```

```python
import math
import numpy as np
import ml_dtypes
import concourse.bass as bass
import concourse.mybir as mybir
from concourse.bass_utils import run_bass_kernel_spmd

F32 = mybir.dt.float32
BF16 = mybir.dt.bfloat16
AF = mybir.ActivationFunctionType
ALU = mybir.AluOpType

NCORES = 8
D = 1024
DFF = 2816
DEPTH = 4
T = 2192
NTILES = [(0, 144)] + [(144 + 512 * i, 512) for i in range(4)]
ALPHA = (2 * DEPTH) ** 0.25
EPS = 1e-5
TYPE_S, TYPE_M, TYPE_P = 0, 1, 2


def token_tiles(nt):
    if nt == 0:
        return [(TYPE_S, 0, 128, 16, 8), (TYPE_M, 128, 16, 1, 16)]
    c0 = NTILES[nt][0]
    return [(TYPE_P, c0 + 128 * r, 128, 2, 64) for r in range(4)]


C_ID = 0
C_MASK = 128
C_NEG = C_MASK + 384
C_LTRI = C_NEG + 384
C_BLK = C_LTRI + 384
C_RM = C_BLK + 48
C_SEL = C_RM + 656
C_END = C_SEL + 48

V_LNG, V_LNB, V_LB, V_HGN, V_RETN, V_CW, V_CB, V_MN = 0, 12, 24, 28, 30, 32, 48, 52
V_ROWS = 54


def host_consts():
    c = np.zeros((128, C_END), np.float32)
    c[:, C_ID:C_ID + 128] = np.eye(128, dtype=np.float32)
    j = np.arange(128)[:, None]
    i = np.arange(128)[None, :]
    for ty, L, n in ((TYPE_S, 8, 128), (TYPE_M, 16, 16), (TYPE_P, 64, 128)):
        same = (j // L == i // L) & (j < n) & (i < n)
        m = (same & (j <= i)).astype(np.float32)
        c[:, C_MASK + 128 * ty:C_MASK + 128 * ty + 128] = m
        c[:, C_NEG + 128 * ty:C_NEG + 128 * ty + 128] = (m - 1.0) * 30000.0
        c[:, C_LTRI + 128 * ty:C_LTRI + 128 * ty + 128] = same.astype(np.float32)
        s = np.arange(16)[None, :]
        c[:, C_BLK + 16 * ty:C_BLK + 16 * ty + 16] = ((j // L == s) & (j < n)).astype(np.float32)
    rm = np.ones(656, np.float32)
    rm[0:128:8] = 0.0
    rm[128] = 0.0
    rm[144::64] = 0.0
    c[:, C_RM:C_RM + 656] = rm[None, :]
    for s_ in range(16):
        for r_ in range(3):
            c[s_ * 8 + 5 + r_, C_SEL + s_ * 3 + r_] = 1.0
    half = 128
    inv_freq = (np.float32(10000.0) ** (-np.arange(half, dtype=np.float32) / np.float32(half))).astype(np.float32)
    pos = np.zeros(T, np.float32)
    pos[0:128] = 16384 + (np.arange(128) % 8)
    pos[128:144] = np.arange(16)
    pos[144:] = 16 + np.arange(2048)
    ang = (pos[None, :].astype(np.float32) * inv_freq[:, None]).astype(np.float32)
    cs = np.stack([np.cos(ang), np.sin(ang)], axis=1).astype(np.float32)
    pidx = np.zeros(208, np.float64)
    pidx[0:128] = np.arange(128) % 8
    pidx[128:144] = np.arange(16)
    pidx[144:208] = np.arange(64)
    ret = np.zeros((128, 4, 2, 208), np.float32)
    gam = []
    for h in range(4):
        lg = math.log(1.0 - 2.0 ** (-5.0 - h))
        gam.append(lg)
        g = (pidx + 1.0) * lg
        ret[:, h, 0, :] = np.exp(g)[None, :]
        ret[:, h, 1, :] = (np.exp(-g) / 16.0)[None, :]
    return c, cs, ret, gam


class Res:
    __slots__ = ("name", "w", "rd", "dsem", "dcnt", "excl")

    def __init__(self, name, excl=False):
        self.name = name
        self.excl = excl
        self.w = None
        self.rd = {}
        self.dsem = None
        self.dcnt = 0


class Eng:
    def __init__(self, name, eng, sem):
        self.name, self.eng, self.sem = name, eng, sem
        self.cnt = 0
        self.seen = {}


class Builder:
    def __init__(self):
        nc = bass.Bass("TRN2", target_bir_lowering=False)
        self.nc = nc
        self.pe = Eng("pe", nc.tensor, nc.semaphore("s_pe").__enter__())
        self.dve = Eng("dve", nc.vector, nc.semaphore("s_dve").__enter__())
        self.act = Eng("act", nc.scalar, nc.semaphore("s_act").__enter__())
        self.pool = Eng("pool", nc.gpsimd, nc.semaphore("s_pool").__enter__())
        self.sp = Eng("sp", nc.sync, None)
        self.compute = [self.pe, self.dve, self.act, self.pool]
        self.nsem = 4
        self.out_waits = []
        self.dma_tags = {}
        self.semcache = {}
        self.arena = nc.alloc_sbuf_tensor("arena", [128, 53000], F32)
        self.aoff = 0
        self.psum = nc.alloc_psum_tensor("psum", [128, 4096], F32)
        self.bank = [Res("bank%d" % b, excl=True) for b in range(8)]

    def alloc(self, nfloats):
        o = self.aoff
        self.aoff += nfloats
        assert self.aoff <= 53000, self.aoff
        return o

    def view(self, off, shape, dt=F32):
        n = int(np.prod(shape))
        if dt == F32:
            a = self.arena[:, off:off + n]
        else:
            a = self.arena[:, off:off + (n + 1) // 2].bitcast(BF16)
        if len(shape) == 2:
            return a.rearrange("p (a b) -> p a b", b=shape[1])
        if len(shape) == 3:
            return a.rearrange("p (a b c) -> p a b c", b=shape[1], c=shape[2])
        return a

    def pb(self, b, n=512, p0=0, p1=128, c0=0):
        return self.psum[p0:p1, b * 512 + c0:b * 512 + c0 + n]

    def _wait(self, E, tag):
        key, sem, val = tag
        if E is self.pe and key == "pe":
            return
        if E.seen.get(key, 0) >= val:
            return
        E.eng.wait_ge(sem, val)
        E.seen[key] = val

    def _deps(self, E, reads, writes):
        for r in reads:
            if r.w is not None:
                self._wait(E, r.w)
            if r.excl:
                for key, tag in list(r.rd.items()):
                    if key != E.name:
                        self._wait(E, tag)
        for w in writes:
            if w.w is not None:
                self._wait(E, w.w)
            for tag in list(w.rd.values()):
                self._wait(E, tag)

    def op(self, E, emit, r=(), w=(), sig=True):
        self._deps(E, r, w)
        ins = emit()
        if sig:
            E.cnt += 1
            ins.then_inc(E.sem, 1)
            tag = (E.name, E.sem, E.cnt)
        else:
            tag = (E.name, E.sem, E.cnt + 1)
        for x in r:
            o = x.rd.get(E.name)
            if o is None or o[2] < tag[2]:
                x.rd[E.name] = tag
        for x in w:
            x.w = tag
            x.rd = {}
        return ins

    def _dsem(self, res):
        if res.dsem is None:
            ent = self.semcache.get(res.name)
            if ent is None:
                ent = [self.nc.semaphore("d_" + res.name).__enter__(), 0]
                self.semcache[res.name] = ent
                self.nsem += 1
            res.dsem = ent[0]
            res.dcnt = ent[1]
        return res.dsem

    def dma_in(self, Q, out_ap, in_ap, res, **kw):
        self._deps(Q, (), (res,))
        sem = self._dsem(res)
        Q.eng.dma_start(out=out_ap, in_=in_ap, **kw).then_inc(sem, 16)
        res.dcnt += 16
        self.semcache[res.name][1] = res.dcnt
        res.w = ("d_" + res.name, sem, res.dcnt)
        res.rd = {}
        self.dma_tags[res.w[0]] = res.w

    def dma_out(self, Q, out_ap, in_ap, res, final=True, **kw):
        self._deps(Q, (res,), ())
        sem = self._dsem(res)
        Q.eng.dma_start(out=out_ap, in_=in_ap, **kw).then_inc(sem, 16)
        res.dcnt += 16
        self.semcache[res.name][1] = res.dcnt
        tag = ("d_" + res.name, sem, res.dcnt)
        res.rd["dma"] = tag
        self.out_waits.append(tag)
        self.dma_tags[tag[0]] = tag

    def barrier(self, touch=()):
        for E in self.compute + [self.sp]:
            for F in self.compute:
                if E is not F and F.cnt > 0:
                    self._wait(E, (F.name, F.sem, F.cnt))
            for tag in self.dma_tags.values():
                self._wait(E, tag)
        for res in touch:
            for F in self.compute:
                if F.cnt > 0:
                    res.rd[F.name] = (F.name, F.sem, F.cnt)
            for tag in self.dma_tags.values():
                res.rd[tag[0]] = tag
        self.dma_tags = {}

    def finish(self):
        for E in self.compute:
            if E.cnt > 0:
                self._wait(self.sp, (E.name, E.sem, E.cnt))
        last = {}
        for key, sem, val in self.out_waits:
            if key not in last or last[key][1] < val:
                last[key] = (sem, val)
        for key, (sem, val) in last.items():
            self.sp.eng.wait_ge(sem, val)
            self.act.eng.wait_ge(sem, val)


def bcast_mid(ap, n):
    pat = [list(x) for x in ap.ap]
    return bass.AP(ap.tensor, ap.offset, [pat[0], [0, n]] + pat[1:])


def bcast_last(ap, n):
    pat = [list(x) for x in ap.ap]
    if len(pat) == 3:
        pat = pat[:2]
    return bass.AP(ap.tensor, ap.offset, pat + [[0, n]])


def build_program(stages=None):
    K = Builder()
    nc = K.nc
    pe, dve, act, pool, sp = K.pe, K.dve, K.act, K.pool, K.sp
    TT = nc.tensor
    V = nc.vector
    A = nc.scalar
    G = nc.gpsimd

    def din(name, shape, dt=F32):
        return nc.dram_tensor(name, list(shape), dt, kind="ExternalInput").ap()

    def dout(name, shape):
        return nc.dram_tensor(name, list(shape), F32, kind="ExternalOutput").ap()

    xin = din("xin", [T, D])
    c128_d = din("c128", [128, C_END])
    cs_d = din("cs", [128, 2, T])
    rett_d = din("rett", [128, 4, 2, 208])
    vecs_d = din("vecs", [64, D])
    mhead_d = din("mhead", [3, 32])
    wg_d = din("ffn_w_gate", [DEPTH, 2, D, DFF])
    wu_d = din("ffn_w_up", [DEPTH, 2, D, DFF])
    wd_d = din("ffn_w_down", [DEPTH, 2, DFF, D])
    hgwi_d = din("hg_w_in", [2, D, 4096])
    hgwo_d = din("hg_w_o", [2, D, D])
    rwi_d = din("ret_w_in", [1, D, 6144])
    rwo_d = din("ret_w_o", [1, 2048, D])
    mwi_d = din("m_w_in", [1, D, 6176])
    mwo_d = din("m_w_o", [1, 2048, D])
    sthg_d = din("st_hg", [2, 16, 8, 128, 128])
    stret_d = din("st_ret", [1, 16, 4, 256, 512])
    stssm_d = din("st_ssm", [1, 16, 32, 128, 64])
    stconv_d = din("st_conv", [1, 16, 3, 4096])

    y_d = dout("y", [T, D])
    ohgp_d = dout("o_hg_p", [2, 8, 128, 128])
    ohgs_d = dout("o_hg_s", [2, 16, 8, 128, 128])
    oretp_d = dout("o_ret_p", [1, 4, 256, 512])
    orets_d = dout("o_ret_s", [1, 16, 4, 256, 512])
    ossmp_d = dout("o_ssm_p", [1, 32, 128, 64])
    ossms_d = dout("o_ssm_s", [1, 16, 32, 128, 64])
    oconvp_d = dout("o_conv_p", [1, 3, 4096])
    oconvs_d = dout("o_conv_s", [1, 16, 3, 4096])

    o_hf = K.alloc(8 * T)
    o_hb = K.alloc(8 * T // 2)
    o_c = K.alloc(C_END)
    o_vt = K.alloc(8 * 64)
    o_small = K.alloc(512)
    o_wq = [K.alloc(3072) for _ in range(4)]
    o_scr = K.alloc(0)
    SCR_END = 53000

    hf = K.view(o_hf, [8, T])
    hb = K.view(o_hb, [8, T], BF16)
    c128 = K.arena[:, o_c:o_c + C_END]
    vecT = K.view(o_vt, [8, 64])
    small = K.arena[:, o_small:o_small + 512]
    R_hf = [[Res("hf%d_%d" % (c, n)) for n in range(5)] for c in range(8)]
    R_hb = [Res("hb%d" % n) for n in range(5)]
    R_c = Res("consts")
    R_vt = Res("vecT")
    R_small = Res("small")
    R_wq = [Res("wq%d" % i) for i in range(4)]

    ident = c128[:, C_ID:C_ID + 128]

    def cmask(ty, n):
        return c128[0:n, C_MASK + 128 * ty:C_MASK + 128 * ty + n]

    def cneg(ty, n):
        return c128[0:n, C_NEG + 128 * ty:C_NEG + 128 * ty + n]

    def cltri(ty, n):
        return c128[0:n, C_LTRI + 128 * ty:C_LTRI + 128 * ty + n]

    def cblk(ty, n, nsub):
        return c128[0:n, C_BLK + 16 * ty:C_BLK + 16 * ty + nsub]

    ones_f = small[:, 0:128]
    ones_b = small[:, 128:192].bitcast(BF16)
    lbcol = small[:, 192:208].rearrange("p (j h) -> p j h", h=8)
    omlcol = small[:, 208:224].rearrange("p (j h) -> p j h", h=8)
    nomlcol = small[:, 224:240].rearrange("p (j h) -> p j h", h=8)
    mh_bc = small[:, 240:336].rearrange("p (r h) -> p r h", h=32)
    negA = small[:, 336:368]
    dcol = small[:, 368:384]
    lbtmp = small[:, 384:448]

    K.dma_in(sp, c128, c128_d, R_c)
    K.op(dve, lambda: V.memset(ones_f, 1.0), w=(R_small,))
    K.op(dve, lambda: V.memset(ones_b, 1.0), w=(R_small,))
    import os as _os
    if "nomh" not in _os.environ.get("KDBG", ""):
        K.dma_in(sp, mh_bc, bass.AP(mhead_d.tensor, 0, [[0, 128], [32, 3], [1, 32]]), R_small)
        with nc.allow_non_contiguous_dma(reason="tiny const"):
            K.dma_in(sp, dcol[0:64, :], bass.AP(mhead_d.tensor, 64, [[0, 64], [2, 16]]), R_small)
            K.dma_in(sp, dcol[64:128, :], bass.AP(mhead_d.tensor, 65, [[0, 64], [2, 16]]), R_small)

    class Scr:
        def __init__(self, extra=False):
            self.off = o_scr
            self.end = SCR_END
            self.extra = [o_wq[2], o_wq[2] + 6144] if extra else None

        def get(self, n):
            if self.off + n <= self.end:
                o = self.off
                self.off += n
                return o
            assert self.extra is not None and self.extra[0] + n <= self.extra[1], "scratch overflow"
            o = self.extra[0]
            self.extra[0] += n
            return o

    units = []
    wstate = {"next": 0}

    def emit_loads(upto):
        while wstate["next"] <= min(upto, len(units) - 1):
            u = units[wstate["next"]]
            for res_list, dst, src in u:
                K._deps(pool, (), res_list)
                sem = K._dsem(res_list[0])
                G.dma_start(out=dst, in_=src).then_inc(sem, 16)
                res_list[0].dcnt += 16
                K.semcache[res_list[0].name][1] = res_list[0].dcnt
                tag = ("d_" + res_list[0].name, sem, res_list[0].dcnt)
                for rr in res_list:
                    rr.w = tag
                    rr.rd = {}
            wstate["next"] += 1

    def wview(slot_floats_off, shape):
        return K.view(slot_floats_off, shape, BF16)

    plan = []
    for i in range(DEPTH):
        plan.append(("ffn", i, 0))
        kind = i % 3
        plan.append((("hg", i // 3), ("ret", 0), ("mam", 0))[kind])
        plan.append(("ffn", i, 1))

    ffn_groups = [(4 * g, 4) for g in range(5)] + [(20, 2)]
    unit_index = {}
    big = [0]

    def add_unit(key, entries):
        unit_index[key] = len(units)
        units.append(entries)

    for ph in plan:
        if ph[0] == "ffn":
            _, i, s = ph
            for gi, (j0, gn) in enumerate(ffn_groups):
                slot = big[0] % 2
                big[0] += 1
                base = o_wq[2 * slot]
                rl = [R_wq[2 * slot], R_wq[2 * slot + 1]]
                wg_v = wview(base, [8, 512])
                wu_v = wview(base + 2048, [8, 512])
                wd_v = wview(base + 4096, [4, 1024])
                ent = [
                    (rl, wg_v[:, :, 0:gn * 128], wg_d[i, s].rearrange("(k p) n -> p k n", p=128)[:, :, j0 * 128:(j0 + gn) * 128]),
                    (rl, wu_v[:, :, 0:gn * 128], wu_d[i, s].rearrange("(k p) n -> p k n", p=128)[:, :, j0 * 128:(j0 + gn) * 128]),
                    (rl, wd_v[:, 0:gn, :], wd_d[i, s].rearrange("(j p) n -> p j n", p=128)[:, j0:j0 + gn, :]),
                ]
                add_unit(("ffn", i, s, gi), ent)
            if big[0] % 2 == 1:
                pass
        else:
            msl = [0]

            def mslot():
                s_ = msl[0] % 2
                msl[0] += 1
                return o_wq[s_], [R_wq[s_]]

            if ph[0] == "hg":
                j = ph[1]
                wsrc = hgwi_d[j].rearrange("(k p) n -> p k n", p=128)
                for h in range(8):
                    base, rl = mslot()
                    wi_v = wview(base, [8, 512])
                    wo_v = wview(base + 2048, [1024])
                    ent = []
                    for b4 in range(4):
                        ent.append((rl, wi_v[:, :, b4 * 128:(b4 + 1) * 128], wsrc[:, :, b4 * 1024 + h * 128:b4 * 1024 + (h + 1) * 128]))
                    ent.append((rl, wo_v, hgwo_d[j, h * 128:(h + 1) * 128, :]))
                    add_unit(("hg", j, h), ent)
            elif ph[0] == "ret":
                wsrc = rwi_d[0].rearrange("(k p) n -> p k n", p=128)
                for h in range(4):
                    v_ = wview(o_wq[3], [8, 512])
                    add_unit(("ret", h, "qk"), [
                        ([R_wq[3]], v_[:, :, 0:256], wsrc[:, :, h * 256:(h + 1) * 256]),
                        ([R_wq[3]], v_[:, :, 256:512], wsrc[:, :, 1024 + h * 256:1024 + (h + 1) * 256])])
                    v_ = wview(o_wq[1], [8, 512])
                    add_unit(("ret", h, "v"), [([R_wq[1]], v_, wsrc[:, :, 2048 + h * 512:2048 + (h + 1) * 512])])
                    v_ = wview(o_wq[2], [8, 512])
                    add_unit(("ret", h, "g"), [([R_wq[2]], v_, wsrc[:, :, 4096 + h * 512:4096 + (h + 1) * 512])])
                    v_ = wview(o_wq[0], [4, 1024])
                    add_unit(("ret", h, "o"), [([R_wq[0]], v_, rwo_d[0, h * 512:(h + 1) * 512, :].rearrange("(c p) n -> p c n", p=128))])
            else:
                wsrc = mwi_d[0].rearrange("(k p) n -> p k n", p=128)
                base, rl = mslot()
                v_ = wview(base, [8, 32])
                add_unit(("mam", "dt"), [(rl, v_, wsrc[:, :, 6144:6176])])
                for g in range(8):
                    base, rl = mslot()
                    v_ = wview(base, [8, 768])
                    ent = [
                        (rl, v_[:, :, 0:256], wsrc[:, :, g * 256:(g + 1) * 256]),
                        (rl, v_[:, :, 256:512], wsrc[:, :, 2048 + g * 256:2048 + (g + 1) * 256]),
                        (rl, v_[:, :, 512:640], wsrc[:, :, 4096 + g * 128:4096 + (g + 1) * 128]),
                        (rl, v_[:, :, 640:768], wsrc[:, :, 5120 + g * 128:5120 + (g + 1) * 128]),
                    ]
                    add_unit(("mam", g, "in"), ent)
                    base, rl = mslot()
                    v_ = wview(base, [2, 1024])
                    add_unit(("mam", g, "o"), [(rl, v_, mwo_d[0, g * 256:(g + 1) * 256, :].rearrange("(c p) n -> p c n", p=128))])

    def use_unit(key):
        idx = unit_index[key]
        emit_loads(idx)
        return units[idx]

    def done_unit(key):
        emit_loads(unit_index[key] + 1)

    first_acc = {"v": True}

    def accumulate(m, nt, ps_ap, bank_res):
        c0, n = NTILES[nt]
        dst = hf[:, m, c0:c0 + n]
        if first_acc["v"]:
            K.op(dve, lambda: V.scalar_tensor_tensor(dst, dst, ALPHA, ps_ap, ALU.mult, ALU.add),
                 r=(bank_res, R_hf[m][nt]), w=(R_hf[m][nt],))
        else:
            K.op(dve, lambda: V.tensor_tensor(dst, dst, ps_ap, ALU.add),
                 r=(bank_res, R_hf[m][nt]), w=(R_hf[m][nt],))

    def load_vecs():
        scr = Scr()
        o = scr.get(1024)
        vtok = K.arena[0:64, o:o + 1024]
        R = Res("vecstage")
        K.dma_in(sp, vtok, vecs_d, R)
        for c in range(8):
            ps = K.pb(c % 4, 64)
            K.op(pe, lambda: TT.transpose(ps, vtok[:, c * 128:(c + 1) * 128], ident[0:64, 0:64]),
                 r=(R, R_c), w=(K.bank[c % 4],))
            K.op(act, lambda: A.copy(vecT[:, c, :], ps), r=(K.bank[c % 4],), w=(R_vt,))
        lg = vecT[:, :, V_LB:V_LB + 4]
        mx = lbtmp[:, 0:8]
        ex = lbtmp[:, 8:40].rearrange("p (h d) -> p h d", d=4)
        sm = lbtmp[:, 40:48]
        K.op(dve, lambda: V.tensor_reduce(mx, lg, mybir.AxisListType.X, ALU.max), r=(R_vt,), w=(R_small,))
        K.op(dve, lambda: V.tensor_tensor(ex, lg, bcast_last(mx, 4), ALU.subtract), r=(R_vt, R_small), w=(R_small,))
        K.op(act, lambda: A.activation(ex, ex, AF.Exp), r=(R_small,), w=(R_small,))
        K.op(dve, lambda: V.tensor_reduce(sm, ex, mybir.AxisListType.X, ALU.add), r=(R_small,), w=(R_small,))
        K.op(dve, lambda: V.reciprocal(sm, sm), r=(R_small,), w=(R_small,))
        K.op(dve, lambda: V.memset(lbcol[:, 0, :], 0.0), w=(R_small,))
        t3 = lbtmp[:, 48:56]
        K.op(dve, lambda: V.tensor_tensor(t3, ex[:, :, 1], ex[:, :, 2], ALU.add), r=(R_small,), w=(R_small,))
        K.op(dve, lambda: V.tensor_tensor(t3, t3, ex[:, :, 3], ALU.add), r=(R_small,), w=(R_small,))
        K.op(dve, lambda: V.tensor_tensor(lbcol[:, 1, :], t3, sm, ALU.mult), r=(R_small,), w=(R_small,))
        lb_all = small[:, 192:208]
        K.op(dve, lambda: V.tensor_scalar(small[:, 208:224], lb_all, -1.0, 1.0, ALU.mult, ALU.add), r=(R_small,), w=(R_small,))
        K.op(dve, lambda: V.tensor_scalar(small[:, 224:240], lb_all, 1.0, -1.0, ALU.mult, ALU.add), r=(R_small,), w=(R_small,))
        K.op(act, lambda: A.activation(negA, mh_bc[:, 1, :], AF.Exp), r=(R_small,), w=(R_small,))
        K.op(dve, lambda: V.tensor_scalar(negA, negA, -1.0, None, ALU.mult), r=(R_small,), w=(R_small,))

    def load_input():
        scr = Scr()
        xs = [K.arena[:, o:o + 1024] for o in (scr.get(1024), scr.get(1024))]
        Rx = [Res("xs0"), Res("xs1")]
        tiles = [(0, 128), (128, 16)] + [(144 + 128 * r, 128) for r in range(16)]
        import os
        if "nometa" in os.environ.get("KDBG", ""):
            tiles = [t_ for t_ in tiles if t_[1] == 128]
        if "ntiles" in os.environ.get("KDBG", ""):
            tiles = tiles[:int(os.environ["KNT"])]
        for ti, (c0, n) in enumerate(tiles):
            s = ti % 2
            K.dma_in(sp, xs[s][0:n, :], xin[c0:c0 + n, :], Rx[s])
            nt = 0 if c0 < 144 else 1 + (c0 - 144) // 512
            for half in range(2):
                b = (2 * ti + half) % 4
                ps = K.psum[:, b * 512:b * 512 + 512].rearrange("p (c n) -> p c n", c=4)
                for cc in range(4):
                    c = half * 4 + cc
                    K.op(pe, lambda: TT.transpose(ps[:, cc, 0:n], xs[s][0:n, c * 128:(c + 1) * 128], ident[0:n, 0:n]),
                         r=(Rx[s], R_c), w=(K.bank[b],), sig=(cc == 3))
                wr = [R_hf[half * 4 + cc][nt] for cc in range(4)]
                if "nohf" not in os.environ.get("KDBG", ""):
                    K.op(act, lambda: A.copy(hf[:, half * 4:half * 4 + 4, c0:c0 + n], ps[:, :, 0:n]), r=(K.bank[b],), w=wr)
                if "nohb" not in os.environ.get("KDBG", ""):
                    K.op(dve, lambda: V.tensor_copy(hb[:, half * 4:half * 4 + 4, c0:c0 + n], ps[:, :, 0:n]), r=(K.bank[b],), w=(R_hb[nt],))

    def store_output():
        scr = Scr()
        ys = [K.arena[:, o:o + 1024] for o in (scr.get(1024), scr.get(1024))]
        Ry = [Res("ys0"), Res("ys1")]
        tiles = [(0, 128), (128, 16)] + [(144 + 128 * r, 128) for r in range(16)]
        for ti, (c0, n) in enumerate(tiles):
            s = ti % 2
            nt = 0 if c0 < 144 else 1 + (c0 - 144) // 512
            for half in range(2):
                b = (2 * ti + half) % 4
                ps = K.psum[:, b * 512:b * 512 + 512]
                for cc in range(4):
                    c = half * 4 + cc
                    K.op(pe, lambda: TT.transpose(ps[0:n, cc * 128:(cc + 1) * 128], hf[:, c, c0:c0 + n], ident),
                         r=(R_hf[c][nt], R_c), w=(K.bank[b],), sig=(cc == 3))
                if half == 0:
                    K.op(act, lambda: A.copy(ys[s][0:n, 0:512], ps[0:n, :]), r=(K.bank[b],), w=(Ry[s],))
                else:
                    K.op(dve, lambda: V.tensor_copy(ys[s][0:n, 512:1024], ps[0:n, :]), r=(K.bank[b],), w=(Ry[s],))
            K.dma_out(sp, y_d[c0:c0 + n, :], ys[s][0:n, :], Ry[s])

    LN_OFF = o_scr + 3072
    ln_xb = K.view(LN_OFF, [8, 512], BF16)
    ln_sq = K.view(LN_OFF + 2048, [8, 512], BF16)
    ln_st = [K.view(LN_OFF + 4096, [4, 512]), K.view(LN_OFF + 6144, [4, 512])]
    LN_R = {"xb": Res("ln_xb"), "sq": Res("ln_sq"), "st": [Res("ln_st0"), Res("ln_st1")]}

    def ln_closures(li, lj):
        xb, sq = ln_xb, ln_sq
        Rxb, Rsq = LN_R["xb"], LN_R["sq"]
        row = li * 3 + lj

        def stats(nt):
            c0, n = NTILES[nt]
            st, Rst = ln_st[nt % 2], LN_R["st"][nt % 2]
            hft = hf[:, :, c0:c0 + n]
            rall = [R_hf[c][nt] for c in range(8)]
            K.op(pool, lambda: G.tensor_copy(xb[:, :, 0:n], hft), r=rall, w=(Rxb,))
            K.op(act, lambda: A.activation(sq[:, :, 0:n], hft, AF.Square), r=rall, w=(Rsq,))
            for c in range(8):
                K.op(pe, lambda: TT.matmul(K.pb(0, n), ones_b, xb[:, c, 0:n], start=(c == 0), stop=(c == 7)),
                     r=(Rxb, R_small), w=(K.bank[0],), sig=(c == 7))
            for c in range(8):
                K.op(pe, lambda: TT.matmul(K.pb(1, n), ones_b, sq[:, c, 0:n], start=(c == 0), stop=(c == 7)),
                     r=(Rsq, R_small), w=(K.bank[1],), sig=(c == 7))
            mean, var, rstd, nmr = (st[:, q, 0:n] for q in range(4))
            K.op(dve, lambda: V.tensor_scalar(mean, K.pb(0, n), 1.0 / D, None, ALU.mult), r=(K.bank[0],), w=(Rst,))
            K.op(dve, lambda: V.tensor_tensor(var, mean, mean, ALU.mult), r=(Rst,), w=(Rst,))
            K.op(dve, lambda: V.scalar_tensor_tensor(var, K.pb(1, n), 1.0 / D, var, ALU.mult, ALU.subtract), r=(K.bank[1], Rst), w=(Rst,))
            K.op(act, lambda: A.activation(rstd, var, AF.Ln, bias=EPS), r=(Rst,), w=(Rst,))
            K.op(act, lambda: A.activation(rstd, rstd, AF.Exp, scale=-0.5), r=(Rst,), w=(Rst,))
            K.op(dve, lambda: V.scalar_tensor_tensor(nmr, mean, -1.0, rstd, ALU.mult, ALU.mult), r=(Rst,), w=(Rst,))

        def apply(nt):
            c0, n = NTILES[nt]
            st, Rst = ln_st[nt % 2], LN_R["st"][nt % 2]
            hft = hf[:, :, c0:c0 + n]
            rall = [R_hf[c][nt] for c in range(8)]
            rstd, nmr = st[:, 2, 0:n], st[:, 3, 0:n]
            K.op(dve, lambda: V.tensor_tensor(hft, hft, bcast_mid(rstd, 8), ALU.mult), r=rall + [Rst], w=rall)
            K.op(pool, lambda: G.tensor_tensor(hft, hft, bcast_mid(nmr, 8), ALU.add), r=rall + [Rst], w=rall)
            for c in range(8):
                K.op(act, lambda: A.activation(hf[:, c, c0:c0 + n], hf[:, c, c0:c0 + n], AF.Identity,
                                               bias=vecT[:, c, V_LNB + row:V_LNB + row + 1],
                                               scale=vecT[:, c, V_LNG + row:V_LNG + row + 1]),
                     r=(R_hf[c][nt], R_vt), w=(R_hf[c][nt],))
            K.op(dve, lambda: V.tensor_copy(hb[:, :, c0:c0 + n], hft), r=rall, w=(R_hb[nt],))

        return stats, apply

    def layernorm(li, lj):
        stats, apply = ln_closures(li, lj)
        stats(0)
        for nt in range(5):
            if nt + 1 < 5:
                stats(nt + 1)
            apply(nt)
        first_acc["v"] = True

    ffn_actb = [K.view(o_scr + o, [4, 512], BF16) for o in (0, 1024)]
    ffn_sgb = [K.arena[:, o_scr + o:o_scr + o + 512] for o in (2048, 2560)]
    FFN_R = {"act": [Res("act0"), Res("act1")], "sg": [Res("sg0"), Res("sg1")]}

    def ffn(i, s, ln=None):
        actb, sgb = ffn_actb, ffn_sgb
        Ract, Rsg = FFN_R["act"], FFN_R["sg"]
        cnt = {"gu": 0, "y": 0, "a": 0}
        for gi, (j0, gn) in enumerate(ffn_groups):
            u = use_unit(("ffn", i, s, gi))
            rl = u[0][0]
            wg_v, wu_v, wd_v = u[0][1], u[1][1], u[2][1]
            for nt, (c0, n) in enumerate(NTILES):
                a = cnt["a"] % 2
                cnt["a"] += 1
                for jj in range(gn):
                    p = cnt["gu"] % 2
                    cnt["gu"] += 1
                    bg, bu = p, 2 + p
                    for k in range(8):
                        K.op(pe, lambda: TT.matmul(K.pb(bg, n), wg_v[:, k, jj * 128:(jj + 1) * 128], hb[:, k, c0:c0 + n], start=(k == 0), stop=(k == 7)),
                             r=rl + [R_hb[nt]], w=(K.bank[bg],), sig=(k == 7))
                    for k in range(8):
                        K.op(pe, lambda: TT.matmul(K.pb(bu, n), wu_v[:, k, jj * 128:(jj + 1) * 128], hb[:, k, c0:c0 + n], start=(k == 0), stop=(k == 7)),
                             r=rl + [R_hb[nt]], w=(K.bank[bu],), sig=(k == 7))
                    K.op(act, lambda: A.activation(sgb[p][:, 0:n], K.pb(bg, n), AF.Silu), r=(K.bank[bg],), w=(Rsg[p],))
                    K.op(dve, lambda: V.scalar_tensor_tensor(actb[a][:, jj, 0:n], K.pb(bu, n), 0.5, sgb[p][:, 0:n], ALU.mult, ALU.mult),
                         r=(K.bank[bu], Rsg[p]), w=(Ract[a],))
                for m in range(8):
                    by = 4 + cnt["y"] % 4
                    cnt["y"] += 1
                    for jj in range(gn):
                        K.op(pe, lambda: TT.matmul(K.pb(by, n), wd_v[:, jj, m * 128:(m + 1) * 128], actb[a][:, jj, 0:n], start=(jj == 0), stop=(jj == gn - 1)),
                             r=rl + [Ract[a]], w=(K.bank[by],), sig=(jj == gn - 1))
                    accumulate(m, nt, K.pb(by, n), K.bank[by])
                if ln is not None and gi == len(ffn_groups) - 1 and nt >= 1:
                    ln[0](nt - 1)
                    ln[1](nt - 1)
            first_acc["v"] = False
            done_unit(("ffn", i, s, gi))
        if ln is not None:
            ln[0](4)
            ln[1](4)
            first_acc["v"] = True

    def hgrn(j):
        K.barrier(touch=(R_wq[2], R_wq[3]))
        scr = Scr(extra=True)

        def buf(n):
            o = scr.get(n)
            return K.arena[:, o:o + n]
        sig_, lf, gcs, eng_ = (buf(512) for _ in range(4))
        setA = []
        for q in range(2):
            setA.append({
                "qh": buf(512), "kt": buf(512), "eg": buf(512), "sgate": buf(512),
                "vtok": K.view(scr.get(512), [4, 128]),
                "R": {nm: Res("hg_%s_%d" % (nm, q)) for nm in ("qh", "kt", "eg", "sgate", "vtok")},
            })
        attsb2 = [buf(128), buf(128)]
        ktok2 = [buf(128), buf(128)]
        vblk_s = K.view(scr.get(2048), [16, 128])
        upr_s = K.view(scr.get(2048), [16, 128])
        vblk_p = [K.view(scr.get(256), [2, 128]), K.view(scr.get(256), [2, 128])]
        upr_p = [K.view(scr.get(256), [2, 128]), K.view(scr.get(256), [2, 128])]
        Scur = [buf(128), buf(128)]
        S0 = K.view(scr.get(2048), [16, 128])
        osq = buf(512)
        t1 = osq
        rstd = buf(512)
        yT = K.view(scr.get(256), [512], BF16)
        R = {nm: Res("hg_" + nm) for nm in "sig lf gcs eng S0 osq rstd yT".split()}
        R["t1"] = R["osq"]
        R2 = {nm: [Res("hg_%s_a" % nm), Res("hg_%s_b" % nm)] for nm in ("attsb", "ktok", "vblk", "upr")}
        RS = [Res("hg_S0_"), Res("hg_S1_")]
        rmt = c128[:, C_RM:C_RM + 656]
        tcount = [0]
        st8 = {"sidx": 0}
        units_h = [None] * 8

        def stageA(h, nt, q):
            c0, n = NTILES[nt]
            SA = setA[q]
            RA = SA["R"]
            if nt == 0:
                units_h[h] = use_unit(("hg", j, h))
                K.dma_in(sp, S0, sthg_d[j, :, h].rearrange("s k v -> k s v"), R["S0"])
            if nt == 1 and h < 7:
                emit_loads(unit_index[("hg", j, h + 1)])
            u = units_h[h]
            rl = u[0][0]
            wq_, wz_, wi_, wgt_ = (u[b][1] for b in range(4))
            lb_c = lbcol[:, j, h:h + 1]
            oml_c = omlcol[:, j, h:h + 1]
            noml_c = nomlcol[:, j, h:h + 1]
            for bi, wv in enumerate((wq_, wz_, wgt_)):
                for k in range(8):
                    K.op(pe, lambda: TT.matmul(K.pb(bi, n), wv[:, k, :], hb[:, k, c0:c0 + n], start=(k == 0), stop=(k == 7)),
                         r=rl + [R_hb[nt]], w=(K.bank[bi],), sig=(k == 7))
            tts = token_tiles(nt)
            for ti, (ty, tc0, tn, nsub, L) in enumerate(tts):
                for k in range(8):
                    K.op(pe, lambda: TT.matmul(K.pb(3, 128, 0, tn, ti * 128), hb[:, k, tc0:tc0 + tn], wi_[:, k, :], start=(k == 0), stop=(k == 7)),
                         r=rl + [R_hb[nt]], w=(K.bank[3],), sig=(k == 7))
            qh, kt, eg, sgate, vtok = SA["qh"], SA["kt"], SA["eg"], SA["sgate"], SA["vtok"]
            K.op(act, lambda: A.activation(qh[:, 0:n], K.pb(0, n), AF.Silu), r=(K.bank[0],), w=(RA["qh"],))
            K.op(act, lambda: A.activation(sgate[:, 0:n], K.pb(2, n), AF.Silu), r=(K.bank[2],), w=(RA["sgate"],))
            K.op(act, lambda: A.activation(sig_[:, 0:n], K.pb(1, n), AF.Sigmoid), r=(K.bank[1],), w=(R["sig"],))
            for ti, (ty, tc0, tn, nsub, L) in enumerate(tts):
                K.op(act, lambda: A.copy(vtok[0:tn, ti, :], K.pb(3, 128, 0, tn, ti * 128)), r=(K.bank[3],), w=(RA["vtok"],))
            K.op(act, lambda: A.activation(lf[:, 0:n], sig_[:, 0:n], AF.Ln, bias=lb_c, scale=oml_c), r=(R["sig"], R_small), w=(R["lf"],))
            K.op(dve, lambda: V.tensor_scalar(kt[:, 0:n], sig_[:, 0:n], noml_c, oml_c, ALU.mult, ALU.add), r=(R["sig"], R_small), w=(RA["kt"],))
            rmo = 0 if nt == 0 else 144
            K.op(dve, lambda: V.tensor_tensor_scan(gcs[:, 0:n], rmt[:, rmo:rmo + n], lf[:, 0:n], 0.0, ALU.mult, ALU.add),
                 r=(R["lf"], R_c), w=(R["gcs"],))
            K.op(act, lambda: A.activation(eg[:, 0:n], gcs[:, 0:n], AF.Exp), r=(R["gcs"],), w=(RA["eg"],))
            K.op(act, lambda: A.activation(eng_[:, 0:n], gcs[:, 0:n], AF.Exp, scale=-1.0), r=(R["gcs"],), w=(R["eng"],))
            K.op(dve, lambda: V.tensor_tensor(qh[:, 0:n], qh[:, 0:n], eg[:, 0:n], ALU.mult), r=(RA["qh"], RA["eg"]), w=(RA["qh"],))
            K.op(pool, lambda: G.tensor_tensor(kt[:, 0:n], kt[:, 0:n], eng_[:, 0:n], ALU.mult), r=(RA["kt"], R["eng"]), w=(RA["kt"],))

        def stageB(h, nt, q):
            c0, n = NTILES[nt]
            SA = setA[q]
            RA = SA["R"]
            qh, kt, eg, sgate, vtok = SA["qh"], SA["kt"], SA["eg"], SA["sgate"], SA["vtok"]
            u = units_h[h]
            rl = u[0][0]
            wo_ = u[4][1]
            ng_c = vecT[:, h, V_HGN + j:V_HGN + j + 1]
            if nt == 0:
                st8["sidx"] = 0
                K.op(dve, lambda: V.memset(Scur[0], 0.0), w=(RS[0],))
            for ti, (ty, tc0, tn, nsub, L) in enumerate(token_tiles(nt)):
                lo = tc0 - c0
                pp = tcount[0] % 2
                tcount[0] += 1
                attsb, ktok = attsb2[pp], ktok2[pp]
                Ratt, Rkt = R2["attsb"][pp], R2["ktok"][pp]
                if ty == TYPE_S:
                    vblk, upr = vblk_s, upr_s
                    Rvb, Rup = R2["vblk"][0], R2["upr"][0]
                else:
                    vblk, upr = vblk_p[pp], upr_p[pp]
                    Rvb, Rup = R2["vblk"][pp], R2["upr"][pp]
                K.op(pe, lambda: TT.matmul(K.pb(4, tn, 0, tn), kt[:, lo:lo + tn], qh[:, lo:lo + tn], start=True, stop=True),
                     r=(RA["kt"], RA["qh"]), w=(K.bank[4],))
                K.op(dve, lambda: V.tensor_tensor(attsb[0:tn, 0:tn], K.pb(4, tn, 0, tn), cmask(ty, tn), ALU.mult), r=(K.bank[4], R_c), w=(Ratt,))
                K.op(pe, lambda: TT.transpose(K.pb(5, 128, 0, tn), kt[:, lo:lo + tn], ident), r=(RA["kt"], R_c), w=(K.bank[5],))
                K.op(act, lambda: A.copy(ktok[0:tn, :], K.pb(5, 128, 0, tn)), r=(K.bank[5],), w=(Rkt,))
                K.op(pool, lambda: G.tensor_tensor(vblk[0:tn, 0:nsub, :], bcast_mid(vtok[0:tn, ti, :], nsub), bcast_last(cblk(ty, tn, nsub), 128), ALU.mult),
                     r=(RA["vtok"], R_c), w=(Rvb,))
                K.op(pe, lambda: TT.matmul(K.pb(7, tn, 0, 128, lo), vtok[0:tn, ti, :], attsb[0:tn, 0:tn], start=True, stop=False),
                     r=(RA["vtok"], Ratt), w=(K.bank[7],), sig=False)
                for s0 in range(0, nsub, 4):
                    sn = min(4, nsub - s0)
                    K.op(pe, lambda: TT.matmul(K.pb(6, sn * 128), ktok[0:tn, :], vblk[0:tn, s0:s0 + sn, :], start=True, stop=True),
                         r=(Rkt, Rvb), w=(K.bank[6],))
                    dview = eg[:, lo + (s0 + 1) * L - 1:lo + (s0 + sn) * L:L] if sn > 1 else eg[:, lo + (s0 + 1) * L - 1:lo + (s0 + 1) * L]
                    K.op(dve, lambda: V.tensor_tensor(upr[:, s0:s0 + sn, :], K.pb(6, sn * 128).rearrange("p (s v) -> p s v", v=128), bcast_last(dview, 128), ALU.mult),
                         r=(K.bank[6], RA["eg"]), w=(Rup,))
                if ty == TYPE_S:
                    for s_ in range(16):
                        K.op(pe, lambda: TT.matmul(K.pb(7, L, 0, 128, lo + s_ * L), S0[:, s_, :], qh[:, lo + s_ * L:lo + (s_ + 1) * L], start=False, stop=(s_ == 15)),
                             r=(R["S0"], RA["qh"]), w=(K.bank[7],), sig=(s_ == 15))
                    dv_ = eg[:, lo + L - 1:lo + 16 * L:L]
                    K.op(dve, lambda: V.tensor_tensor(S0, S0, bcast_last(dv_, 128), ALU.mult), r=(R["S0"], RA["eg"]), w=(R["S0"],))
                    K.op(dve, lambda: V.tensor_tensor(S0, S0, upr, ALU.add), r=(Rup, R["S0"]), w=(R["S0"],))
                    K.dma_out(sp, ohgs_d[j, :, h].rearrange("s k v -> k s v"), S0, R["S0"])
                else:
                    for s_ in range(nsub):
                        cur = st8["sidx"] % 2
                        K.op(pe, lambda: TT.matmul(K.pb(7, L, 0, 128, lo + s_ * L), Scur[cur], qh[:, lo + s_ * L:lo + (s_ + 1) * L], start=False, stop=(s_ == nsub - 1)),
                             r=(RS[cur], RA["qh"]), w=(K.bank[7],), sig=(s_ == nsub - 1))
                        dcl = eg[:, lo + (s_ + 1) * L - 1:lo + (s_ + 1) * L]
                        K.op(dve, lambda: V.scalar_tensor_tensor(Scur[1 - cur], Scur[cur], dcl, upr[:, s_, :], ALU.mult, ALU.add),
                             r=(RS[cur], RA["eg"], Rup), w=(RS[1 - cur],))
                        st8["sidx"] += 1
            K.op(act, lambda: A.activation(osq[:, 0:n], K.pb(7, n), AF.Square), r=(K.bank[7],), w=(R["osq"],))
            K.op(pe, lambda: TT.matmul(K.pb(4, n), ones_f, osq[:, 0:n], start=True, stop=True), r=(R["osq"], R_small), w=(K.bank[4],))
            K.op(act, lambda: A.activation(rstd[:, 0:n], K.pb(4, n), AF.Ln, bias=EPS, scale=1.0 / 128), r=(K.bank[4],), w=(R["rstd"],))
            K.op(act, lambda: A.activation(rstd[:, 0:n], rstd[:, 0:n], AF.Exp, scale=-0.5), r=(R["rstd"],), w=(R["rstd"],))
            K.op(dve, lambda: V.tensor_tensor(t1[:, 0:n], K.pb(7, n), rstd[:, 0:n], ALU.mult), r=(K.bank[7], R["rstd"]), w=(R["t1"],))
            K.op(dve, lambda: V.scalar_tensor_tensor(yT[:, 0:n], t1[:, 0:n], ng_c, sgate[:, 0:n], ALU.mult, ALU.mult), r=(R["t1"], RA["sgate"], R_vt), w=(R["yT"],))
            if nt == 4:
                fin = st8["sidx"] % 2
                K.dma_out(sp, ohgp_d[j, h], Scur[fin], RS[fin])

        def stageC(h, nt, q):
            c0, n = NTILES[nt]
            u = units_h[h]
            rl = u[0][0]
            wo_ = u[4][1]
            for m in range(8):
                b = 4 + (m % 3)
                K.op(pe, lambda: TT.matmul(K.pb(b, n), wo_[:, m * 128:(m + 1) * 128], yT[:, 0:n], start=True, stop=True),
                     r=rl + [R["yT"]], w=(K.bank[b],))
                accumulate(m, nt, K.pb(b, n), K.bank[b])
            if nt == 4:
                first_acc["v"] = False

        steps = [(h, nt) for h in range(8) for nt in range(5)]
        stageA(steps[0][0], steps[0][1], 0)
        for i_, (h, nt) in enumerate(steps):
            stageB(h, nt, i_ % 2)
            if i_ + 1 < len(steps):
                stageA(steps[i_ + 1][0], steps[i_ + 1][1], (i_ + 1) % 2)
            stageC(h, nt, i_ % 2)
        K.barrier(touch=(R_wq[2], R_wq[3]))
        done_unit(("hg", j, 7))

    def retention(rconst):
        K.barrier()
        scr = Scr(extra=False)

        def buf(n):
            o = scr.get(n)
            return K.arena[:, o:o + n]
        cs = K.view(scr.get(1024), [2, 512])
        rt = K.view(scr.get(416), [2, 208])
        qk = K.view(scr.get(2048), [4, 512])
        ta, tb = buf(512), buf(512)
        vtok = buf(512)
        attsb, ktok = buf(128), buf(256)
        vb = buf(512)
        Sc = K.view(scr.get(1024), [2, 512])
        S0_2 = [K.view(scr.get(1024), [2, 512]), K.view(scr.get(1024), [2, 512])]
        osb = K.view(scr.get(512), [4, 128])
        osq = K.view(scr.get(512), [4, 128])
        stt = K.view(scr.get(512), [4, 128])
        sgate = K.view(scr.get(512), [4, 128])
        yT = K.view(scr.get(256), [4, 128], BF16)
        names = "cs rt qk ta tb vtok attsb ktok vb Sc S0 osb osq stt sgate yT".split()
        R = {nm: Res("rt_" + nm) for nm in names}
        RS0 = [R["S0"], Res("rt_S0b")]
        gam = rconst
        U2 = K.psum[:, 1024:2048].rearrange("p (c v) -> p c v", v=512)
        for h in range(4):
            uqk = use_unit(("ret", h, "qk"))
            uv = use_unit(("ret", h, "v"))
            ug = use_unit(("ret", h, "g"))
            uo = use_unit(("ret", h, "o"))
            wq_v, wk_v = uqk[0][1], uqk[1][1]
            rl_qk = uqk[0][0]
            wv_v, wg_v, wo_v = uv[0][1], ug[0][1], uo[0][1]
            rl_v, rl_g, rl_o = uv[0][0], ug[0][0], uo[0][0]
            K.dma_in(sp, rt, rett_d[:, h], R["rt"])
            dS, dM, dP = math.exp(8 * gam[h]), math.exp(16 * gam[h]), math.exp(64 * gam[h])
            K.op(dve, lambda: V.memset(Sc, 0.0), w=(R["Sc"],))
            for nt, (c0, n) in enumerate(NTILES):
                K.dma_in(sp, cs[:, :, 0:n], cs_d[:, :, c0:c0 + n], R["cs"])
                for qi, wv in enumerate((wq_v, wk_v)):
                    for half in range(2):
                        b = qi * 2 + half
                        for k in range(8):
                            K.op(pe, lambda: TT.matmul(K.pb(b, n), wv[:, k, half * 128:(half + 1) * 128], hb[:, k, c0:c0 + n], start=(k == 0), stop=(k == 7)),
                                 r=rl_qk + [R_hb[nt]], w=(K.bank[b],), sig=(k == 7))
                cosv, sinv = cs[:, 0, 0:n], cs[:, 1, 0:n]
                for qi in range(2):
                    b1, b2 = qi * 2, qi * 2 + 1
                    x1o, x2o = qk[:, qi * 2, 0:n], qk[:, qi * 2 + 1, 0:n]
                    K.op(dve, lambda: V.tensor_tensor(ta[:, 0:n], K.pb(b1, n), cosv, ALU.mult), r=(K.bank[b1], R["cs"]), w=(R["ta"],))
                    K.op(dve, lambda: V.tensor_tensor(tb[:, 0:n], K.pb(b2, n), sinv, ALU.mult), r=(K.bank[b2], R["cs"]), w=(R["tb"],))
                    K.op(pool, lambda: G.tensor_tensor(x1o, ta[:, 0:n], tb[:, 0:n], ALU.subtract), r=(R["ta"], R["tb"]), w=(R["qk"],))
                    K.op(dve, lambda: V.tensor_tensor(ta[:, 0:n], K.pb(b1, n), sinv, ALU.mult), r=(K.bank[b1], R["cs"]), w=(R["ta"],))
                    K.op(dve, lambda: V.tensor_tensor(tb[:, 0:n], K.pb(b2, n), cosv, ALU.mult), r=(K.bank[b2], R["cs"]), w=(R["tb"],))
                    K.op(pool, lambda: G.tensor_tensor(x2o, ta[:, 0:n], tb[:, 0:n], ALU.add), r=(R["ta"], R["tb"]), w=(R["qk"],))
                    for xo in (x1o, x2o):
                        if nt == 0:
                            K.op(dve, lambda: V.tensor_tensor(xo, xo, rt[:, qi, 0:144], ALU.mult), r=(R["qk"], R["rt"]), w=(R["qk"],))
                        else:
                            x3 = xo.rearrange("p (a b) -> p a b", b=64)
                            K.op(dve, lambda: V.tensor_tensor(x3, x3, bcast_mid(rt[:, qi, 144:208], 8), ALU.mult), r=(R["qk"], R["rt"]), w=(R["qk"],))
                for ti, (ty, tc0, tn, nsub, L) in enumerate(token_tiles(nt)):
                    lo = tc0 - c0
                    dd = (dS, dM, dP)[ty]
                    for k in range(8):
                        K.op(pe, lambda: TT.matmul(K.pb(4, 512, 0, tn), hb[:, k, tc0:tc0 + tn], wv_v[:, k, :], start=(k == 0), stop=(k == 7)),
                             r=rl_v + [R_hb[nt]], w=(K.bank[4],), sig=(k == 7))
                    K.op(act, lambda: A.copy(vtok[0:tn, :], K.pb(4, 512, 0, tn)), r=(K.bank[4],), w=(R["vtok"],))
                    for vc in range(4):
                        for k in range(8):
                            K.op(pe, lambda: TT.matmul(K.pb(5, tn, 0, 128, vc * 128), wg_v[:, k, vc * 128:(vc + 1) * 128], hb[:, k, tc0:tc0 + tn], start=(k == 0), stop=(k == 7)),
                                 r=rl_g + [R_hb[nt]], w=(K.bank[5],), sig=(k == 7))
                    K.op(act, lambda: A.activation(sgate[:, :, 0:tn], K.pb(5, 512).rearrange("p (c t) -> p c t", t=128)[:, :, 0:tn], AF.Silu), r=(K.bank[5],), w=(R["sgate"],))
                    for kc in range(2):
                        K.op(pe, lambda: TT.matmul(K.pb(6, tn, 0, tn), qk[:, 2 + kc, lo:lo + tn], qk[:, kc, lo:lo + tn], start=(kc == 0), stop=(kc == 1)),
                             r=(R["qk"],), w=(K.bank[6],), sig=(kc == 1))
                    K.op(dve, lambda: V.tensor_tensor(attsb[0:tn, 0:tn], K.pb(6, tn, 0, tn), cmask(ty, tn), ALU.mult), r=(K.bank[6], R_c), w=(R["attsb"],))
                    for kc in range(2):
                        K.op(pe, lambda: TT.transpose(K.pb(6, 128, 0, tn, 128 + kc * 128), qk[:, 2 + kc, lo:lo + tn], ident), r=(R["qk"], R_c), w=(K.bank[6],), sig=(kc == 1))
                    K.op(act, lambda: A.copy(ktok[0:tn, :], K.pb(6, 256, 0, tn, 128)), r=(K.bank[6],), w=(R["ktok"],))
                    for vc in range(4):
                        K.op(pe, lambda: TT.matmul(K.pb(7, tn, 0, 128, vc * 128), vtok[0:tn, vc * 128:(vc + 1) * 128], attsb[0:tn, 0:tn], start=True, stop=True),
                             r=(R["vtok"], R["attsb"]), w=(K.bank[7],), sig=(vc == 3))
                    K.op(act, lambda: A.copy(osb[:, :, 0:tn], K.pb(7, 512).rearrange("p (c t) -> p c t", t=128)[:, :, 0:tn]), r=(K.bank[7],), w=(R["osb"],))
                    for s_ in range(nsub):
                        if ty == TYPE_S:
                            S0, RS0_ = S0_2[s_ % 2], RS0[s_ % 2]
                            K.dma_in(sp, S0, stret_d[0, s_, h].rearrange("(c p) v -> p c v", p=128), RS0_)
                            Sx, Rx_ = S0, RS0_
                        else:
                            Sx, Rx_ = Sc, R["Sc"]
                        for vc in range(4):
                            for kc in range(2):
                                K.op(pe, lambda: TT.matmul(K.pb(4, L, 0, 128, vc * 64), Sx[:, kc, vc * 128:(vc + 1) * 128], qk[:, kc, lo + s_ * L:lo + (s_ + 1) * L], start=(kc == 0), stop=(kc == 1)),
                                     r=(Rx_, R["qk"]), w=(K.bank[4],), sig=(kc == 1 and vc == 3))
                        K.op(dve, lambda: V.tensor_tensor(osb[:, :, s_ * L:(s_ + 1) * L], osb[:, :, s_ * L:(s_ + 1) * L], K.pb(4, 256).rearrange("p (c t) -> p c t", t=64)[:, :, 0:L], ALU.add),
                             r=(K.bank[4], R["osb"]), w=(R["osb"],))
                        if nsub > 1:
                            K.op(act, lambda: A.mul(vb[0:tn, :], vtok[0:tn, :], cblk(ty, tn, nsub)[:, s_:s_ + 1]), r=(R["vtok"], R_c), w=(R["vb"],))
                            vsrc, rv = vb, R["vb"]
                        else:
                            vsrc, rv = vtok, R["vtok"]
                        for kc in range(2):
                            K.op(pe, lambda: TT.matmul(K.pb(2 + kc, 512), ktok[0:tn, kc * 128:(kc + 1) * 128], vsrc[0:tn, :], start=True, stop=True),
                                 r=(R["ktok"], rv), w=(K.bank[2 + kc],))
                        K.op(dve, lambda: V.tensor_tensor(Sx, Sx, U2, ALU.add), r=(Rx_, K.bank[2], K.bank[3]), w=(Rx_,))
                        K.op(act, lambda: A.mul(Sx, Sx, dd), r=(Rx_,), w=(Rx_,))
                        if ty == TYPE_S:
                            K.dma_out(sp, orets_d[0, s_, h].rearrange("(c p) v -> p c v", p=128), S0, RS0_)
                    K.op(act, lambda: A.activation(osq[:, :, 0:tn], osb[:, :, 0:tn], AF.Square), r=(R["osb"],), w=(R["osq"],))
                    for vc in range(4):
                        K.op(pe, lambda: TT.matmul(K.pb(6, tn), ones_f, osb[:, vc, 0:tn], start=(vc == 0), stop=(vc == 3)), r=(R["osb"], R_small), w=(K.bank[6],), sig=(vc == 3))
                    mean, var, rstd, nmr = (stt[:, q, 0:tn] for q in range(4))
                    K.op(dve, lambda: V.tensor_scalar(mean, K.pb(6, tn), 1.0 / 512, None, ALU.mult), r=(K.bank[6],), w=(R["stt"],))
                    for vc in range(4):
                        K.op(pe, lambda: TT.matmul(K.pb(6, tn), ones_f, osq[:, vc, 0:tn], start=(vc == 0), stop=(vc == 3)), r=(R["osq"], R_small), w=(K.bank[6],), sig=(vc == 3))
                    K.op(dve, lambda: V.tensor_tensor(var, mean, mean, ALU.mult), r=(R["stt"],), w=(R["stt"],))
                    K.op(dve, lambda: V.scalar_tensor_tensor(var, K.pb(6, tn), 1.0 / 512, var, ALU.mult, ALU.subtract), r=(K.bank[6], R["stt"]), w=(R["stt"],))
                    K.op(act, lambda: A.activation(rstd, var, AF.Ln, bias=EPS), r=(R["stt"],), w=(R["stt"],))
                    K.op(act, lambda: A.activation(rstd, rstd, AF.Exp, scale=-0.5), r=(R["stt"],), w=(R["stt"],))
                    K.op(dve, lambda: V.scalar_tensor_tensor(nmr, mean, -1.0, rstd, ALU.mult, ALU.mult), r=(R["stt"],), w=(R["stt"],))
                    K.op(dve, lambda: V.tensor_tensor(osb[:, :, 0:tn], osb[:, :, 0:tn], bcast_mid(rstd, 4), ALU.mult), r=(R["osb"], R["stt"]), w=(R["osb"],))
                    K.op(dve, lambda: V.tensor_tensor(osb[:, :, 0:tn], osb[:, :, 0:tn], bcast_mid(nmr, 4), ALU.add), r=(R["osb"], R["stt"]), w=(R["osb"],))
                    for vc in range(4):
                        ci_ = h * 4 + vc
                        ngc = vecT[:, ci_ % 8, V_RETN + ci_ // 8:V_RETN + ci_ // 8 + 1]
                        K.op(dve, lambda: V.scalar_tensor_tensor(yT[:, vc, 0:tn], osb[:, vc, 0:tn], ngc, sgate[:, vc, 0:tn], ALU.mult, ALU.mult),
                             r=(R["osb"], R["sgate"], R_vt), w=(R["yT"],))
                    for m in range(8):
                        b = m % 4
                        for vc in range(4):
                            K.op(pe, lambda: TT.matmul(K.pb(b, tn), wo_v[:, vc, m * 128:(m + 1) * 128], yT[:, vc, 0:tn], start=(vc == 0), stop=(vc == 3)),
                                 r=rl_o + [R["yT"]], w=(K.bank[b],), sig=(vc == 3))
                        dst = hf[:, m, tc0:tc0 + tn]
                        ps_ = K.pb(b, tn)
                        if first_acc["v"]:
                            K.op(dve, lambda: V.scalar_tensor_tensor(dst, dst, ALPHA, ps_, ALU.mult, ALU.add), r=(K.bank[b], R_hf[m][nt]), w=(R_hf[m][nt],))
                        else:
                            K.op(dve, lambda: V.tensor_tensor(dst, dst, ps_, ALU.add), r=(K.bank[b], R_hf[m][nt]), w=(R_hf[m][nt],))
            first_acc["v"] = False
            K.dma_out(sp, oretp_d[0, h].rearrange("(c p) v -> p c v", p=128), Sc, R["Sc"])
            if h < 3:
                done_unit(("ret", h, "o"))
        K.barrier()
        done_unit(("ret", 3, "o"))

    def mamba():
        K.barrier(touch=(R_wq[2], R_wq[3]))
        scr = Scr(extra=True)

        def buf(n):
            o = scr.get(n)
            return K.arena[:, o:o + n]
        NTT = 18
        dt_t = K.view(scr.get(NTT * 32), [NTT, 32])
        g_t = K.view(scr.get(NTT * 32), [NTT, 32])
        dtw_t = K.view(scr.get(NTT * 32), [NTT, 32])
        tmp32 = buf(32)
        o_pre = scr.get(4 * 520)
        pre = K.view(o_pre, [4, 520])
        y2 = K.view(o_pre, [2, 512])
        rstd = K.arena[:, o_pre + 1024:o_pre + 1536]
        spre = K.view(scr.get(4 * 176), [4, 176])
        mpre = K.view(scr.get(4 * 19), [4, 19])
        halo = K.view(scr.get(4 * 3), [4, 3])
        cv = K.view(scr.get(4 * 512), [4, 512])
        cacc = buf(128)
        sz = K.view(scr.get(1024), [2, 512])
        ctmp = buf(128)
        xtok = buf(256)
        btok = buf(128)
        rhsd4 = K.view(scr.get(512), [4, 128])
        tmpd4_2 = [K.view(scr.get(512), [4, 128]), K.view(scr.get(512), [4, 128])]
        egbc4_2 = [K.view(scr.get(512), [4, 128]), K.view(scr.get(512), [4, 128])]
        tmpd4, egbc4 = tmpd4_2[0], egbc4_2[0]
        attsb4 = K.view(scr.get(512), [4, 128])
        qh4 = K.view(scr.get(512), [4, 128])
        vdt4 = K.view(scr.get(256), [4, 64])
        vpr4 = K.view(scr.get(256), [4, 64])
        rhsd, tmpd, egbc, attsb, qh = rhsd4[:, 0, :], tmpd4[:, 0, :], egbc4[:, 0, :], attsb4[:, 0, :], qh4[:, 0, :]
        decT = tmpd4[:, 1, :]
        vpr, vdt = vpr4[:, 0, :], vdt4[:, 0, :]
        o_vblk = scr.get(1024)
        vblk = K.view(o_vblk, [16, 64])
        vblk4 = K.arena[:, o_vblk:o_vblk + 512]
        ysq = K.view(o_vblk, [2, 512])
        hist_tok = K.arena[0:48, o_vblk:o_vblk + 512]
        o_upr = scr.get(1024)
        upr = K.view(o_upr, [16, 64])
        upr4 = K.view(o_upr, [4, 2, 64])
        c48 = K.arena[0:48, o_upr:o_upr + 512]
        ptok = K.arena[0:3, o_upr + 512:o_upr + 1024]
        S4 = [K.view(scr.get(256), [4, 64]), K.view(scr.get(256), [4, 64])]
        S0 = K.view(scr.get(1024), [16, 64])
        yT = K.view(scr.get(512), [2, 512], BF16)
        names = "vdt dt g dtw tmp32 pre spre mpre halo cv cacc sz hist c48 ctmp xtok btok rhsd tmpd egbc attsb qh vpr vblk upr S0 yT".split()
        R = {nm: Res("mb_" + nm) for nm in names}
        R["ysq"] = R["vblk"]
        R["hist"] = R["vblk"]
        R["c48"] = R["upr"]
        R["ptok"] = R["upr"]
        R["y2"] = R["pre"]
        R["rstd"] = R["pre"]
        R["decT"] = R["tmpd"]
        R_td = [R["tmpd"], Res("mb_tmpd1")]
        R_eg = [R["egbc"], Res("mb_egbc1")]
        RS4 = [Res("mb_S4_0"), Res("mb_S4_1")]
        all_tt = []
        for nt in range(5):
            for tt_ in token_tiles(nt):
                all_tt.append((nt,) + tt_)
        sel = c128[:, C_SEL:C_SEL + 48]
        udt = use_unit(("mam", "dt"))
        wdt = udt[0][1]
        for tix, (nt, ty, tc0, tn, nsub, L) in enumerate(all_tt):
            for k in range(8):
                K.op(pe, lambda: TT.matmul(K.pb(0, 32, 0, tn), hb[:, k, tc0:tc0 + tn], wdt[:, k, :], start=(k == 0), stop=(k == 7)),
                     r=udt[0][0] + [R_hb[nt]], w=(K.bank[0],), sig=(k == 7))
            d_ = dt_t[0:tn, tix, :]
            K.op(dve, lambda: V.tensor_tensor(d_, K.pb(0, 32, 0, tn), mh_bc[0:tn, 0, :], ALU.add), r=(K.bank[0], R_small), w=(R["dt"],))
            K.op(act, lambda: A.activation(d_, d_, AF.Exp), r=(R["dt"],), w=(R["dt"],))
            K.op(act, lambda: A.activation(d_, d_, AF.Ln, bias=1.0), r=(R["dt"],), w=(R["dt"],))
            la = tmp32[0:tn, :]
            K.op(dve, lambda: V.tensor_tensor(la, d_, negA[0:tn, :], ALU.mult), r=(R["dt"], R_small), w=(R["tmp32"],))
            K.op(pe, lambda: TT.matmul(K.pb(1, 32, 0, tn), cmask(ty, tn), la, start=True, stop=True), r=(R["tmp32"], R_c), w=(K.bank[1],))
            K.op(pe, lambda: TT.matmul(K.pb(2, 32, 0, tn), cltri(ty, tn), la, start=True, stop=True), r=(R["tmp32"], R_c), w=(K.bank[2],))
            K.op(act, lambda: A.copy(g_t[0:tn, tix, :], K.pb(1, 32, 0, tn)), r=(K.bank[1],), w=(R["g"],))
            w_ = dtw_t[0:tn, tix, :]
            K.op(dve, lambda: V.tensor_tensor(w_, K.pb(2, 32, 0, tn), g_t[0:tn, tix, :], ALU.subtract), r=(K.bank[2], R["g"]), w=(R["dtw"],))
            K.op(act, lambda: A.activation(w_, w_, AF.Exp), r=(R["dtw"],), w=(R["dtw"],))
            K.op(dve, lambda: V.tensor_tensor(w_, w_, d_, ALU.mult), r=(R["dtw"], R["dt"]), w=(R["dtw"],))
        done_unit(("mam", "dt"))
        hist_rows = stconv_d[0].rearrange("s r c -> (s r) c")
        oconvs_rows = oconvs_d[0].rearrange("s r c -> (s r) c")
        btiles = [(tix_, tt_[1], tt_[3]) for tix_, tt_ in enumerate(all_tt) if tt_[1] != TYPE_S]
        btile_idx = {bt[0]: i_ for i_, bt in enumerate(btiles)}

        def decay_stage(g, bt, slot):
            tix_, ty_, tn_ = bt
            gq = g_t[0:tn_, tix_, 4 * g:4 * g + 4]
            td, eg_ = tmpd4_2[slot], egbc4_2[slot]
            Rtd, Reg = R_td[slot], R_eg[slot]
            K.op(dve, lambda: V.tensor_tensor(rhsd4[0:tn_, :, 0:tn_], bcast_mid(ident[0:tn_, 0:tn_], 4), bcast_last(gq, tn_), ALU.mult), r=(R["g"], R_c), w=(R["rhsd"],))
            gb4 = K.pb(5, 512).rearrange("p (h t) -> p h t", t=128)
            for hh in range(4):
                K.op(pe, lambda: TT.matmul(K.pb(5, tn_, 0, 128, hh * 128), ones_f[0:tn_, :], rhsd4[0:tn_, hh, 0:tn_], start=True, stop=True), r=(R["rhsd"], R_small), w=(K.bank[5],), sig=(hh == 3))
            K.op(act, lambda: A.activation(eg_[:, :, 0:tn_], gb4[:, :, 0:tn_], AF.Exp), r=(K.bank[5],), w=(Reg,))
            K.op(dve, lambda: V.tensor_tensor(td[0:tn_, :, 0:tn_], gb4[0:tn_, :, 0:tn_], bcast_last(gq, tn_), ALU.subtract), r=(K.bank[5], R["g"]), w=(Rtd,))
            K.op(pool, lambda: G.tensor_tensor(td[0:tn_, :, 0:tn_], td[0:tn_, :, 0:tn_], bcast_mid(cneg(ty_, tn_), 4), ALU.add), r=(Rtd, R_c), w=(Rtd,))
            K.op(act, lambda: A.activation(td[0:tn_, :, 0:tn_], td[0:tn_, :, 0:tn_], AF.Exp), r=(Rtd,), w=(Rtd,))

        for g in range(8):
            uin = use_unit(("mam", g, "in"))
            rl = uin[0][0]
            wz_, wx_, wB_, wC_ = (uin[b][1] for b in range(4))
            uo = use_unit(("mam", g, "o"))
            wo_v, rl_o = uo[0][1], uo[0][0]
            chunks = [(wx_[:, :, 0:128], 2 * g), (wx_[:, :, 128:256], 2 * g + 1), (wB_, 16 + g), (wC_, 24 + g)]
            sidx = 0
            K.op(dve, lambda: V.memset(S4[0], 0.0), w=(RS4[0],))
            for ci, (wv, cch) in enumerate(chunks):
                K.dma_in(sp, hist_tok[:, ci * 128:(ci + 1) * 128], hist_rows[:, cch * 128:(cch + 1) * 128], R["hist"])
            for ci, (wv, cch) in enumerate(chunks):
                K.op(pe, lambda: TT.transpose(K.pb(6, 48), hist_tok[:, ci * 128:(ci + 1) * 128], ident[0:48, 0:48]), r=(R["hist"], R_c), w=(K.bank[6],))
                K.op(act, lambda: A.copy(spre[:, ci, :].rearrange("p (s t) -> p s t", t=11)[:, :, 0:3], K.pb(6, 48).rearrange("p (s r) -> p s r", r=3)),
                     r=(K.bank[6],), w=(R["spre"],))
            K.op(dve, lambda: V.memset(mpre[:, :, 0:3], 0.0), w=(R["mpre"],))
            tix = 0
            for nt, (c0, n) in enumerate(NTILES):
                if nt == 0:
                    pass
                for ci, (wv, cch) in enumerate(chunks):
                    for k in range(8):
                        K.op(pe, lambda: TT.matmul(K.pb(ci, n), wv[:, k, :], hb[:, k, c0:c0 + n], start=(k == 0), stop=(k == 7)),
                             r=rl + [R_hb[nt]], w=(K.bank[ci],), sig=(k == 7))
                for zc in range(2):
                    for k in range(8):
                        K.op(pe, lambda: TT.matmul(K.pb(4 + zc, n), wz_[:, k, zc * 128:(zc + 1) * 128], hb[:, k, c0:c0 + n], start=(k == 0), stop=(k == 7)),
                             r=rl + [R_hb[nt]], w=(K.bank[4 + zc],), sig=(k == 7))
                    K.op(act, lambda: A.activation(sz[:, zc, 0:n], K.pb(4 + zc, n), AF.Silu), r=(K.bank[4 + zc],), w=(R["sz"],))
                for ci, (wv, cch) in enumerate(chunks):
                    cc8, rr = cch % 8, cch // 8
                    wcol = [vecT[:, cc8, V_CW + 4 * t_ + rr:V_CW + 4 * t_ + rr + 1] for t_ in range(4)]
                    bcol = vecT[:, cc8, V_CB + rr:V_CB + rr + 1]
                    if nt == 0:
                        sp3 = spre[:, ci, :].rearrange("p (s t) -> p s t", t=11)
                        K.op(act, lambda: A.copy(sp3[:, :, 3:11], K.pb(ci, 128).rearrange("p (s t) -> p s t", t=8)), r=(K.bank[ci],), w=(R["spre"],))
                        K.op(act, lambda: A.copy(mpre[:, ci, 3:19], K.pb(ci, 16, 0, 128, 128)), r=(K.bank[ci],), w=(R["mpre"],))
                        K.op(dve, lambda: V.tensor_copy(cacc, K.pb(ci, 128)), r=(K.bank[ci],), w=(R["cacc"],))
                        K.op(pe, lambda: TT.transpose(K.pb(6, 128), cacc, ident), r=(R["cacc"], R_c), w=(K.bank[6],))
                        K.op(act, lambda: A.copy(ctmp, K.pb(6, 128)), r=(K.bank[6],), w=(R["ctmp"],))
                        K.op(pe, lambda: TT.matmul(K.pb(6, 128, 0, 48, 128), sel, ctmp, start=True, stop=True), r=(R["ctmp"], R_c), w=(K.bank[6],))
                        K.op(act, lambda: A.copy(c48[:, ci * 128:(ci + 1) * 128], K.pb(6, 128, 0, 48, 128)), r=(K.bank[6],), w=(R["c48"],))
                        co = cv[:, ci, 0:128].rearrange("p (s t) -> p s t", t=8)
                        K.op(act, lambda: A.activation(co, sp3[:, :, 3:11], AF.Identity, bias=bcol, scale=wcol[3]), r=(R["spre"], R_vt), w=(R["cv"],))
                        for t_ in range(3):
                            K.op(dve, lambda: V.scalar_tensor_tensor(co, sp3[:, :, t_:t_ + 8], wcol[t_], co, ALU.mult, ALU.add), r=(R["spre"], R["cv"], R_vt), w=(R["cv"],))
                        cm = cv[:, ci, 128:144]
                        K.op(act, lambda: A.activation(cm, mpre[:, ci, 3:19], AF.Identity, bias=bcol, scale=wcol[3]), r=(R["mpre"], R_vt), w=(R["cv"],))
                        for t_ in range(3):
                            K.op(dve, lambda: V.scalar_tensor_tensor(cm, mpre[:, ci, t_:t_ + 16], wcol[t_], cm, ALU.mult, ALU.add), r=(R["mpre"], R["cv"], R_vt), w=(R["cv"],))
                        K.op(dve, lambda: V.tensor_copy(halo[:, ci, :], mpre[:, ci, 16:19]), r=(R["mpre"],), w=(R["halo"],))
                    else:
                        K.op(dve, lambda: V.tensor_copy(pre[:, ci, 0:3], halo[:, ci, :]), r=(R["halo"],), w=(R["pre"],))
                        K.op(act, lambda: A.copy(pre[:, ci, 3:3 + n], K.pb(ci, n)), r=(K.bank[ci],), w=(R["pre"],))
                        K.op(dve, lambda: V.tensor_copy(halo[:, ci, :], pre[:, ci, n:n + 3]), r=(R["pre"],), w=(R["halo"],))
                        co = cv[:, ci, 0:n]
                        K.op(act, lambda: A.activation(co, pre[:, ci, 3:3 + n], AF.Identity, bias=bcol, scale=wcol[3]), r=(R["pre"], R_vt), w=(R["cv"],))
                        for t_ in range(3):
                            K.op(dve, lambda: V.scalar_tensor_tensor(co, pre[:, ci, t_:t_ + n], wcol[t_], co, ALU.mult, ALU.add), r=(R["pre"], R["cv"], R_vt), w=(R["cv"],))
                        if nt == 4:
                            K.op(pe, lambda: TT.transpose(K.pb(6, 128, 0, 3), pre[:, ci, n:n + 3], ident), r=(R["pre"], R_c), w=(K.bank[6],))
                            K.op(act, lambda: A.copy(ptok[:, ci * 128:(ci + 1) * 128], K.pb(6, 128, 0, 3)), r=(K.bank[6],), w=(R["ptok"],))
                    K.op(act, lambda: A.activation(cv[:, ci, 0:n], cv[:, ci, 0:n], AF.Silu), r=(R["cv"],), w=(R["cv"],))
                if nt == 0:
                    for ci, (wv, cch) in enumerate(chunks):
                        K.dma_out(sp, oconvs_rows[:, cch * 128:(cch + 1) * 128], c48[:, ci * 128:(ci + 1) * 128], R["c48"])
                if nt == 4:
                    for ci, (wv, cch) in enumerate(chunks):
                        K.dma_out(sp, oconvp_d[0, :, cch * 128:(cch + 1) * 128], ptok[:, ci * 128:(ci + 1) * 128], R["ptok"])
                for ti, (ty, tc0, tn, nsub, L) in enumerate(token_tiles(nt)):
                    lo = tc0 - c0
                    for xc in range(2):
                        K.op(pe, lambda: TT.transpose(K.pb(6, 128, 0, tn, xc * 128), cv[:, xc, lo:lo + tn], ident), r=(R["cv"], R_c), w=(K.bank[6],), sig=False)
                    K.op(pe, lambda: TT.transpose(K.pb(6, 128, 0, tn, 256), cv[:, 2, lo:lo + tn], ident), r=(R["cv"], R_c), w=(K.bank[6],))
                    K.op(act, lambda: A.copy(xtok[0:tn, :], K.pb(6, 256, 0, tn)), r=(K.bank[6],), w=(R["xtok"],))
                    K.op(act, lambda: A.copy(btok[0:tn, :], K.pb(6, 128, 0, tn, 256)), r=(K.bank[6],), w=(R["btok"],))
                    K.op(pe, lambda: TT.matmul(K.pb(7, tn, 0, tn), cv[:, 2, lo:lo + tn], cv[:, 3, lo:lo + tn], start=True, stop=True), r=(R["cv"],), w=(K.bank[7],))
                    if ty == TYPE_S:
                        for hh in range(4):
                            hd = 4 * g + hh
                            gcol = g_t[0:tn, tix, hd:hd + 1]
                            K.op(dve, lambda: V.tensor_scalar(rhsd[0:tn, 0:tn], ident[0:tn, 0:tn], gcol, None, ALU.mult), r=(R["g"], R_c), w=(R["rhsd"],))
                            K.op(pe, lambda: TT.matmul(K.pb(5, tn, 0, 128, 0), ones_f[0:tn, :], rhsd[0:tn, 0:tn], start=True, stop=True), r=(R["rhsd"], R_small), w=(K.bank[5],))
                            K.op(act, lambda: A.activation(egbc[:, 0:tn], K.pb(5, tn), AF.Exp), r=(K.bank[5],), w=(R["egbc"],))
                            K.op(dve, lambda: V.scalar_tensor_tensor(tmpd[0:tn, 0:tn], K.pb(5, tn, 0, tn), gcol, cneg(ty, tn), ALU.subtract, ALU.add), r=(K.bank[5], R_c, R["g"]), w=(R["tmpd"],))
                            K.op(act, lambda: A.activation(decT[0:tn, 0:tn], tmpd[0:tn, 0:tn], AF.Exp), r=(R["tmpd"],), w=(R["decT"],))
                            K.op(dve, lambda: V.tensor_tensor(attsb[0:tn, 0:tn], K.pb(7, tn, 0, tn), decT[0:tn, 0:tn], ALU.mult), r=(K.bank[7], R["decT"]), w=(R["attsb"],))
                            K.op(pool, lambda: G.tensor_tensor(qh[:, 0:tn], cv[:, 3, lo:lo + tn], egbc[:, 0:tn], ALU.mult), r=(R["cv"], R["egbc"]), w=(R["qh"],))
                            K.op(dve, lambda: V.tensor_scalar(vpr[0:tn, :], xtok[0:tn, hh * 64:(hh + 1) * 64], dtw_t[0:tn, tix, hd:hd + 1], None, ALU.mult), r=(R["xtok"], R["dtw"]), w=(R["vpr"],))
                            K.op(dve, lambda: V.tensor_scalar(vdt[0:tn, :], xtok[0:tn, hh * 64:(hh + 1) * 64], dt_t[0:tn, tix, hd:hd + 1], None, ALU.mult), r=(R["xtok"], R["dt"]), w=(R["vdt"],))
                            K.op(pool, lambda: G.tensor_tensor(vblk[0:tn, 0:nsub, :], bcast_mid(vpr[0:tn, :], nsub), bcast_last(cblk(ty, tn, nsub), 64), ALU.mult), r=(R["vpr"], R_c), w=(R["vblk"],))
                            for s0 in range(0, nsub, 8):
                                sn = min(8, nsub - s0)
                                K.op(pe, lambda: TT.matmul(K.pb(4, sn * 64), btok[0:tn, :], vblk[0:tn, s0:s0 + sn, :], start=True, stop=True), r=(R["btok"], R["vblk"]), w=(K.bank[4],))
                                K.op(act, lambda: A.copy(upr[:, s0:s0 + sn, :], K.pb(4, sn * 64).rearrange("p (s v) -> p s v", v=64)), r=(K.bank[4],), w=(R["upr"],))
                            po = 64 * (hh % 2)
                            ob = 2 + (hh // 2)
                            K.op(pe, lambda: TT.matmul(K.pb(ob, tn, po, po + 64, lo), vdt[0:tn, :], attsb[0:tn, 0:tn], start=True, stop=False),
                                 r=(R["vdt"], R["attsb"]), w=(K.bank[ob],), sig=False)
                            K.dma_in(sp, S0, stssm_d[0, :, hd].rearrange("s k v -> k s v"), R["S0"])
                            for s_ in range(16):
                                K.op(pe, lambda: TT.matmul(K.pb(ob, L, po, po + 64, lo + s_ * L), S0[:, s_, :], qh[:, s_ * L:(s_ + 1) * L], start=False, stop=(s_ == 15)),
                                     r=(R["S0"], R["qh"]), w=(K.bank[ob],), sig=(s_ == 15))
                            dv_ = egbc[:, L - 1:16 * L:L]
                            K.op(dve, lambda: V.tensor_tensor(S0, S0, bcast_last(dv_, 64), ALU.mult), r=(R["S0"], R["egbc"]), w=(R["S0"],))
                            K.op(dve, lambda: V.tensor_tensor(S0, S0, upr, ALU.add), r=(R["upr"], R["S0"]), w=(R["S0"],))
                            K.dma_out(sp, ossms_d[0, :, hd].rearrange("s k v -> k s v"), S0, R["S0"])
                    else:
                        bi = btile_idx[tix]
                        slot = bi % 2
                        if bi == 0:
                            decay_stage(g, btiles[0], 0)
                        tmpd4, egbc4 = tmpd4_2[slot], egbc4_2[slot]
                        Rtd, Reg = R_td[slot], R_eg[slot]
                        dq = dt_t[0:tn, tix, 4 * g:4 * g + 4]
                        wq4 = dtw_t[0:tn, tix, 4 * g:4 * g + 4]
                        x4 = xtok[0:tn, :].rearrange("p (h v) -> p h v", v=64)
                        K.op(dve, lambda: V.tensor_tensor(vpr4[0:tn], x4, bcast_last(wq4, 64), ALU.mult), r=(R["xtok"], R["dtw"]), w=(R["vpr"],))
                        K.op(pool, lambda: G.tensor_tensor(qh4[:, :, 0:tn], egbc4[:, :, 0:tn], bcast_mid(cv[:, 3, lo:lo + tn], 4), ALU.mult), r=(R["cv"], Reg), w=(R["qh"],))
                        if nsub == 1:
                            urhs, rur = vpr4[0:tn].rearrange("p h v -> p (h v)"), R["vpr"]
                        else:
                            vp = vpr4[0:tn]
                            pat = [list(x_) for x_ in vp.ap]
                            in0 = bass.AP(vp.tensor, vp.offset, [pat[0], pat[1], [0, nsub], pat[2]])
                            bk = cblk(ty, tn, nsub)
                            pb_ = [list(x_) for x_ in bk.ap]
                            in1 = bass.AP(bk.tensor, bk.offset, [pb_[0], [0, 4], pb_[1], [0, 64]])
                            v4 = vblk4[0:tn, :].rearrange("p (h s v) -> p h s v", s=nsub, v=64)
                            K.op(dve, lambda: V.tensor_tensor(v4, in0, in1, ALU.mult), r=(R["vpr"], R_c), w=(R["vblk"],))
                            urhs, rur = vblk4[0:tn, :], R["vblk"]
                        K.op(pe, lambda: TT.matmul(K.pb(4, 4 * nsub * 64), btok[0:tn, :], urhs, start=True, stop=True), r=(R["btok"], rur), w=(K.bank[4],))
                        K.op(act, lambda: A.copy(upr4[:, :, 0:nsub, :], K.pb(4, 4 * nsub * 64).rearrange("p (h s v) -> p h s v", s=nsub, v=64)), r=(K.bank[4],), w=(R["upr"],))
                        K.op(dve, lambda: V.tensor_tensor(attsb4[0:tn, :, 0:tn], tmpd4[0:tn, :, 0:tn], bcast_mid(K.pb(7, tn, 0, tn), 4), ALU.mult), r=(K.bank[7], Rtd), w=(R["attsb"],))
                        K.op(dve, lambda: V.tensor_tensor(vdt4[0:tn], x4, bcast_last(dq, 64), ALU.mult), r=(R["xtok"], R["dt"]), w=(R["vdt"],))
                        for hh in range(4):
                            po = 64 * (hh % 2)
                            ob = 2 + (hh // 2)
                            K.op(pe, lambda: TT.matmul(K.pb(ob, tn, po, po + 64, lo), vdt4[0:tn, hh, :], attsb4[0:tn, hh, 0:tn], start=True, stop=False),
                                 r=(R["vdt"], R["attsb"]), w=(K.bank[ob],), sig=False)
                        for s_ in range(nsub):
                            cur = sidx % 2
                            for hh in range(4):
                                po = 64 * (hh % 2)
                                ob = 2 + (hh // 2)
                                K.op(pe, lambda: TT.matmul(K.pb(ob, L, po, po + 64, lo + s_ * L), S4[cur][:, hh, :], qh4[:, hh, s_ * L:(s_ + 1) * L], start=False, stop=(s_ == nsub - 1)),
                                     r=(RS4[cur], R["qh"]), w=(K.bank[ob],), sig=(hh == 3))
                            dcl = egbc4[:, :, (s_ + 1) * L - 1:(s_ + 1) * L]
                            K.op(dve, lambda: V.tensor_tensor(S4[1 - cur], S4[cur], bcast_last(dcl, 64), ALU.mult), r=(RS4[cur], Reg), w=(RS4[1 - cur],))
                            K.op(dve, lambda: V.tensor_tensor(S4[1 - cur], S4[1 - cur], upr4[:, :, s_, :], ALU.add), r=(R["upr"], RS4[1 - cur]), w=(RS4[1 - cur],))
                            sidx += 1
                        if bi + 1 < len(btiles):
                            decay_stage(g, btiles[bi + 1], (bi + 1) % 2)
                    tix += 1
                for pc in range(2):
                    cch = 2 * g + pc
                    K.op(dve, lambda: V.scalar_tensor_tensor(y2[:, pc, 0:n], cv[:, pc, 0:n], dcol[:, cch:cch + 1], K.pb(2 + pc, n), ALU.mult, ALU.add),
                         r=(R["cv"], K.bank[2 + pc], R_small), w=(R["y2"],))
                    K.op(pool, lambda: G.tensor_tensor(y2[:, pc, 0:n], y2[:, pc, 0:n], sz[:, pc, 0:n], ALU.mult), r=(R["y2"], R["sz"]), w=(R["y2"],))
                    K.op(act, lambda: A.activation(ysq[:, pc, 0:n], y2[:, pc, 0:n], AF.Square), r=(R["y2"],), w=(R["ysq"],))
                for pc in range(2):
                    K.op(pe, lambda: TT.matmul(K.pb(7, n), ones_f, ysq[:, pc, 0:n], start=(pc == 0), stop=(pc == 1)), r=(R["ysq"], R_small), w=(K.bank[7],), sig=(pc == 1))
                K.op(act, lambda: A.activation(rstd[:, 0:n], K.pb(7, n), AF.Ln, bias=EPS, scale=1.0 / 256), r=(K.bank[7],), w=(R["rstd"],))
                K.op(act, lambda: A.activation(rstd[:, 0:n], rstd[:, 0:n], AF.Exp, scale=-0.5), r=(R["rstd"],), w=(R["rstd"],))
                for pc in range(2):
                    cch = 2 * g + pc
                    ngc = vecT[:, cch % 8, V_MN + cch // 8:V_MN + cch // 8 + 1]
                    K.op(dve, lambda: V.scalar_tensor_tensor(yT[:, pc, 0:n], y2[:, pc, 0:n], ngc, rstd[:, 0:n], ALU.mult, ALU.mult), r=(R["y2"], R["rstd"], R_vt), w=(R["yT"],))
                for m in range(8):
                    b = m % 2
                    for pc in range(2):
                        K.op(pe, lambda: TT.matmul(K.pb(b, n), wo_v[:, pc, m * 128:(m + 1) * 128], yT[:, pc, 0:n], start=(pc == 0), stop=(pc == 1)),
                             r=rl_o + [R["yT"]], w=(K.bank[b],), sig=(pc == 1))
                    accumulate(m, nt, K.pb(b, n), K.bank[b])
            first_acc["v"] = False
            fin = sidx % 2
            for hh in range(4):
                K.dma_out(sp, ossmp_d[0, 4 * g + hh], S4[fin][:, hh, :], RS4[fin])
            if g < 7:
                done_unit(("mam", g, "o"))
        K.barrier(touch=(R_wq[2], R_wq[3]))
        done_unit(("mam", 7, "o"))

    rconst = host_consts()[3]
    import os
    dbg = os.environ.get("KDBG", "")
    if "novecs" not in dbg:
        load_vecs()
        K.barrier()
    if "noin" not in dbg:
        load_input()
    K.barrier()
    first_acc["v"] = True
    nsub_done = 0
    for i in range(DEPTH):
        for sub in range(3):
            if stages is not None and nsub_done >= stages:
                break
            if sub == 0:
                ffn(i, 0, ln_closures(i, 0))
            elif sub == 1:
                kind = i % 3
                if kind == 0:
                    hgrn(i // 3)
                elif kind == 1:
                    retention(rconst)
                else:
                    mamba()
                layernorm(i, 1)
            else:
                ffn(i, 1, ln_closures(i, 2))
            nsub_done += 1
    K.barrier()
    if "noout" not in dbg:
        store_output()
    K.finish()
    return nc


_CACHE = {}


def kernel(x_prompt, x_sample, state_hgrn, state_ret, state_ssm, state_conv, meta_tokens, ln_g, ln_b,
           ffn_w_gate, ffn_w_up, ffn_w_down, hg_lb_logits, hg_w_in, hg_norm_g, hg_w_o,
           ret_w_in, ret_norm_g, ret_w_o, m_w_in, m_conv_w, m_conv_b, m_dt_bias, m_a_log, m_d,
           m_norm_g, m_w_o):
    f = lambda a: np.ascontiguousarray(np.asarray(a, dtype=np.float32))
    x_prompt, x_sample, meta_tokens = f(x_prompt), f(x_sample), f(meta_tokens)
    c128, cs, rett, _ = host_consts()
    vecs = np.zeros((64, D), np.float32)
    vecs[V_LNG:V_LNG + 12] = f(ln_g).reshape(12, D)
    vecs[V_LNB:V_LNB + 12] = f(ln_b).reshape(12, D)
    vecs[V_LB:V_LB + 4] = f(hg_lb_logits)
    vecs[V_HGN:V_HGN + 2] = f(hg_norm_g)
    vecs[V_RETN:V_RETN + 2] = f(ret_norm_g).reshape(2, D)
    vecs[V_CW:V_CW + 16] = f(m_conv_w).reshape(16, D)
    vecs[V_CB:V_CB + 4] = f(m_conv_b).reshape(4, D)
    vecs[V_MN:V_MN + 2] = f(m_norm_g).reshape(2, D)
    mhead = np.stack([f(m_dt_bias)[0], f(m_a_log)[0], f(m_d)[0]], axis=0)
    shared = {
        "c128": c128, "cs": cs, "rett": rett, "vecs": vecs, "mhead": mhead,
        "ffn_w_gate": f(ffn_w_gate), "ffn_w_up": f(ffn_w_up), "ffn_w_down": f(ffn_w_down),
        "hg_w_in": f(hg_w_in), "hg_w_o": f(hg_w_o), "ret_w_in": f(ret_w_in), "ret_w_o": f(ret_w_o),
        "m_w_in": f(m_w_in), "m_w_o": f(m_w_o),
    }
    state_hgrn, state_ret, state_ssm, state_conv = f(state_hgrn), f(state_ret), f(state_ssm), f(state_conv)
    in_maps = []
    for c in range(NCORES):
        sl = slice(16 * c, 16 * c + 16)
        xin = np.concatenate([x_sample[sl].reshape(128, D), meta_tokens, x_prompt[c]], axis=0)
        m = dict(shared)
        m["xin"] = np.ascontiguousarray(xin)
        m["st_hg"] = np.ascontiguousarray(state_hgrn[:, sl])
        m["st_ret"] = np.ascontiguousarray(state_ret[:, sl])
        m["st_ssm"] = np.ascontiguousarray(state_ssm[:, sl])
        m["st_conv"] = np.ascontiguousarray(state_conv[:, sl])
        in_maps.append(m)
    if "nc" not in _CACHE:
        _CACHE["nc"] = build_program()
    res = run_bass_kernel_spmd(_CACHE["nc"], in_maps, core_ids=list(range(NCORES)))
    rs = res.results
    y = [r["y"] for r in rs]
    y_prompt = np.stack([yy[144:] for yy in y], axis=0)
    y_sample = np.concatenate([yy[0:128].reshape(16, 8, D) for yy in y], axis=0)
    cat1 = lambda k: np.concatenate([r[k] for r in rs], axis=1)
    stk1 = lambda k: np.stack([r[k] for r in rs], axis=1)
    return (y_prompt.astype(np.float32), y_sample.astype(np.float32),
            stk1("o_hg_p"), cat1("o_hg_s"), stk1("o_ret_p"), cat1("o_ret_s"),
            stk1("o_ssm_p"), cat1("o_ssm_s"), stk1("o_conv_p"), cat1("o_conv_s"))
```

```python
import math
import numpy as np
import ml_dtypes
import concourse.bass as bass
import concourse.mybir as mybir
from concourse.bass_utils import run_bass_kernel_spmd

F32 = mybir.dt.float32
BF16 = mybir.dt.bfloat16
AF = mybir.ActivationFunctionType
ALU = mybir.AluOpType

NCORES = 8
D = 1024
DFF = 2816
DEPTH = 4
T = 2192
NTILES = [(0, 144)] + [(144 + 512 * i, 512) for i in range(4)]
ALPHA = (2 * DEPTH) ** 0.25
EPS = 1e-5
TYPE_S, TYPE_M, TYPE_P = 0, 1, 2


def token_tiles(nt):
    if nt == 0:
        return [(TYPE_S, 0, 128, 16, 8), (TYPE_M, 128, 16, 1, 16)]
    c0 = NTILES[nt][0]
    return [(TYPE_P, c0 + 128 * r, 128, 2, 64) for r in range(4)]


C_ID = 0
C_MASK = 128
C_NEG = C_MASK + 384
C_LTRI = C_NEG + 384
C_BLK = C_LTRI + 384
C_RM = C_BLK + 48
C_SEL = C_RM + 656
C_END = C_SEL + 48

V_LNG, V_LNB, V_LB, V_HGN, V_RETN, V_CW, V_CB, V_MN = 0, 12, 24, 28, 30, 32, 48, 52
V_ROWS = 54


def host_consts():
    c = np.zeros((128, C_END), np.float32)
    c[:, C_ID:C_ID + 128] = np.eye(128, dtype=np.float32)
    j = np.arange(128)[:, None]
    i = np.arange(128)[None, :]
    for ty, L, n in ((TYPE_S, 8, 128), (TYPE_M, 16, 16), (TYPE_P, 64, 128)):
        same = (j // L == i // L) & (j < n) & (i < n)
        m = (same & (j <= i)).astype(np.float32)
        c[:, C_MASK + 128 * ty:C_MASK + 128 * ty + 128] = m
        c[:, C_NEG + 128 * ty:C_NEG + 128 * ty + 128] = (m - 1.0) * 30000.0
        c[:, C_LTRI + 128 * ty:C_LTRI + 128 * ty + 128] = same.astype(np.float32)
        s = np.arange(16)[None, :]
        c[:, C_BLK + 16 * ty:C_BLK + 16 * ty + 16] = ((j // L == s) & (j < n)).astype(np.float32)
    rm = np.ones(656, np.float32)
    rm[0:128:8] = 0.0
    rm[128] = 0.0
    rm[144::64] = 0.0
    c[:, C_RM:C_RM + 656] = rm[None, :]
    for s_ in range(16):
        for r_ in range(3):
            c[s_ * 8 + 5 + r_, C_SEL + s_ * 3 + r_] = 1.0
    half = 128
    inv_freq = (np.float32(10000.0) ** (-np.arange(half, dtype=np.float32) / np.float32(half))).astype(np.float32)
    pos = np.zeros(T, np.float32)
    pos[0:128] = 16384 + (np.arange(128) % 8)
    pos[128:144] = np.arange(16)
    pos[144:] = 16 + np.arange(2048)
    ang = (pos[None, :].astype(np.float32) * inv_freq[:, None]).astype(np.float32)
    cs = np.stack([np.cos(ang), np.sin(ang)], axis=1).astype(np.float32)
    pidx = np.zeros(208, np.float64)
    pidx[0:128] = np.arange(128) % 8
    pidx[128:144] = np.arange(16)
    pidx[144:208] = np.arange(64)
    ret = np.zeros((128, 4, 2, 208), np.float32)
    gam = []
    for h in range(4):
        lg = math.log(1.0 - 2.0 ** (-5.0 - h))
        gam.append(lg)
        g = (pidx + 1.0) * lg
        ret[:, h, 0, :] = np.exp(g)[None, :]
        ret[:, h, 1, :] = (np.exp(-g) / 16.0)[None, :]
    return c, cs, ret, gam


class Res:
    __slots__ = ("name", "w", "rd", "dsem", "dcnt", "excl")

    def __init__(self, name, excl=False):
        self.name = name
        self.excl = excl
        self.w = None
        self.rd = {}
        self.dsem = None
        self.dcnt = 0


class Eng:
    def __init__(self, name, eng, sem):
        self.name, self.eng, self.sem = name, eng, sem
        self.cnt = 0
        self.seen = {}


class Builder:
    def __init__(self):
        nc = bass.Bass("TRN2", target_bir_lowering=False)
        self.nc = nc
        self.pe = Eng("pe", nc.tensor, nc.semaphore("s_pe").__enter__())
        self.dve = Eng("dve", nc.vector, nc.semaphore("s_dve").__enter__())
        self.act = Eng("act", nc.scalar, nc.semaphore("s_act").__enter__())
        self.pool = Eng("pool", nc.gpsimd, nc.semaphore("s_pool").__enter__())
        self.sp = Eng("sp", nc.sync, None)
        self.compute = [self.pe, self.dve, self.act, self.pool]
        self.nsem = 4
        self.out_waits = []
        self.dma_tags = {}
        self.semcache = {}
        self.arena = nc.alloc_sbuf_tensor("arena", [128, 53000], F32)
        self.aoff = 0
        self.psum = nc.alloc_psum_tensor("psum", [128, 4096], F32)
        self.bank = [Res("bank%d" % b, excl=True) for b in range(8)]

    def alloc(self, nfloats):
        o = self.aoff
        self.aoff += nfloats
        assert self.aoff <= 53000, self.aoff
        return o

    def view(self, off, shape, dt=F32):
        n = int(np.prod(shape))
        if dt == F32:
            a = self.arena[:, off:off + n]
        else:
            a = self.arena[:, off:off + (n + 1) // 2].bitcast(BF16)
        if len(shape) == 2:
            return a.rearrange("p (a b) -> p a b", b=shape[1])
        if len(shape) == 3:
            return a.rearrange("p (a b c) -> p a b c", b=shape[1], c=shape[2])
        return a

    def pb(self, b, n=512, p0=0, p1=128, c0=0):
        return self.psum[p0:p1, b * 512 + c0:b * 512 + c0 + n]

    def _wait(self, E, tag):
        key, sem, val = tag
        if E is self.pe and key == "pe":
            return
        if E.seen.get(key, 0) >= val:
            return
        E.eng.wait_ge(sem, val)
        E.seen[key] = val

    def _deps(self, E, reads, writes):
        for r in reads:
            if r.w is not None:
                self._wait(E, r.w)
            if r.excl:
                for key, tag in list(r.rd.items()):
                    if key != E.name:
                        self._wait(E, tag)
        for w in writes:
            if w.w is not None:
                self._wait(E, w.w)
            for tag in list(w.rd.values()):
                self._wait(E, tag)

    def op(self, E, emit, r=(), w=(), sig=True):
        self._deps(E, r, w)
        ins = emit()
        if sig:
            E.cnt += 1
            ins.then_inc(E.sem, 1)
            tag = (E.name, E.sem, E.cnt)
        else:
            tag = (E.name, E.sem, E.cnt + 1)
        for x in r:
            o = x.rd.get(E.name)
            if o is None or o[2] < tag[2]:
                x.rd[E.name] = tag
        for x in w:
            x.w = tag
            x.rd = {}
        return ins

    def _dsem(self, res):
        if res.dsem is None:
            ent = self.semcache.get(res.name)
            if ent is None:
                ent = [self.nc.semaphore("d_" + res.name).__enter__(), 0]
                self.semcache[res.name] = ent
                self.nsem += 1
            res.dsem = ent[0]
            res.dcnt = ent[1]
        return res.dsem

    def dma_in(self, Q, out_ap, in_ap, res, **kw):
        self._deps(Q, (), (res,))
        sem = self._dsem(res)
        Q.eng.dma_start(out=out_ap, in_=in_ap, **kw).then_inc(sem, 16)
        res.dcnt += 16
        self.semcache[res.name][1] = res.dcnt
        res.w = ("d_" + res.name, sem, res.dcnt)
        res.rd = {}
        self.dma_tags[res.w[0]] = res.w

    def dma_out(self, Q, out_ap, in_ap, res, final=True, **kw):
        self._deps(Q, (res,), ())
        sem = self._dsem(res)
        Q.eng.dma_start(out=out_ap, in_=in_ap, **kw).then_inc(sem, 16)
        res.dcnt += 16
        self.semcache[res.name][1] = res.dcnt
        tag = ("d_" + res.name, sem, res.dcnt)
        res.rd["dma"] = tag
        self.out_waits.append(tag)
        self.dma_tags[tag[0]] = tag

    def barrier(self, touch=()):
        for E in self.compute + [self.sp]:
            for F in self.compute:
                if E is not F and F.cnt > 0:
                    self._wait(E, (F.name, F.sem, F.cnt))
            for tag in self.dma_tags.values():
                self._wait(E, tag)
        for res in touch:
            for F in self.compute:
                if F.cnt > 0:
                    res.rd[F.name] = (F.name, F.sem, F.cnt)
            for tag in self.dma_tags.values():
                res.rd[tag[0]] = tag
        self.dma_tags = {}

    def finish(self):
        for E in self.compute:
            if E.cnt > 0:
                self._wait(self.sp, (E.name, E.sem, E.cnt))
        last = {}
        for key, sem, val in self.out_waits:
            if key not in last or last[key][1] < val:
                last[key] = (sem, val)
        for key, (sem, val) in last.items():
            self.sp.eng.wait_ge(sem, val)
            self.act.eng.wait_ge(sem, val)


def bcast_mid(ap, n):
    pat = [list(x) for x in ap.ap]
    return bass.AP(ap.tensor, ap.offset, [pat[0], [0, n]] + pat[1:])


def bcast_last(ap, n):
    pat = [list(x) for x in ap.ap]
    if len(pat) == 3:
        pat = pat[:2]
    return bass.AP(ap.tensor, ap.offset, pat + [[0, n]])


def build_program(stages=None):
    K = Builder()
    nc = K.nc
    pe, dve, act, pool, sp = K.pe, K.dve, K.act, K.pool, K.sp
    TT = nc.tensor
    V = nc.vector
    A = nc.scalar
    G = nc.gpsimd

    def din(name, shape, dt=F32):
        return nc.dram_tensor(name, list(shape), dt, kind="ExternalInput").ap()

    def dout(name, shape):
        return nc.dram_tensor(name, list(shape), F32, kind="ExternalOutput").ap()

    xin = din("xin", [T, D])
    c128_d = din("c128", [128, C_END])
    cs_d = din("cs", [128, 2, T])
    rett_d = din("rett", [128, 4, 2, 208])
    vecs_d = din("vecs", [64, D])
    mhead_d = din("mhead", [3, 32])
    wg_d = din("ffn_w_gate", [DEPTH, 2, D, DFF])
    wu_d = din("ffn_w_up", [DEPTH, 2, D, DFF])
    wd_d = din("ffn_w_down", [DEPTH, 2, DFF, D])
    hgwi_d = din("hg_w_in", [2, D, 4096])
    hgwo_d = din("hg_w_o", [2, D, D])
    rwi_d = din("ret_w_in", [1, D, 6144])
    rwo_d = din("ret_w_o", [1, 2048, D])
    mwi_d = din("m_w_in", [1, D, 6176])
    mwo_d = din("m_w_o", [1, 2048, D])
    sthg_d = din("st_hg", [2, 16, 8, 128, 128])
    stret_d = din("st_ret", [1, 16, 4, 256, 512])
    stssm_d = din("st_ssm", [1, 16, 32, 128, 64])
    stconv_d = din("st_conv", [1, 16, 3, 4096])

    y_d = dout("y", [T, D])
    ohgp_d = dout("o_hg_p", [2, 8, 128, 128])
    ohgs_d = dout("o_hg_s", [2, 16, 8, 128, 128])
    oretp_d = dout("o_ret_p", [1, 4, 256, 512])
    orets_d = dout("o_ret_s", [1, 16, 4, 256, 512])
    ossmp_d = dout("o_ssm_p", [1, 32, 128, 64])
    ossms_d = dout("o_ssm_s", [1, 16, 32, 128, 64])
    oconvp_d = dout("o_conv_p", [1, 3, 4096])
    oconvs_d = dout("o_conv_s", [1, 16, 3, 4096])

    o_hf = K.alloc(8 * T)
    o_hb = K.alloc(8 * T // 2)
    o_c = K.alloc(C_END)
    o_vt = K.alloc(8 * 64)
    o_small = K.alloc(512)
    o_wq = [K.alloc(3072) for _ in range(4)]
    o_scr = K.alloc(0)
    SCR_END = 53000

    hf = K.view(o_hf, [8, T])
    hb = K.view(o_hb, [8, T], BF16)
    c128 = K.arena[:, o_c:o_c + C_END]
    vecT = K.view(o_vt, [8, 64])
    small = K.arena[:, o_small:o_small + 512]
    R_hf = [[Res("hf%d_%d" % (c, n)) for n in range(5)] for c in range(8)]
    R_hb = [Res("hb%d" % n) for n in range(5)]
    R_c = Res("consts")
    R_vt = Res("vecT")
    R_small = Res("small")
    R_wq = [Res("wq%d" % i) for i in range(4)]

    ident = c128[:, C_ID:C_ID + 128]

    def cmask(ty, n):
        return c128[0:n, C_MASK + 128 * ty:C_MASK + 128 * ty + n]

    def cneg(ty, n):
        return c128[0:n, C_NEG + 128 * ty:C_NEG + 128 * ty + n]

    def cltri(ty, n):
        return c128[0:n, C_LTRI + 128 * ty:C_LTRI + 128 * ty + n]

    def cblk(ty, n, nsub):
        return c128[0:n, C_BLK + 16 * ty:C_BLK + 16 * ty + nsub]

    ones_f = small[:, 0:128]
    ones_b = small[:, 128:192].bitcast(BF16)
    lbcol = small[:, 192:208].rearrange("p (j h) -> p j h", h=8)
    omlcol = small[:, 208:224].rearrange("p (j h) -> p j h", h=8)
    nomlcol = small[:, 224:240].rearrange("p (j h) -> p j h", h=8)
    mh_bc = small[:, 240:336].rearrange("p (r h) -> p r h", h=32)
    negA = small[:, 336:368]
    dcol = small[:, 368:384]
    lbtmp = small[:, 384:448]

    K.dma_in(sp, c128, c128_d, R_c)
    K.op(dve, lambda: V.memset(ones_f, 1.0), w=(R_small,))
    K.op(dve, lambda: V.memset(ones_b, 1.0), w=(R_small,))
    import os as _os
    if "nomh" not in _os.environ.get("KDBG", ""):
        K.dma_in(sp, mh_bc, bass.AP(mhead_d.tensor, 0, [[0, 128], [32, 3], [1, 32]]), R_small)
        with nc.allow_non_contiguous_dma(reason="tiny const"):
            K.dma_in(sp, dcol[0:64, :], bass.AP(mhead_d.tensor, 64, [[0, 64], [2, 16]]), R_small)
            K.dma_in(sp, dcol[64:128, :], bass.AP(mhead_d.tensor, 65, [[0, 64], [2, 16]]), R_small)

    class Scr:
        def __init__(self, extra=False):
            self.off = o_scr
            self.end = SCR_END
            self.extra = [o_wq[2], o_wq[2] + 6144] if extra else None

        def get(self, n):
            if self.off + n <= self.end:
                o = self.off
                self.off += n
                return o
            assert self.extra is not None and self.extra[0] + n <= self.extra[1], "scratch overflow"
            o = self.extra[0]
            self.extra[0] += n
            return o

    units = []
    wstate = {"next": 0}

    def emit_loads(upto):
        while wstate["next"] <= min(upto, len(units) - 1):
            u = units[wstate["next"]]
            for res_list, dst, src in u:
                K._deps(pool, (), res_list)
                sem = K._dsem(res_list[0])
                G.dma_start(out=dst, in_=src).then_inc(sem, 16)
                res_list[0].dcnt += 16
                K.semcache[res_list[0].name][1] = res_list[0].dcnt
                tag = ("d_" + res_list[0].name, sem, res_list[0].dcnt)
                for rr in res_list:
                    rr.w = tag
                    rr.rd = {}
            wstate["next"] += 1

    def wview(slot_floats_off, shape):
        return K.view(slot_floats_off, shape, BF16)

    plan = []
    for i in range(DEPTH):
        plan.append(("ffn", i, 0))
        kind = i % 3
        plan.append((("hg", i // 3), ("ret", 0), ("mam", 0))[kind])
        plan.append(("ffn", i, 1))

    ffn_groups = [(4 * g, 4) for g in range(5)] + [(20, 2)]
    unit_index = {}
    big = [0]

    def add_unit(key, entries):
        unit_index[key] = len(units)
        units.append(entries)

    for ph in plan:
        if ph[0] == "ffn":
            _, i, s = ph
            for gi, (j0, gn) in enumerate(ffn_groups):
                slot = big[0] % 2
                big[0] += 1
                base = o_wq[2 * slot]
                rl = [R_wq[2 * slot], R_wq[2 * slot + 1]]
                wg_v = wview(base, [8, 512])
                wu_v = wview(base + 2048, [8, 512])
                wd_v = wview(base + 4096, [4, 1024])
                ent = [
                    (rl, wg_v[:, :, 0:gn * 128], wg_d[i, s].rearrange("(k p) n -> p k n", p=128)[:, :, j0 * 128:(j0 + gn) * 128]),
                    (rl, wu_v[:, :, 0:gn * 128], wu_d[i, s].rearrange("(k p) n -> p k n", p=128)[:, :, j0 * 128:(j0 + gn) * 128]),
                    (rl, wd_v[:, 0:gn, :], wd_d[i, s].rearrange("(j p) n -> p j n", p=128)[:, j0:j0 + gn, :]),
                ]
                add_unit(("ffn", i, s, gi), ent)
            if big[0] % 2 == 1:
                pass
        else:
            msl = [0]

            def mslot():
                s_ = msl[0] % 2
                msl[0] += 1
                return o_wq[s_], [R_wq[s_]]

            if ph[0] == "hg":
                j = ph[1]
                wsrc = hgwi_d[j].rearrange("(k p) n -> p k n", p=128)
                for h in range(8):
                    base, rl = mslot()
                    wi_v = wview(base, [8, 512])
                    wo_v = wview(base + 2048, [1024])
                    ent = []
                    for b4 in range(4):
                        ent.append((rl, wi_v[:, :, b4 * 128:(b4 + 1) * 128], wsrc[:, :, b4 * 1024 + h * 128:b4 * 1024 + (h + 1) * 128]))
                    ent.append((rl, wo_v, hgwo_d[j, h * 128:(h + 1) * 128, :]))
                    add_unit(("hg", j, h), ent)
            elif ph[0] == "ret":
                wsrc = rwi_d[0].rearrange("(k p) n -> p k n", p=128)
                for h in range(4):
                    v_ = wview(o_wq[3], [8, 512])
                    add_unit(("ret", h, "qk"), [
                        ([R_wq[3]], v_[:, :, 0:256], wsrc[:, :, h * 256:(h + 1) * 256]),
                        ([R_wq[3]], v_[:, :, 256:512], wsrc[:, :, 1024 + h * 256:1024 + (h + 1) * 256])])
                    v_ = wview(o_wq[1], [8, 512])
                    add_unit(("ret", h, "v"), [([R_wq[1]], v_, wsrc[:, :, 2048 + h * 512:2048 + (h + 1) * 512])])
                    v_ = wview(o_wq[2], [8, 512])
                    add_unit(("ret", h, "g"), [([R_wq[2]], v_, wsrc[:, :, 4096 + h * 512:4096 + (h + 1) * 512])])
                    v_ = wview(o_wq[0], [4, 1024])
                    add_unit(("ret", h, "o"), [([R_wq[0]], v_, rwo_d[0, h * 512:(h + 1) * 512, :].rearrange("(c p) n -> p c n", p=128))])
            else:
                wsrc = mwi_d[0].rearrange("(k p) n -> p k n", p=128)
                base, rl = mslot()
                v_ = wview(base, [8, 32])
                add_unit(("mam", "dt"), [(rl, v_, wsrc[:, :, 6144:6176])])
                for g in range(8):
                    base, rl = mslot()
                    v_ = wview(base, [8, 768])
                    ent = [
                        (rl, v_[:, :, 0:256], wsrc[:, :, g * 256:(g + 1) * 256]),
                        (rl, v_[:, :, 256:512], wsrc[:, :, 2048 + g * 256:2048 + (g + 1) * 256]),
                        (rl, v_[:, :, 512:640], wsrc[:, :, 4096 + g * 128:4096 + (g + 1) * 128]),
                        (rl, v_[:, :, 640:768], wsrc[:, :, 5120 + g * 128:5120 + (g + 1) * 128]),
                    ]
                    add_unit(("mam", g, "in"), ent)
                    base, rl = mslot()
                    v_ = wview(base, [2, 1024])
                    add_unit(("mam", g, "o"), [(rl, v_, mwo_d[0, g * 256:(g + 1) * 256, :].rearrange("(c p) n -> p c n", p=128))])

    def use_unit(key):
        idx = unit_index[key]
        emit_loads(idx)
        return units[idx]

    def done_unit(key):
        emit_loads(unit_index[key] + 1)

    first_acc = {"v": True}

    def accumulate(m, nt, ps_ap, bank_res):
        c0, n = NTILES[nt]
        dst = hf[:, m, c0:c0 + n]
        if first_acc["v"]:
            K.op(dve, lambda: V.scalar_tensor_tensor(dst, dst, ALPHA, ps_ap, ALU.mult, ALU.add),
                 r=(bank_res, R_hf[m][nt]), w=(R_hf[m][nt],))
        else:
            K.op(dve, lambda: V.tensor_tensor(dst, dst, ps_ap, ALU.add),
                 r=(bank_res, R_hf[m][nt]), w=(R_hf[m][nt],))

    def load_vecs():
        scr = Scr()
        o = scr.get(1024)
        vtok = K.arena[0:64, o:o + 1024]
        R = Res("vecstage")
        K.dma_in(sp, vtok, vecs_d, R)
        for c in range(8):
            ps = K.pb(c % 4, 64)
            K.op(pe, lambda: TT.transpose(ps, vtok[:, c * 128:(c + 1) * 128], ident[0:64, 0:64]),
                 r=(R, R_c), w=(K.bank[c % 4],))
            K.op(act, lambda: A.copy(vecT[:, c, :], ps), r=(K.bank[c % 4],), w=(R_vt,))
        lg = vecT[:, :, V_LB:V_LB + 4]
        mx = lbtmp[:, 0:8]
        ex = lbtmp[:, 8:40].rearrange("p (h d) -> p h d", d=4)
        sm = lbtmp[:, 40:48]
        K.op(dve, lambda: V.tensor_reduce(mx, lg, mybir.AxisListType.X, ALU.max), r=(R_vt,), w=(R_small,))
        K.op(dve, lambda: V.tensor_tensor(ex, lg, bcast_last(mx, 4), ALU.subtract), r=(R_vt, R_small), w=(R_small,))
        K.op(act, lambda: A.activation(ex, ex, AF.Exp), r=(R_small,), w=(R_small,))
        K.op(dve, lambda: V.tensor_reduce(sm, ex, mybir.AxisListType.X, ALU.add), r=(R_small,), w=(R_small,))
        K.op(dve, lambda: V.reciprocal(sm, sm), r=(R_small,), w=(R_small,))
        K.op(dve, lambda: V.memset(lbcol[:, 0, :], 0.0), w=(R_small,))
        t3 = lbtmp[:, 48:56]
        K.op(dve, lambda: V.tensor_tensor(t3, ex[:, :, 1], ex[:, :, 2], ALU.add), r=(R_small,), w=(R_small,))
        K.op(dve, lambda: V.tensor_tensor(t3, t3, ex[:, :, 3], ALU.add), r=(R_small,), w=(R_small,))
        K.op(dve, lambda: V.tensor_tensor(lbcol[:, 1, :], t3, sm, ALU.mult), r=(R_small,), w=(R_small,))
        lb_all = small[:, 192:208]
        K.op(dve, lambda: V.tensor_scalar(small[:, 208:224], lb_all, -1.0, 1.0, ALU.mult, ALU.add), r=(R_small,), w=(R_small,))
        K.op(dve, lambda: V.tensor_scalar(small[:, 224:240], lb_all, 1.0, -1.0, ALU.mult, ALU.add), r=(R_small,), w=(R_small,))
        K.op(act, lambda: A.activation(negA, mh_bc[:, 1, :], AF.Exp), r=(R_small,), w=(R_small,))
        K.op(dve, lambda: V.tensor_scalar(negA, negA, -1.0, None, ALU.mult), r=(R_small,), w=(R_small,))

    def load_input():
        scr = Scr()
        xs = [K.arena[:, o:o + 1024] for o in (scr.get(1024), scr.get(1024))]
        Rx = [Res("xs0"), Res("xs1")]
        tiles = [(0, 128), (128, 16)] + [(144 + 128 * r, 128) for r in range(16)]
        import os
        if "nometa" in os.environ.get("KDBG", ""):
            tiles = [t_ for t_ in tiles if t_[1] == 128]
        if "ntiles" in os.environ.get("KDBG", ""):
            tiles = tiles[:int(os.environ["KNT"])]
        for ti, (c0, n) in enumerate(tiles):
            s = ti % 2
            K.dma_in(sp, xs[s][0:n, :], xin[c0:c0 + n, :], Rx[s])
            nt = 0 if c0 < 144 else 1 + (c0 - 144) // 512
            for half in range(2):
                b = (2 * ti + half) % 4
                ps = K.psum[:, b * 512:b * 512 + 512].rearrange("p (c n) -> p c n", c=4)
                for cc in range(4):
                    c = half * 4 + cc
                    K.op(pe, lambda: TT.transpose(ps[:, cc, 0:n], xs[s][0:n, c * 128:(c + 1) * 128], ident[0:n, 0:n]),
                         r=(Rx[s], R_c), w=(K.bank[b],), sig=(cc == 3))
                wr = [R_hf[half * 4 + cc][nt] for cc in range(4)]
                if "nohf" not in os.environ.get("KDBG", ""):
                    K.op(act, lambda: A.copy(hf[:, half * 4:half * 4 + 4, c0:c0 + n], ps[:, :, 0:n]), r=(K.bank[b],), w=wr)
                if "nohb" not in os.environ.get("KDBG", ""):
                    K.op(dve, lambda: V.tensor_copy(hb[:, half * 4:half * 4 + 4, c0:c0 + n], ps[:, :, 0:n]), r=(K.bank[b],), w=(R_hb[nt],))

    def store_output():
        scr = Scr()
        ys = [K.arena[:, o:o + 1024] for o in (scr.get(1024), scr.get(1024))]
        Ry = [Res("ys0"), Res("ys1")]
        tiles = [(0, 128), (128, 16)] + [(144 + 128 * r, 128) for r in range(16)]
        for ti, (c0, n) in enumerate(tiles):
            s = ti % 2
            nt = 0 if c0 < 144 else 1 + (c0 - 144) // 512
            for half in range(2):
                b = (2 * ti + half) % 4
                ps = K.psum[:, b * 512:b * 512 + 512]
                for cc in range(4):
                    c = half * 4 + cc
                    K.op(pe, lambda: TT.transpose(ps[0:n, cc * 128:(cc + 1) * 128], hf[:, c, c0:c0 + n], ident),
                         r=(R_hf[c][nt], R_c), w=(K.bank[b],), sig=(cc == 3))
                if half == 0:
                    K.op(act, lambda: A.copy(ys[s][0:n, 0:512], ps[0:n, :]), r=(K.bank[b],), w=(Ry[s],))
                else:
                    K.op(dve, lambda: V.tensor_copy(ys[s][0:n, 512:1024], ps[0:n, :]), r=(K.bank[b],), w=(Ry[s],))
            K.dma_out(sp, y_d[c0:c0 + n, :], ys[s][0:n, :], Ry[s])

    LN_OFF = o_scr + 3072
    ln_xb = K.view(LN_OFF, [8, 512], BF16)
    ln_sq = K.view(LN_OFF + 2048, [8, 512], BF16)
    ln_st = [K.view(LN_OFF + 4096, [4, 512]), K.view(LN_OFF + 6144, [4, 512])]
    LN_R = {"xb": Res("ln_xb"), "sq": Res("ln_sq"), "st": [Res("ln_st0"), Res("ln_st1")]}

    def ln_closures(li, lj):
        xb, sq = ln_xb, ln_sq
        Rxb, Rsq = LN_R["xb"], LN_R["sq"]
        row = li * 3 + lj

        def stats(nt):
            c0, n = NTILES[nt]
            st, Rst = ln_st[nt % 2], LN_R["st"][nt % 2]
            hft = hf[:, :, c0:c0 + n]
            rall = [R_hf[c][nt] for c in range(8)]
            K.op(pool, lambda: G.tensor_copy(xb[:, :, 0:n], hft), r=rall, w=(Rxb,))
            K.op(act, lambda: A.activation(sq[:, :, 0:n], hft, AF.Square), r=rall, w=(Rsq,))
            for c in range(8):
                K.op(pe, lambda: TT.matmul(K.pb(0, n), ones_b, xb[:, c, 0:n], start=(c == 0), stop=(c == 7)),
                     r=(Rxb, R_small), w=(K.bank[0],), sig=(c == 7))
            for c in range(8):
                K.op(pe, lambda: TT.matmul(K.pb(1, n), ones_b, sq[:, c, 0:n], start=(c == 0), stop=(c == 7)),
                     r=(Rsq, R_small), w=(K.bank[1],), sig=(c == 7))
            mean, var, rstd, nmr = (st[:, q, 0:n] for q in range(4))
            K.op(dve, lambda: V.tensor_scalar(mean, K.pb(0, n), 1.0 / D, None, ALU.mult), r=(K.bank[0],), w=(Rst,))
            K.op(dve, lambda: V.tensor_tensor(var, mean, mean, ALU.mult), r=(Rst,), w=(Rst,))
            K.op(dve, lambda: V.scalar_tensor_tensor(var, K.pb(1, n), 1.0 / D, var, ALU.mult, ALU.subtract), r=(K.bank[1], Rst), w=(Rst,))
            K.op(act, lambda: A.activation(rstd, var, AF.Ln, bias=EPS), r=(Rst,), w=(Rst,))
            K.op(act, lambda: A.activation(rstd, rstd, AF.Exp, scale=-0.5), r=(Rst,), w=(Rst,))
            K.op(dve, lambda: V.scalar_tensor_tensor(nmr, mean, -1.0, rstd, ALU.mult, ALU.mult), r=(Rst,), w=(Rst,))

        def apply(nt):
            c0, n = NTILES[nt]
            st, Rst = ln_st[nt % 2], LN_R["st"][nt % 2]
            hft = hf[:, :, c0:c0 + n]
            rall = [R_hf[c][nt] for c in range(8)]
            rstd, nmr = st[:, 2, 0:n], st[:, 3, 0:n]
            K.op(dve, lambda: V.tensor_tensor(hft, hft, bcast_mid(rstd, 8), ALU.mult), r=rall + [Rst], w=rall)
            K.op(pool, lambda: G.tensor_tensor(hft, hft, bcast_mid(nmr, 8), ALU.add), r=rall + [Rst], w=rall)
            for c in range(8):
                K.op(act, lambda: A.activation(hf[:, c, c0:c0 + n], hf[:, c, c0:c0 + n], AF.Identity,
                                               bias=vecT[:, c, V_LNB + row:V_LNB + row + 1],
                                               scale=vecT[:, c, V_LNG + row:V_LNG + row + 1]),
                     r=(R_hf[c][nt], R_vt), w=(R_hf[c][nt],))
            K.op(dve, lambda: V.tensor_copy(hb[:, :, c0:c0 + n], hft), r=rall, w=(R_hb[nt],))

        return stats, apply

    def layernorm(li, lj):
        stats, apply = ln_closures(li, lj)
        stats(0)
        for nt in range(5):
            if nt + 1 < 5:
                stats(nt + 1)
            apply(nt)
        first_acc["v"] = True

    ffn_actb = [K.view(o_scr + o, [4, 512], BF16) for o in (0, 1024)]
    ffn_sgb = [K.arena[:, o_scr + o:o_scr + o + 512] for o in (2048, 2560)]
    FFN_R = {"act": [Res("act0"), Res("act1")], "sg": [Res("sg0"), Res("sg1")]}

    def ffn(i, s, ln=None):
        actb, sgb = ffn_actb, ffn_sgb
        Ract, Rsg = FFN_R["act"], FFN_R["sg"]
        cnt = {"gu": 0, "y": 0, "a": 0}
        for gi, (j0, gn) in enumerate(ffn_groups):
            u = use_unit(("ffn", i, s, gi))
            rl = u[0][0]
            wg_v, wu_v, wd_v = u[0][1], u[1][1], u[2][1]
            for nt, (c0, n) in enumerate(NTILES):
                a = cnt["a"] % 2
                cnt["a"] += 1
                for jj in range(gn):
                    p = cnt["gu"] % 2
                    cnt["gu"] += 1
                    bg, bu = p, 2 + p
                    for k in range(8):
                        K.op(pe, lambda: TT.matmul(K.pb(bg, n), wg_v[:, k, jj * 128:(jj + 1) * 128], hb[:, k, c0:c0 + n], start=(k == 0), stop=(k == 7)),
                             r=rl + [R_hb[nt]], w=(K.bank[bg],), sig=(k == 7))
                    for k in range(8):
                        K.op(pe, lambda: TT.matmul(K.pb(bu, n), wu_v[:, k, jj * 128:(jj + 1) * 128], hb[:, k, c0:c0 + n], start=(k == 0), stop=(k == 7)),
                             r=rl + [R_hb[nt]], w=(K.bank[bu],), sig=(k == 7))
                    K.op(act, lambda: A.activation(sgb[p][:, 0:n], K.pb(bg, n), AF.Silu), r=(K.bank[bg],), w=(Rsg[p],))
                    K.op(dve, lambda: V.scalar_tensor_tensor(actb[a][:, jj, 0:n], K.pb(bu, n), 0.5, sgb[p][:, 0:n], ALU.mult, ALU.mult),
                         r=(K.bank[bu], Rsg[p]), w=(Ract[a],))
                for m in range(8):
                    by = 4 + cnt["y"] % 4
                    cnt["y"] += 1
                    for jj in range(gn):
                        K.op(pe, lambda: TT.matmul(K.pb(by, n), wd_v[:, jj, m * 128:(m + 1) * 128], actb[a][:, jj, 0:n], start=(jj == 0), stop=(jj == gn - 1)),
                             r=rl + [Ract[a]], w=(K.bank[by],), sig=(jj == gn - 1))
                    accumulate(m, nt, K.pb(by, n), K.bank[by])
                if ln is not None and gi == len(ffn_groups) - 1 and nt >= 1:
                    ln[0](nt - 1)
                    ln[1](nt - 1)
            first_acc["v"] = False
            done_unit(("ffn", i, s, gi))
        if ln is not None:
            ln[0](4)
            ln[1](4)
            first_acc["v"] = True

    def hgrn(j):
        K.barrier(touch=(R_wq[2], R_wq[3]))
        scr = Scr(extra=True)

        def buf(n):
            o = scr.get(n)
            return K.arena[:, o:o + n]
        sig_, lf, gcs, eng_ = (buf(512) for _ in range(4))
        setA = []
        for q in range(2):
            setA.append({
                "qh": buf(512), "kt": buf(512), "eg": buf(512), "sgate": buf(512),
                "vtok": K.view(scr.get(512), [4, 128]),
                "R": {nm: Res("hg_%s_%d" % (nm, q)) for nm in ("qh", "kt", "eg", "sgate", "vtok")},
            })
        attsb2 = [buf(128), buf(128)]
        ktok2 = [buf(128), buf(128)]
        vblk_s = K.view(scr.get(2048), [16, 128])
        upr_s = K.view(scr.get(2048), [16, 128])
        vblk_p = [K.view(scr.get(256), [2, 128]), K.view(scr.get(256), [2, 128])]
        upr_p = [K.view(scr.get(256), [2, 128]), K.view(scr.get(256), [2, 128])]
        Scur = [buf(128), buf(128)]
        S0 = K.view(scr.get(2048), [16, 128])
        osq = buf(512)
        t1 = osq
        rstd = buf(512)
        yT = K.view(scr.get(256), [512], BF16)
        R = {nm: Res("hg_" + nm) for nm in "sig lf gcs eng S0 osq rstd yT".split()}
        R["t1"] = R["osq"]
        R2 = {nm: [Res("hg_%s_a" % nm), Res("hg_%s_b" % nm)] for nm in ("attsb", "ktok", "vblk", "upr")}
        RS = [Res("hg_S0_"), Res("hg_S1_")]
        rmt = c128[:, C_RM:C_RM + 656]
        tcount = [0]
        st8 = {"sidx": 0}
        units_h = [None] * 8

        def stageA(h, nt, q):
            c0, n = NTILES[nt]
            SA = setA[q]
            RA = SA["R"]
            if nt == 0:
                units_h[h] = use_unit(("hg", j, h))
                K.dma_in(sp, S0, sthg_d[j, :, h].rearrange("s k v -> k s v"), R["S0"])
            if nt == 1 and h < 7:
                emit_loads(unit_index[("hg", j, h + 1)])
            u = units_h[h]
            rl = u[0][0]
            wq_, wz_, wi_, wgt_ = (u[b][1] for b in range(4))
            lb_c = lbcol[:, j, h:h + 1]
            oml_c = omlcol[:, j, h:h + 1]
            noml_c = nomlcol[:, j, h:h + 1]
            for bi, wv in enumerate((wq_, wz_, wgt_)):
                for k in range(8):
                    K.op(pe, lambda: TT.matmul(K.pb(bi, n), wv[:, k, :], hb[:, k, c0:c0 + n], start=(k == 0), stop=(k == 7)),
                         r=rl + [R_hb[nt]], w=(K.bank[bi],), sig=(k == 7))
            tts = token_tiles(nt)
            for ti, (ty, tc0, tn, nsub, L) in enumerate(tts):
                for k in range(8):
                    K.op(pe, lambda: TT.matmul(K.pb(3, 128, 0, tn, ti * 128), hb[:, k, tc0:tc0 + tn], wi_[:, k, :], start=(k == 0), stop=(k == 7)),
                         r=rl + [R_hb[nt]], w=(K.bank[3],), sig=(k == 7))
            qh, kt, eg, sgate, vtok = SA["qh"], SA["kt"], SA["eg"], SA["sgate"], SA["vtok"]
            K.op(act, lambda: A.activation(qh[:, 0:n], K.pb(0, n), AF.Silu), r=(K.bank[0],), w=(RA["qh"],))
            K.op(act, lambda: A.activation(sgate[:, 0:n], K.pb(2, n), AF.Silu), r=(K.bank[2],), w=(RA["sgate"],))
            K.op(act, lambda: A.activation(sig_[:, 0:n], K.pb(1, n), AF.Sigmoid), r=(K.bank[1],), w=(R["sig"],))
            for ti, (ty, tc0, tn, nsub, L) in enumerate(tts):
                K.op(act, lambda: A.copy(vtok[0:tn, ti, :], K.pb(3, 128, 0, tn, ti * 128)), r=(K.bank[3],), w=(RA["vtok"],))
            K.op(act, lambda: A.activation(lf[:, 0:n], sig_[:, 0:n], AF.Ln, bias=lb_c, scale=oml_c), r=(R["sig"], R_small), w=(R["lf"],))
            K.op(dve, lambda: V.tensor_scalar(kt[:, 0:n], sig_[:, 0:n], noml_c, oml_c, ALU.mult, ALU.add), r=(R["sig"], R_small), w=(RA["kt"],))
            rmo = 0 if nt == 0 else 144
            K.op(dve, lambda: V.tensor_tensor_scan(gcs[:, 0:n], rmt[:, rmo:rmo + n], lf[:, 0:n], 0.0, ALU.mult, ALU.add),
                 r=(R["lf"], R_c), w=(R["gcs"],))
            K.op(act, lambda: A.activation(eg[:, 0:n], gcs[:, 0:n], AF.Exp), r=(R["gcs"],), w=(RA["eg"],))
            K.op(act, lambda: A.activation(eng_[:, 0:n], gcs[:, 0:n], AF.Exp, scale=-1.0), r=(R["gcs"],), w=(R["eng"],))
            K.op(dve, lambda: V.tensor_tensor(qh[:, 0:n], qh[:, 0:n], eg[:, 0:n], ALU.mult), r=(RA["qh"], RA["eg"]), w=(RA["qh"],))
            K.op(pool, lambda: G.tensor_tensor(kt[:, 0:n], kt[:, 0:n], eng_[:, 0:n], ALU.mult), r=(RA["kt"], R["eng"]), w=(RA["kt"],))

        def stageB(h, nt, q):
            c0, n = NTILES[nt]
            SA = setA[q]
            RA = SA["R"]
            qh, kt, eg, sgate, vtok = SA["qh"], SA["kt"], SA["eg"], SA["sgate"], SA["vtok"]
            u = units_h[h]
            rl = u[0][0]
            wo_ = u[4][1]
            ng_c = vecT[:, h, V_HGN + j:V_HGN + j + 1]
            if nt == 0:
                st8["sidx"] = 0
                K.op(dve, lambda: V.memset(Scur[0], 0.0), w=(RS[0],))
            for ti, (ty, tc0, tn, nsub, L) in enumerate(token_tiles(nt)):
                lo = tc0 - c0
                pp = tcount[0] % 2
                tcount[0] += 1
                attsb, ktok = attsb2[pp], ktok2[pp]
                Ratt, Rkt = R2["attsb"][pp], R2["ktok"][pp]
                if ty == TYPE_S:
                    vblk, upr = vblk_s, upr_s
                    Rvb, Rup = R2["vblk"][0], R2["upr"][0]
                else:
                    vblk, upr = vblk_p[pp], upr_p[pp]
                    Rvb, Rup = R2["vblk"][pp], R2["upr"][pp]
                K.op(pe, lambda: TT.matmul(K.pb(4, tn, 0, tn), kt[:, lo:lo + tn], qh[:, lo:lo + tn], start=True, stop=True),
                     r=(RA["kt"], RA["qh"]), w=(K.bank[4],))
                K.op(dve, lambda: V.tensor_tensor(attsb[0:tn, 0:tn], K.pb(4, tn, 0, tn), cmask(ty, tn), ALU.mult), r=(K.bank[4], R_c), w=(Ratt,))
                K.op(pe, lambda: TT.transpose(K.pb(5, 128, 0, tn), kt[:, lo:lo + tn], ident), r=(RA["kt"], R_c), w=(K.bank[5],))
                K.op(act, lambda: A.copy(ktok[0:tn, :], K.pb(5, 128, 0, tn)), r=(K.bank[5],), w=(Rkt,))
                K.op(pool, lambda: G.tensor_tensor(vblk[0:tn, 0:nsub, :], bcast_mid(vtok[0:tn, ti, :], nsub), bcast_last(cblk(ty, tn, nsub), 128), ALU.mult),
                     r=(RA["vtok"], R_c), w=(Rvb,))
                K.op(pe, lambda: TT.matmul(K.pb(7, tn, 0, 128, lo), vtok[0:tn, ti, :], attsb[0:tn, 0:tn], start=True, stop=False),
                     r=(RA["vtok"], Ratt), w=(K.bank[7],), sig=False)
                for s0 in range(0, nsub, 4):
                    sn = min(4, nsub - s0)
                    K.op(pe, lambda: TT.matmul(K.pb(6, sn * 128), ktok[0:tn, :], vblk[0:tn, s0:s0 + sn, :], start=True, stop=True),
                         r=(Rkt, Rvb), w=(K.bank[6],))
                    dview = eg[:, lo + (s0 + 1) * L - 1:lo + (s0 + sn) * L:L] if sn > 1 else eg[:, lo + (s0 + 1) * L - 1:lo + (s0 + 1) * L]
                    K.op(dve, lambda: V.tensor_tensor(upr[:, s0:s0 + sn, :], K.pb(6, sn * 128).rearrange("p (s v) -> p s v", v=128), bcast_last(dview, 128), ALU.mult),
                         r=(K.bank[6], RA["eg"]), w=(Rup,))
                if ty == TYPE_S:
                    for s_ in range(16):
                        K.op(pe, lambda: TT.matmul(K.pb(7, L, 0, 128, lo + s_ * L), S0[:, s_, :], qh[:, lo + s_ * L:lo + (s_ + 1) * L], start=False, stop=(s_ == 15)),
                             r=(R["S0"], RA["qh"]), w=(K.bank[7],), sig=(s_ == 15))
                    dv_ = eg[:, lo + L - 1:lo + 16 * L:L]
                    K.op(dve, lambda: V.tensor_tensor(S0, S0, bcast_last(dv_, 128), ALU.mult), r=(R["S0"], RA["eg"]), w=(R["S0"],))
                    K.op(dve, lambda: V.tensor_tensor(S0, S0, upr, ALU.add), r=(Rup, R["S0"]), w=(R["S0"],))
                    K.dma_out(sp, ohgs_d[j, :, h].rearrange("s k v -> k s v"), S0, R["S0"])
                else:
                    for s_ in range(nsub):
                        cur = st8["sidx"] % 2
                        K.op(pe, lambda: TT.matmul(K.pb(7, L, 0, 128, lo + s_ * L), Scur[cur], qh[:, lo + s_ * L:lo + (s_ + 1) * L], start=False, stop=(s_ == nsub - 1)),
                             r=(RS[cur], RA["qh"]), w=(K.bank[7],), sig=(s_ == nsub - 1))
                        dcl = eg[:, lo + (s_ + 1) * L - 1:lo + (s_ + 1) * L]
                        K.op(dve, lambda: V.scalar_tensor_tensor(Scur[1 - cur], Scur[cur], dcl, upr[:, s_, :], ALU.mult, ALU.add),
                             r=(RS[cur], RA["eg"], Rup), w=(RS[1 - cur],))
                        st8["sidx"] += 1
            K.op(act, lambda: A.activation(osq[:, 0:n], K.pb(7, n), AF.Square), r=(K.bank[7],), w=(R["osq"],))
            K.op(pe, lambda: TT.matmul(K.pb(4, n), ones_f, osq[:, 0:n], start=True, stop=True), r=(R["osq"], R_small), w=(K.bank[4],))
            K.op(act, lambda: A.activation(rstd[:, 0:n], K.pb(4, n), AF.Ln, bias=EPS, scale=1.0 / 128), r=(K.bank[4],), w=(R["rstd"],))
            K.op(act, lambda: A.activation(rstd[:, 0:n], rstd[:, 0:n], AF.Exp, scale=-0.5), r=(R["rstd"],), w=(R["rstd"],))
            K.op(dve, lambda: V.tensor_tensor(t1[:, 0:n], K.pb(7, n), rstd[:, 0:n], ALU.mult), r=(K.bank[7], R["rstd"]), w=(R["t1"],))
            K.op(dve, lambda: V.scalar_tensor_tensor(yT[:, 0:n], t1[:, 0:n], ng_c, sgate[:, 0:n], ALU.mult, ALU.mult), r=(R["t1"], RA["sgate"], R_vt), w=(R["yT"],))
            if nt == 4:
                fin = st8["sidx"] % 2
                K.dma_out(sp, ohgp_d[j, h], Scur[fin], RS[fin])

        def stageC(h, nt, q):
            c0, n = NTILES[nt]
            u = units_h[h]
            rl = u[0][0]
            wo_ = u[4][1]
            for m in range(8):
                b = 4 + (m % 3)
                K.op(pe, lambda: TT.matmul(K.pb(b, n), wo_[:, m * 128:(m + 1) * 128], yT[:, 0:n], start=True, stop=True),
                     r=rl + [R["yT"]], w=(K.bank[b],))
                accumulate(m, nt, K.pb(b, n), K.bank[b])
            if nt == 4:
                first_acc["v"] = False

        steps = [(h, nt) for h in range(8) for nt in range(5)]
        stageA(steps[0][0], steps[0][1], 0)
        for i_, (h, nt) in enumerate(steps):
            stageB(h, nt, i_ % 2)
            if i_ + 1 < len(steps):
                stageA(steps[i_ + 1][0], steps[i_ + 1][1], (i_ + 1) % 2)
            stageC(h, nt, i_ % 2)
        K.barrier(touch=(R_wq[2], R_wq[3]))
        done_unit(("hg", j, 7))

    def retention(rconst):
        K.barrier()
        scr = Scr(extra=False)

        def buf(n):
            o = scr.get(n)
            return K.arena[:, o:o + n]
        cs = K.view(scr.get(1024), [2, 512])
        rt = K.view(scr.get(416), [2, 208])
        qk = K.view(scr.get(2048), [4, 512])
        ta, tb = buf(512), buf(512)
        vtok = buf(512)
        attsb, ktok = buf(128), buf(256)
        vb = buf(512)
        Sc = K.view(scr.get(1024), [2, 512])
        S0_2 = [K.view(scr.get(1024), [2, 512]), K.view(scr.get(1024), [2, 512])]
        osb = K.view(scr.get(512), [4, 128])
        osq = K.view(scr.get(512), [4, 128])
        stt = K.view(scr.get(512), [4, 128])
        sgate = K.view(scr.get(512), [4, 128])
        yT = K.view(scr.get(256), [4, 128], BF16)
        names = "cs rt qk ta tb vtok attsb ktok vb Sc S0 osb osq stt sgate yT".split()
        R = {nm: Res("rt_" + nm) for nm in names}
        RS0 = [R["S0"], Res("rt_S0b")]
        gam = rconst
        U2 = K.psum[:, 1024:2048].rearrange("p (c v) -> p c v", v=512)
        for h in range(4):
            uqk = use_unit(("ret", h, "qk"))
            uv = use_unit(("ret", h, "v"))
            ug = use_unit(("ret", h, "g"))
            uo = use_unit(("ret", h, "o"))
            wq_v, wk_v = uqk[0][1], uqk[1][1]
            rl_qk = uqk[0][0]
            wv_v, wg_v, wo_v = uv[0][1], ug[0][1], uo[0][1]
            rl_v, rl_g, rl_o = uv[0][0], ug[0][0], uo[0][0]
            K.dma_in(sp, rt, rett_d[:, h], R["rt"])
            dS, dM, dP = math.exp(8 * gam[h]), math.exp(16 * gam[h]), math.exp(64 * gam[h])
            K.op(dve, lambda: V.memset(Sc, 0.0), w=(R["Sc"],))
            for nt, (c0, n) in enumerate(NTILES):
                K.dma_in(sp, cs[:, :, 0:n], cs_d[:, :, c0:c0 + n], R["cs"])
                for qi, wv in enumerate((wq_v, wk_v)):
                    for half in range(2):
                        b = qi * 2 + half
                        for k in range(8):
                            K.op(pe, lambda: TT.matmul(K.pb(b, n), wv[:, k, half * 128:(half + 1) * 128], hb[:, k, c0:c0 + n], start=(k == 0), stop=(k == 7)),
                                 r=rl_qk + [R_hb[nt]], w=(K.bank[b],), sig=(k == 7))
                cosv, sinv = cs[:, 0, 0:n], cs[:, 1, 0:n]
                for qi in range(2):
                    b1, b2 = qi * 2, qi * 2 + 1
                    x1o, x2o = qk[:, qi * 2, 0:n], qk[:, qi * 2 + 1, 0:n]
                    K.op(dve, lambda: V.tensor_tensor(ta[:, 0:n], K.pb(b1, n), cosv, ALU.mult), r=(K.bank[b1], R["cs"]), w=(R["ta"],))
                    K.op(dve, lambda: V.tensor_tensor(tb[:, 0:n], K.pb(b2, n), sinv, ALU.mult), r=(K.bank[b2], R["cs"]), w=(R["tb"],))
                    K.op(pool, lambda: G.tensor_tensor(x1o, ta[:, 0:n], tb[:, 0:n], ALU.subtract), r=(R["ta"], R["tb"]), w=(R["qk"],))
                    K.op(dve, lambda: V.tensor_tensor(ta[:, 0:n], K.pb(b1, n), sinv, ALU.mult), r=(K.bank[b1], R["cs"]), w=(R["ta"],))
                    K.op(dve, lambda: V.tensor_tensor(tb[:, 0:n], K.pb(b2, n), cosv, ALU.mult), r=(K.bank[b2], R["cs"]), w=(R["tb"],))
                    K.op(pool, lambda: G.tensor_tensor(x2o, ta[:, 0:n], tb[:, 0:n], ALU.add), r=(R["ta"], R["tb"]), w=(R["qk"],))
                    for xo in (x1o, x2o):
                        if nt == 0:
                            K.op(dve, lambda: V.tensor_tensor(xo, xo, rt[:, qi, 0:144], ALU.mult), r=(R["qk"], R["rt"]), w=(R["qk"],))
                        else:
                            x3 = xo.rearrange("p (a b) -> p a b", b=64)
                            K.op(dve, lambda: V.tensor_tensor(x3, x3, bcast_mid(rt[:, qi, 144:208], 8), ALU.mult), r=(R["qk"], R["rt"]), w=(R["qk"],))
                for ti, (ty, tc0, tn, nsub, L) in enumerate(token_tiles(nt)):
                    lo = tc0 - c0
                    dd = (dS, dM, dP)[ty]
                    for k in range(8):
                        K.op(pe, lambda: TT.matmul(K.pb(4, 512, 0, tn), hb[:, k, tc0:tc0 + tn], wv_v[:, k, :], start=(k == 0), stop=(k == 7)),
                             r=rl_v + [R_hb[nt]], w=(K.bank[4],), sig=(k == 7))
                    K.op(act, lambda: A.copy(vtok[0:tn, :], K.pb(4, 512, 0, tn)), r=(K.bank[4],), w=(R["vtok"],))
                    for vc in range(4):
                        for k in range(8):
                            K.op(pe, lambda: TT.matmul(K.pb(5, tn, 0, 128, vc * 128), wg_v[:, k, vc * 128:(vc + 1) * 128], hb[:, k, tc0:tc0 + tn], start=(k == 0), stop=(k == 7)),
                                 r=rl_g + [R_hb[nt]], w=(K.bank[5],), sig=(k == 7))
                    K.op(act, lambda: A.activation(sgate[:, :, 0:tn], K.pb(5, 512).rearrange("p (c t) -> p c t", t=128)[:, :, 0:tn], AF.Silu), r=(K.bank[5],), w=(R["sgate"],))
                    for kc in range(2):
                        K.op(pe, lambda: TT.matmul(K.pb(6, tn, 0, tn), qk[:, 2 + kc, lo:lo + tn], qk[:, kc, lo:lo + tn], start=(kc == 0), stop=(kc == 1)),
                             r=(R["qk"],), w=(K.bank[6],), sig=(kc == 1))
                    K.op(dve, lambda: V.tensor_tensor(attsb[0:tn, 0:tn], K.pb(6, tn, 0, tn), cmask(ty, tn), ALU.mult), r=(K.bank[6], R_c), w=(R["attsb"],))
                    for kc in range(2):
                        K.op(pe, lambda: TT.transpose(K.pb(6, 128, 0, tn, 128 + kc * 128), qk[:, 2 + kc, lo:lo + tn], ident), r=(R["qk"], R_c), w=(K.bank[6],), sig=(kc == 1))
                    K.op(act, lambda: A.copy(ktok[0:tn, :], K.pb(6, 256, 0, tn, 128)), r=(K.bank[6],), w=(R["ktok"],))
                    for vc in range(4):
                        K.op(pe, lambda: TT.matmul(K.pb(7, tn, 0, 128, vc * 128), vtok[0:tn, vc * 128:(vc + 1) * 128], attsb[0:tn, 0:tn], start=True, stop=True),
                             r=(R["vtok"], R["attsb"]), w=(K.bank[7],), sig=(vc == 3))
                    K.op(act, lambda: A.copy(osb[:, :, 0:tn], K.pb(7, 512).rearrange("p (c t) -> p c t", t=128)[:, :, 0:tn]), r=(K.bank[7],), w=(R["osb"],))
                    for s_ in range(nsub):
                        if ty == TYPE_S:
                            S0, RS0_ = S0_2[s_ % 2], RS0[s_ % 2]
                            K.dma_in(sp, S0, stret_d[0, s_, h].rearrange("(c p) v -> p c v", p=128), RS0_)
                            Sx, Rx_ = S0, RS0_
                        else:
                            Sx, Rx_ = Sc, R["Sc"]
                        for vc in range(4):
                            for kc in range(2):
                                K.op(pe, lambda: TT.matmul(K.pb(4, L, 0, 128, vc * 64), Sx[:, kc, vc * 128:(vc + 1) * 128], qk[:, kc, lo + s_ * L:lo + (s_ + 1) * L], start=(kc == 0), stop=(kc == 1)),
                                     r=(Rx_, R["qk"]), w=(K.bank[4],), sig=(kc == 1 and vc == 3))
                        K.op(dve, lambda: V.tensor_tensor(osb[:, :, s_ * L:(s_ + 1) * L], osb[:, :, s_ * L:(s_ + 1) * L], K.pb(4, 256).rearrange("p (c t) -> p c t", t=64)[:, :, 0:L], ALU.add),
                             r=(K.bank[4], R["osb"]), w=(R["osb"],))
                        if nsub > 1:
                            K.op(act, lambda: A.mul(vb[0:tn, :], vtok[0:tn, :], cblk(ty, tn, nsub)[:, s_:s_ + 1]), r=(R["vtok"], R_c), w=(R["vb"],))
                            vsrc, rv = vb, R["vb"]
                        else:
                            vsrc, rv = vtok, R["vtok"]
                        for kc in range(2):
                            K.op(pe, lambda: TT.matmul(K.pb(2 + kc, 512), ktok[0:tn, kc * 128:(kc + 1) * 128], vsrc[0:tn, :], start=True, stop=True),
                                 r=(R["ktok"], rv), w=(K.bank[2 + kc],))
                        K.op(dve, lambda: V.tensor_tensor(Sx, Sx, U2, ALU.add), r=(Rx_, K.bank[2], K.bank[3]), w=(Rx_,))
                        K.op(act, lambda: A.mul(Sx, Sx, dd), r=(Rx_,), w=(Rx_,))
                        if ty == TYPE_S:
                            K.dma_out(sp, orets_d[0, s_, h].rearrange("(c p) v -> p c v", p=128), S0, RS0_)
                    K.op(act, lambda: A.activation(osq[:, :, 0:tn], osb[:, :, 0:tn], AF.Square), r=(R["osb"],), w=(R["osq"],))
                    for vc in range(4):
                        K.op(pe, lambda: TT.matmul(K.pb(6, tn), ones_f, osb[:, vc, 0:tn], start=(vc == 0), stop=(vc == 3)), r=(R["osb"], R_small), w=(K.bank[6],), sig=(vc == 3))
                    mean, var, rstd, nmr = (stt[:, q, 0:tn] for q in range(4))
                    K.op(dve, lambda: V.tensor_scalar(mean, K.pb(6, tn), 1.0 / 512, None, ALU.mult), r=(K.bank[6],), w=(R["stt"],))
                    for vc in range(4):
                        K.op(pe, lambda: TT.matmul(K.pb(6, tn), ones_f, osq[:, vc, 0:tn], start=(vc == 0), stop=(vc == 3)), r=(R["osq"], R_small), w=(K.bank[6],), sig=(vc == 3))
                    K.op(dve, lambda: V.tensor_tensor(var, mean, mean, ALU.mult), r=(R["stt"],), w=(R["stt"],))
                    K.op(dve, lambda: V.scalar_tensor_tensor(var, K.pb(6, tn), 1.0 / 512, var, ALU.mult, ALU.subtract), r=(K.bank[6], R["stt"]), w=(R["stt"],))
                    K.op(act, lambda: A.activation(rstd, var, AF.Ln, bias=EPS), r=(R["stt"],), w=(R["stt"],))
                    K.op(act, lambda: A.activation(rstd, rstd, AF.Exp, scale=-0.5), r=(R["stt"],), w=(R["stt"],))
                    K.op(dve, lambda: V.scalar_tensor_tensor(nmr, mean, -1.0, rstd, ALU.mult, ALU.mult), r=(R["stt"],), w=(R["stt"],))
                    K.op(dve, lambda: V.tensor_tensor(osb[:, :, 0:tn], osb[:, :, 0:tn], bcast_mid(rstd, 4), ALU.mult), r=(R["osb"], R["stt"]), w=(R["osb"],))
                    K.op(dve, lambda: V.tensor_tensor(osb[:, :, 0:tn], osb[:, :, 0:tn], bcast_mid(nmr, 4), ALU.add), r=(R["osb"], R["stt"]), w=(R["osb"],))
                    for vc in range(4):
                        ci_ = h * 4 + vc
                        ngc = vecT[:, ci_ % 8, V_RETN + ci_ // 8:V_RETN + ci_ // 8 + 1]
                        K.op(dve, lambda: V.scalar_tensor_tensor(yT[:, vc, 0:tn], osb[:, vc, 0:tn], ngc, sgate[:, vc, 0:tn], ALU.mult, ALU.mult),
                             r=(R["osb"], R["sgate"], R_vt), w=(R["yT"],))
                    for m in range(8):
                        b = m % 4
                        for vc in range(4):
                            K.op(pe, lambda: TT.matmul(K.pb(b, tn), wo_v[:, vc, m * 128:(m + 1) * 128], yT[:, vc, 0:tn], start=(vc == 0), stop=(vc == 3)),
                                 r=rl_o + [R["yT"]], w=(K.bank[b],), sig=(vc == 3))
                        dst = hf[:, m, tc0:tc0 + tn]
                        ps_ = K.pb(b, tn)
                        if first_acc["v"]:
                            K.op(dve, lambda: V.scalar_tensor_tensor(dst, dst, ALPHA, ps_, ALU.mult, ALU.add), r=(K.bank[b], R_hf[m][nt]), w=(R_hf[m][nt],))
                        else:
                            K.op(dve, lambda: V.tensor_tensor(dst, dst, ps_, ALU.add), r=(K.bank[b], R_hf[m][nt]), w=(R_hf[m][nt],))
            first_acc["v"] = False
            K.dma_out(sp, oretp_d[0, h].rearrange("(c p) v -> p c v", p=128), Sc, R["Sc"])
            if h < 3:
                done_unit(("ret", h, "o"))
        K.barrier()
        done_unit(("ret", 3, "o"))

    def mamba():
        K.barrier(touch=(R_wq[2], R_wq[3]))
        scr = Scr(extra=True)

        def buf(n):
            o = scr.get(n)
            return K.arena[:, o:o + n]
        NTT = 18
        dt_t = K.view(scr.get(NTT * 32), [NTT, 32])
        g_t = K.view(scr.get(NTT * 32), [NTT, 32])
        dtw_t = K.view(scr.get(NTT * 32), [NTT, 32])
        tmp32 = buf(32)
        o_pre = scr.get(4 * 520)
        pre = K.view(o_pre, [4, 520])
        y2 = K.view(o_pre, [2, 512])
        rstd = K.arena[:, o_pre + 1024:o_pre + 1536]
        spre = K.view(scr.get(4 * 176), [4, 176])
        mpre = K.view(scr.get(4 * 19), [4, 19])
        halo = K.view(scr.get(4 * 3), [4, 3])
        cv = K.view(scr.get(4 * 512), [4, 512])
        cacc = buf(128)
        sz = K.view(scr.get(1024), [2, 512])
        ctmp = buf(128)
        xtok = buf(256)
        btok = buf(128)
        rhsd4 = K.view(scr.get(512), [4, 128])
        tmpd4_2 = [K.view(scr.get(512), [4, 128]), K.view(scr.get(512), [4, 128])]
        egbc4_2 = [K.view(scr.get(512), [4, 128]), K.view(scr.get(512), [4, 128])]
        tmpd4, egbc4 = tmpd4_2[0], egbc4_2[0]
        attsb4 = K.view(scr.get(512), [4, 128])
        qh4 = K.view(scr.get(512), [4, 128])
        vdt4 = K.view(scr.get(256), [4, 64])
        vpr4 = K.view(scr.get(256), [4, 64])
        rhsd, tmpd, egbc, attsb, qh = rhsd4[:, 0, :], tmpd4[:, 0, :], egbc4[:, 0, :], attsb4[:, 0, :], qh4[:, 0, :]
        decT = tmpd4[:, 1, :]
        vpr, vdt = vpr4[:, 0, :], vdt4[:, 0, :]
        o_vblk = scr.get(1024)
        vblk = K.view(o_vblk, [16, 64])
        vblk4 = K.arena[:, o_vblk:o_vblk + 512]
        ysq = K.view(o_vblk, [2, 512])
        hist_tok = K.arena[0:48, o_vblk:o_vblk + 512]
        o_upr = scr.get(1024)
        upr = K.view(o_upr, [16, 64])
        upr4 = K.view(o_upr, [4, 2, 64])
        c48 = K.arena[0:48, o_upr:o_upr + 512]
        ptok = K.arena[0:3, o_upr + 512:o_upr + 1024]
        S4 = [K.view(scr.get(256), [4, 64]), K.view(scr.get(256), [4, 64])]
        S0 = K.view(scr.get(1024), [16, 64])
        yT = K.view(scr.get(512), [2, 512], BF16)
        names = "vdt dt g dtw tmp32 pre spre mpre halo cv cacc sz hist c48 ctmp xtok btok rhsd tmpd egbc attsb qh vpr vblk upr S0 yT".split()
        R = {nm: Res("mb_" + nm) for nm in names}
        R["ysq"] = R["vblk"]
        R["hist"] = R["vblk"]
        R["c48"] = R["upr"]
        R["ptok"] = R["upr"]
        R["y2"] = R["pre"]
        R["rstd"] = R["pre"]
        R["decT"] = R["tmpd"]
        R_td = [R["tmpd"], Res("mb_tmpd1")]
        R_eg = [R["egbc"], Res("mb_egbc1")]
        RS4 = [Res("mb_S4_0"), Res("mb_S4_1")]
        all_tt = []
        for nt in range(5):
            for tt_ in token_tiles(nt):
                all_tt.append((nt,) + tt_)
        sel = c128[:, C_SEL:C_SEL + 48]
        udt = use_unit(("mam", "dt"))
        wdt = udt[0][1]
        for tix, (nt, ty, tc0, tn, nsub, L) in enumerate(all_tt):
            for k in range(8):
                K.op(pe, lambda: TT.matmul(K.pb(0, 32, 0, tn), hb[:, k, tc0:tc0 + tn], wdt[:, k, :], start=(k == 0), stop=(k == 7)),
                     r=udt[0][0] + [R_hb[nt]], w=(K.bank[0],), sig=(k == 7))
            d_ = dt_t[0:tn, tix, :]
            K.op(dve, lambda: V.tensor_tensor(d_, K.pb(0, 32, 0, tn), mh_bc[0:tn, 0, :], ALU.add), r=(K.bank[0], R_small), w=(R["dt"],))
            K.op(act, lambda: A.activation(d_, d_, AF.Exp), r=(R["dt"],), w=(R["dt"],))
            K.op(act, lambda: A.activation(d_, d_, AF.Ln, bias=1.0), r=(R["dt"],), w=(R["dt"],))
            la = tmp32[0:tn, :]
            K.op(dve, lambda: V.tensor_tensor(la, d_, negA[0:tn, :], ALU.mult), r=(R["dt"], R_small), w=(R["tmp32"],))
            K.op(pe, lambda: TT.matmul(K.pb(1, 32, 0, tn), cmask(ty, tn), la, start=True, stop=True), r=(R["tmp32"], R_c), w=(K.bank[1],))
            K.op(pe, lambda: TT.matmul(K.pb(2, 32, 0, tn), cltri(ty, tn), la, start=True, stop=True), r=(R["tmp32"], R_c), w=(K.bank[2],))
            K.op(act, lambda: A.copy(g_t[0:tn, tix, :], K.pb(1, 32, 0, tn)), r=(K.bank[1],), w=(R["g"],))
            w_ = dtw_t[0:tn, tix, :]
            K.op(dve, lambda: V.tensor_tensor(w_, K.pb(2, 32, 0, tn), g_t[0:tn, tix, :], ALU.subtract), r=(K.bank[2], R["g"]), w=(R["dtw"],))
            K.op(act, lambda: A.activation(w_, w_, AF.Exp), r=(R["dtw"],), w=(R["dtw"],))
            K.op(dve, lambda: V.tensor_tensor(w_, w_, d_, ALU.mult), r=(R["dtw"], R["dt"]), w=(R["dtw"],))
        done_unit(("mam", "dt"))
        hist_rows = stconv_d[0].rearrange("s r c -> (s r) c")
        oconvs_rows = oconvs_d[0].rearrange("s r c -> (s r) c")
        btiles = [(tix_, tt_[1], tt_[3]) for tix_, tt_ in enumerate(all_tt) if tt_[1] != TYPE_S]
        btile_idx = {bt[0]: i_ for i_, bt in enumerate(btiles)}

        def decay_stage(g, bt, slot):
            tix_, ty_, tn_ = bt
            gq = g_t[0:tn_, tix_, 4 * g:4 * g + 4]
            td, eg_ = tmpd4_2[slot], egbc4_2[slot]
            Rtd, Reg = R_td[slot], R_eg[slot]
            rhsn4 = attsb4
            K.op(dve, lambda: V.tensor_tensor(rhsd4[0:tn_, :, 0:tn_], bcast_mid(ident[0:tn_, 0:tn_], 4), bcast_last(gq, tn_), ALU.mult), r=(R["g"], R_c), w=(R["rhsd"],))
            K.op(dve, lambda: V.tensor_scalar(rhsn4[0:tn_, :, 0:tn_], rhsd4[0:tn_, :, 0:tn_], -1.0, None, ALU.mult), r=(R["rhsd"],), w=(R["attsb"],))
            gb4 = K.pb(5, 512).rearrange("p (h t) -> p h t", t=128)
            da4 = K.pb(6, 512).rearrange("p (h t) -> p h t", t=128)
            for hh in range(4):
                K.op(pe, lambda: TT.matmul(K.pb(5, tn_, 0, 128, hh * 128), ones_f[0:tn_, :], rhsd4[0:tn_, hh, 0:tn_], start=True, stop=True), r=(R["rhsd"], R_small), w=(K.bank[5],), sig=(hh == 3))
            for hh in range(4):
                o_ = K.pb(6, tn_, 0, tn_, hh * 128)
                K.op(pe, lambda: TT.matmul(o_, ones_f[0:tn_, 0:tn_], rhsd4[0:tn_, hh, 0:tn_], start=True, stop=False), r=(R["rhsd"], R_small), w=(K.bank[6],), sig=False)
                K.op(pe, lambda: TT.matmul(o_, rhsn4[0:tn_, hh, 0:tn_], ones_f[0:tn_, 0:tn_], start=False, stop=False), r=(R["attsb"], R_small), w=(K.bank[6],), sig=False)
                K.op(pe, lambda: TT.matmul(o_, ident[0:tn_, 0:tn_], cneg(ty_, tn_), start=False, stop=True), r=(R_c,), w=(K.bank[6],), sig=(hh == 3))
            K.op(act, lambda: A.activation(eg_[:, :, 0:tn_], gb4[:, :, 0:tn_], AF.Exp), r=(K.bank[5],), w=(Reg,))
            K.op(act, lambda: A.activation(td[0:tn_, :, 0:tn_], da4[0:tn_, :, 0:tn_], AF.Exp), r=(K.bank[6],), w=(Rtd,))

        for g in range(8):
            uin = use_unit(("mam", g, "in"))
            rl = uin[0][0]
            wz_, wx_, wB_, wC_ = (uin[b][1] for b in range(4))
            uo = use_unit(("mam", g, "o"))
            wo_v, rl_o = uo[0][1], uo[0][0]
            chunks = [(wx_[:, :, 0:128], 2 * g), (wx_[:, :, 128:256], 2 * g + 1), (wB_, 16 + g), (wC_, 24 + g)]
            sidx = 0
            K.op(dve, lambda: V.memset(S4[0], 0.0), w=(RS4[0],))
            for ci, (wv, cch) in enumerate(chunks):
                K.dma_in(sp, hist_tok[:, ci * 128:(ci + 1) * 128], hist_rows[:, cch * 128:(cch + 1) * 128], R["hist"])
            for ci, (wv, cch) in enumerate(chunks):
                K.op(pe, lambda: TT.transpose(K.pb(6, 48), hist_tok[:, ci * 128:(ci + 1) * 128], ident[0:48, 0:48]), r=(R["hist"], R_c), w=(K.bank[6],))
                K.op(act, lambda: A.copy(spre[:, ci, :].rearrange("p (s t) -> p s t", t=11)[:, :, 0:3], K.pb(6, 48).rearrange("p (s r) -> p s r", r=3)),
                     r=(K.bank[6],), w=(R["spre"],))
            K.op(dve, lambda: V.memset(mpre[:, :, 0:3], 0.0), w=(R["mpre"],))
            tix = 0
            for nt, (c0, n) in enumerate(NTILES):
                if nt == 0:
                    pass
                for ci, (wv, cch) in enumerate(chunks):
                    for k in range(8):
                        K.op(pe, lambda: TT.matmul(K.pb(ci, n), wv[:, k, :], hb[:, k, c0:c0 + n], start=(k == 0), stop=(k == 7)),
                             r=rl + [R_hb[nt]], w=(K.bank[ci],), sig=(k == 7))
                for zc in range(2):
                    for k in range(8):
                        K.op(pe, lambda: TT.matmul(K.pb(4 + zc, n), wz_[:, k, zc * 128:(zc + 1) * 128], hb[:, k, c0:c0 + n], start=(k == 0), stop=(k == 7)),
                             r=rl + [R_hb[nt]], w=(K.bank[4 + zc],), sig=(k == 7))
                    K.op(act, lambda: A.activation(sz[:, zc, 0:n], K.pb(4 + zc, n), AF.Silu), r=(K.bank[4 + zc],), w=(R["sz"],))
                for ci, (wv, cch) in enumerate(chunks):
                    cc8, rr = cch % 8, cch // 8
                    wcol = [vecT[:, cc8, V_CW + 4 * t_ + rr:V_CW + 4 * t_ + rr + 1] for t_ in range(4)]
                    bcol = vecT[:, cc8, V_CB + rr:V_CB + rr + 1]
                    if nt == 0:
                        sp3 = spre[:, ci, :].rearrange("p (s t) -> p s t", t=11)
                        K.op(act, lambda: A.copy(sp3[:, :, 3:11], K.pb(ci, 128).rearrange("p (s t) -> p s t", t=8)), r=(K.bank[ci],), w=(R["spre"],))
                        K.op(act, lambda: A.copy(mpre[:, ci, 3:19], K.pb(ci, 16, 0, 128, 128)), r=(K.bank[ci],), w=(R["mpre"],))
                        K.op(dve, lambda: V.tensor_copy(cacc, K.pb(ci, 128)), r=(K.bank[ci],), w=(R["cacc"],))
                        K.op(pe, lambda: TT.transpose(K.pb(6, 128), cacc, ident), r=(R["cacc"], R_c), w=(K.bank[6],))
                        K.op(act, lambda: A.copy(ctmp, K.pb(6, 128)), r=(K.bank[6],), w=(R["ctmp"],))
                        K.op(pe, lambda: TT.matmul(K.pb(6, 128, 0, 48, 128), sel, ctmp, start=True, stop=True), r=(R["ctmp"], R_c), w=(K.bank[6],))
                        K.op(act, lambda: A.copy(c48[:, ci * 128:(ci + 1) * 128], K.pb(6, 128, 0, 48, 128)), r=(K.bank[6],), w=(R["c48"],))
                        co = cv[:, ci, 0:128].rearrange("p (s t) -> p s t", t=8)
                        K.op(act, lambda: A.activation(co, sp3[:, :, 3:11], AF.Identity, bias=bcol, scale=wcol[3]), r=(R["spre"], R_vt), w=(R["cv"],))
                        for t_ in range(3):
                            K.op(dve, lambda: V.scalar_tensor_tensor(co, sp3[:, :, t_:t_ + 8], wcol[t_], co, ALU.mult, ALU.add), r=(R["spre"], R["cv"], R_vt), w=(R["cv"],))
                        cm = cv[:, ci, 128:144]
                        K.op(act, lambda: A.activation(cm, mpre[:, ci, 3:19], AF.Identity, bias=bcol, scale=wcol[3]), r=(R["mpre"], R_vt), w=(R["cv"],))
                        for t_ in range(3):
                            K.op(dve, lambda: V.scalar_tensor_tensor(cm, mpre[:, ci, t_:t_ + 16], wcol[t_], cm, ALU.mult, ALU.add), r=(R["mpre"], R["cv"], R_vt), w=(R["cv"],))
                        K.op(dve, lambda: V.tensor_copy(halo[:, ci, :], mpre[:, ci, 16:19]), r=(R["mpre"],), w=(R["halo"],))
                    else:
                        K.op(dve, lambda: V.tensor_copy(pre[:, ci, 0:3], halo[:, ci, :]), r=(R["halo"],), w=(R["pre"],))
                        K.op(act, lambda: A.copy(pre[:, ci, 3:3 + n], K.pb(ci, n)), r=(K.bank[ci],), w=(R["pre"],))
                        K.op(dve, lambda: V.tensor_copy(halo[:, ci, :], pre[:, ci, n:n + 3]), r=(R["pre"],), w=(R["halo"],))
                        co = cv[:, ci, 0:n]
                        K.op(act, lambda: A.activation(co, pre[:, ci, 3:3 + n], AF.Identity, bias=bcol, scale=wcol[3]), r=(R["pre"], R_vt), w=(R["cv"],))
                        for t_ in range(3):
                            K.op(dve, lambda: V.scalar_tensor_tensor(co, pre[:, ci, t_:t_ + n], wcol[t_], co, ALU.mult, ALU.add), r=(R["pre"], R["cv"], R_vt), w=(R["cv"],))
                        if nt == 4:
                            K.op(pe, lambda: TT.transpose(K.pb(6, 128, 0, 3), pre[:, ci, n:n + 3], ident), r=(R["pre"], R_c), w=(K.bank[6],))
                            K.op(act, lambda: A.copy(ptok[:, ci * 128:(ci + 1) * 128], K.pb(6, 128, 0, 3)), r=(K.bank[6],), w=(R["ptok"],))
                    K.op(act, lambda: A.activation(cv[:, ci, 0:n], cv[:, ci, 0:n], AF.Silu), r=(R["cv"],), w=(R["cv"],))
                if nt == 0:
                    for ci, (wv, cch) in enumerate(chunks):
                        K.dma_out(sp, oconvs_rows[:, cch * 128:(cch + 1) * 128], c48[:, ci * 128:(ci + 1) * 128], R["c48"])
                if nt == 4:
                    for ci, (wv, cch) in enumerate(chunks):
                        K.dma_out(sp, oconvp_d[0, :, cch * 128:(cch + 1) * 128], ptok[:, ci * 128:(ci + 1) * 128], R["ptok"])
                for ti, (ty, tc0, tn, nsub, L) in enumerate(token_tiles(nt)):
                    lo = tc0 - c0
                    for xc in range(2):
                        K.op(pe, lambda: TT.transpose(K.pb(6, 128, 0, tn, xc * 128), cv[:, xc, lo:lo + tn], ident), r=(R["cv"], R_c), w=(K.bank[6],), sig=False)
                    K.op(pe, lambda: TT.transpose(K.pb(6, 128, 0, tn, 256), cv[:, 2, lo:lo + tn], ident), r=(R["cv"], R_c), w=(K.bank[6],))
                    K.op(act, lambda: A.copy(xtok[0:tn, :], K.pb(6, 256, 0, tn)), r=(K.bank[6],), w=(R["xtok"],))
                    K.op(act, lambda: A.copy(btok[0:tn, :], K.pb(6, 128, 0, tn, 256)), r=(K.bank[6],), w=(R["btok"],))
                    K.op(pe, lambda: TT.matmul(K.pb(7, tn, 0, tn), cv[:, 2, lo:lo + tn], cv[:, 3, lo:lo + tn], start=True, stop=True), r=(R["cv"],), w=(K.bank[7],))
                    if ty == TYPE_S:
                        for hh in range(4):
                            hd = 4 * g + hh
                            gcol = g_t[0:tn, tix, hd:hd + 1]
                            K.op(dve, lambda: V.tensor_scalar(rhsd[0:tn, 0:tn], ident[0:tn, 0:tn], gcol, None, ALU.mult), r=(R["g"], R_c), w=(R["rhsd"],))
                            K.op(pe, lambda: TT.matmul(K.pb(5, tn, 0, 128, 0), ones_f[0:tn, :], rhsd[0:tn, 0:tn], start=True, stop=True), r=(R["rhsd"], R_small), w=(K.bank[5],))
                            K.op(act, lambda: A.activation(egbc[:, 0:tn], K.pb(5, tn), AF.Exp), r=(K.bank[5],), w=(R["egbc"],))
                            K.op(dve, lambda: V.scalar_tensor_tensor(tmpd[0:tn, 0:tn], K.pb(5, tn, 0, tn), gcol, cneg(ty, tn), ALU.subtract, ALU.add), r=(K.bank[5], R_c, R["g"]), w=(R["tmpd"],))
                            K.op(act, lambda: A.activation(decT[0:tn, 0:tn], tmpd[0:tn, 0:tn], AF.Exp), r=(R["tmpd"],), w=(R["decT"],))
                            K.op(dve, lambda: V.tensor_tensor(attsb[0:tn, 0:tn], K.pb(7, tn, 0, tn), decT[0:tn, 0:tn], ALU.mult), r=(K.bank[7], R["decT"]), w=(R["attsb"],))
                            K.op(pool, lambda: G.tensor_tensor(qh[:, 0:tn], cv[:, 3, lo:lo + tn], egbc[:, 0:tn], ALU.mult), r=(R["cv"], R["egbc"]), w=(R["qh"],))
                            K.op(dve, lambda: V.tensor_scalar(vpr[0:tn, :], xtok[0:tn, hh * 64:(hh + 1) * 64], dtw_t[0:tn, tix, hd:hd + 1], None, ALU.mult), r=(R["xtok"], R["dtw"]), w=(R["vpr"],))
                            K.op(dve, lambda: V.tensor_scalar(vdt[0:tn, :], xtok[0:tn, hh * 64:(hh + 1) * 64], dt_t[0:tn, tix, hd:hd + 1], None, ALU.mult), r=(R["xtok"], R["dt"]), w=(R["vdt"],))
                            K.op(pool, lambda: G.tensor_tensor(vblk[0:tn, 0:nsub, :], bcast_mid(vpr[0:tn, :], nsub), bcast_last(cblk(ty, tn, nsub), 64), ALU.mult), r=(R["vpr"], R_c), w=(R["vblk"],))
                            for s0 in range(0, nsub, 8):
                                sn = min(8, nsub - s0)
                                K.op(pe, lambda: TT.matmul(K.pb(4, sn * 64), btok[0:tn, :], vblk[0:tn, s0:s0 + sn, :], start=True, stop=True), r=(R["btok"], R["vblk"]), w=(K.bank[4],))
                                K.op(act, lambda: A.copy(upr[:, s0:s0 + sn, :], K.pb(4, sn * 64).rearrange("p (s v) -> p s v", v=64)), r=(K.bank[4],), w=(R["upr"],))
                            po = 64 * (hh % 2)
                            ob = 2 + (hh // 2)
                            K.op(pe, lambda: TT.matmul(K.pb(ob, tn, po, po + 64, lo), vdt[0:tn, :], attsb[0:tn, 0:tn], start=True, stop=False),
                                 r=(R["vdt"], R["attsb"]), w=(K.bank[ob],), sig=False)
                            K.dma_in(sp, S0, stssm_d[0, :, hd].rearrange("s k v -> k s v"), R["S0"])
                            for s_ in range(16):
                                K.op(pe, lambda: TT.matmul(K.pb(ob, L, po, po + 64, lo + s_ * L), S0[:, s_, :], qh[:, s_ * L:(s_ + 1) * L], start=False, stop=(s_ == 15)),
                                     r=(R["S0"], R["qh"]), w=(K.bank[ob],), sig=(s_ == 15))
                            dv_ = egbc[:, L - 1:16 * L:L]
                            K.op(dve, lambda: V.tensor_tensor(S0, S0, bcast_last(dv_, 64), ALU.mult), r=(R["S0"], R["egbc"]), w=(R["S0"],))
                            K.op(dve, lambda: V.tensor_tensor(S0, S0, upr, ALU.add), r=(R["upr"], R["S0"]), w=(R["S0"],))
                            K.dma_out(sp, ossms_d[0, :, hd].rearrange("s k v -> k s v"), S0, R["S0"])
                    else:
                        bi = btile_idx[tix]
                        slot = bi % 2
                        if bi == 0:
                            decay_stage(g, btiles[0], 0)
                        tmpd4, egbc4 = tmpd4_2[slot], egbc4_2[slot]
                        Rtd, Reg = R_td[slot], R_eg[slot]
                        dq = dt_t[0:tn, tix, 4 * g:4 * g + 4]
                        wq4 = dtw_t[0:tn, tix, 4 * g:4 * g + 4]
                        x4 = xtok[0:tn, :].rearrange("p (h v) -> p h v", v=64)
                        K.op(dve, lambda: V.tensor_tensor(vpr4[0:tn], x4, bcast_last(wq4, 64), ALU.mult), r=(R["xtok"], R["dtw"]), w=(R["vpr"],))
                        K.op(pool, lambda: G.tensor_tensor(qh4[:, :, 0:tn], egbc4[:, :, 0:tn], bcast_mid(cv[:, 3, lo:lo + tn], 4), ALU.mult), r=(R["cv"], Reg), w=(R["qh"],))
                        if nsub == 1:
                            urhs, rur = vpr4[0:tn].rearrange("p h v -> p (h v)"), R["vpr"]
                        else:
                            vp = vpr4[0:tn]
                            pat = [list(x_) for x_ in vp.ap]
                            in0 = bass.AP(vp.tensor, vp.offset, [pat[0], pat[1], [0, nsub], pat[2]])
                            bk = cblk(ty, tn, nsub)
                            pb_ = [list(x_) for x_ in bk.ap]
                            in1 = bass.AP(bk.tensor, bk.offset, [pb_[0], [0, 4], pb_[1], [0, 64]])
                            v4 = vblk4[0:tn, :].rearrange("p (h s v) -> p h s v", s=nsub, v=64)
                            K.op(dve, lambda: V.tensor_tensor(v4, in0, in1, ALU.mult), r=(R["vpr"], R_c), w=(R["vblk"],))
                            urhs, rur = vblk4[0:tn, :], R["vblk"]
                        K.op(pe, lambda: TT.matmul(K.pb(4, 4 * nsub * 64), btok[0:tn, :], urhs, start=True, stop=True), r=(R["btok"], rur), w=(K.bank[4],))
                        K.op(act, lambda: A.copy(upr4[:, :, 0:nsub, :], K.pb(4, 4 * nsub * 64).rearrange("p (h s v) -> p h s v", s=nsub, v=64)), r=(K.bank[4],), w=(R["upr"],))
                        K.op(dve, lambda: V.tensor_tensor(attsb4[0:tn, :, 0:tn], tmpd4[0:tn, :, 0:tn], bcast_mid(K.pb(7, tn, 0, tn), 4), ALU.mult), r=(K.bank[7], Rtd), w=(R["attsb"],))
                        K.op(dve, lambda: V.tensor_tensor(vdt4[0:tn], x4, bcast_last(dq, 64), ALU.mult), r=(R["xtok"], R["dt"]), w=(R["vdt"],))
                        for hh in range(4):
                            po = 64 * (hh % 2)
                            ob = 2 + (hh // 2)
                            K.op(pe, lambda: TT.matmul(K.pb(ob, tn, po, po + 64, lo), vdt4[0:tn, hh, :], attsb4[0:tn, hh, 0:tn], start=True, stop=False),
                                 r=(R["vdt"], R["attsb"]), w=(K.bank[ob],), sig=False)
                        for s_ in range(nsub):
                            cur = sidx % 2
                            for hh in range(4):
                                po = 64 * (hh % 2)
                                ob = 2 + (hh // 2)
                                K.op(pe, lambda: TT.matmul(K.pb(ob, L, po, po + 64, lo + s_ * L), S4[cur][:, hh, :], qh4[:, hh, s_ * L:(s_ + 1) * L], start=False, stop=(s_ == nsub - 1)),
                                     r=(RS4[cur], R["qh"]), w=(K.bank[ob],), sig=(hh == 3))
                            dcl = egbc4[:, :, (s_ + 1) * L - 1:(s_ + 1) * L]
                            K.op(dve, lambda: V.tensor_tensor(S4[1 - cur], S4[cur], bcast_last(dcl, 64), ALU.mult), r=(RS4[cur], Reg), w=(RS4[1 - cur],))
                            K.op(dve, lambda: V.tensor_tensor(S4[1 - cur], S4[1 - cur], upr4[:, :, s_, :], ALU.add), r=(R["upr"], RS4[1 - cur]), w=(RS4[1 - cur],))
                            sidx += 1
                        if bi + 1 < len(btiles):
                            decay_stage(g, btiles[bi + 1], (bi + 1) % 2)
                    tix += 1
                for pc in range(2):
                    cch = 2 * g + pc
                    K.op(dve, lambda: V.scalar_tensor_tensor(y2[:, pc, 0:n], cv[:, pc, 0:n], dcol[:, cch:cch + 1], K.pb(2 + pc, n), ALU.mult, ALU.add),
                         r=(R["cv"], K.bank[2 + pc], R_small), w=(R["y2"],))
                    K.op(pool, lambda: G.tensor_tensor(y2[:, pc, 0:n], y2[:, pc, 0:n], sz[:, pc, 0:n], ALU.mult), r=(R["y2"], R["sz"]), w=(R["y2"],))
                    K.op(act, lambda: A.activation(ysq[:, pc, 0:n], y2[:, pc, 0:n], AF.Square), r=(R["y2"],), w=(R["ysq"],))
                for pc in range(2):
                    K.op(pe, lambda: TT.matmul(K.pb(7, n), ones_f, ysq[:, pc, 0:n], start=(pc == 0), stop=(pc == 1)), r=(R["ysq"], R_small), w=(K.bank[7],), sig=(pc == 1))
                K.op(act, lambda: A.activation(rstd[:, 0:n], K.pb(7, n), AF.Ln, bias=EPS, scale=1.0 / 256), r=(K.bank[7],), w=(R["rstd"],))
                K.op(act, lambda: A.activation(rstd[:, 0:n], rstd[:, 0:n], AF.Exp, scale=-0.5), r=(R["rstd"],), w=(R["rstd"],))
                for pc in range(2):
                    cch = 2 * g + pc
                    ngc = vecT[:, cch % 8, V_MN + cch // 8:V_MN + cch // 8 + 1]
                    K.op(dve, lambda: V.scalar_tensor_tensor(yT[:, pc, 0:n], y2[:, pc, 0:n], ngc, rstd[:, 0:n], ALU.mult, ALU.mult), r=(R["y2"], R["rstd"], R_vt), w=(R["yT"],))
                for m in range(8):
                    b = m % 2
                    for pc in range(2):
                        K.op(pe, lambda: TT.matmul(K.pb(b, n), wo_v[:, pc, m * 128:(m + 1) * 128], yT[:, pc, 0:n], start=(pc == 0), stop=(pc == 1)),
                             r=rl_o + [R["yT"]], w=(K.bank[b],), sig=(pc == 1))
                    accumulate(m, nt, K.pb(b, n), K.bank[b])
            first_acc["v"] = False
            fin = sidx % 2
            for hh in range(4):
                K.dma_out(sp, ossmp_d[0, 4 * g + hh], S4[fin][:, hh, :], RS4[fin])
            if g < 7:
                done_unit(("mam", g, "o"))
        K.barrier(touch=(R_wq[2], R_wq[3]))
        done_unit(("mam", 7, "o"))

    rconst = host_consts()[3]
    import os
    dbg = os.environ.get("KDBG", "")
    if "novecs" not in dbg:
        load_vecs()
        K.barrier()
    if "noin" not in dbg:
        load_input()
    K.barrier()
    first_acc["v"] = True
    nsub_done = 0
    for i in range(DEPTH):
        for sub in range(3):
            if stages is not None and nsub_done >= stages:
                break
            if sub == 0:
                ffn(i, 0, ln_closures(i, 0))
            elif sub == 1:
                kind = i % 3
                if kind == 0:
                    hgrn(i // 3)
                elif kind == 1:
                    retention(rconst)
                else:
                    mamba()
                layernorm(i, 1)
            else:
                ffn(i, 1, ln_closures(i, 2))
            nsub_done += 1
    K.barrier()
    if "noout" not in dbg:
        store_output()
    K.finish()
    return nc


_CACHE = {}


def kernel(x_prompt, x_sample, state_hgrn, state_ret, state_ssm, state_conv, meta_tokens, ln_g, ln_b,
           ffn_w_gate, ffn_w_up, ffn_w_down, hg_lb_logits, hg_w_in, hg_norm_g, hg_w_o,
           ret_w_in, ret_norm_g, ret_w_o, m_w_in, m_conv_w, m_conv_b, m_dt_bias, m_a_log, m_d,
           m_norm_g, m_w_o):
    f = lambda a: np.ascontiguousarray(np.asarray(a, dtype=np.float32))
    x_prompt, x_sample, meta_tokens = f(x_prompt), f(x_sample), f(meta_tokens)
    c128, cs, rett, _ = host_consts()
    vecs = np.zeros((64, D), np.float32)
    vecs[V_LNG:V_LNG + 12] = f(ln_g).reshape(12, D)
    vecs[V_LNB:V_LNB + 12] = f(ln_b).reshape(12, D)
    vecs[V_LB:V_LB + 4] = f(hg_lb_logits)
    vecs[V_HGN:V_HGN + 2] = f(hg_norm_g)
    vecs[V_RETN:V_RETN + 2] = f(ret_norm_g).reshape(2, D)
    vecs[V_CW:V_CW + 16] = f(m_conv_w).reshape(16, D)
    vecs[V_CB:V_CB + 4] = f(m_conv_b).reshape(4, D)
    vecs[V_MN:V_MN + 2] = f(m_norm_g).reshape(2, D)
    mhead = np.stack([f(m_dt_bias)[0], f(m_a_log)[0], f(m_d)[0]], axis=0)
    shared = {
        "c128": c128, "cs": cs, "rett": rett, "vecs": vecs, "mhead": mhead,
        "ffn_w_gate": f(ffn_w_gate), "ffn_w_up": f(ffn_w_up), "ffn_w_down": f(ffn_w_down),
        "hg_w_in": f(hg_w_in), "hg_w_o": f(hg_w_o), "ret_w_in": f(ret_w_in), "ret_w_o": f(ret_w_o),
        "m_w_in": f(m_w_in), "m_w_o": f(m_w_o),
    }
    state_hgrn, state_ret, state_ssm, state_conv = f(state_hgrn), f(state_ret), f(state_ssm), f(state_conv)
    in_maps = []
    for c in range(NCORES):
        sl = slice(16 * c, 16 * c + 16)
        xin = np.concatenate([x_sample[sl].reshape(128, D), meta_tokens, x_prompt[c]], axis=0)
        m = dict(shared)
        m["xin"] = np.ascontiguousarray(xin)
        m["st_hg"] = np.ascontiguousarray(state_hgrn[:, sl])
        m["st_ret"] = np.ascontiguousarray(state_ret[:, sl])
        m["st_ssm"] = np.ascontiguousarray(state_ssm[:, sl])
        m["st_conv"] = np.ascontiguousarray(state_conv[:, sl])
        in_maps.append(m)
    if "nc" not in _CACHE:
        _CACHE["nc"] = build_program()
    res = run_bass_kernel_spmd(_CACHE["nc"], in_maps, core_ids=list(range(NCORES)))
    rs = res.results
    y = [r["y"] for r in rs]
    y_prompt = np.stack([yy[144:] for yy in y], axis=0)
    y_sample = np.concatenate([yy[0:128].reshape(16, 8, D) for yy in y], axis=0)
    cat1 = lambda k: np.concatenate([r[k] for r in rs], axis=1)
    stk1 = lambda k: np.stack([r[k] for r in rs], axis=1)
    return (y_prompt.astype(np.float32), y_sample.astype(np.float32),
            stk1("o_hg_p"), cat1("o_hg_s"), stk1("o_ret_p"), cat1("o_ret_s"),
            stk1("o_ssm_p"), cat1("o_ssm_s"), stk1("o_conv_p"), cat1("o_conv_s"))
```

```python
import math
import numpy as np
import ml_dtypes
import concourse.bass as bass
import concourse.mybir as mybir
from concourse.bass_utils import run_bass_kernel_spmd

F32 = mybir.dt.float32
BF16 = mybir.dt.bfloat16
AF = mybir.ActivationFunctionType
ALU = mybir.AluOpType

NCORES = 8
D = 1024
DFF = 2816
DEPTH = 4
T = 2192
NTILES = [(0, 144)] + [(144 + 512 * i, 512) for i in range(4)]
ALPHA = (2 * DEPTH) ** 0.25
EPS = 1e-5
TYPE_S, TYPE_M, TYPE_P = 0, 1, 2


def token_tiles(nt):
    if nt == 0:
        return [(TYPE_S, 0, 128, 16, 8), (TYPE_M, 128, 16, 1, 16)]
    c0 = NTILES[nt][0]
    return [(TYPE_P, c0 + 128 * r, 128, 2, 64) for r in range(4)]


C_ID = 0
C_MASK = 128
C_NEG = C_MASK + 384
C_LTRI = C_NEG + 384
C_BLK = C_LTRI + 384
C_RM = C_BLK + 48
C_SEL = C_RM + 656
C_END = C_SEL + 48

V_LNG, V_LNB, V_LB, V_HGN, V_RETN, V_CW, V_CB, V_MN = 0, 12, 24, 28, 30, 32, 48, 52
V_ROWS = 54


def host_consts():
    c = np.zeros((128, C_END), np.float32)
    c[:, C_ID:C_ID + 128] = np.eye(128, dtype=np.float32)
    j = np.arange(128)[:, None]
    i = np.arange(128)[None, :]
    for ty, L, n in ((TYPE_S, 8, 128), (TYPE_M, 16, 16), (TYPE_P, 64, 128)):
        same = (j // L == i // L) & (j < n) & (i < n)
        m = (same & (j <= i)).astype(np.float32)
        c[:, C_MASK + 128 * ty:C_MASK + 128 * ty + 128] = m
        c[:, C_NEG + 128 * ty:C_NEG + 128 * ty + 128] = (m - 1.0) * 30000.0
        c[:, C_LTRI + 128 * ty:C_LTRI + 128 * ty + 128] = same.astype(np.float32)
        s = np.arange(16)[None, :]
        c[:, C_BLK + 16 * ty:C_BLK + 16 * ty + 16] = ((j // L == s) & (j < n)).astype(np.float32)
    rm = np.ones(656, np.float32)
    rm[0:128:8] = 0.0
    rm[128] = 0.0
    rm[144::64] = 0.0
    c[:, C_RM:C_RM + 656] = rm[None, :]
    for s_ in range(16):
        for r_ in range(3):
            c[s_ * 8 + 5 + r_, C_SEL + s_ * 3 + r_] = 1.0
    half = 128
    inv_freq = (np.float32(10000.0) ** (-np.arange(half, dtype=np.float32) / np.float32(half))).astype(np.float32)
    pos = np.zeros(T, np.float32)
    pos[0:128] = 16384 + (np.arange(128) % 8)
    pos[128:144] = np.arange(16)
    pos[144:] = 16 + np.arange(2048)
    ang = (pos[None, :].astype(np.float32) * inv_freq[:, None]).astype(np.float32)
    cs = np.stack([np.cos(ang), np.sin(ang)], axis=1).astype(np.float32)
    pidx = np.zeros(208, np.float64)
    pidx[0:128] = np.arange(128) % 8
    pidx[128:144] = np.arange(16)
    pidx[144:208] = np.arange(64)
    ret = np.zeros((128, 4, 2, 208), np.float32)
    gam = []
    for h in range(4):
        lg = math.log(1.0 - 2.0 ** (-5.0 - h))
        gam.append(lg)
        g = (pidx + 1.0) * lg
        ret[:, h, 0, :] = np.exp(g)[None, :]
        ret[:, h, 1, :] = (np.exp(-g) / 16.0)[None, :]
    return c, cs, ret, gam


class Res:
    __slots__ = ("name", "w", "rd", "dsem", "dcnt", "excl")

    def __init__(self, name, excl=False):
        self.name = name
        self.excl = excl
        self.w = None
        self.rd = {}
        self.dsem = None
        self.dcnt = 0


class Eng:
    def __init__(self, name, eng, sem):
        self.name, self.eng, self.sem = name, eng, sem
        self.cnt = 0
        self.seen = {}


class Builder:
    def __init__(self):
        nc = bass.Bass("TRN2", target_bir_lowering=False)
        self.nc = nc
        self.pe = Eng("pe", nc.tensor, nc.semaphore("s_pe").__enter__())
        self.dve = Eng("dve", nc.vector, nc.semaphore("s_dve").__enter__())
        self.act = Eng("act", nc.scalar, nc.semaphore("s_act").__enter__())
        self.pool = Eng("pool", nc.gpsimd, nc.semaphore("s_pool").__enter__())
        self.sp = Eng("sp", nc.sync, None)
        self.compute = [self.pe, self.dve, self.act, self.pool]
        self.nsem = 4
        self.out_waits = []
        self.dma_tags = {}
        self.semcache = {}
        self.arena = nc.alloc_sbuf_tensor("arena", [128, 53000], F32)
        self.aoff = 0
        self.psum = nc.alloc_psum_tensor("psum", [128, 4096], F32)
        self.bank = [Res("bank%d" % b, excl=True) for b in range(8)]

    def alloc(self, nfloats):
        o = self.aoff
        self.aoff += nfloats
        assert self.aoff <= 53000, self.aoff
        return o

    def view(self, off, shape, dt=F32):
        n = int(np.prod(shape))
        if dt == F32:
            a = self.arena[:, off:off + n]
        else:
            a = self.arena[:, off:off + (n + 1) // 2].bitcast(BF16)
        if len(shape) == 2:
            return a.rearrange("p (a b) -> p a b", b=shape[1])
        if len(shape) == 3:
            return a.rearrange("p (a b c) -> p a b c", b=shape[1], c=shape[2])
        return a

    def pb(self, b, n=512, p0=0, p1=128, c0=0):
        return self.psum[p0:p1, b * 512 + c0:b * 512 + c0 + n]

    def _wait(self, E, tag):
        key, sem, val = tag
        if E is self.pe and key == "pe":
            return
        if E.seen.get(key, 0) >= val:
            return
        E.eng.wait_ge(sem, val)
        E.seen[key] = val

    def _deps(self, E, reads, writes):
        for r in reads:
            if r.w is not None:
                self._wait(E, r.w)
            if r.excl:
                for key, tag in list(r.rd.items()):
                    if key != E.name:
                        self._wait(E, tag)
        for w in writes:
            if w.w is not None:
                self._wait(E, w.w)
            for tag in list(w.rd.values()):
                self._wait(E, tag)

    def op(self, E, emit, r=(), w=(), sig=True):
        self._deps(E, r, w)
        ins = emit()
        if sig:
            E.cnt += 1
            ins.then_inc(E.sem, 1)
            tag = (E.name, E.sem, E.cnt)
        else:
            tag = (E.name, E.sem, E.cnt + 1)
        for x in r:
            o = x.rd.get(E.name)
            if o is None or o[2] < tag[2]:
                x.rd[E.name] = tag
        for x in w:
            x.w = tag
            x.rd = {}
        return ins

    def _dsem(self, res):
        if res.dsem is None:
            ent = self.semcache.get(res.name)
            if ent is None:
                ent = [self.nc.semaphore("d_" + res.name).__enter__(), 0]
                self.semcache[res.name] = ent
                self.nsem += 1
            res.dsem = ent[0]
            res.dcnt = ent[1]
        return res.dsem

    def dma_in(self, Q, out_ap, in_ap, res, **kw):
        self._deps(Q, (), (res,))
        sem = self._dsem(res)
        Q.eng.dma_start(out=out_ap, in_=in_ap, **kw).then_inc(sem, 16)
        res.dcnt += 16
        self.semcache[res.name][1] = res.dcnt
        res.w = ("d_" + res.name, sem, res.dcnt)
        res.rd = {}
        self.dma_tags[res.w[0]] = res.w

    def dma_out(self, Q, out_ap, in_ap, res, final=True, **kw):
        self._deps(Q, (res,), ())
        sem = self._dsem(res)
        Q.eng.dma_start(out=out_ap, in_=in_ap, **kw).then_inc(sem, 16)
        res.dcnt += 16
        self.semcache[res.name][1] = res.dcnt
        tag = ("d_" + res.name, sem, res.dcnt)
        res.rd["dma"] = tag
        self.out_waits.append(tag)
        self.dma_tags[tag[0]] = tag

    def barrier(self, touch=()):
        for E in self.compute + [self.sp]:
            for F in self.compute:
                if E is not F and F.cnt > 0:
                    self._wait(E, (F.name, F.sem, F.cnt))
            for tag in self.dma_tags.values():
                self._wait(E, tag)
        for res in touch:
            for F in self.compute:
                if F.cnt > 0:
                    res.rd[F.name] = (F.name, F.sem, F.cnt)
            for tag in self.dma_tags.values():
                res.rd[tag[0]] = tag
        self.dma_tags = {}

    def finish(self):
        for E in self.compute:
            if E.cnt > 0:
                self._wait(self.sp, (E.name, E.sem, E.cnt))
        last = {}
        for key, sem, val in self.out_waits:
            if key not in last or last[key][1] < val:
                last[key] = (sem, val)
        for key, (sem, val) in last.items():
            self.sp.eng.wait_ge(sem, val)
            self.act.eng.wait_ge(sem, val)


def bcast_mid(ap, n):
    pat = [list(x) for x in ap.ap]
    return bass.AP(ap.tensor, ap.offset, [pat[0], [0, n]] + pat[1:])


def bcast_last(ap, n):
    pat = [list(x) for x in ap.ap]
    if len(pat) == 3:
        pat = pat[:2]
    return bass.AP(ap.tensor, ap.offset, pat + [[0, n]])


def build_program(stages=None):
    K = Builder()
    nc = K.nc
    pe, dve, act, pool, sp = K.pe, K.dve, K.act, K.pool, K.sp
    TT = nc.tensor
    V = nc.vector
    A = nc.scalar
    G = nc.gpsimd

    def din(name, shape, dt=F32):
        return nc.dram_tensor(name, list(shape), dt, kind="ExternalInput").ap()

    def dout(name, shape):
        return nc.dram_tensor(name, list(shape), F32, kind="ExternalOutput").ap()

    xin = din("xin", [T, D])
    c128_d = din("c128", [128, C_END])
    cs_d = din("cs", [128, 2, T])
    rett_d = din("rett", [128, 4, 2, 208])
    vecs_d = din("vecs", [64, D])
    mhead_d = din("mhead", [3, 32])
    wg_d = din("ffn_w_gate", [DEPTH, 2, D, DFF])
    wu_d = din("ffn_w_up", [DEPTH, 2, D, DFF])
    wd_d = din("ffn_w_down", [DEPTH, 2, DFF, D])
    hgwi_d = din("hg_w_in", [2, D, 4096])
    hgwo_d = din("hg_w_o", [2, D, D])
    rwi_d = din("ret_w_in", [1, D, 6144])
    rwo_d = din("ret_w_o", [1, 2048, D])
    mwi_d = din("m_w_in", [1, D, 6176])
    mwo_d = din("m_w_o", [1, 2048, D])
    sthg_d = din("st_hg", [2, 16, 8, 128, 128])
    stret_d = din("st_ret", [1, 16, 4, 256, 512])
    stssm_d = din("st_ssm", [1, 16, 32, 128, 64])
    stconv_d = din("st_conv", [1, 16, 3, 4096])

    y_d = dout("y", [T, D])
    ohgp_d = dout("o_hg_p", [2, 8, 128, 128])
    ohgs_d = dout("o_hg_s", [2, 16, 8, 128, 128])
    oretp_d = dout("o_ret_p", [1, 4, 256, 512])
    orets_d = dout("o_ret_s", [1, 16, 4, 256, 512])
    ossmp_d = dout("o_ssm_p", [1, 32, 128, 64])
    ossms_d = dout("o_ssm_s", [1, 16, 32, 128, 64])
    oconvp_d = dout("o_conv_p", [1, 3, 4096])
    oconvs_d = dout("o_conv_s", [1, 16, 3, 4096])

    o_hf = K.alloc(8 * T)
    o_hb = K.alloc(8 * T // 2)
    o_c = K.alloc(C_END)
    o_vt = K.alloc(8 * 64)
    o_small = K.alloc(512)
    o_wq = [K.alloc(3072) for _ in range(4)]
    o_scr = K.alloc(0)
    SCR_END = 53000

    hf = K.view(o_hf, [8, T])
    hb = K.view(o_hb, [8, T], BF16)
    c128 = K.arena[:, o_c:o_c + C_END]
    vecT = K.view(o_vt, [8, 64])
    small = K.arena[:, o_small:o_small + 512]
    R_hf = [[Res("hf%d_%d" % (c, n)) for n in range(5)] for c in range(8)]
    R_hb = [Res("hb%d" % n) for n in range(5)]
    R_c = Res("consts")
    R_vt = Res("vecT")
    R_small = Res("small")
    R_wq = [Res("wq%d" % i) for i in range(4)]

    ident = c128[:, C_ID:C_ID + 128]

    def cmask(ty, n):
        return c128[0:n, C_MASK + 128 * ty:C_MASK + 128 * ty + n]

    def cneg(ty, n):
        return c128[0:n, C_NEG + 128 * ty:C_NEG + 128 * ty + n]

    def cltri(ty, n):
        return c128[0:n, C_LTRI + 128 * ty:C_LTRI + 128 * ty + n]

    def cblk(ty, n, nsub):
        return c128[0:n, C_BLK + 16 * ty:C_BLK + 16 * ty + nsub]

    ones_f = small[:, 0:128]
    ones_b = small[:, 128:192].bitcast(BF16)
    lbcol = small[:, 192:208].rearrange("p (j h) -> p j h", h=8)
    omlcol = small[:, 208:224].rearrange("p (j h) -> p j h", h=8)
    nomlcol = small[:, 224:240].rearrange("p (j h) -> p j h", h=8)
    mh_bc = small[:, 240:336].rearrange("p (r h) -> p r h", h=32)
    negA = small[:, 336:368]
    dcol = small[:, 368:384]
    lbtmp = small[:, 384:448]

    K.dma_in(sp, c128, c128_d, R_c)
    K.op(dve, lambda: V.memset(ones_f, 1.0), w=(R_small,))
    K.op(dve, lambda: V.memset(ones_b, 1.0), w=(R_small,))
    import os as _os
    if "nomh" not in _os.environ.get("KDBG", ""):
        K.dma_in(sp, mh_bc, bass.AP(mhead_d.tensor, 0, [[0, 128], [32, 3], [1, 32]]), R_small)
        with nc.allow_non_contiguous_dma(reason="tiny const"):
            K.dma_in(sp, dcol[0:64, :], bass.AP(mhead_d.tensor, 64, [[0, 64], [2, 16]]), R_small)
            K.dma_in(sp, dcol[64:128, :], bass.AP(mhead_d.tensor, 65, [[0, 64], [2, 16]]), R_small)

    class Scr:
        def __init__(self, extra=False):
            self.off = o_scr
            self.end = SCR_END
            self.extra = [o_wq[2], o_wq[2] + 6144] if extra else None

        def get(self, n):
            if self.off + n <= self.end:
                o = self.off
                self.off += n
                return o
            assert self.extra is not None and self.extra[0] + n <= self.extra[1], "scratch overflow"
            o = self.extra[0]
            self.extra[0] += n
            return o

    units = []
    wstate = {"next": 0}

    def emit_loads(upto):
        while wstate["next"] <= min(upto, len(units) - 1):
            u = units[wstate["next"]]
            for res_list, dst, src in u:
                K._deps(pool, (), res_list)
                sem = K._dsem(res_list[0])
                G.dma_start(out=dst, in_=src).then_inc(sem, 16)
                res_list[0].dcnt += 16
                K.semcache[res_list[0].name][1] = res_list[0].dcnt
                tag = ("d_" + res_list[0].name, sem, res_list[0].dcnt)
                for rr in res_list:
                    rr.w = tag
                    rr.rd = {}
            wstate["next"] += 1

    def wview(slot_floats_off, shape):
        return K.view(slot_floats_off, shape, BF16)

    plan = []
    for i in range(DEPTH):
        plan.append(("ffn", i, 0))
        kind = i % 3
        plan.append((("hg", i // 3), ("ret", 0), ("mam", 0))[kind])
        plan.append(("ffn", i, 1))

    ffn_groups = [(4 * g, 4) for g in range(5)] + [(20, 2)]
    unit_index = {}
    big = [0]

    def add_unit(key, entries):
        unit_index[key] = len(units)
        units.append(entries)

    for ph in plan:
        if ph[0] == "ffn":
            _, i, s = ph
            for gi, (j0, gn) in enumerate(ffn_groups):
                slot = big[0] % 2
                big[0] += 1
                base = o_wq[2 * slot]
                rl = [R_wq[2 * slot], R_wq[2 * slot + 1]]
                wg_v = wview(base, [8, 512])
                wu_v = wview(base + 2048, [8, 512])
                wd_v = wview(base + 4096, [4, 1024])
                ent = [
                    (rl, wg_v[:, :, 0:gn * 128], wg_d[i, s].rearrange("(k p) n -> p k n", p=128)[:, :, j0 * 128:(j0 + gn) * 128]),
                    (rl, wu_v[:, :, 0:gn * 128], wu_d[i, s].rearrange("(k p) n -> p k n", p=128)[:, :, j0 * 128:(j0 + gn) * 128]),
                    (rl, wd_v[:, 0:gn, :], wd_d[i, s].rearrange("(j p) n -> p j n", p=128)[:, j0:j0 + gn, :]),
                ]
                add_unit(("ffn", i, s, gi), ent)
            if big[0] % 2 == 1:
                pass
        else:
            msl = [0]

            def mslot():
                s_ = msl[0] % 2
                msl[0] += 1
                return o_wq[s_], [R_wq[s_]]

            if ph[0] == "hg":
                j = ph[1]
                wsrc = hgwi_d[j].rearrange("(k p) n -> p k n", p=128)
                for h in range(8):
                    base, rl = mslot()
                    wi_v = wview(base, [8, 512])
                    wo_v = wview(base + 2048, [1024])
                    ent = []
                    for b4 in range(4):
                        ent.append((rl, wi_v[:, :, b4 * 128:(b4 + 1) * 128], wsrc[:, :, b4 * 1024 + h * 128:b4 * 1024 + (h + 1) * 128]))
                    ent.append((rl, wo_v, hgwo_d[j, h * 128:(h + 1) * 128, :]))
                    add_unit(("hg", j, h), ent)
            elif ph[0] == "ret":
                wsrc = rwi_d[0].rearrange("(k p) n -> p k n", p=128)
                for h in range(4):
                    v_ = wview(o_wq[3], [8, 512])
                    add_unit(("ret", h, "qk"), [
                        ([R_wq[3]], v_[:, :, 0:256], wsrc[:, :, h * 256:(h + 1) * 256]),
                        ([R_wq[3]], v_[:, :, 256:512], wsrc[:, :, 1024 + h * 256:1024 + (h + 1) * 256])])
                    v_ = wview(o_wq[1], [8, 512])
                    add_unit(("ret", h, "v"), [([R_wq[1]], v_, wsrc[:, :, 2048 + h * 512:2048 + (h + 1) * 512])])
                    v_ = wview(o_wq[2], [8, 512])
                    add_unit(("ret", h, "g"), [([R_wq[2]], v_, wsrc[:, :, 4096 + h * 512:4096 + (h + 1) * 512])])
                    v_ = wview(o_wq[0], [4, 1024])
                    add_unit(("ret", h, "o"), [([R_wq[0]], v_, rwo_d[0, h * 512:(h + 1) * 512, :].rearrange("(c p) n -> p c n", p=128))])
            else:
                wsrc = mwi_d[0].rearrange("(k p) n -> p k n", p=128)
                base, rl = mslot()
                v_ = wview(base, [8, 32])
                add_unit(("mam", "dt"), [(rl, v_, wsrc[:, :, 6144:6176])])
                for g in range(8):
                    base, rl = mslot()
                    v_ = wview(base, [8, 768])
                    ent = [
                        (rl, v_[:, :, 0:256], wsrc[:, :, g * 256:(g + 1) * 256]),
                        (rl, v_[:, :, 256:512], wsrc[:, :, 2048 + g * 256:2048 + (g + 1) * 256]),
                        (rl, v_[:, :, 512:640], wsrc[:, :, 4096 + g * 128:4096 + (g + 1) * 128]),
                        (rl, v_[:, :, 640:768], wsrc[:, :, 5120 + g * 128:5120 + (g + 1) * 128]),
                    ]
                    add_unit(("mam", g, "in"), ent)
                    base, rl = mslot()
                    v_ = wview(base, [2, 1024])
                    add_unit(("mam", g, "o"), [(rl, v_, mwo_d[0, g * 256:(g + 1) * 256, :].rearrange("(c p) n -> p c n", p=128))])

    def use_unit(key):
        idx = unit_index[key]
        emit_loads(idx)
        return units[idx]

    def done_unit(key):
        emit_loads(unit_index[key] + 1)

    first_acc = {"v": True}

    def accumulate(m, nt, ps_ap, bank_res):
        c0, n = NTILES[nt]
        dst = hf[:, m, c0:c0 + n]
        if first_acc["v"]:
            K.op(dve, lambda: V.scalar_tensor_tensor(dst, dst, ALPHA, ps_ap, ALU.mult, ALU.add),
                 r=(bank_res, R_hf[m][nt]), w=(R_hf[m][nt],))
        else:
            K.op(dve, lambda: V.tensor_tensor(dst, dst, ps_ap, ALU.add),
                 r=(bank_res, R_hf[m][nt]), w=(R_hf[m][nt],))

    def load_vecs():
        scr = Scr()
        o = scr.get(1024)
        vtok = K.arena[0:64, o:o + 1024]
        R = Res("vecstage")
        K.dma_in(sp, vtok, vecs_d, R)
        for c in range(8):
            ps = K.pb(c % 4, 64)
            K.op(pe, lambda: TT.transpose(ps, vtok[:, c * 128:(c + 1) * 128], ident[0:64, 0:64]),
                 r=(R, R_c), w=(K.bank[c % 4],))
            K.op(act, lambda: A.copy(vecT[:, c, :], ps), r=(K.bank[c % 4],), w=(R_vt,))
        lg = vecT[:, :, V_LB:V_LB + 4]
        mx = lbtmp[:, 0:8]
        ex = lbtmp[:, 8:40].rearrange("p (h d) -> p h d", d=4)
        sm = lbtmp[:, 40:48]
        K.op(dve, lambda: V.tensor_reduce(mx, lg, mybir.AxisListType.X, ALU.max), r=(R_vt,), w=(R_small,))
        K.op(dve, lambda: V.tensor_tensor(ex, lg, bcast_last(mx, 4), ALU.subtract), r=(R_vt, R_small), w=(R_small,))
        K.op(act, lambda: A.activation(ex, ex, AF.Exp), r=(R_small,), w=(R_small,))
        K.op(dve, lambda: V.tensor_reduce(sm, ex, mybir.AxisListType.X, ALU.add), r=(R_small,), w=(R_small,))
        K.op(dve, lambda: V.reciprocal(sm, sm), r=(R_small,), w=(R_small,))
        K.op(dve, lambda: V.memset(lbcol[:, 0, :], 0.0), w=(R_small,))
        t3 = lbtmp[:, 48:56]
        K.op(dve, lambda: V.tensor_tensor(t3, ex[:, :, 1], ex[:, :, 2], ALU.add), r=(R_small,), w=(R_small,))
        K.op(dve, lambda: V.tensor_tensor(t3, t3, ex[:, :, 3], ALU.add), r=(R_small,), w=(R_small,))
        K.op(dve, lambda: V.tensor_tensor(lbcol[:, 1, :], t3, sm, ALU.mult), r=(R_small,), w=(R_small,))
        lb_all = small[:, 192:208]
        K.op(dve, lambda: V.tensor_scalar(small[:, 208:224], lb_all, -1.0, 1.0, ALU.mult, ALU.add), r=(R_small,), w=(R_small,))
        K.op(dve, lambda: V.tensor_scalar(small[:, 224:240], lb_all, 1.0, -1.0, ALU.mult, ALU.add), r=(R_small,), w=(R_small,))
        K.op(act, lambda: A.activation(negA, mh_bc[:, 1, :], AF.Exp), r=(R_small,), w=(R_small,))
        K.op(dve, lambda: V.tensor_scalar(negA, negA, -1.0, None, ALU.mult), r=(R_small,), w=(R_small,))

    def load_input():
        scr = Scr()
        xs = [K.arena[:, o:o + 1024] for o in (scr.get(1024), scr.get(1024))]
        Rx = [Res("xs0"), Res("xs1")]
        tiles = [(0, 128), (128, 16)] + [(144 + 128 * r, 128) for r in range(16)]
        import os
        if "nometa" in os.environ.get("KDBG", ""):
            tiles = [t_ for t_ in tiles if t_[1] == 128]
        if "ntiles" in os.environ.get("KDBG", ""):
            tiles = tiles[:int(os.environ["KNT"])]
        for ti, (c0, n) in enumerate(tiles):
            s = ti % 2
            K.dma_in(sp, xs[s][0:n, :], xin[c0:c0 + n, :], Rx[s])
            nt = 0 if c0 < 144 else 1 + (c0 - 144) // 512
            for half in range(2):
                b = (2 * ti + half) % 4
                ps = K.psum[:, b * 512:b * 512 + 512].rearrange("p (c n) -> p c n", c=4)
                for cc in range(4):
                    c = half * 4 + cc
                    K.op(pe, lambda: TT.transpose(ps[:, cc, 0:n], xs[s][0:n, c * 128:(c + 1) * 128], ident[0:n, 0:n]),
                         r=(Rx[s], R_c), w=(K.bank[b],), sig=(cc == 3))
                wr = [R_hf[half * 4 + cc][nt] for cc in range(4)]
                if "nohf" not in os.environ.get("KDBG", ""):
                    K.op(act, lambda: A.copy(hf[:, half * 4:half * 4 + 4, c0:c0 + n], ps[:, :, 0:n]), r=(K.bank[b],), w=wr)
                if "nohb" not in os.environ.get("KDBG", ""):
                    K.op(dve, lambda: V.tensor_copy(hb[:, half * 4:half * 4 + 4, c0:c0 + n], ps[:, :, 0:n]), r=(K.bank[b],), w=(R_hb[nt],))

    def store_output():
        scr = Scr()
        ys = [K.arena[:, o:o + 1024] for o in (scr.get(1024), scr.get(1024))]
        Ry = [Res("ys0"), Res("ys1")]
        tiles = [(0, 128), (128, 16)] + [(144 + 128 * r, 128) for r in range(16)]
        for ti, (c0, n) in enumerate(tiles):
            s = ti % 2
            nt = 0 if c0 < 144 else 1 + (c0 - 144) // 512
            for half in range(2):
                b = (2 * ti + half) % 4
                ps = K.psum[:, b * 512:b * 512 + 512]
                for cc in range(4):
                    c = half * 4 + cc
                    K.op(pe, lambda: TT.transpose(ps[0:n, cc * 128:(cc + 1) * 128], hf[:, c, c0:c0 + n], ident),
                         r=(R_hf[c][nt], R_c), w=(K.bank[b],), sig=(cc == 3))
                if half == 0:
                    K.op(act, lambda: A.copy(ys[s][0:n, 0:512], ps[0:n, :]), r=(K.bank[b],), w=(Ry[s],))
                else:
                    K.op(dve, lambda: V.tensor_copy(ys[s][0:n, 512:1024], ps[0:n, :]), r=(K.bank[b],), w=(Ry[s],))
            K.dma_out(sp, y_d[c0:c0 + n, :], ys[s][0:n, :], Ry[s])

    LN_OFF = o_scr + 3072
    ln_xb = K.view(LN_OFF, [8, 512], BF16)
    ln_sq = K.view(LN_OFF + 2048, [8, 512], BF16)
    ln_st = [K.view(LN_OFF + 4096, [4, 512]), K.view(LN_OFF + 6144, [4, 512])]
    LN_R = {"xb": Res("ln_xb"), "sq": Res("ln_sq"), "st": [Res("ln_st0"), Res("ln_st1")]}

    def ln_closures(li, lj):
        xb, sq = ln_xb, ln_sq
        Rxb, Rsq = LN_R["xb"], LN_R["sq"]
        row = li * 3 + lj

        def stats(nt):
            c0, n = NTILES[nt]
            st, Rst = ln_st[nt % 2], LN_R["st"][nt % 2]
            hft = hf[:, :, c0:c0 + n]
            rall = [R_hf[c][nt] for c in range(8)]
            K.op(act, lambda: A.copy(xb[:, :, 0:n], hft), r=rall, w=(Rxb,))
            K.op(act, lambda: A.activation(sq[:, :, 0:n], hft, AF.Square), r=rall, w=(Rsq,))
            for c in range(8):
                K.op(pe, lambda: TT.matmul(K.pb(0, n), ones_b, xb[:, c, 0:n], start=(c == 0), stop=(c == 7)),
                     r=(Rxb, R_small), w=(K.bank[0],), sig=(c == 7))
            for c in range(8):
                K.op(pe, lambda: TT.matmul(K.pb(1, n), ones_b, sq[:, c, 0:n], start=(c == 0), stop=(c == 7)),
                     r=(Rsq, R_small), w=(K.bank[1],), sig=(c == 7))
            mean, var, rstd, nmr = (st[:, q, 0:n] for q in range(4))
            K.op(dve, lambda: V.tensor_scalar(mean, K.pb(0, n), 1.0 / D, None, ALU.mult), r=(K.bank[0],), w=(Rst,))
            K.op(dve, lambda: V.tensor_tensor(var, mean, mean, ALU.mult), r=(Rst,), w=(Rst,))
            K.op(dve, lambda: V.scalar_tensor_tensor(var, K.pb(1, n), 1.0 / D, var, ALU.mult, ALU.subtract), r=(K.bank[1], Rst), w=(Rst,))
            K.op(act, lambda: A.activation(rstd, var, AF.Ln, bias=EPS), r=(Rst,), w=(Rst,))
            K.op(act, lambda: A.activation(rstd, rstd, AF.Exp, scale=-0.5), r=(Rst,), w=(Rst,))
            K.op(dve, lambda: V.scalar_tensor_tensor(nmr, mean, -1.0, rstd, ALU.mult, ALU.mult), r=(Rst,), w=(Rst,))

        def apply(nt):
            c0, n = NTILES[nt]
            st, Rst = ln_st[nt % 2], LN_R["st"][nt % 2]
            hft = hf[:, :, c0:c0 + n]
            rall = [R_hf[c][nt] for c in range(8)]
            rstd, nmr = st[:, 2, 0:n], st[:, 3, 0:n]
            K.op(dve, lambda: V.tensor_tensor(hft, hft, bcast_mid(rstd, 8), ALU.mult), r=rall + [Rst], w=rall)
            K.op(dve, lambda: V.tensor_tensor(hft, hft, bcast_mid(nmr, 8), ALU.add), r=rall + [Rst], w=rall)
            for c in range(8):
                K.op(act, lambda: A.activation(hf[:, c, c0:c0 + n], hf[:, c, c0:c0 + n], AF.Identity,
                                               bias=vecT[:, c, V_LNB + row:V_LNB + row + 1],
                                               scale=vecT[:, c, V_LNG + row:V_LNG + row + 1]),
                     r=(R_hf[c][nt], R_vt), w=(R_hf[c][nt],))
            K.op(dve, lambda: V.tensor_copy(hb[:, :, c0:c0 + n], hft), r=rall, w=(R_hb[nt],))

        return stats, apply

    def layernorm(li, lj):
        stats, apply = ln_closures(li, lj)
        stats(0)
        for nt in range(5):
            if nt + 1 < 5:
                stats(nt + 1)
            apply(nt)
        first_acc["v"] = True

    ffn_actb = [K.view(o_scr + o, [4, 512], BF16) for o in (0, 1024)]
    ffn_sgb = [K.arena[:, o_scr + o:o_scr + o + 512] for o in (2048, 2560)]
    FFN_R = {"act": [Res("act0"), Res("act1")], "sg": [Res("sg0"), Res("sg1")]}

    def ffn(i, s, ln=None):
        actb, sgb = ffn_actb, ffn_sgb
        Ract, Rsg = FFN_R["act"], FFN_R["sg"]
        cnt = {"gu": 0, "y": 0, "a": 0}
        for gi, (j0, gn) in enumerate(ffn_groups):
            u = use_unit(("ffn", i, s, gi))
            rl = u[0][0]
            wg_v, wu_v, wd_v = u[0][1], u[1][1], u[2][1]
            for nt, (c0, n) in enumerate(NTILES):
                a = cnt["a"] % 2
                cnt["a"] += 1
                for jj in range(gn):
                    p = cnt["gu"] % 2
                    cnt["gu"] += 1
                    bg, bu = p, 2 + p
                    for k in range(8):
                        K.op(pe, lambda: TT.matmul(K.pb(bg, n), wg_v[:, k, jj * 128:(jj + 1) * 128], hb[:, k, c0:c0 + n], start=(k == 0), stop=(k == 7)),
                             r=rl + [R_hb[nt]], w=(K.bank[bg],), sig=(k == 7))
                    for k in range(8):
                        K.op(pe, lambda: TT.matmul(K.pb(bu, n), wu_v[:, k, jj * 128:(jj + 1) * 128], hb[:, k, c0:c0 + n], start=(k == 0), stop=(k == 7)),
                             r=rl + [R_hb[nt]], w=(K.bank[bu],), sig=(k == 7))
                    K.op(act, lambda: A.activation(sgb[p][:, 0:n], K.pb(bg, n), AF.Silu), r=(K.bank[bg],), w=(Rsg[p],))
                    K.op(dve, lambda: V.scalar_tensor_tensor(actb[a][:, jj, 0:n], K.pb(bu, n), 0.5, sgb[p][:, 0:n], ALU.mult, ALU.mult),
                         r=(K.bank[bu], Rsg[p]), w=(Ract[a],))
                for m in range(8):
                    by = 4 + cnt["y"] % 4
                    cnt["y"] += 1
                    for jj in range(gn):
                        K.op(pe, lambda: TT.matmul(K.pb(by, n), wd_v[:, jj, m * 128:(m + 1) * 128], actb[a][:, jj, 0:n], start=(jj == 0), stop=(jj == gn - 1)),
                             r=rl + [Ract[a]], w=(K.bank[by],), sig=(jj == gn - 1))
                    accumulate(m, nt, K.pb(by, n), K.bank[by])
                if ln is not None and gi == len(ffn_groups) - 1 and nt >= 1:
                    ln[0](nt - 1)
                    ln[1](nt - 1)
            first_acc["v"] = False
            done_unit(("ffn", i, s, gi))
        if ln is not None:
            ln[0](4)
            ln[1](4)
            first_acc["v"] = True

    def hgrn(j):
        K.barrier(touch=(R_wq[2], R_wq[3]))
        scr = Scr(extra=True)

        def buf(n):
            o = scr.get(n)
            return K.arena[:, o:o + n]
        sig_, lf, gcs, eng_ = (buf(512) for _ in range(4))
        setA = []
        for q in range(2):
            setA.append({
                "qh": buf(512), "kt": buf(512), "eg": buf(512), "sgate": buf(512),
                "vtok": K.view(scr.get(512), [4, 128]),
                "R": {nm: Res("hg_%s_%d" % (nm, q)) for nm in ("qh", "kt", "eg", "sgate", "vtok")},
            })
        attsb2 = [buf(128), buf(128)]
        ktok2 = [buf(128), buf(128)]
        vblk_s = K.view(scr.get(2048), [16, 128])
        upr_s = K.view(scr.get(2048), [16, 128])
        vblk_p = [K.view(scr.get(256), [2, 128]), K.view(scr.get(256), [2, 128])]
        upr_p = [K.view(scr.get(256), [2, 128]), K.view(scr.get(256), [2, 128])]
        Scur = [buf(128), buf(128)]
        S0 = K.view(scr.get(2048), [16, 128])
        osq = buf(512)
        t1 = osq
        rstd = buf(512)
        yT = K.view(scr.get(256), [512], BF16)
        R = {nm: Res("hg_" + nm) for nm in "sig lf gcs eng S0 osq rstd yT".split()}
        R["t1"] = R["osq"]
        R2 = {nm: [Res("hg_%s_a" % nm), Res("hg_%s_b" % nm)] for nm in ("attsb", "ktok", "vblk", "upr")}
        RS = [Res("hg_S0_"), Res("hg_S1_")]
        rmt = c128[:, C_RM:C_RM + 656]
        tcount = [0]
        st8 = {"sidx": 0}
        units_h = [None] * 8

        def stageA(h, nt, q):
            c0, n = NTILES[nt]
            SA = setA[q]
            RA = SA["R"]
            if nt == 0:
                units_h[h] = use_unit(("hg", j, h))
                K.dma_in(sp, S0, sthg_d[j, :, h].rearrange("s k v -> k s v"), R["S0"])
            if nt == 1 and h < 7:
                emit_loads(unit_index[("hg", j, h + 1)])
            u = units_h[h]
            rl = u[0][0]
            wq_, wz_, wi_, wgt_ = (u[b][1] for b in range(4))
            lb_c = lbcol[:, j, h:h + 1]
            oml_c = omlcol[:, j, h:h + 1]
            noml_c = nomlcol[:, j, h:h + 1]
            for bi, wv in enumerate((wq_, wz_, wgt_)):
                for k in range(8):
                    K.op(pe, lambda: TT.matmul(K.pb(bi, n), wv[:, k, :], hb[:, k, c0:c0 + n], start=(k == 0), stop=(k == 7)),
                         r=rl + [R_hb[nt]], w=(K.bank[bi],), sig=(k == 7))
            tts = token_tiles(nt)
            for ti, (ty, tc0, tn, nsub, L) in enumerate(tts):
                for k in range(8):
                    K.op(pe, lambda: TT.matmul(K.pb(3, 128, 0, tn, ti * 128), hb[:, k, tc0:tc0 + tn], wi_[:, k, :], start=(k == 0), stop=(k == 7)),
                         r=rl + [R_hb[nt]], w=(K.bank[3],), sig=(k == 7))
            qh, kt, eg, sgate, vtok = SA["qh"], SA["kt"], SA["eg"], SA["sgate"], SA["vtok"]
            K.op(act, lambda: A.activation(qh[:, 0:n], K.pb(0, n), AF.Silu), r=(K.bank[0],), w=(RA["qh"],))
            K.op(act, lambda: A.activation(sgate[:, 0:n], K.pb(2, n), AF.Silu), r=(K.bank[2],), w=(RA["sgate"],))
            K.op(act, lambda: A.activation(sig_[:, 0:n], K.pb(1, n), AF.Sigmoid), r=(K.bank[1],), w=(R["sig"],))
            for ti, (ty, tc0, tn, nsub, L) in enumerate(tts):
                K.op(act, lambda: A.copy(vtok[0:tn, ti, :], K.pb(3, 128, 0, tn, ti * 128)), r=(K.bank[3],), w=(RA["vtok"],))
            K.op(act, lambda: A.activation(lf[:, 0:n], sig_[:, 0:n], AF.Ln, bias=lb_c, scale=oml_c), r=(R["sig"], R_small), w=(R["lf"],))
            K.op(dve, lambda: V.tensor_scalar(kt[:, 0:n], sig_[:, 0:n], noml_c, oml_c, ALU.mult, ALU.add), r=(R["sig"], R_small), w=(RA["kt"],))
            rmo = 0 if nt == 0 else 144
            K.op(dve, lambda: V.tensor_tensor_scan(gcs[:, 0:n], rmt[:, rmo:rmo + n], lf[:, 0:n], 0.0, ALU.mult, ALU.add),
                 r=(R["lf"], R_c), w=(R["gcs"],))
            K.op(act, lambda: A.activation(eg[:, 0:n], gcs[:, 0:n], AF.Exp), r=(R["gcs"],), w=(RA["eg"],))
            K.op(act, lambda: A.activation(eng_[:, 0:n], gcs[:, 0:n], AF.Exp, scale=-1.0), r=(R["gcs"],), w=(R["eng"],))
            K.op(dve, lambda: V.tensor_tensor(qh[:, 0:n], qh[:, 0:n], eg[:, 0:n], ALU.mult), r=(RA["qh"], RA["eg"]), w=(RA["qh"],))
            K.op(pool, lambda: G.tensor_tensor(kt[:, 0:n], kt[:, 0:n], eng_[:, 0:n], ALU.mult), r=(RA["kt"], R["eng"]), w=(RA["kt"],))

        def stageB(h, nt, q):
            c0, n = NTILES[nt]
            SA = setA[q]
            RA = SA["R"]
            qh, kt, eg, sgate, vtok = SA["qh"], SA["kt"], SA["eg"], SA["sgate"], SA["vtok"]
            u = units_h[h]
            rl = u[0][0]
            wo_ = u[4][1]
            ng_c = vecT[:, h, V_HGN + j:V_HGN + j + 1]
            if nt == 0:
                st8["sidx"] = 0
                K.op(dve, lambda: V.memset(Scur[0], 0.0), w=(RS[0],))
            for ti, (ty, tc0, tn, nsub, L) in enumerate(token_tiles(nt)):
                lo = tc0 - c0
                pp = tcount[0] % 2
                tcount[0] += 1
                attsb, ktok = attsb2[pp], ktok2[pp]
                Ratt, Rkt = R2["attsb"][pp], R2["ktok"][pp]
                if ty == TYPE_S:
                    vblk, upr = vblk_s, upr_s
                    Rvb, Rup = R2["vblk"][0], R2["upr"][0]
                else:
                    vblk, upr = vblk_p[pp], upr_p[pp]
                    Rvb, Rup = R2["vblk"][pp], R2["upr"][pp]
                K.op(pe, lambda: TT.matmul(K.pb(4, tn, 0, tn), kt[:, lo:lo + tn], qh[:, lo:lo + tn], start=True, stop=True),
                     r=(RA["kt"], RA["qh"]), w=(K.bank[4],))
                K.op(dve, lambda: V.tensor_tensor(attsb[0:tn, 0:tn], K.pb(4, tn, 0, tn), cmask(ty, tn), ALU.mult), r=(K.bank[4], R_c), w=(Ratt,))
                K.op(pe, lambda: TT.transpose(K.pb(5, 128, 0, tn), kt[:, lo:lo + tn], ident), r=(RA["kt"], R_c), w=(K.bank[5],))
                K.op(act, lambda: A.copy(ktok[0:tn, :], K.pb(5, 128, 0, tn)), r=(K.bank[5],), w=(Rkt,))
                K.op(dve, lambda: V.tensor_tensor(vblk[0:tn, 0:nsub, :], bcast_mid(vtok[0:tn, ti, :], nsub), bcast_last(cblk(ty, tn, nsub), 128), ALU.mult),
                     r=(RA["vtok"], R_c), w=(Rvb,))
                K.op(pe, lambda: TT.matmul(K.pb(7, tn, 0, 128, lo), vtok[0:tn, ti, :], attsb[0:tn, 0:tn], start=True, stop=False),
                     r=(RA["vtok"], Ratt), w=(K.bank[7],), sig=False)
                for s0 in range(0, nsub, 4):
                    sn = min(4, nsub - s0)
                    K.op(pe, lambda: TT.matmul(K.pb(6, sn * 128), ktok[0:tn, :], vblk[0:tn, s0:s0 + sn, :], start=True, stop=True),
                         r=(Rkt, Rvb), w=(K.bank[6],))
                    dview = eg[:, lo + (s0 + 1) * L - 1:lo + (s0 + sn) * L:L] if sn > 1 else eg[:, lo + (s0 + 1) * L - 1:lo + (s0 + 1) * L]
                    K.op(dve, lambda: V.tensor_tensor(upr[:, s0:s0 + sn, :], K.pb(6, sn * 128).rearrange("p (s v) -> p s v", v=128), bcast_last(dview, 128), ALU.mult),
                         r=(K.bank[6], RA["eg"]), w=(Rup,))
                if ty == TYPE_S:
                    for s_ in range(16):
                        K.op(pe, lambda: TT.matmul(K.pb(7, L, 0, 128, lo + s_ * L), S0[:, s_, :], qh[:, lo + s_ * L:lo + (s_ + 1) * L], start=False, stop=(s_ == 15)),
                             r=(R["S0"], RA["qh"]), w=(K.bank[7],), sig=(s_ == 15))
                    dv_ = eg[:, lo + L - 1:lo + 16 * L:L]
                    K.op(dve, lambda: V.tensor_tensor(S0, S0, bcast_last(dv_, 128), ALU.mult), r=(R["S0"], RA["eg"]), w=(R["S0"],))
                    K.op(dve, lambda: V.tensor_tensor(S0, S0, upr, ALU.add), r=(Rup, R["S0"]), w=(R["S0"],))
                    K.dma_out(sp, ohgs_d[j, :, h].rearrange("s k v -> k s v"), S0, R["S0"])
                else:
                    for s_ in range(nsub):
                        cur = st8["sidx"] % 2
                        K.op(pe, lambda: TT.matmul(K.pb(7, L, 0, 128, lo + s_ * L), Scur[cur], qh[:, lo + s_ * L:lo + (s_ + 1) * L], start=False, stop=(s_ == nsub - 1)),
                             r=(RS[cur], RA["qh"]), w=(K.bank[7],), sig=(s_ == nsub - 1))
                        dcl = eg[:, lo + (s_ + 1) * L - 1:lo + (s_ + 1) * L]
                        K.op(dve, lambda: V.scalar_tensor_tensor(Scur[1 - cur], Scur[cur], dcl, upr[:, s_, :], ALU.mult, ALU.add),
                             r=(RS[cur], RA["eg"], Rup), w=(RS[1 - cur],))
                        st8["sidx"] += 1
            K.op(act, lambda: A.activation(osq[:, 0:n], K.pb(7, n), AF.Square), r=(K.bank[7],), w=(R["osq"],))
            K.op(pe, lambda: TT.matmul(K.pb(4, n), ones_f, osq[:, 0:n], start=True, stop=True), r=(R["osq"], R_small), w=(K.bank[4],))
            K.op(act, lambda: A.activation(rstd[:, 0:n], K.pb(4, n), AF.Ln, bias=EPS, scale=1.0 / 128), r=(K.bank[4],), w=(R["rstd"],))
            K.op(act, lambda: A.activation(rstd[:, 0:n], rstd[:, 0:n], AF.Exp, scale=-0.5), r=(R["rstd"],), w=(R["rstd"],))
            K.op(dve, lambda: V.tensor_tensor(t1[:, 0:n], K.pb(7, n), rstd[:, 0:n], ALU.mult), r=(K.bank[7], R["rstd"]), w=(R["t1"],))
            K.op(dve, lambda: V.scalar_tensor_tensor(yT[:, 0:n], t1[:, 0:n], ng_c, sgate[:, 0:n], ALU.mult, ALU.mult), r=(R["t1"], RA["sgate"], R_vt), w=(R["yT"],))
            if nt == 4:
                fin = st8["sidx"] % 2
                K.dma_out(sp, ohgp_d[j, h], Scur[fin], RS[fin])

        def stageC(h, nt, q):
            c0, n = NTILES[nt]
            u = units_h[h]
            rl = u[0][0]
            wo_ = u[4][1]
            for m in range(8):
                b = 4 + (m % 3)
                K.op(pe, lambda: TT.matmul(K.pb(b, n), wo_[:, m * 128:(m + 1) * 128], yT[:, 0:n], start=True, stop=True),
                     r=rl + [R["yT"]], w=(K.bank[b],))
                accumulate(m, nt, K.pb(b, n), K.bank[b])
            if nt == 4:
                first_acc["v"] = False

        steps = [(h, nt) for h in range(8) for nt in range(5)]
        stageA(steps[0][0], steps[0][1], 0)
        for i_, (h, nt) in enumerate(steps):
            stageB(h, nt, i_ % 2)
            if i_ + 1 < len(steps):
                stageA(steps[i_ + 1][0], steps[i_ + 1][1], (i_ + 1) % 2)
            stageC(h, nt, i_ % 2)
        K.barrier(touch=(R_wq[2], R_wq[3]))
        done_unit(("hg", j, 7))

    def retention(rconst):
        K.barrier()
        scr = Scr(extra=False)

        def buf(n):
            o = scr.get(n)
            return K.arena[:, o:o + n]
        cs = K.view(scr.get(1024), [2, 512])
        rt = K.view(scr.get(416), [2, 208])
        qk = K.view(scr.get(2048), [4, 512])
        ta, tb = buf(512), buf(512)
        vtok = buf(512)
        attsb, ktok = buf(128), buf(256)
        vb = buf(512)
        Sc2 = [K.view(scr.get(1024), [2, 512]), K.view(scr.get(1024), [2, 512])]
        S0 = K.view(scr.get(1024), [2, 512])
        blkd = K.view(scr.get(48), [3, 16])
        osb = K.view(scr.get(512), [4, 128])
        osq = K.view(scr.get(512), [4, 128])
        stt = K.view(scr.get(512), [4, 128])
        sgate = K.view(scr.get(512), [4, 128])
        yT = K.view(scr.get(256), [4, 128], BF16)
        names = "cs rt qk ta tb vtok attsb ktok vb Sc S0 osb osq stt sgate yT".split()
        R = {nm: Res("rt_" + nm) for nm in names}
        RSc = [R["Sc"], Res("rt_Scb")]
        R["blkd"] = Res("rt_blkd")
        gam = rconst
        U2 = K.psum[:, 1024:2048].rearrange("p (c v) -> p c v", v=512)
        for h in range(4):
            uqk = use_unit(("ret", h, "qk"))
            uv = use_unit(("ret", h, "v"))
            ug = use_unit(("ret", h, "g"))
            uo = use_unit(("ret", h, "o"))
            wq_v, wk_v = uqk[0][1], uqk[1][1]
            rl_qk = uqk[0][0]
            wv_v, wg_v, wo_v = uv[0][1], ug[0][1], uo[0][1]
            rl_v, rl_g, rl_o = uv[0][0], ug[0][0], uo[0][0]
            K.dma_in(sp, rt, rett_d[:, h], R["rt"])
            dS, dM, dP = math.exp(8 * gam[h]), math.exp(16 * gam[h]), math.exp(64 * gam[h])
            K.op(dve, lambda: V.memset(Sc2[0], 0.0), w=(RSc[0],))
            for ty_, dd_ in ((TYPE_S, dS), (TYPE_M, dM), (TYPE_P, dP)):
                K.op(dve, lambda: V.tensor_scalar(blkd[:, ty_, :], c128[:, C_BLK + 16 * ty_:C_BLK + 16 * ty_ + 16], dd_, None, ALU.mult), r=(R_c,), w=(R["blkd"],))
            sidx = 0
            for nt, (c0, n) in enumerate(NTILES):
                K.dma_in(sp, cs[:, :, 0:n], cs_d[:, :, c0:c0 + n], R["cs"])
                for qi, wv in enumerate((wq_v, wk_v)):
                    for half in range(2):
                        b = qi * 2 + half
                        for k in range(8):
                            K.op(pe, lambda: TT.matmul(K.pb(b, n), wv[:, k, half * 128:(half + 1) * 128], hb[:, k, c0:c0 + n], start=(k == 0), stop=(k == 7)),
                                 r=rl_qk + [R_hb[nt]], w=(K.bank[b],), sig=(k == 7))
                cosv, sinv = cs[:, 0, 0:n], cs[:, 1, 0:n]
                for qi in range(2):
                    b1, b2 = qi * 2, qi * 2 + 1
                    x1o, x2o = qk[:, qi * 2, 0:n], qk[:, qi * 2 + 1, 0:n]
                    K.op(dve, lambda: V.tensor_tensor(ta[:, 0:n], K.pb(b1, n), cosv, ALU.mult), r=(K.bank[b1], R["cs"]), w=(R["ta"],))
                    K.op(dve, lambda: V.tensor_tensor(tb[:, 0:n], K.pb(b2, n), sinv, ALU.mult), r=(K.bank[b2], R["cs"]), w=(R["tb"],))
                    K.op(dve, lambda: V.tensor_tensor(x1o, ta[:, 0:n], tb[:, 0:n], ALU.subtract), r=(R["ta"], R["tb"]), w=(R["qk"],))
                    K.op(dve, lambda: V.tensor_tensor(ta[:, 0:n], K.pb(b1, n), sinv, ALU.mult), r=(K.bank[b1], R["cs"]), w=(R["ta"],))
                    K.op(dve, lambda: V.tensor_tensor(tb[:, 0:n], K.pb(b2, n), cosv, ALU.mult), r=(K.bank[b2], R["cs"]), w=(R["tb"],))
                    K.op(dve, lambda: V.tensor_tensor(x2o, ta[:, 0:n], tb[:, 0:n], ALU.add), r=(R["ta"], R["tb"]), w=(R["qk"],))
                    for xo in (x1o, x2o):
                        if nt == 0:
                            K.op(dve, lambda: V.tensor_tensor(xo, xo, rt[:, qi, 0:144], ALU.mult), r=(R["qk"], R["rt"]), w=(R["qk"],))
                        else:
                            x3 = xo.rearrange("p (a b) -> p a b", b=64)
                            K.op(dve, lambda: V.tensor_tensor(x3, x3, bcast_mid(rt[:, qi, 144:208], 8), ALU.mult), r=(R["qk"], R["rt"]), w=(R["qk"],))
                for ti, (ty, tc0, tn, nsub, L) in enumerate(token_tiles(nt)):
                    lo = tc0 - c0
                    dd = (dS, dM, dP)[ty]
                    for k in range(8):
                        K.op(pe, lambda: TT.matmul(K.pb(4, 512, 0, tn), hb[:, k, tc0:tc0 + tn], wv_v[:, k, :], start=(k == 0), stop=(k == 7)),
                             r=rl_v + [R_hb[nt]], w=(K.bank[4],), sig=(k == 7))
                    K.op(act, lambda: A.copy(vtok[0:tn, :], K.pb(4, 512, 0, tn)), r=(K.bank[4],), w=(R["vtok"],))
                    for vc in range(4):
                        for k in range(8):
                            K.op(pe, lambda: TT.matmul(K.pb(5, tn, 0, 128, vc * 128), wg_v[:, k, vc * 128:(vc + 1) * 128], hb[:, k, tc0:tc0 + tn], start=(k == 0), stop=(k == 7)),
                                 r=rl_g + [R_hb[nt]], w=(K.bank[5],), sig=(k == 7))
                    K.op(act, lambda: A.activation(sgate[:, :, 0:tn], K.pb(5, 512).rearrange("p (c t) -> p c t", t=128)[:, :, 0:tn], AF.Silu), r=(K.bank[5],), w=(R["sgate"],))
                    for kc in range(2):
                        K.op(pe, lambda: TT.matmul(K.pb(6, tn, 0, tn), qk[:, 2 + kc, lo:lo + tn], qk[:, kc, lo:lo + tn], start=(kc == 0), stop=(kc == 1)),
                             r=(R["qk"],), w=(K.bank[6],), sig=(kc == 1))
                    K.op(dve, lambda: V.tensor_tensor(attsb[0:tn, 0:tn], K.pb(6, tn, 0, tn), cmask(ty, tn), ALU.mult), r=(K.bank[6], R_c), w=(R["attsb"],))
                    for kc in range(2):
                        K.op(pe, lambda: TT.transpose(K.pb(6, 128, 0, tn, 128 + kc * 128), qk[:, 2 + kc, lo:lo + tn], ident), r=(R["qk"], R_c), w=(K.bank[6],), sig=(kc == 1))
                    K.op(act, lambda: A.copy(ktok[0:tn, :], K.pb(6, 256, 0, tn, 128)), r=(K.bank[6],), w=(R["ktok"],))
                    for vc in range(4):
                        K.op(pe, lambda: TT.matmul(K.pb(7, tn, 0, 128, vc * 128), vtok[0:tn, vc * 128:(vc + 1) * 128], attsb[0:tn, 0:tn], start=True, stop=True),
                             r=(R["vtok"], R["attsb"]), w=(K.bank[7],), sig=(vc == 3))
                    K.op(act, lambda: A.copy(osb[:, :, 0:tn], K.pb(7, 512).rearrange("p (c t) -> p c t", t=128)[:, :, 0:tn]), r=(K.bank[7],), w=(R["osb"],))
                    for s_ in range(nsub):
                        cur = sidx % 2
                        if ty == TYPE_S:
                            K.dma_in(sp, S0, stret_d[0, s_, h].rearrange("(c p) v -> p c v", p=128), R["S0"])
                            Sx, Rx_ = S0, R["S0"]
                        else:
                            Sx, Rx_ = Sc2[cur], RSc[cur]
                        K.op(act, lambda: A.mul(vb[0:tn, :], vtok[0:tn, :], blkd[0:tn, ty, s_:s_ + 1]), r=(R["vtok"], R["blkd"]), w=(R["vb"],))
                        for kc in range(2):
                            K.op(pe, lambda: TT.matmul(K.pb(2 + kc, 512), ktok[0:tn, kc * 128:(kc + 1) * 128], vb[0:tn, :], start=True, stop=True),
                                 r=(R["ktok"], R["vb"]), w=(K.bank[2 + kc],))
                        for vc in range(4):
                            for kc in range(2):
                                K.op(pe, lambda: TT.matmul(K.pb(4, L, 0, 128, vc * 64), Sx[:, kc, vc * 128:(vc + 1) * 128], qk[:, kc, lo + s_ * L:lo + (s_ + 1) * L], start=(kc == 0), stop=(kc == 1)),
                                     r=(Rx_, R["qk"]), w=(K.bank[4],), sig=(kc == 1 and vc == 3))
                        K.op(dve, lambda: V.tensor_tensor(osb[:, :, s_ * L:(s_ + 1) * L], osb[:, :, s_ * L:(s_ + 1) * L], K.pb(4, 256).rearrange("p (c t) -> p c t", t=64)[:, :, 0:L], ALU.add),
                             r=(K.bank[4], R["osb"]), w=(R["osb"],))
                        if ty == TYPE_S:
                            K.op(dve, lambda: V.scalar_tensor_tensor(S0, S0, dd, U2, ALU.mult, ALU.add), r=(R["S0"], K.bank[2], K.bank[3]), w=(R["S0"],))
                            K.dma_out(sp, orets_d[0, s_, h].rearrange("(c p) v -> p c v", p=128), S0, R["S0"])
                        else:
                            K.op(dve, lambda: V.scalar_tensor_tensor(Sc2[1 - cur], Sc2[cur], dd, U2, ALU.mult, ALU.add), r=(RSc[cur], K.bank[2], K.bank[3]), w=(RSc[1 - cur],))
                            sidx += 1
                    K.op(act, lambda: A.activation(osq[:, :, 0:tn], osb[:, :, 0:tn], AF.Square), r=(R["osb"],), w=(R["osq"],))
                    for vc in range(4):
                        K.op(pe, lambda: TT.matmul(K.pb(6, tn), ones_f, osb[:, vc, 0:tn], start=(vc == 0), stop=(vc == 3)), r=(R["osb"], R_small), w=(K.bank[6],), sig=(vc == 3))
                    mean, var, rstd, nmr = (stt[:, q, 0:tn] for q in range(4))
                    K.op(dve, lambda: V.tensor_scalar(mean, K.pb(6, tn), 1.0 / 512, None, ALU.mult), r=(K.bank[6],), w=(R["stt"],))
                    for vc in range(4):
                        K.op(pe, lambda: TT.matmul(K.pb(6, tn), ones_f, osq[:, vc, 0:tn], start=(vc == 0), stop=(vc == 3)), r=(R["osq"], R_small), w=(K.bank[6],), sig=(vc == 3))
                    K.op(dve, lambda: V.tensor_tensor(var, mean, mean, ALU.mult), r=(R["stt"],), w=(R["stt"],))
                    K.op(dve, lambda: V.scalar_tensor_tensor(var, K.pb(6, tn), 1.0 / 512, var, ALU.mult, ALU.subtract), r=(K.bank[6], R["stt"]), w=(R["stt"],))
                    K.op(act, lambda: A.activation(rstd, var, AF.Ln, bias=EPS), r=(R["stt"],), w=(R["stt"],))
                    K.op(act, lambda: A.activation(rstd, rstd, AF.Exp, scale=-0.5), r=(R["stt"],), w=(R["stt"],))
                    K.op(dve, lambda: V.scalar_tensor_tensor(nmr, mean, -1.0, rstd, ALU.mult, ALU.mult), r=(R["stt"],), w=(R["stt"],))
                    K.op(dve, lambda: V.tensor_tensor(osb[:, :, 0:tn], osb[:, :, 0:tn], bcast_mid(rstd, 4), ALU.mult), r=(R["osb"], R["stt"]), w=(R["osb"],))
                    K.op(dve, lambda: V.tensor_tensor(osb[:, :, 0:tn], osb[:, :, 0:tn], bcast_mid(nmr, 4), ALU.add), r=(R["osb"], R["stt"]), w=(R["osb"],))
                    for vc in range(4):
                        ci_ = h * 4 + vc
                        ngc = vecT[:, ci_ % 8, V_RETN + ci_ // 8:V_RETN + ci_ // 8 + 1]
                        K.op(dve, lambda: V.scalar_tensor_tensor(yT[:, vc, 0:tn], osb[:, vc, 0:tn], ngc, sgate[:, vc, 0:tn], ALU.mult, ALU.mult),
                             r=(R["osb"], R["sgate"], R_vt), w=(R["yT"],))
                    for m in range(8):
                        b = m % 4
                        for vc in range(4):
                            K.op(pe, lambda: TT.matmul(K.pb(b, tn), wo_v[:, vc, m * 128:(m + 1) * 128], yT[:, vc, 0:tn], start=(vc == 0), stop=(vc == 3)),
                                 r=rl_o + [R["yT"]], w=(K.bank[b],), sig=(vc == 3))
                        dst = hf[:, m, tc0:tc0 + tn]
                        ps_ = K.pb(b, tn)
                        if first_acc["v"]:
                            K.op(dve, lambda: V.scalar_tensor_tensor(dst, dst, ALPHA, ps_, ALU.mult, ALU.add), r=(K.bank[b], R_hf[m][nt]), w=(R_hf[m][nt],))
                        else:
                            K.op(dve, lambda: V.tensor_tensor(dst, dst, ps_, ALU.add), r=(K.bank[b], R_hf[m][nt]), w=(R_hf[m][nt],))
            first_acc["v"] = False
            K.dma_out(sp, oretp_d[0, h].rearrange("(c p) v -> p c v", p=128), Sc2[sidx % 2], RSc[sidx % 2])
            if h < 3:
                done_unit(("ret", h, "o"))
        K.barrier()
        done_unit(("ret", 3, "o"))

    def mamba():
        K.barrier(touch=(R_wq[2], R_wq[3]))
        scr = Scr(extra=True)

        def buf(n):
            o = scr.get(n)
            return K.arena[:, o:o + n]
        NTT = 18
        dt_t = K.view(scr.get(NTT * 32), [NTT, 32])
        g_t = K.view(scr.get(NTT * 32), [NTT, 32])
        dtw_t = K.view(scr.get(NTT * 32), [NTT, 32])
        tmp32 = buf(32)
        o_pre = scr.get(4 * 520)
        pre = K.view(o_pre, [4, 520])
        y2 = K.view(o_pre, [2, 512])
        rstd = K.arena[:, o_pre + 1024:o_pre + 1536]
        spre = K.view(scr.get(4 * 176), [4, 176])
        mpre = K.view(scr.get(4 * 19), [4, 19])
        halo = K.view(scr.get(4 * 3), [4, 3])
        cv = K.view(scr.get(4 * 512), [4, 512])
        cacc = buf(128)
        sz = K.view(scr.get(1024), [2, 512])
        ctmp = buf(128)
        xtok = buf(256)
        btok = buf(128)
        rhsd4 = K.view(scr.get(512), [4, 128])
        tmpd4_2 = [K.view(scr.get(512), [4, 128]), K.view(scr.get(512), [4, 128])]
        egbc4_2 = [K.view(scr.get(512), [4, 128]), K.view(scr.get(512), [4, 128])]
        tmpd4, egbc4 = tmpd4_2[0], egbc4_2[0]
        attsb4 = K.view(scr.get(512), [4, 128])
        qh4 = K.view(scr.get(512), [4, 128])
        vdt4 = K.view(scr.get(256), [4, 64])
        vpr4 = K.view(scr.get(256), [4, 64])
        rhsd, tmpd, egbc, attsb, qh = rhsd4[:, 0, :], tmpd4[:, 0, :], egbc4[:, 0, :], attsb4[:, 0, :], qh4[:, 0, :]
        decT = tmpd4[:, 1, :]
        vpr, vdt = vpr4[:, 0, :], vdt4[:, 0, :]
        o_vblk = scr.get(1024)
        vblk = K.view(o_vblk, [16, 64])
        vblk4 = K.arena[:, o_vblk:o_vblk + 512]
        ysq = K.view(o_vblk, [2, 512])
        hist_tok = K.arena[0:48, o_vblk:o_vblk + 512]
        o_upr = scr.get(1024)
        upr = K.view(o_upr, [16, 64])
        upr4 = K.view(o_upr, [4, 2, 64])
        c48 = K.arena[0:48, o_upr:o_upr + 512]
        ptok = K.arena[0:3, o_upr + 512:o_upr + 1024]
        S4 = [K.view(scr.get(256), [4, 64]), K.view(scr.get(256), [4, 64])]
        S0 = K.view(scr.get(1024), [16, 64])
        yT = K.view(scr.get(512), [2, 512], BF16)
        names = "vdt dt g dtw tmp32 pre spre mpre halo cv cacc sz hist c48 ctmp xtok btok rhsd tmpd egbc attsb qh vpr vblk upr S0 yT".split()
        R = {nm: Res("mb_" + nm) for nm in names}
        R["ysq"] = R["vblk"]
        R["hist"] = R["vblk"]
        R["c48"] = R["upr"]
        R["ptok"] = R["upr"]
        R["y2"] = R["pre"]
        R["rstd"] = R["pre"]
        R["decT"] = R["tmpd"]
        R_td = [R["tmpd"], Res("mb_tmpd1")]
        R_eg = [R["egbc"], Res("mb_egbc1")]
        RS4 = [Res("mb_S4_0"), Res("mb_S4_1")]
        all_tt = []
        for nt in range(5):
            for tt_ in token_tiles(nt):
                all_tt.append((nt,) + tt_)
        sel = c128[:, C_SEL:C_SEL + 48]
        udt = use_unit(("mam", "dt"))
        wdt = udt[0][1]
        for tix, (nt, ty, tc0, tn, nsub, L) in enumerate(all_tt):
            for k in range(8):
                K.op(pe, lambda: TT.matmul(K.pb(0, 32, 0, tn), hb[:, k, tc0:tc0 + tn], wdt[:, k, :], start=(k == 0), stop=(k == 7)),
                     r=udt[0][0] + [R_hb[nt]], w=(K.bank[0],), sig=(k == 7))
            d_ = dt_t[0:tn, tix, :]
            K.op(dve, lambda: V.tensor_tensor(d_, K.pb(0, 32, 0, tn), mh_bc[0:tn, 0, :], ALU.add), r=(K.bank[0], R_small), w=(R["dt"],))
            K.op(act, lambda: A.activation(d_, d_, AF.Exp), r=(R["dt"],), w=(R["dt"],))
            K.op(act, lambda: A.activation(d_, d_, AF.Ln, bias=1.0), r=(R["dt"],), w=(R["dt"],))
            la = tmp32[0:tn, :]
            K.op(dve, lambda: V.tensor_tensor(la, d_, negA[0:tn, :], ALU.mult), r=(R["dt"], R_small), w=(R["tmp32"],))
            K.op(pe, lambda: TT.matmul(K.pb(1, 32, 0, tn), cmask(ty, tn), la, start=True, stop=True), r=(R["tmp32"], R_c), w=(K.bank[1],))
            K.op(pe, lambda: TT.matmul(K.pb(2, 32, 0, tn), cltri(ty, tn), la, start=True, stop=True), r=(R["tmp32"], R_c), w=(K.bank[2],))
            K.op(act, lambda: A.copy(g_t[0:tn, tix, :], K.pb(1, 32, 0, tn)), r=(K.bank[1],), w=(R["g"],))
            w_ = dtw_t[0:tn, tix, :]
            K.op(dve, lambda: V.tensor_tensor(w_, K.pb(2, 32, 0, tn), g_t[0:tn, tix, :], ALU.subtract), r=(K.bank[2], R["g"]), w=(R["dtw"],))
            K.op(act, lambda: A.activation(w_, w_, AF.Exp), r=(R["dtw"],), w=(R["dtw"],))
            K.op(dve, lambda: V.tensor_tensor(w_, w_, d_, ALU.mult), r=(R["dtw"], R["dt"]), w=(R["dtw"],))
        done_unit(("mam", "dt"))
        hist_rows = stconv_d[0].rearrange("s r c -> (s r) c")
        oconvs_rows = oconvs_d[0].rearrange("s r c -> (s r) c")
        btiles = [(tix_, tt_[1], tt_[3]) for tix_, tt_ in enumerate(all_tt) if tt_[1] != TYPE_S]
        btile_idx = {bt[0]: i_ for i_, bt in enumerate(btiles)}

        def decay_stage(g, bt, slot):
            tix_, ty_, tn_ = bt
            gq = g_t[0:tn_, tix_, 4 * g:4 * g + 4]
            td, eg_ = tmpd4_2[slot], egbc4_2[slot]
            Rtd, Reg = R_td[slot], R_eg[slot]
            K.op(dve, lambda: V.tensor_tensor(rhsd4[0:tn_, :, 0:tn_], bcast_mid(ident[0:tn_, 0:tn_], 4), bcast_last(gq, tn_), ALU.mult), r=(R["g"], R_c), w=(R["rhsd"],))
            gb4 = K.pb(5, 512).rearrange("p (h t) -> p h t", t=128)
            for hh in range(4):
                K.op(pe, lambda: TT.matmul(K.pb(5, tn_, 0, 128, hh * 128), ones_f[0:tn_, :], rhsd4[0:tn_, hh, 0:tn_], start=True, stop=True), r=(R["rhsd"], R_small), w=(K.bank[5],), sig=(hh == 3))
            K.op(act, lambda: A.activation(eg_[:, :, 0:tn_], gb4[:, :, 0:tn_], AF.Exp), r=(K.bank[5],), w=(Reg,))
            K.op(dve, lambda: V.tensor_tensor(td[0:tn_, :, 0:tn_], gb4[0:tn_, :, 0:tn_], bcast_last(gq, tn_), ALU.subtract), r=(K.bank[5], R["g"]), w=(Rtd,))
            K.op(pool, lambda: G.tensor_tensor(td[0:tn_, :, 0:tn_], td[0:tn_, :, 0:tn_], bcast_mid(cneg(ty_, tn_), 4), ALU.add), r=(Rtd, R_c), w=(Rtd,))
            K.op(act, lambda: A.activation(td[0:tn_, :, 0:tn_], td[0:tn_, :, 0:tn_], AF.Exp), r=(Rtd,), w=(Rtd,))

        for g in range(8):
            uin = use_unit(("mam", g, "in"))
            rl = uin[0][0]
            wz_, wx_, wB_, wC_ = (uin[b][1] for b in range(4))
            uo = use_unit(("mam", g, "o"))
            wo_v, rl_o = uo[0][1], uo[0][0]
            chunks = [(wx_[:, :, 0:128], 2 * g), (wx_[:, :, 128:256], 2 * g + 1), (wB_, 16 + g), (wC_, 24 + g)]
            sidx = 0
            K.op(dve, lambda: V.memset(S4[0], 0.0), w=(RS4[0],))
            for ci, (wv, cch) in enumerate(chunks):
                K.dma_in(sp, hist_tok[:, ci * 128:(ci + 1) * 128], hist_rows[:, cch * 128:(cch + 1) * 128], R["hist"])
            for ci, (wv, cch) in enumerate(chunks):
                K.op(pe, lambda: TT.transpose(K.pb(6, 48), hist_tok[:, ci * 128:(ci + 1) * 128], ident[0:48, 0:48]), r=(R["hist"], R_c), w=(K.bank[6],))
                K.op(act, lambda: A.copy(spre[:, ci, :].rearrange("p (s t) -> p s t", t=11)[:, :, 0:3], K.pb(6, 48).rearrange("p (s r) -> p s r", r=3)),
                     r=(K.bank[6],), w=(R["spre"],))
            K.op(dve, lambda: V.memset(mpre[:, :, 0:3], 0.0), w=(R["mpre"],))
            tix = 0
            for nt, (c0, n) in enumerate(NTILES):
                if nt == 0:
                    pass
                for ci, (wv, cch) in enumerate(chunks):
                    for k in range(8):
                        K.op(pe, lambda: TT.matmul(K.pb(ci, n), wv[:, k, :], hb[:, k, c0:c0 + n], start=(k == 0), stop=(k == 7)),
                             r=rl + [R_hb[nt]], w=(K.bank[ci],), sig=(k == 7))
                for zc in range(2):
                    for k in range(8):
                        K.op(pe, lambda: TT.matmul(K.pb(4 + zc, n), wz_[:, k, zc * 128:(zc + 1) * 128], hb[:, k, c0:c0 + n], start=(k == 0), stop=(k == 7)),
                             r=rl + [R_hb[nt]], w=(K.bank[4 + zc],), sig=(k == 7))
                    K.op(act, lambda: A.activation(sz[:, zc, 0:n], K.pb(4 + zc, n), AF.Silu), r=(K.bank[4 + zc],), w=(R["sz"],))
                for ci, (wv, cch) in enumerate(chunks):
                    cc8, rr = cch % 8, cch // 8
                    wcol = [vecT[:, cc8, V_CW + 4 * t_ + rr:V_CW + 4 * t_ + rr + 1] for t_ in range(4)]
                    bcol = vecT[:, cc8, V_CB + rr:V_CB + rr + 1]
                    if nt == 0:
                        sp3 = spre[:, ci, :].rearrange("p (s t) -> p s t", t=11)
                        K.op(act, lambda: A.copy(sp3[:, :, 3:11], K.pb(ci, 128).rearrange("p (s t) -> p s t", t=8)), r=(K.bank[ci],), w=(R["spre"],))
                        K.op(act, lambda: A.copy(mpre[:, ci, 3:19], K.pb(ci, 16, 0, 128, 128)), r=(K.bank[ci],), w=(R["mpre"],))
                        K.op(dve, lambda: V.tensor_copy(cacc, K.pb(ci, 128)), r=(K.bank[ci],), w=(R["cacc"],))
                        K.op(pe, lambda: TT.transpose(K.pb(6, 128), cacc, ident), r=(R["cacc"], R_c), w=(K.bank[6],))
                        K.op(act, lambda: A.copy(ctmp, K.pb(6, 128)), r=(K.bank[6],), w=(R["ctmp"],))
                        K.op(pe, lambda: TT.matmul(K.pb(6, 128, 0, 48, 128), sel, ctmp, start=True, stop=True), r=(R["ctmp"], R_c), w=(K.bank[6],))
                        K.op(act, lambda: A.copy(c48[:, ci * 128:(ci + 1) * 128], K.pb(6, 128, 0, 48, 128)), r=(K.bank[6],), w=(R["c48"],))
                        co = cv[:, ci, 0:128].rearrange("p (s t) -> p s t", t=8)
                        K.op(act, lambda: A.activation(co, sp3[:, :, 3:11], AF.Identity, bias=bcol, scale=wcol[3]), r=(R["spre"], R_vt), w=(R["cv"],))
                        for t_ in range(3):
                            K.op(dve, lambda: V.scalar_tensor_tensor(co, sp3[:, :, t_:t_ + 8], wcol[t_], co, ALU.mult, ALU.add), r=(R["spre"], R["cv"], R_vt), w=(R["cv"],))
                        cm = cv[:, ci, 128:144]
                        K.op(act, lambda: A.activation(cm, mpre[:, ci, 3:19], AF.Identity, bias=bcol, scale=wcol[3]), r=(R["mpre"], R_vt), w=(R["cv"],))
                        for t_ in range(3):
                            K.op(dve, lambda: V.scalar_tensor_tensor(cm, mpre[:, ci, t_:t_ + 16], wcol[t_], cm, ALU.mult, ALU.add), r=(R["mpre"], R["cv"], R_vt), w=(R["cv"],))
                        K.op(dve, lambda: V.tensor_copy(halo[:, ci, :], mpre[:, ci, 16:19]), r=(R["mpre"],), w=(R["halo"],))
                    else:
                        K.op(dve, lambda: V.tensor_copy(pre[:, ci, 0:3], halo[:, ci, :]), r=(R["halo"],), w=(R["pre"],))
                        K.op(act, lambda: A.copy(pre[:, ci, 3:3 + n], K.pb(ci, n)), r=(K.bank[ci],), w=(R["pre"],))
                        K.op(dve, lambda: V.tensor_copy(halo[:, ci, :], pre[:, ci, n:n + 3]), r=(R["pre"],), w=(R["halo"],))
                        co = cv[:, ci, 0:n]
                        K.op(act, lambda: A.activation(co, pre[:, ci, 3:3 + n], AF.Identity, bias=bcol, scale=wcol[3]), r=(R["pre"], R_vt), w=(R["cv"],))
                        for t_ in range(3):
                            K.op(dve, lambda: V.scalar_tensor_tensor(co, pre[:, ci, t_:t_ + n], wcol[t_], co, ALU.mult, ALU.add), r=(R["pre"], R["cv"], R_vt), w=(R["cv"],))
                        if nt == 4:
                            K.op(pe, lambda: TT.transpose(K.pb(6, 128, 0, 3), pre[:, ci, n:n + 3], ident), r=(R["pre"], R_c), w=(K.bank[6],))
                            K.op(act, lambda: A.copy(ptok[:, ci * 128:(ci + 1) * 128], K.pb(6, 128, 0, 3)), r=(K.bank[6],), w=(R["ptok"],))
                    K.op(act, lambda: A.activation(cv[:, ci, 0:n], cv[:, ci, 0:n], AF.Silu), r=(R["cv"],), w=(R["cv"],))
                if nt == 0:
                    for ci, (wv, cch) in enumerate(chunks):
                        K.dma_out(sp, oconvs_rows[:, cch * 128:(cch + 1) * 128], c48[:, ci * 128:(ci + 1) * 128], R["c48"])
                if nt == 4:
                    for ci, (wv, cch) in enumerate(chunks):
                        K.dma_out(sp, oconvp_d[0, :, cch * 128:(cch + 1) * 128], ptok[:, ci * 128:(ci + 1) * 128], R["ptok"])
                for ti, (ty, tc0, tn, nsub, L) in enumerate(token_tiles(nt)):
                    lo = tc0 - c0
                    for xc in range(2):
                        K.op(pe, lambda: TT.transpose(K.pb(6, 128, 0, tn, xc * 128), cv[:, xc, lo:lo + tn], ident), r=(R["cv"], R_c), w=(K.bank[6],), sig=False)
                    K.op(pe, lambda: TT.transpose(K.pb(6, 128, 0, tn, 256), cv[:, 2, lo:lo + tn], ident), r=(R["cv"], R_c), w=(K.bank[6],))
                    K.op(act, lambda: A.copy(xtok[0:tn, :], K.pb(6, 256, 0, tn)), r=(K.bank[6],), w=(R["xtok"],))
                    K.op(act, lambda: A.copy(btok[0:tn, :], K.pb(6, 128, 0, tn, 256)), r=(K.bank[6],), w=(R["btok"],))
                    K.op(pe, lambda: TT.matmul(K.pb(7, tn, 0, tn), cv[:, 2, lo:lo + tn], cv[:, 3, lo:lo + tn], start=True, stop=True), r=(R["cv"],), w=(K.bank[7],))
                    if ty == TYPE_S:
                        for hh in range(4):
                            hd = 4 * g + hh
                            gcol = g_t[0:tn, tix, hd:hd + 1]
                            K.op(dve, lambda: V.tensor_scalar(rhsd[0:tn, 0:tn], ident[0:tn, 0:tn], gcol, None, ALU.mult), r=(R["g"], R_c), w=(R["rhsd"],))
                            K.op(pe, lambda: TT.matmul(K.pb(5, tn, 0, 128, 0), ones_f[0:tn, :], rhsd[0:tn, 0:tn], start=True, stop=True), r=(R["rhsd"], R_small), w=(K.bank[5],))
                            K.op(act, lambda: A.activation(egbc[:, 0:tn], K.pb(5, tn), AF.Exp), r=(K.bank[5],), w=(R["egbc"],))
                            K.op(dve, lambda: V.scalar_tensor_tensor(tmpd[0:tn, 0:tn], K.pb(5, tn, 0, tn), gcol, cneg(ty, tn), ALU.subtract, ALU.add), r=(K.bank[5], R_c, R["g"]), w=(R["tmpd"],))
                            K.op(act, lambda: A.activation(decT[0:tn, 0:tn], tmpd[0:tn, 0:tn], AF.Exp), r=(R["tmpd"],), w=(R["decT"],))
                            K.op(dve, lambda: V.tensor_tensor(attsb[0:tn, 0:tn], K.pb(7, tn, 0, tn), decT[0:tn, 0:tn], ALU.mult), r=(K.bank[7], R["decT"]), w=(R["attsb"],))
                            K.op(pool, lambda: G.tensor_tensor(qh[:, 0:tn], cv[:, 3, lo:lo + tn], egbc[:, 0:tn], ALU.mult), r=(R["cv"], R["egbc"]), w=(R["qh"],))
                            K.op(dve, lambda: V.tensor_scalar(vpr[0:tn, :], xtok[0:tn, hh * 64:(hh + 1) * 64], dtw_t[0:tn, tix, hd:hd + 1], None, ALU.mult), r=(R["xtok"], R["dtw"]), w=(R["vpr"],))
                            K.op(dve, lambda: V.tensor_scalar(vdt[0:tn, :], xtok[0:tn, hh * 64:(hh + 1) * 64], dt_t[0:tn, tix, hd:hd + 1], None, ALU.mult), r=(R["xtok"], R["dt"]), w=(R["vdt"],))
                            K.op(pool, lambda: G.tensor_tensor(vblk[0:tn, 0:nsub, :], bcast_mid(vpr[0:tn, :], nsub), bcast_last(cblk(ty, tn, nsub), 64), ALU.mult), r=(R["vpr"], R_c), w=(R["vblk"],))
                            for s0 in range(0, nsub, 8):
                                sn = min(8, nsub - s0)
                                K.op(pe, lambda: TT.matmul(K.pb(4, sn * 64), btok[0:tn, :], vblk[0:tn, s0:s0 + sn, :], start=True, stop=True), r=(R["btok"], R["vblk"]), w=(K.bank[4],))
                                K.op(act, lambda: A.copy(upr[:, s0:s0 + sn, :], K.pb(4, sn * 64).rearrange("p (s v) -> p s v", v=64)), r=(K.bank[4],), w=(R["upr"],))
                            po = 64 * (hh % 2)
                            ob = 2 + (hh // 2)
                            K.op(pe, lambda: TT.matmul(K.pb(ob, tn, po, po + 64, lo), vdt[0:tn, :], attsb[0:tn, 0:tn], start=True, stop=False),
                                 r=(R["vdt"], R["attsb"]), w=(K.bank[ob],), sig=False)
                            K.dma_in(sp, S0, stssm_d[0, :, hd].rearrange("s k v -> k s v"), R["S0"])
                            for s_ in range(16):
                                K.op(pe, lambda: TT.matmul(K.pb(ob, L, po, po + 64, lo + s_ * L), S0[:, s_, :], qh[:, s_ * L:(s_ + 1) * L], start=False, stop=(s_ == 15)),
                                     r=(R["S0"], R["qh"]), w=(K.bank[ob],), sig=(s_ == 15))
                            dv_ = egbc[:, L - 1:16 * L:L]
                            K.op(dve, lambda: V.tensor_tensor(S0, S0, bcast_last(dv_, 64), ALU.mult), r=(R["S0"], R["egbc"]), w=(R["S0"],))
                            K.op(dve, lambda: V.tensor_tensor(S0, S0, upr, ALU.add), r=(R["upr"], R["S0"]), w=(R["S0"],))
                            K.dma_out(sp, ossms_d[0, :, hd].rearrange("s k v -> k s v"), S0, R["S0"])
                    else:
                        bi = btile_idx[tix]
                        slot = bi % 2
                        if bi == 0:
                            decay_stage(g, btiles[0], 0)
                        tmpd4, egbc4 = tmpd4_2[slot], egbc4_2[slot]
                        Rtd, Reg = R_td[slot], R_eg[slot]
                        dq = dt_t[0:tn, tix, 4 * g:4 * g + 4]
                        wq4 = dtw_t[0:tn, tix, 4 * g:4 * g + 4]
                        x4 = xtok[0:tn, :].rearrange("p (h v) -> p h v", v=64)
                        K.op(dve, lambda: V.tensor_tensor(vpr4[0:tn], x4, bcast_last(wq4, 64), ALU.mult), r=(R["xtok"], R["dtw"]), w=(R["vpr"],))
                        K.op(dve, lambda: V.tensor_tensor(qh4[:, :, 0:tn], egbc4[:, :, 0:tn], bcast_mid(cv[:, 3, lo:lo + tn], 4), ALU.mult), r=(R["cv"], Reg), w=(R["qh"],))
                        if nsub == 1:
                            urhs, rur = vpr4[0:tn].rearrange("p h v -> p (h v)"), R["vpr"]
                        else:
                            vp = vpr4[0:tn]
                            pat = [list(x_) for x_ in vp.ap]
                            in0 = bass.AP(vp.tensor, vp.offset, [pat[0], pat[1], [0, nsub], pat[2]])
                            bk = cblk(ty, tn, nsub)
                            pb_ = [list(x_) for x_ in bk.ap]
                            in1 = bass.AP(bk.tensor, bk.offset, [pb_[0], [0, 4], pb_[1], [0, 64]])
                            v4 = vblk4[0:tn, :].rearrange("p (h s v) -> p h s v", s=nsub, v=64)
                            K.op(dve, lambda: V.tensor_tensor(v4, in0, in1, ALU.mult), r=(R["vpr"], R_c), w=(R["vblk"],))
                            urhs, rur = vblk4[0:tn, :], R["vblk"]
                        K.op(pe, lambda: TT.matmul(K.pb(4, 4 * nsub * 64), btok[0:tn, :], urhs, start=True, stop=True), r=(R["btok"], rur), w=(K.bank[4],))
                        K.op(act, lambda: A.copy(upr4[:, :, 0:nsub, :], K.pb(4, 4 * nsub * 64).rearrange("p (h s v) -> p h s v", s=nsub, v=64)), r=(K.bank[4],), w=(R["upr"],))
                        K.op(dve, lambda: V.tensor_tensor(attsb4[0:tn, :, 0:tn], tmpd4[0:tn, :, 0:tn], bcast_mid(K.pb(7, tn, 0, tn), 4), ALU.mult), r=(K.bank[7], Rtd), w=(R["attsb"],))
                        K.op(dve, lambda: V.tensor_tensor(vdt4[0:tn], x4, bcast_last(dq, 64), ALU.mult), r=(R["xtok"], R["dt"]), w=(R["vdt"],))
                        for hh in range(4):
                            po = 64 * (hh % 2)
                            ob = 2 + (hh // 2)
                            K.op(pe, lambda: TT.matmul(K.pb(ob, tn, po, po + 64, lo), vdt4[0:tn, hh, :], attsb4[0:tn, hh, 0:tn], start=True, stop=False),
                                 r=(R["vdt"], R["attsb"]), w=(K.bank[ob],), sig=False)
                        for s_ in range(nsub):
                            cur = sidx % 2
                            for hh in range(4):
                                po = 64 * (hh % 2)
                                ob = 2 + (hh // 2)
                                K.op(pe, lambda: TT.matmul(K.pb(ob, L, po, po + 64, lo + s_ * L), S4[cur][:, hh, :], qh4[:, hh, s_ * L:(s_ + 1) * L], start=False, stop=(s_ == nsub - 1)),
                                     r=(RS4[cur], R["qh"]), w=(K.bank[ob],), sig=(hh == 3))
                            dcl = egbc4[:, :, (s_ + 1) * L - 1:(s_ + 1) * L]
                            K.op(dve, lambda: V.tensor_tensor(S4[1 - cur], S4[cur], bcast_last(dcl, 64), ALU.mult), r=(RS4[cur], Reg), w=(RS4[1 - cur],))
                            K.op(dve, lambda: V.tensor_tensor(S4[1 - cur], S4[1 - cur], upr4[:, :, s_, :], ALU.add), r=(R["upr"], RS4[1 - cur]), w=(RS4[1 - cur],))
                            sidx += 1
                        if bi + 1 < len(btiles):
                            decay_stage(g, btiles[bi + 1], (bi + 1) % 2)
                    tix += 1
                for pc in range(2):
                    cch = 2 * g + pc
                    K.op(dve, lambda: V.scalar_tensor_tensor(y2[:, pc, 0:n], cv[:, pc, 0:n], dcol[:, cch:cch + 1], K.pb(2 + pc, n), ALU.mult, ALU.add),
                         r=(R["cv"], K.bank[2 + pc], R_small), w=(R["y2"],))
                    K.op(dve, lambda: V.tensor_tensor(y2[:, pc, 0:n], y2[:, pc, 0:n], sz[:, pc, 0:n], ALU.mult), r=(R["y2"], R["sz"]), w=(R["y2"],))
                    K.op(act, lambda: A.activation(ysq[:, pc, 0:n], y2[:, pc, 0:n], AF.Square), r=(R["y2"],), w=(R["ysq"],))
                for pc in range(2):
                    K.op(pe, lambda: TT.matmul(K.pb(7, n), ones_f, ysq[:, pc, 0:n], start=(pc == 0), stop=(pc == 1)), r=(R["ysq"], R_small), w=(K.bank[7],), sig=(pc == 1))
                K.op(act, lambda: A.activation(rstd[:, 0:n], K.pb(7, n), AF.Ln, bias=EPS, scale=1.0 / 256), r=(K.bank[7],), w=(R["rstd"],))
                K.op(act, lambda: A.activation(rstd[:, 0:n], rstd[:, 0:n], AF.Exp, scale=-0.5), r=(R["rstd"],), w=(R["rstd"],))
                for pc in range(2):
                    cch = 2 * g + pc
                    ngc = vecT[:, cch % 8, V_MN + cch // 8:V_MN + cch // 8 + 1]
                    K.op(dve, lambda: V.scalar_tensor_tensor(yT[:, pc, 0:n], y2[:, pc, 0:n], ngc, rstd[:, 0:n], ALU.mult, ALU.mult), r=(R["y2"], R["rstd"], R_vt), w=(R["yT"],))
                for m in range(8):
                    b = m % 2
                    for pc in range(2):
                        K.op(pe, lambda: TT.matmul(K.pb(b, n), wo_v[:, pc, m * 128:(m + 1) * 128], yT[:, pc, 0:n], start=(pc == 0), stop=(pc == 1)),
                             r=rl_o + [R["yT"]], w=(K.bank[b],), sig=(pc == 1))
                    accumulate(m, nt, K.pb(b, n), K.bank[b])
            first_acc["v"] = False
            fin = sidx % 2
            for hh in range(4):
                K.dma_out(sp, ossmp_d[0, 4 * g + hh], S4[fin][:, hh, :], RS4[fin])
            if g < 7:
                done_unit(("mam", g, "o"))
        K.barrier(touch=(R_wq[2], R_wq[3]))
        done_unit(("mam", 7, "o"))

    rconst = host_consts()[3]
    import os
    dbg = os.environ.get("KDBG", "")
    if "novecs" not in dbg:
        load_vecs()
        K.barrier()
    if "noin" not in dbg:
        load_input()
    K.barrier()
    first_acc["v"] = True
    nsub_done = 0
    for i in range(DEPTH):
        for sub in range(3):
            if stages is not None and nsub_done >= stages:
                break
            if sub == 0:
                ffn(i, 0, ln_closures(i, 0))
            elif sub == 1:
                kind = i % 3
                if kind == 0:
                    hgrn(i // 3)
                elif kind == 1:
                    retention(rconst)
                else:
                    mamba()
                layernorm(i, 1)
            else:
                ffn(i, 1, ln_closures(i, 2))
            nsub_done += 1
    K.barrier()
    if "noout" not in dbg:
        store_output()
    K.finish()
    return nc


_CACHE = {}


def kernel(x_prompt, x_sample, state_hgrn, state_ret, state_ssm, state_conv, meta_tokens, ln_g, ln_b,
           ffn_w_gate, ffn_w_up, ffn_w_down, hg_lb_logits, hg_w_in, hg_norm_g, hg_w_o,
           ret_w_in, ret_norm_g, ret_w_o, m_w_in, m_conv_w, m_conv_b, m_dt_bias, m_a_log, m_d,
           m_norm_g, m_w_o):
    f = lambda a: np.ascontiguousarray(np.asarray(a, dtype=np.float32))
    x_prompt, x_sample, meta_tokens = f(x_prompt), f(x_sample), f(meta_tokens)
    c128, cs, rett, _ = host_consts()
    vecs = np.zeros((64, D), np.float32)
    vecs[V_LNG:V_LNG + 12] = f(ln_g).reshape(12, D)
    vecs[V_LNB:V_LNB + 12] = f(ln_b).reshape(12, D)
    vecs[V_LB:V_LB + 4] = f(hg_lb_logits)
    vecs[V_HGN:V_HGN + 2] = f(hg_norm_g)
    vecs[V_RETN:V_RETN + 2] = f(ret_norm_g).reshape(2, D)
    vecs[V_CW:V_CW + 16] = f(m_conv_w).reshape(16, D)
    vecs[V_CB:V_CB + 4] = f(m_conv_b).reshape(4, D)
    vecs[V_MN:V_MN + 2] = f(m_norm_g).reshape(2, D)
    mhead = np.stack([f(m_dt_bias)[0], f(m_a_log)[0], f(m_d)[0]], axis=0)
    shared = {
        "c128": c128, "cs": cs, "rett": rett, "vecs": vecs, "mhead": mhead,
        "ffn_w_gate": f(ffn_w_gate), "ffn_w_up": f(ffn_w_up), "ffn_w_down": f(ffn_w_down),
        "hg_w_in": f(hg_w_in), "hg_w_o": f(hg_w_o), "ret_w_in": f(ret_w_in), "ret_w_o": f(ret_w_o),
        "m_w_in": f(m_w_in), "m_w_o": f(m_w_o),
    }
    state_hgrn, state_ret, state_ssm, state_conv = f(state_hgrn), f(state_ret), f(state_ssm), f(state_conv)
    in_maps = []
    for c in range(NCORES):
        sl = slice(16 * c, 16 * c + 16)
        xin = np.concatenate([x_sample[sl].reshape(128, D), meta_tokens, x_prompt[c]], axis=0)
        m = dict(shared)
        m["xin"] = np.ascontiguousarray(xin)
        m["st_hg"] = np.ascontiguousarray(state_hgrn[:, sl])
        m["st_ret"] = np.ascontiguousarray(state_ret[:, sl])
        m["st_ssm"] = np.ascontiguousarray(state_ssm[:, sl])
        m["st_conv"] = np.ascontiguousarray(state_conv[:, sl])
        in_maps.append(m)
    if "nc" not in _CACHE:
        _CACHE["nc"] = build_program()
    res = run_bass_kernel_spmd(_CACHE["nc"], in_maps, core_ids=list(range(NCORES)))
    rs = res.results
    y = [r["y"] for r in rs]
    y_prompt = np.stack([yy[144:] for yy in y], axis=0)
    y_sample = np.concatenate([yy[0:128].reshape(16, 8, D) for yy in y], axis=0)
    cat1 = lambda k: np.concatenate([r[k] for r in rs], axis=1)
    stk1 = lambda k: np.stack([r[k] for r in rs], axis=1)
    return (y_prompt.astype(np.float32), y_sample.astype(np.float32),
            stk1("o_hg_p"), cat1("o_hg_s"), stk1("o_ret_p"), cat1("o_ret_s"),
            stk1("o_ssm_p"), cat1("o_ssm_s"), stk1("o_conv_p"), cat1("o_conv_s"))
```
